# Optimizing a Trainium2 kernel written in Bass

```python
import math
import jax, jax.numpy as jnp
from jax import lax
import numpy as np

D_MODEL = 1024
BATCH = 16
SEQ = 2048
DEPTH = 1

PLE_DIM = 256
HEAD_DIM = 64
ATTN_WIDTH = D_MODEL // 2
ATTN_HEADS = ATTN_WIDTH // HEAD_DIM
SSM_WIDTH = D_MODEL - ATTN_WIDTH
SSM_GROUP = 16
SSM_GROUPS = SSM_WIDTH // SSM_GROUP
SSM_STATE = 64
MIX_WIDTH = ATTN_WIDTH + SSM_WIDTH
IN_WIDTH = 4 * ATTN_WIDTH + 2 * SSM_WIDTH
DILATED_CONFIGS = ((128, 1), (512, 4), (2048, 16))
BLOCK_Q = 128
EPS = 1e-6
DT_MIN = 1e-3
DT_MAX = 1e-1

kernel_name = "hymba_s5_longnet_hybrid"


def rms_norm(x, gain):
    xf = x.astype(jnp.float32)
    y = xf * lax.rsqrt(jnp.mean(xf * xf, axis=-1, keepdims=True) + EPS) * gain.astype(jnp.float32)
    return y.astype(x.dtype)


def banded_causal_attention(q, k, v, w):
    n, l, dh = q.shape
    nb = -(-l // BLOCK_Q)
    lp = nb * BLOCK_Q
    nk = BLOCK_Q + w
    qb = jnp.pad(q, ((0, 0), (0, lp - l), (0, 0))).reshape(n, nb, BLOCK_Q, dh)
    kp = jnp.pad(k, ((0, 0), (w, lp - l), (0, 0)))
    vp = jnp.pad(v, ((0, 0), (w, lp - l), (0, 0)))
    block_start = jnp.arange(nb) * BLOCK_Q
    idx = block_start[:, None] + jnp.arange(nk)[None, :]
    kb = kp[:, idx]
    vb = vp[:, idx]
    s = jnp.einsum('nbqd,nbkd->nbqk', qb, kb)
    qi = jnp.arange(BLOCK_Q)[:, None]
    kj = jnp.arange(nk)[None, :]
    key_pos = block_start[:, None, None] + kj[None] - w
    valid = (kj >= qi)[None] & (kj <= qi + w)[None] & (key_pos >= 0)
    s = jnp.where(valid[None], s, -jnp.inf)
    m = jnp.max(s, axis=-1, keepdims=True)
    e = jnp.exp(s - m)
    den = jnp.sum(e, axis=-1)
    o = jnp.einsum('nbqk,nbkd->nbqd', e, vb) / den[..., None]
    lse = m[..., 0] + jnp.log(den)
    return o.reshape(n, lp, dh)[:, :l], lse.reshape(n, lp)[:, :l]


def dilated_window_attention(q, k, v, window, dilation):
    b, s, h, dh = q.shape
    l = s // dilation
    w = window // dilation

    def to_classes(t):
        return t.reshape(b, l, dilation, h, dh).transpose(0, 2, 3, 1, 4).reshape(b * dilation * h, l, dh)

    o, lse = banded_causal_attention(to_classes(q), to_classes(k), to_classes(v), w)
    o = o.reshape(b, dilation, h, l, dh).transpose(0, 3, 1, 2, 4).reshape(b, s, h, dh)
    lse = lse.reshape(b, dilation, h, l).transpose(0, 3, 1, 2).reshape(b, s, h)
    return o, lse


def dilated_mixture_attention(q, k, v, q_gain, k_gain):
    b, s, _ = q.shape

    def heads(t):
        return t.reshape(b, s, ATTN_HEADS, HEAD_DIM).astype(jnp.float32)

    qh = rms_norm(heads(q), q_gain) * (HEAD_DIM ** -0.5)
    kh = rms_norm(heads(k), k_gain)
    vh = heads(v)
    outs, lses = [], []
    for window, dilation in DILATED_CONFIGS:
        o, lse = dilated_window_attention(qh, kh, vh, window, dilation)
        outs.append(o)
        lses.append(lse)
    wts = jax.nn.softmax(jnp.stack(lses), axis=0)
    o = jnp.sum(wts[..., None] * jnp.stack(outs), axis=0)
    return o.reshape(b, s, ATTN_WIDTH).astype(q.dtype)


def s5_glu(u, lam_re, lam_im, log_dt, b_re, b_im, c_re, c_im, d_skip, w_glu, b_glu):
    f32 = jnp.float32
    bsz, s, _ = u.shape
    uf = u.astype(f32)
    ug = uf.reshape(bsz, s, SSM_GROUPS, SSM_GROUP)
    lam_re = lam_re.astype(f32); lam_im = lam_im.astype(f32)
    dt = jnp.exp(log_dt.astype(f32))[:, None]
    mag = jnp.exp(lam_re * dt)
    a_re = mag * jnp.cos(lam_im * dt)
    a_im = mag * jnp.sin(lam_im * dt)
    den = lam_re * lam_re + lam_im * lam_im
    num_re = a_re - 1.0
    coef_re = (num_re * lam_re + a_im * lam_im) / den
    coef_im = (a_im * lam_re - num_re * lam_im) / den
    b_re = b_re.astype(f32); b_im = b_im.astype(f32)
    bb_re = coef_re[..., None] * b_re - coef_im[..., None] * b_im
    bb_im = coef_re[..., None] * b_im + coef_im[..., None] * b_re
    bu_re = jnp.einsum('bsgc,gnc->bsgn', ug, bb_re)
    bu_im = jnp.einsum('bsgc,gnc->bsgn', ug, bb_im)
    ar = jnp.broadcast_to(a_re, bu_re.shape)
    ai = jnp.broadcast_to(a_im, bu_re.shape)

    def combine(left, right):
        ar1, ai1, br1, bi1 = left
        ar2, ai2, br2, bi2 = right
        return (ar2 * ar1 - ai2 * ai1,
                ar2 * ai1 + ai2 * ar1,
                ar2 * br1 - ai2 * bi1 + br2,
                ar2 * bi1 + ai2 * br1 + bi2)

    _, _, xr, xi = lax.associative_scan(combine, (ar, ai, bu_re, bu_im), axis=1)
    y = (jnp.einsum('bsgn,gcn->bsgc', xr, c_re.astype(f32))
         - jnp.einsum('bsgn,gcn->bsgc', xi, c_im.astype(f32))).reshape(bsz, s, SSM_WIDTH)
    y = y + d_skip.astype(f32) * uf
    yg = jax.nn.gelu(y, approximate=False)
    out = yg * jax.nn.sigmoid(yg @ w_glu.astype(f32) + b_glu.astype(f32))
    return out.astype(u.dtype)


def setup_inputs(seed: int = 0) -> dict:
    key = jax.random.key(seed)
    ks = jax.random.split(key, 24)
    f32 = jnp.float32
    nrm = lambda k, shape, scale: jax.random.normal(k, shape, f32) * scale
    x = nrm(ks[0], (BATCH, SEQ, D_MODEL), 1.0)
    p = nrm(ks[1], (DEPTH, BATCH, SEQ, PLE_DIM), 1.0)
    mix_norm = 1.0 + nrm(ks[2], (DEPTH, D_MODEL), 0.02)
    w_in = nrm(ks[3], (DEPTH, D_MODEL, IN_WIDTH), D_MODEL ** -0.5)
    q_norm = 1.0 + nrm(ks[4], (DEPTH, HEAD_DIM), 0.02)
    k_norm = 1.0 + nrm(ks[5], (DEPTH, HEAD_DIM), 0.02)
    n_idx = jnp.arange(SSM_STATE, dtype=f32)
    lambda_re = -0.5 + nrm(ks[6], (DEPTH, SSM_GROUPS, SSM_STATE), 0.01)
    lambda_im = math.pi * n_idx + nrm(ks[7], (DEPTH, SSM_GROUPS, SSM_STATE), 0.01)
    log_dt = jax.random.uniform(ks[8], (DEPTH, SSM_GROUPS), f32, math.log(DT_MIN), math.log(DT_MAX))
    b_re = nrm(ks[9], (DEPTH, SSM_GROUPS, SSM_STATE, SSM_GROUP), (2.0 * SSM_GROUP) ** -0.5)
    b_im = nrm(ks[10], (DEPTH, SSM_GROUPS, SSM_STATE, SSM_GROUP), (2.0 * SSM_GROUP) ** -0.5)
    c_re = nrm(ks[11], (DEPTH, SSM_GROUPS, SSM_GROUP, SSM_STATE), (2.0 * SSM_STATE) ** -0.5)
    c_im = nrm(ks[12], (DEPTH, SSM_GROUPS, SSM_GROUP, SSM_STATE), (2.0 * SSM_STATE) ** -0.5)
    d_skip = nrm(ks[13], (DEPTH, SSM_WIDTH), 1.0)
    w_glu = nrm(ks[14], (DEPTH, SSM_WIDTH, SSM_WIDTH), SSM_WIDTH ** -0.5)
    b_glu = nrm(ks[15], (DEPTH, SSM_WIDTH), 0.02)
    w_out = nrm(ks[16], (DEPTH, MIX_WIDTH, D_MODEL), MIX_WIDTH ** -0.5)
    ple_norm = 1.0 + nrm(ks[17], (DEPTH, D_MODEL), 0.02)
    w_ple_gate = nrm(ks[18], (DEPTH, D_MODEL, D_MODEL), D_MODEL ** -0.5)
    w_ple_proj = nrm(ks[19], (DEPTH, PLE_DIM, D_MODEL), PLE_DIM ** -0.5)
    return {"x": x, "p": p, "mix_norm": mix_norm, "w_in": w_in, "q_norm": q_norm,
            "k_norm": k_norm, "lambda_re": lambda_re, "lambda_im": lambda_im, "log_dt": log_dt,
            "b_re": b_re, "b_im": b_im, "c_re": c_re, "c_im": c_im, "d_skip": d_skip,
            "w_glu": w_glu, "b_glu": b_glu, "w_out": w_out, "ple_norm": ple_norm,
            "w_ple_gate": w_ple_gate, "w_ple_proj": w_ple_proj}


def reference(x, p, mix_norm, w_in, q_norm, k_norm, lambda_re, lambda_im, log_dt,
              b_re, b_im, c_re, c_im, d_skip, w_glu, b_glu, w_out, ple_norm,
              w_ple_gate, w_ple_proj):
    A = ATTN_WIDTH
    splits = [A, 2 * A, 3 * A, 4 * A, 4 * A + SSM_WIDTH]
    h = x
    for i in range(DEPTH):
        xn = rms_norm(h, mix_norm[i])
        z = xn @ w_in[i]
        q, k, v, gate_a, u, gate_s = jnp.split(z, splits, axis=-1)
        attn = dilated_mixture_attention(q, k, v, q_norm[i], k_norm[i]) * jax.nn.silu(gate_a)
        ssm = s5_glu(u, lambda_re[i], lambda_im[i], log_dt[i], b_re[i], b_im[i], c_re[i],
                     c_im[i], d_skip[i], w_glu[i], b_glu[i]) * jax.nn.silu(gate_s)
        h = h + jnp.concatenate([attn, ssm], axis=-1) @ w_out[i]
        gate = jax.nn.sigmoid(rms_norm(h, ple_norm[i]) @ w_ple_gate[i])
        h = h + gate * (p[i] @ w_ple_proj[i])
    return h
```

```python
import math
from contextlib import ExitStack
import numpy as np
import concourse.bass as bass
import concourse.mybir as mybir
from concourse.bass_utils import run_bass_kernel_spmd

F32 = mybir.dt.float32
BF16 = mybir.dt.bfloat16
F32R = mybir.dt.float32r
I32 = mybir.dt.int32
ALU = mybir.AluOpType
AF = mybir.ActivationFunctionType
AX = mybir.AxisListType

NCORES = 8
SEQ = 2048
DM = 1024
EPS = 1e-6
MAGIC = 12582912.0
TWO_PI = 2.0 * math.pi
EXACT_MASK = False
ALPHA = 1.0 if EXACT_MASK else 38.2305


class Sched:
    ENGS = ("pe", "act", "dve", "pool", "sp")

    def __init__(self, nc, n_dma_sems=24):
        self.nc = nc
        self.cnt = {e: 0 for e in self.ENGS}
        self.sem = {}
        self.seen = {e: {} for e in self.ENGS}
        self.last_w = {}
        self.readers = {}
        self.dma_sems = []
        self.dma_uses = []
        self.n_dma_sems = n_dma_sems
        self.dma_rr = 0

    def alloc(self, stack):
        for e in self.ENGS:
            if e == "sp":
                continue
            self.sem[e] = stack.enter_context(self.nc.semaphore("s_" + e))
        for i in range(self.n_dma_sems):
            self.dma_sems.append(stack.enter_context(self.nc.semaphore("d%d" % i)))
            self.dma_uses.append(0)

    def _need(self, eng, tok, waits):
        key, sem, val, src = tok
        if self.seen[eng].get(key, 0) >= val:
            return
        self.seen[eng][key] = val
        waits.append((sem, val))

    dead = False

    def op(self, eng, fn, reads=(), writes=(), dma=False):
        if self.dead:
            return None
        waits = []
        for r in reads:
            t = self.last_w.get(r)
            if t is not None:
                self._need(eng, t, waits)
        for w in writes:
            t = self.last_w.get(w)
            if t is not None and (dma or t[3] != eng or eng != "pe"):
                self._need(eng, t, waits)
            for t in self.readers.get(w, {}).values():
                if dma or t[3] != eng or eng != "pe":
                    self._need(eng, t, waits)
        if dma:
            j = self.dma_rr
            self.dma_rr = (self.dma_rr + 1) % self.n_dma_sems
            k = self.dma_uses[j]
            if k > 0:
                self._need(eng, ("d%d" % j, self.dma_sems[j], 16 * k, None), waits)
            self.dma_uses[j] = k + 1
            tok = ("d%d" % j, self.dma_sems[j], 16 * (k + 1), None)
            inc = (self.dma_sems[j], 16)
        else:
            self.cnt[eng] += 1
            tok = (eng, self.sem[eng], self.cnt[eng], eng)
            inc = (self.sem[eng], 1)
        for w in writes:
            self.last_w[w] = tok
            self.readers[w] = {}
        for r in reads:
            self.readers.setdefault(r, {})[tok[0]] = tok
        self._emit(eng, waits, fn, inc)
        return tok

    def group(self, eng, ops):
        if self.dead:
            return
        waits = []
        for fn, reads, writes in ops:
            for r in reads:
                t = self.last_w.get(r)
                if t is not None:
                    self._need(eng, t, waits)
            for w in writes:
                t = self.last_w.get(w)
                if t is not None and (t[3] != eng or eng != "pe"):
                    self._need(eng, t, waits)
                for t in self.readers.get(w, {}).values():
                    if t[3] != eng or eng != "pe":
                        self._need(eng, t, waits)
        first = True
        for fn, reads, writes in ops:
            self.cnt[eng] += 1
            tok = (eng, self.sem[eng], self.cnt[eng], eng)
            for w in writes:
                self.last_w[w] = tok
                self.readers[w] = {}
            for r in reads:
                self.readers.setdefault(r, {})[tok[0]] = tok
            self._emit(eng, waits if first else [], fn, (self.sem[eng], 1))
            first = False

    def _eng(self, name):
        nc = self.nc
        return {"pe": nc.tensor, "act": nc.scalar, "dve": nc.vector, "pool": nc.gpsimd, "sp": nc.sync}[name]

    def _emit(self, eng, waits, fn, inc):
        e = self._eng(eng)
        for sem, val in waits:
            e.wait_ge(sem, val)
        if fn is not None:
            ins = fn(e)
            ins.then_inc(inc[0], inc[1])

    def barrier(self):
        if self.dead:
            return
        toks = []
        for e in self.ENGS:
            if e != "sp" and self.cnt[e] > 0:
                toks.append((e, self.sem[e], self.cnt[e], e))
        for j in range(self.n_dma_sems):
            if self.dma_uses[j] > 0:
                toks.append(("d%d" % j, self.dma_sems[j], 16 * self.dma_uses[j], None))
        for e in self.ENGS:
            waits = []
            for t in toks:
                self._need(e, t, waits)
            if waits:
                self._emit(e, waits, None, None)


class _Stop(Exception):
    pass


def build_nc(debug=False, stop_after=None):
    nc = bass.Bass("TRN2", target_bir_lowering=False)

    def din(name, shape):
        return nc.dram_tensor(name, shape, F32, kind="ExternalInput").ap()

    x_d = din("x", [2, SEQ, DM])
    p_d = din("p", [2, SEQ, 256])
    mixn_d = din("mix_norm", [DM])
    win_d = din("w_in", [DM, 3072])
    qn_d = din("q_norm", [64])
    kn_d = din("k_norm", [64])
    lre_d = din("lambda_re", [32, 64])
    lim_d = din("lambda_im", [32, 64])
    ldt_d = din("log_dt", [32])
    bre_d = din("b_re", [32, 64, 16])
    bim_d = din("b_im", [32, 64, 16])
    cre_d = din("c_re", [32, 16, 64])
    cim_d = din("c_im", [32, 16, 64])
    dsk_d = din("d_skip", [512])
    wglu_d = din("w_glu", [512, 512])
    bglu_d = din("b_glu", [512])
    wout_d = din("w_out", [DM, DM])
    plen_d = din("ple_norm", [DM])
    wg_d = din("w_ple_gate", [DM, DM])
    wp_d = din("w_ple_proj", [256, DM])
    out_d = nc.dram_tensor("out", [2, SEQ, DM], F32, kind="ExternalOutput").ap()
    dbg = {}
    if debug:
        dbg["xT"] = nc.dram_tensor("dbg_xT", [128, 8, SEQ], F32, kind="ExternalOutput").ap()
        dbg["attnT"] = nc.dram_tensor("dbg_attnT", [128, 4, SEQ], F32, kind="ExternalOutput").ap()
        dbg["ssmT"] = nc.dram_tensor("dbg_ssmT", [128, 4, SEQ], F32, kind="ExternalOutput").ap()
        dbg["ygT"] = nc.dram_tensor("dbg_ygT", [128, 4, SEQ], F32, kind="ExternalOutput").ap()
        dbg["qT"] = nc.dram_tensor("dbg_qT", [128, SEQ], F32, kind="ExternalOutput").ap()
        dbg["T"] = nc.dram_tensor("dbg_T", [128, 32, 128], F32, kind="ExternalOutput").ap()
        dbg["S"] = nc.dram_tensor("dbg_S", [128, 16, 2, 257], F32, kind="ExternalOutput").ap()

    with ExitStack() as top:
        S = Sched(nc)
        S.alloc(top)
        top.enter_context(nc.allow_non_contiguous_dma(reason="small strided parameter loads"))

        uid = [0]

        def mk(stack):
            def sb(name, shape, dt=F32):
                uid[0] += 1
                return stack.enter_context(nc.sbuf_tensor("%s_u%d" % (name, uid[0]), shape, dt))
            return sb

        sbP = mk(top)
        PS = [top.enter_context(nc.psum_tensor("ps%d" % i, [128, 512], F32)) for i in range(8)]
        psn = ["ps%d" % i for i in range(8)]

        def V(fn, r=(), w=()):
            S.op("dve", fn, reads=r, writes=w)

        def A(fn, r=(), w=()):
            S.op("act", fn, reads=r, writes=w)

        def G(fn, r=(), w=()):
            S.op("pool", fn, reads=r, writes=w)

        def T(fn, r=(), w=()):
            S.op("pe", fn, reads=r, writes=w)

        def D(fn, r=(), w=()):
            S.op("sp", fn, reads=r, writes=w, dma=True)

        dq = [0]

        def D2(fn, r=(), w=()):
            dq[0] += 1
            S.op("sp" if dq[0] % 2 else "act", fn, reads=r, writes=w, dma=True)

        def ck(name):
            if stop_after == name:
                S.barrier()
                S.dead = True

        try:
            ident = sbP("ident", [128, 128])
            ones_f = sbP("ones_f", [128, 128])
            identr = sbP("identr", [128, 128], F32 if EXACT_MASK else BF16)
            cneg = sbP("cneg", [128, 2])
            blk1 = sbP("blk1", [128, 128], BF16)
            MT = sbP("MT", [128, 2432], F32 if EXACT_MASK else BF16)
            mixg = sbP("mixg", [128, 8, 1])
            pleg = sbP("pleg", [128, 8, 1])
            qg = sbP("qg", [128, 1])
            kg = sbP("kg", [128, 1])
            bglu = sbP("bglu", [128, 4, 1])
            Tm = sbP("Tm", [128, 32, 128], BF16)
            PGre = sbP("PGre", [128, 32, 64], BF16)
            PGim = sbP("PGim", [128, 32, 64], BF16)
            PCre = sbP("PCre", [128, 16, 128], BF16)
            PCim = sbP("PCim", [128, 16, 128], BF16)
            APR = sbP("APR", [128, 16, 16, 2])
            API = sbP("API", [128, 16, 16, 2])
            attnT = sbP("attnT", [128, 4, SEQ], BF16)
            ssmT = sbP("ssmT", [128, 4, SEQ], BF16)

            G(lambda e: e.memset(ident[:], 1.0), w=["ident"])
            G(lambda e: e.affine_select(out=ident[:], in_=ident[:], pattern=[[-1, 128]], base=0, channel_multiplier=1,
                                        compare_op=ALU.is_equal, fill=0.0), r=["ident"], w=["ident"])
            V(lambda e: e.memset(ones_f[:], 1.0), w=["ones_f"])
            V(lambda e: e.tensor_copy(out=(identr[:].bitcast(F32R) if EXACT_MASK else identr[:]), in_=ident[:]), r=["ident"], w=["identr"])
            V(lambda e: e.memset(cneg[:, 0:1], -0.5), w=["cneg"])
            V(lambda e: e.memset(cneg[:, 1:2], -1.0), w=["cneg"])
            V(lambda e: e.memset(blk1[:], 0.0), w=["blk1"])
            V(lambda e: e.memset(blk1[0:64, 0:64], 1.0), w=["blk1"])
            V(lambda e: e.memset(blk1[64:128, 64:128], 1.0), w=["blk1"])
            ck("consts")

            with ExitStack() as su:
                sb = mk(su)
                NI = 2432
                sb_keep = sb
                msk_scope = ExitStack()
                sb = mk(msk_scope)
                di = sb("di", [128, NI], I32)
                df = sb("df", [128, NI])
                t1 = sb("mt1", [128, NI])
                t2 = sb("mt2", [128, NI])
                acc = sb("macc", [128, NI])
                G(lambda e: e.iota(di[:], pattern=[[1, NI]], base=-384, channel_multiplier=-1), w=["di"])
                V(lambda e: e.tensor_copy(out=df[:], in_=di[:]), r=["di"], w=["df"])
                V(lambda e: e.tensor_scalar(out=acc[:], in0=df[:], scalar1=128.0, scalar2=None, op0=ALU.is_le), r=["df"], w=["acc"])
                for (mod, lim) in ((4.0, 512.0), (16.0, None)):
                    V(lambda e, mod=mod: e.tensor_scalar(out=t1[:], in0=df[:], scalar1=1.0 / mod, scalar2=MAGIC, op0=ALU.mult,
                                                         op1=ALU.add), r=["df"], w=["t1"])
                    V(lambda e, mod=mod: e.tensor_scalar(out=t1[:], in0=t1[:], scalar1=MAGIC, scalar2=mod, op0=ALU.subtract,
                                                         op1=ALU.mult), r=["t1"], w=["t1"])
                    V(lambda e: e.tensor_tensor(out=t1[:], in0=t1[:], in1=df[:], op=ALU.is_equal), r=["t1", "df"], w=["t1"])
                    if lim is not None:
                        V(lambda e, lim=lim: e.tensor_scalar(out=t2[:], in0=df[:], scalar1=lim, scalar2=None, op0=ALU.is_le),
                          r=["df"], w=["t2"])
                        V(lambda e: e.tensor_tensor(out=t1[:], in0=t1[:], in1=t2[:], op=ALU.mult), r=["t1", "t2"], w=["t1"])
                    V(lambda e: e.tensor_tensor(out=acc[:], in0=acc[:], in1=t1[:], op=ALU.add), r=["acc", "t1"], w=["acc"])
                V(lambda e: e.tensor_scalar(out=t2[:], in0=df[:], scalar1=0.0, scalar2=None, op0=ALU.is_ge), r=["df"], w=["t2"])
                V(lambda e: e.tensor_tensor(out=acc[:], in0=acc[:], in1=t2[:], op=ALU.mult), r=["acc", "t2"], w=["acc"])
                if EXACT_MASK:
                    V(lambda e: e.tensor_scalar(out=t1[:], in0=acc[:], scalar1=1.0, scalar2=None, op0=ALU.max), r=["acc"], w=["t1"])
                    A(lambda e: e.activation(out=t1[:], in_=t1[:], func=AF.Ln), r=["t1"], w=["t1"])
                    V(lambda e: e.tensor_scalar(out=t2[:], in0=acc[:], scalar1=0.0, scalar2=-30000.0, op0=ALU.is_equal, op1=ALU.mult),
                      r=["acc"], w=["t2"])
                    V(lambda e: e.tensor_tensor(out=MT[:].bitcast(F32R), in0=t1[:], in1=t2[:], op=ALU.add), r=["t1", "t2"], w=["MT"])
                else:
                    V(lambda e: e.tensor_scalar(out=t1[:], in0=acc[:], scalar1=2.0, scalar2=26.5, op0=ALU.is_equal, op1=ALU.mult),
                      r=["acc"], w=["t1"])
                    V(lambda e: e.tensor_scalar(out=t2[:], in0=acc[:], scalar1=3.0, scalar2=42.0, op0=ALU.is_equal, op1=ALU.mult),
                      r=["acc"], w=["t2"])
                    V(lambda e: e.tensor_tensor(out=t1[:], in0=t1[:], in1=t2[:], op=ALU.add), r=["t1", "t2"], w=["t1"])
                    V(lambda e: e.tensor_scalar(out=t2[:], in0=acc[:], scalar1=0.0, scalar2=-1.0e6, op0=ALU.is_equal, op1=ALU.mult),
                      r=["acc"], w=["t2"])
                    V(lambda e: e.tensor_tensor(out=MT[:], in0=t1[:], in1=t2[:], op=ALU.add), r=["t1", "t2"], w=["MT"])
                S.barrier()
                msk_scope.close()
                sb = sb_keep
                ck("mask")

                LR = sb("LR", [128, 16, 1])
                LI = sb("LI", [128, 16, 1])
                DT = sb("DT", [128, 16, 1])
                dtrow = sb("dtrow", [1, 32])
                D(lambda e: e.dma_start(out=dtrow[:], in_=ldt_d.rearrange("(o g) -> o g", o=1)), w=["dtrow"])
                T(lambda e: e.matmul(PS[5][:, 0:32], lhsT=ones_f[0:1, :], rhs=dtrow[:], start=True, stop=True),
                  r=["ones_f", "dtrow"], w=[psn[5]])
                for j in range(2):
                    V(lambda e, j=j: e.tensor_copy(out=DT[64 * j:64 * j + 64, :, 0], in_=PS[5][64 * j:64 * j + 64, j:32:2]),
                      r=[psn[5]], w=["DT"])
                Ln_ = [sb("Lnat%d" % i, [32, 2, 64]) for i in range(2)]
                for li, (src, dst, dname) in enumerate(((lre_d, LR, "LR"), (lim_d, LI, "LI"))):
                    D(lambda e, li=li, src=src: e.dma_start(out=Ln_[li][:, 0, :], in_=src), w=["Lnat%d" % li])
                    G(lambda e, li=li: e.tensor_copy(out=Ln_[li][:, 1, :], in_=Ln_[li][:, 0, :]), r=["Lnat%d" % li], w=["Lnat%d" % li])
                    bank = 6 + li
                    T(lambda e, li=li, bank=bank: e.transpose(out=PS[bank][:, 0:32], in_=Ln_[li][:].rearrange("p d n -> p (d n)"),
                                                              identity=ident[0:32, 0:32]), r=["Lnat%d" % li, "ident"], w=[psn[bank]])
                    for j in range(2):
                        V(lambda e, dst=dst, j=j, bank=bank: e.tensor_copy(
                            out=dst[64 * j:64 * j + 64, :, 0], in_=PS[bank][64 * j:64 * j + 64, j:32:2]), r=[psn[bank]], w=[dname])
                BR = sb("BR", [128, 16, 16])
                BI = sb("BI", [128, 16, 16])
                for j in range(2):
                    D2(lambda e, j=j: e.dma_start(out=BR[64 * j:64 * j + 64], in_=bre_d.rearrange("(gp j) n c -> j n gp c", j=2)[j]), w=["BR"])
                    D2(lambda e, j=j: e.dma_start(out=BI[64 * j:64 * j + 64], in_=bim_d.rearrange("(gp j) n c -> j n gp c", j=2)[j]), w=["BI"])
                Cn = [sb("Cn%d" % i, [128, 4, 2, 64]) for i in range(2)]
                for ci, src in enumerate((cre_d, cim_d)):
                    D(lambda e, ci=ci, src=src: e.dma_start(out=Cn[ci][:, :, 0, :],
                                                            in_=src.rearrange("(t g) c n -> (g c) t n", t=4)), w=["Cn%d" % ci])
                    G(lambda e, ci=ci: e.tensor_copy(out=Cn[ci][:, :, 1, :], in_=Cn[ci][:, :, 0, :]), r=["Cn%d" % ci], w=["Cn%d" % ci])
                gnat = sb("gnat", [20, 128])
                D(lambda e: e.dma_start(out=gnat[0:8, :], in_=mixn_d.rearrange("(k p) -> k p", p=128)), w=["gnat"])
                D(lambda e: e.dma_start(out=gnat[8:16, :], in_=plen_d.rearrange("(k p) -> k p", p=128)), w=["gnat"])
                D(lambda e: e.dma_start(out=gnat[16:20, :], in_=bglu_d.rearrange("(k p) -> k p", p=128)), w=["gnat"])
                T(lambda e: e.transpose(out=PS[4][:, 0:20], in_=gnat[:], identity=ident[0:20, 0:20]), r=["gnat", "ident"], w=[psn[4]])
                V(lambda e: e.tensor_copy(out=mixg[:, :, 0], in_=PS[4][:, 0:8]), r=[psn[4]], w=["mixg"])
                V(lambda e: e.tensor_copy(out=pleg[:, :, 0], in_=PS[4][:, 8:16]), r=[psn[4]], w=["pleg"])
                V(lambda e: e.tensor_copy(out=bglu[:, :, 0], in_=PS[4][:, 16:20]), r=[psn[4]], w=["bglu"])
                qkrow = sb("qkrow", [2, 2, 64])
                for hh in range(2):
                    D(lambda e, hh=hh: e.dma_start(out=qkrow[0:1, hh, :], in_=qn_d.rearrange("(o n) -> o n", o=1)), w=["qkrow"])
                    D(lambda e, hh=hh: e.dma_start(out=qkrow[1:2, hh, :], in_=kn_d.rearrange("(o n) -> o n", o=1)), w=["qkrow"])
                T(lambda e: e.transpose(out=PS[4][:, 32:34], in_=qkrow[:].rearrange("r h n -> r (h n)"), identity=ident[0:2, 0:2]),
                  r=["qkrow", "ident"], w=[psn[4]])
                V(lambda e: e.tensor_scalar(out=qg[:], in0=PS[4][:, 32:33], scalar1=0.125 * ALPHA, scalar2=None, op0=ALU.mult),
                  r=[psn[4]], w=["qg"])
                V(lambda e: e.tensor_copy(out=kg[:], in_=PS[4][:, 33:34]), r=[psn[4]], w=["kg"])
                CR = sb("CR", [128, 16, 16])
                CI = sb("CI", [128, 16, 16])
                for ci, (dst, dname) in enumerate(((CR, "CR"), (CI, "CI"))):
                    for t in range(4):
                        bank = (ci * 4 + t) % 8
                        T(lambda e, ci=ci, t=t, bank=bank: e.transpose(
                            out=PS[bank][:, 0:128], in_=Cn[ci][:, t, :, :].rearrange("p d n -> p (d n)"), identity=ident[:]),
                          r=["Cn%d" % ci, "ident"], w=[psn[bank]])
                        for j in range(2):
                            V(lambda e, dst=dst, t=t, j=j, bank=bank: e.tensor_copy(
                                out=dst[64 * j:64 * j + 64, 4 * t:4 * t + 4, :],
                                in_=PS[bank][64 * j:64 * j + 64, 0:128].rearrange("p (g q c) -> p g q c", q=2, c=16)[:, :, j, :]),
                              r=[psn[bank]], w=[dname])
                dcol = sb("dcol", [128, 32, 1])
                dnat = sb("dnat", [32, 16])
                d16 = sb("d16", [16, 32])
                Rrep = sb("Rrep", [16, 128])
                D(lambda e: e.dma_start(out=dnat[:], in_=dsk_d.rearrange("(g c) -> g c", c=16)), w=["dnat"])
                T(lambda e: e.transpose(out=PS[4][0:16, 64:96], in_=dnat[:], identity=ident[0:32, 0:32]), r=["dnat", "ident"], w=[psn[4]])
                V(lambda e: e.tensor_copy(out=d16[:], in_=PS[4][0:16, 64:96]), r=[psn[4]], w=["d16"])
                for sp_ in range(8):
                    V(lambda e, sp_=sp_: e.tensor_copy(out=Rrep[:, 16 * sp_:16 * sp_ + 16], in_=ident[0:16, 0:16]), r=["ident"], w=["Rrep"])
                T(lambda e: e.matmul(PS[4][:, 128:160], lhsT=Rrep[:], rhs=d16[:], start=True, stop=True), r=["Rrep", "d16"], w=[psn[4]])
                V(lambda e: e.tensor_copy(out=dcol[:, :, 0], in_=PS[4][:, 128:160]), r=[psn[4]], w=["dcol"])

                def sm(name):
                    return sb("s_" + name, [128, 16, 1])

                dt_ = sm("dt_"); th = sm("th"); lrdt = sm("lrdt"); mag = sm("mag"); magi = sm("magi")
                rr = sm("rr"); sn = sm("sn"); cs = sm("cs"); tA = sm("tA"); tB = sm("tB")
                a_re = sm("a_re"); a_im = sm("a_im"); i_re = sm("i_re"); i_im = sm("i_im")
                c_re = sm("c_re"); c_im = sm("c_im"); den = sm("den")
                A(lambda e: e.activation(out=dt_[:], in_=DT[:], func=AF.Exp), r=["DT"], w=["dt_"])
                V(lambda e: e.tensor_tensor(out=th[:], in0=LI[:], in1=dt_[:], op=ALU.mult), r=["LI", "dt_"], w=["th"])
                V(lambda e: e.tensor_tensor(out=lrdt[:], in0=LR[:], in1=dt_[:], op=ALU.mult), r=["LR", "dt_"], w=["lrdt"])
                A(lambda e: e.activation(out=mag[:], in_=lrdt[:], func=AF.Exp), r=["lrdt"], w=["mag"])
                A(lambda e: e.activation(out=magi[:], in_=lrdt[:], func=AF.Exp, scale=-1.0), r=["lrdt"], w=["magi"])

                def sin_of(dst, shift, dname):
                    V(lambda e: e.tensor_scalar(out=tA[:], in0=th[:], scalar1=shift, scalar2=1.0 / TWO_PI, op0=ALU.add, op1=ALU.mult),
                      r=["th"], w=["tA"])
                    V(lambda e: e.tensor_scalar(out=tA[:], in0=tA[:], scalar1=MAGIC, scalar2=MAGIC, op0=ALU.add, op1=ALU.subtract),
                      r=["tA"], w=["tA"])
                    V(lambda e: e.tensor_scalar(out=tB[:], in0=th[:], scalar1=shift, scalar2=None, op0=ALU.add), r=["th"], w=["tB"])
                    V(lambda e: e.scalar_tensor_tensor(out=rr[:], in0=tA[:], scalar=-TWO_PI, in1=tB[:], op0=ALU.mult, op1=ALU.add),
                      r=["tA", "tB"], w=["rr"])
                    V(lambda e: e.tensor_scalar(out=rr[:], in0=rr[:], scalar1=math.pi, scalar2=-math.pi, op0=ALU.min, op1=ALU.max),
                      r=["rr"], w=["rr"])
                    A(lambda e: e.activation(out=dst[:], in_=rr[:], func=AF.Sin), r=["rr"], w=[dname])

                sin_of(sn, 0.0, "sn")
                sin_of(cs, math.pi / 2.0, "cs")
                V(lambda e: e.tensor_tensor(out=a_re[:], in0=mag[:], in1=cs[:], op=ALU.mult), r=["mag", "cs"], w=["a_re"])
                V(lambda e: e.tensor_tensor(out=a_im[:], in0=mag[:], in1=sn[:], op=ALU.mult), r=["mag", "sn"], w=["a_im"])
                V(lambda e: e.tensor_tensor(out=i_re[:], in0=magi[:], in1=cs[:], op=ALU.mult), r=["magi", "cs"], w=["i_re"])
                V(lambda e: e.scalar_tensor_tensor(out=i_im[:], in0=magi[:], scalar=-1.0, in1=sn[:], op0=ALU.mult, op1=ALU.mult),
                  r=["magi", "sn"], w=["i_im"])
                V(lambda e: e.tensor_tensor(out=den[:], in0=LR[:], in1=LR[:], op=ALU.mult), r=["LR"], w=["den"])
                V(lambda e: e.tensor_tensor(out=tA[:], in0=LI[:], in1=LI[:], op=ALU.mult), r=["LI"], w=["tA"])
                V(lambda e: e.tensor_tensor(out=den[:], in0=den[:], in1=tA[:], op=ALU.add), r=["den", "tA"], w=["den"])
                V(lambda e: e.reciprocal(out=den[:], in_=den[:]), r=["den"], w=["den"])
                V(lambda e: e.tensor_scalar(out=tB[:], in0=a_re[:], scalar1=-1.0, scalar2=None, op0=ALU.add), r=["a_re"], w=["tB"])
                V(lambda e: e.tensor_tensor(out=c_re[:], in0=tB[:], in1=LR[:], op=ALU.mult), r=["tB", "LR"], w=["c_re"])
                V(lambda e: e.tensor_tensor(out=tA[:], in0=a_im[:], in1=LI[:], op=ALU.mult), r=["a_im", "LI"], w=["tA"])
                V(lambda e: e.tensor_tensor(out=c_re[:], in0=c_re[:], in1=tA[:], op=ALU.add), r=["c_re", "tA"], w=["c_re"])
                V(lambda e: e.tensor_tensor(out=c_re[:], in0=c_re[:], in1=den[:], op=ALU.mult), r=["c_re", "den"], w=["c_re"])
                V(lambda e: e.tensor_tensor(out=c_im[:], in0=a_im[:], in1=LR[:], op=ALU.mult), r=["a_im", "LR"], w=["c_im"])
                V(lambda e: e.tensor_tensor(out=tA[:], in0=tB[:], in1=LI[:], op=ALU.mult), r=["tB", "LI"], w=["tA"])
                V(lambda e: e.tensor_tensor(out=c_im[:], in0=c_im[:], in1=tA[:], op=ALU.subtract), r=["c_im", "tA"], w=["c_im"])
                V(lambda e: e.tensor_tensor(out=c_im[:], in0=c_im[:], in1=den[:], op=ALU.mult), r=["c_im", "den"], w=["c_im"])

                tb1 = sb("tb1", [128, 16, 16])
                tb2 = sb("tb2", [128, 16, 16])

                lit = {a_re.name: "a_re", a_im.name: "a_im", i_re.name: "i_re", i_im.name: "i_im",
                       c_re.name: "c_re", c_im.name: "c_im"}

                def nm(t):
                    return lit.get(t.name, t.name)

                tmpE = {"dve": (tb1, tb2, "tb1", "tb2", tA, tB, "tA", "tB"),
                        "pool": (sb("tb1p", [128, 16, 16]), sb("tb2p", [128, 16, 16]), "tb1p", "tb2p",
                                 sm("tAp"), sm("tBp"), "tAp", "tBp")}

                def cmul_b(eng, o_re, o_im, on, x_re, x_im, xn, s_re, s_im, neg_im=False):
                    u1, u2, n1, n2 = tmpE[eng][0:4]
                    sr = s_re[:].to_broadcast([128, 16, 16])
                    si = s_im[:].to_broadcast([128, 16, 16])
                    O = lambda fn, r, w: S.op(eng, fn, reads=r, writes=w)
                    O(lambda e: e.tensor_tensor(out=u1[:], in0=x_re, in1=sr, op=ALU.mult), xn + [nm(s_re)], [n1])
                    O(lambda e: e.tensor_tensor(out=u2[:], in0=x_im, in1=si, op=ALU.mult), xn + [nm(s_im)], [n2])
                    O(lambda e: e.tensor_tensor(out=o_re, in0=u1[:], in1=u2[:], op=ALU.subtract), [n1, n2], [on])
                    O(lambda e: e.tensor_tensor(out=u1[:], in0=x_re, in1=si, op=ALU.mult), xn + [nm(s_im)], [n1])
                    O(lambda e: e.tensor_tensor(out=u2[:], in0=x_im, in1=sr, op=ALU.mult), xn + [nm(s_re)], [n2])
                    if neg_im:
                        O(lambda e: e.tensor_tensor(out=u1[:], in0=u1[:], in1=u2[:], op=ALU.add), [n1, n2], [n1])
                        O(lambda e: e.tensor_scalar(out=o_im, in0=u1[:], scalar1=-1.0, scalar2=None, op0=ALU.mult), [n1], [on])
                    else:
                        O(lambda e: e.tensor_tensor(out=o_im, in0=u1[:], in1=u2[:], op=ALU.add), [n1, n2], [on])

                def cmul_s(eng, o_re, o_im, x_re, x_im, s_re, s_im):
                    u1, u2, n1, n2 = tmpE[eng][4:8]
                    O = lambda fn, r, w: S.op(eng, fn, reads=r, writes=w)
                    O(lambda e: e.tensor_tensor(out=u1[:], in0=x_re[:], in1=s_re[:], op=ALU.mult), [nm(x_re), nm(s_re)], [n1])
                    O(lambda e: e.tensor_tensor(out=u2[:], in0=x_im[:], in1=s_im[:], op=ALU.mult), [nm(x_im), nm(s_im)], [n2])
                    O(lambda e: e.tensor_tensor(out=o_re[:], in0=u1[:], in1=u2[:], op=ALU.subtract), [n1, n2], [nm(o_re)])
                    O(lambda e: e.tensor_tensor(out=u1[:], in0=x_re[:], in1=s_im[:], op=ALU.mult), [nm(x_re), nm(s_im)], [n1])
                    O(lambda e: e.tensor_tensor(out=u2[:], in0=x_im[:], in1=s_re[:], op=ALU.mult), [nm(x_im), nm(s_re)], [n2])
                    O(lambda e: e.tensor_tensor(out=o_im[:], in0=u1[:], in1=u2[:], op=ALU.add), [n1, n2], [nm(o_im)])

                BBr = sb("BBr", [128, 16, 16])
                BBi = sb("BBi", [128, 16, 16])
                cmul_b("dve", BBr[:], BBi[:], "BB", BR[:], BI[:], ["BR", "BI"], c_re, c_im)
                pw_re = [sm("pwr%d" % i) for i in range(9)]
                pw_im = [sm("pwi%d" % i) for i in range(9)]
                iw_re = [sm("iwr%d" % i) for i in range(8)]
                iw_im = [sm("iwi%d" % i) for i in range(8)]
                CPr = sb("CPr", [128, 16, 9, 16])
                CPi = sb("CPi", [128, 16, 9, 16])
                for (eng, lst_re, lst_im) in (("dve", pw_re, pw_im), ("pool", iw_re, iw_im)):
                    S.op(eng, lambda e, t=lst_re[0]: e.memset(t[:], 1.0), writes=[lst_re[0].name])
                    S.op(eng, lambda e, t=lst_im[0]: e.memset(t[:], 0.0), writes=[lst_im[0].name])
                for i in range(1, 8):
                    cmul_s("pool", iw_re[i], iw_im[i], iw_re[i - 1], iw_im[i - 1], i_re, i_im)
                for tau in range(9):
                    if tau + 1 < 9:
                        cmul_s("dve", pw_re[tau + 1], pw_im[tau + 1], pw_re[tau], pw_im[tau], a_re, a_im)
                    cmul_b("dve", CPr[:, :, tau, :], CPi[:, :, tau, :], "CP%d" % tau, CR[:], CI[:], ["CR", "CI"],
                           pw_re[tau], pw_im[tau], neg_im=True)
                Qr = sb("Qr", [128, 16, 8, 16])
                Qi = sb("Qi", [128, 16, 8, 16])
                Hr = sb("Hr", [128, 16, 8, 16])
                Hi = sb("Hi", [128, 16, 8, 16])
                for s_ in range(8):
                    cmul_b("pool", Qr[:, :, s_, :], Qi[:, :, s_, :], "Q%d" % s_, BBr[:], BBi[:], ["BB"], iw_re[s_], iw_im[s_])
                    cmul_b("dve" if s_ < 3 else "pool", Hr[:, :, s_, :], Hi[:, :, s_, :], "H%d" % s_, BBr[:], BBi[:], ["BB"],
                           pw_re[7 - s_], pw_im[7 - s_])
                CPn = ["CP%d" % t for t in range(9)]
                Qn = ["Q%d" % t for t in range(8)]
                Hn = ["H%d" % t for t in range(8)]
                V(lambda e: e.tensor_copy(out=PCre[:].rearrange("p g (t c) -> p g t c", c=16), in_=CPr[:, :, 1:9, :]), r=CPn, w=["PCre"])
                V(lambda e: e.tensor_copy(out=PCim[:].rearrange("p g (t c) -> p g t c", c=16), in_=CPi[:, :, 1:9, :]), r=CPn, w=["PCim"])
                qw_re = [pw_re[8]] + [sm("qwr%d" % i) for i in range(1, 16)]
                qw_im = [pw_im[8]] + [sm("qwi%d" % i) for i in range(1, 16)]
                for jj in range(16):
                    if jj > 0:
                        cmul_s("pool", qw_re[jj], qw_im[jj], qw_re[jj - 1], qw_im[jj - 1], pw_re[8], pw_im[8])
                    G(lambda e, jj=jj: e.tensor_copy(out=APR[:, jj, :, 0:1], in_=qw_re[jj][:]), r=[qw_re[jj].name], w=["APR"])
                    G(lambda e, jj=jj: e.tensor_copy(out=APR[:, jj, :, 1:2], in_=qw_re[jj][:]), r=[qw_re[jj].name], w=["APR"])
                    G(lambda e, jj=jj: e.tensor_scalar(out=API[:, jj, :, 0:1], in0=qw_im[jj][:], scalar1=-1.0, scalar2=None,
                                                       op0=ALU.mult), r=[qw_im[jj].name], w=["API"])
                    G(lambda e, jj=jj: e.tensor_copy(out=API[:, jj, :, 1:2], in_=qw_im[jj][:]), r=[qw_im[jj].name], w=["API"])
                for gp in range(16):
                    for ci, (src, dst, dn_) in enumerate(((Hr, PGre, "PGre"), (Hi, PGim, "PGim"))):
                        bank = (2 * gp + ci) % 8
                        T(lambda e, src=src, gp=gp, bank=bank: e.transpose(
                            out=PS[bank][:, 0:128], in_=src[:, gp, :, :].rearrange("p s c -> p (s c)"), identity=ident[:]),
                          r=Hn + ["ident"], w=[psn[bank]])
                        fcp = (lambda e, dst=dst, gp=gp, bank=bank: e.tensor_copy(
                            out=dst[:, 2 * gp:2 * gp + 2, :], in_=PS[bank][:, 0:128].rearrange("p (j n) -> p j n", j=2)))
                        if ci == 0:
                            V(fcp, r=[psn[bank]], w=[dn_])
                        else:
                            A(lambda e, dst=dst, gp=gp, bank=bank: e.copy(
                                out=dst[:, 2 * gp:2 * gp + 2, :], in_=PS[bank][:, 0:128].rearrange("p (j n) -> p j n", j=2)),
                              r=[psn[bank]], w=[dn_])
                tmask = sb("tmask", [128, 128])
                ttmp = sb("ttmp", [128, 128])
                G(lambda e: e.memset(tmask[:], 1.0), w=["tmask"])
                G(lambda e: e.affine_select(out=tmask[:], in_=tmask[:], pattern=[[16, 8], [0, 16]], base=15, channel_multiplier=-1,
                                            compare_op=ALU.is_ge, fill=0.0), r=["tmask"], w=["tmask"])
                for g in range(32):
                    gp, j = g // 2, g % 2
                    bank = g % 8
                    pb = 64 * j
                    T(lambda e, gp=gp, pb=pb, bank=bank: e.matmul(PS[bank][:, 0:128], lhsT=Qr[pb:pb + 64, gp, :, :].rearrange("p s c -> p (s c)"),
                                                                   rhs=CPr[pb:pb + 64, gp, 0:8, :].rearrange("p s c -> p (s c)"), start=True, stop=False),
                      r=Qn + CPn, w=[psn[bank]])
                    T(lambda e, gp=gp, pb=pb, bank=bank: e.matmul(PS[bank][:, 0:128], lhsT=Qi[pb:pb + 64, gp, :, :].rearrange("p s c -> p (s c)"),
                                                                   rhs=CPi[pb:pb + 64, gp, 0:8, :].rearrange("p s c -> p (s c)"), start=False, stop=True),
                      r=Qn + CPn, w=[psn[bank]])
                    V(lambda e, bank=bank: e.tensor_tensor(out=ttmp[:], in0=PS[bank][:, 0:128], in1=tmask[:], op=ALU.mult),
                      r=[psn[bank], "tmask"], w=["ttmp"])
                    V(lambda e, g=g: e.scalar_tensor_tensor(out=Tm[:, g, :], in0=ident[:], scalar=dcol[:, g, :], in1=ttmp[:],
                                                            op0=ALU.mult, op1=ALU.add), r=["ident", "dcol", "ttmp"], w=["Tm"])
                if debug:
                    dT = sb("dT", [128, 32, 128])
                    V(lambda e: e.tensor_copy(out=dT[:], in_=Tm[:]), r=["Tm"], w=["dT"])
                    D(lambda e: e.dma_start(out=dbg["T"], in_=dT[:]), r=["dT"])
                S.barrier()

            for s in range(2):
                S.barrier()
                with ExitStack() as abc:
                    sb = mk(abc)
                    xT = sb("xT", [128, 8, SEQ], BF16)
                    wst = [None]
                    wbf = [None, None]
                    wctr = [0]

                    def load_w(c0, slot):
                        i = 0
                        D(lambda e: e.dma_start(out=wst[i][:], in_=win_d[:, c0:c0 + 128].rearrange("(k p) c -> p k c", p=128)),
                          w=["wsta%d" % i])
                        G(lambda e: e.tensor_tensor(out=wbf[slot][:], in0=wst[i][:], in1=mixg[:].to_broadcast([128, 8, 128]),
                                                    op=ALU.mult), r=["wsta%d" % i, "mixg"], w=["wbf%d" % slot])
                        return wbf[slot], "wbf%d" % slot

                    ck("ssmsetup")
                    with ExitStack() as a0:
                        sb0 = mk(a0)
                        xt = [sb0("xt%d" % i, [128, DM]) for i in range(4)]
                        sqs = [sb0("sq%d" % i, [128, DM]) for i in range(2)]
                        st4s = [sb0("st4%d" % i, [128, 4]) for i in range(2)]
                        dgs = [sb0("dg%d" % i, [128, 128]) for i in range(2)]
                        rbcs = [sb0("rbc%d" % i, [128, 128]) for i in range(2)]
                        def a0_load(i):
                            xi = xt[i % 4]
                            xn = "xt%d" % (i % 4)
                            sq, sqn = sqs[i % 2], "sq%d" % (i % 2)
                            D(lambda e: e.dma_start(out=xi[:], in_=x_d[s, i * 128:(i + 1) * 128, :]), w=[xn])
                            A(lambda e: e.activation(out=sq[:], in_=xi[:], func=AF.Square), r=[xn], w=[sqn])

                        def a0_stats(i):
                            p2 = i % 2
                            xi = xt[i % 4]
                            xn = "xt%d" % (i % 4)
                            sq, sqn = sqs[p2], "sq%d" % p2
                            st4, stn = st4s[p2], "st4%d" % p2
                            dg, dgn = dgs[p2], "dg%d" % p2
                            rbc, rbn = rbcs[p2], "rbc%d" % p2
                            rb = 2 + p2
                            V(lambda e: e.tensor_reduce(out=st4[:, 0:1], in_=sq[:], axis=AX.X, op=ALU.add), r=[sqn], w=[stn])
                            V(lambda e: e.tensor_scalar(out=st4[:, 1:2], in0=st4[:, 0:1], scalar1=1.0 / DM, scalar2=EPS,
                                                        op0=ALU.mult, op1=ALU.add), r=[stn], w=[stn])
                            G(lambda e: e.tensor_tensor(out=st4[:, 3:4], in0=st4[:, 1:2], in1=cneg[:, 0:1], op=ALU.pow),
                              r=[stn, "cneg"], w=[stn])

                        def a0_stats_b(i):
                            p2 = i % 2
                            st4, stn = st4s[p2], "st4%d" % p2
                            xi = xt[i % 4]
                            xn = "xt%d" % (i % 4)
                            A(lambda e: e.activation(out=xi[:], in_=xi[:], func=AF.Copy, scale=st4[:, 3:4]), r=[xn, stn], w=[xn])

                        def a0_xpose(i):
                            p2 = i % 2
                            xi = xt[i % 4]
                            xn = "xt%d" % (i % 4)
                            for half in range(2):
                                tbk = 4 * p2 + half
                                for q4 in range(4):
                                    kc = half * 4 + q4
                                    T(lambda e, kc=kc, q4=q4, tbk=tbk: e.transpose(
                                        out=PS[tbk][:, q4 * 128:(q4 + 1) * 128], in_=xi[:, kc * 128:(kc + 1) * 128], identity=ident[:]),
                                      r=[xn, "ident"], w=[psn[tbk]])
                                if half == 0:
                                    V(lambda e, half=half, tbk=tbk: e.tensor_copy(
                                        out=xT[:, half * 4:half * 4 + 4, i * 128:(i + 1) * 128],
                                        in_=PS[tbk][:].rearrange("p (k t) -> p k t", k=4)), r=[psn[tbk]], w=["xT"])
                                else:
                                    A(lambda e, half=half, tbk=tbk: e.copy(
                                        out=xT[:, half * 4:half * 4 + 4, i * 128:(i + 1) * 128],
                                        in_=PS[tbk][:].rearrange("p (k t) -> p k t", k=4)), r=[psn[tbk]], w=["xT"])

                        a0_load(0)
                        a0_load(1)
                        a0_stats(0)
                        a0_stats_b(0)
                        for i in range(16):
                            if i + 2 < 16:
                                a0_load(i + 2)
                            if i + 1 < 16:
                                a0_stats(i + 1)
                            a0_xpose(i)
                            if i + 1 < 16:
                                a0_stats_b(i + 1)
                        S.barrier()
                    if debug and s == 0:
                        with ExitStack() as dd:
                            d32 = mk(dd)("d32", [128, 8, SEQ])
                            V(lambda e: e.tensor_copy(out=d32[:], in_=xT[:]), r=["xT"], w=["d32"])
                            D(lambda e: e.dma_start(out=dbg["xT"], in_=d32[:]), r=["d32"])
                            S.barrier()

                    ck("A0")
                    with ExitStack() as bs:
                        sbB = mk(bs)
                        qT = [sbB("qT%d" % i, [128, SEQ], BF16) for i in range(2)]
                        kT = [sbB("kT%d" % i, [128, SEQ], BF16) for i in range(2)]
                        gaT = [sbB("gaT%d" % i, [128, SEQ], BF16) for i in range(2)]
                        Vaug = [sbB("Vaug%d" % i, [128, 16, 2, 128], BF16) for i in range(2)]
                        wB = [sbB("wB%d" % i, [128, 8, 128], BF16) for i in range(8)]
                        wst[0] = sbB("wstaB", [128, 8, 128])
                        sqb = [sbB("sqb%d" % i, [128, 512], BF16) for i in range(2)]
                        tt = [sbB("tt%d" % i, [128, 512]) for i in range(2)]
                        NE = 4
                        Eb = [sbB("Eb%d" % i, [128, 512], BF16) for i in range(NE)]
                        rd = [sbB("rd%d" % i, [128, 512]) for i in range(2)]
                        ot = [sbB("ot%d" % i, [128, 512]) for i in range(2)]
                        oc = [sbB("oc%d" % i, [128, 512]) for i in range(2)]
                        SCB = (4, 5, 6, 7)
                        for i in range(2):
                            V(lambda e, i=i: e.memset(Vaug[i][:], 1.0), w=["Vaug%d" % i])

                        def load_wB(c0, slot):
                            i = 0
                            D(lambda e: e.dma_start(out=wst[i][:], in_=win_d[:, c0:c0 + 128].rearrange("(k p) c -> p k c", p=128)),
                              w=["wsta%d" % i])
                            G(lambda e: e.tensor_tensor(out=wB[slot][:], in0=wst[i][:], in1=mixg[:].to_broadcast([128, 8, 128]),
                                                        op=ALU.mult), r=["wsta%d" % i, "mixg"], w=["wB%d" % slot])
                            return wB[slot], "wB%d" % slot

                        def prep_units(hp):
                            par = hp % 2
                            wq, wqn = load_wB(hp * 128, par * 4 + 0)
                            wk, wkn = load_wB(512 + hp * 128, par * 4 + 1)
                            yield
                            wv, wvn = load_wB(1024 + hp * 128, par * 4 + 2)
                            wa, wan = load_wB(1536 + hp * 128, par * 4 + 3)
                            yield
                            blocks = []
                            for (w_, wn_, dstT, dn, gain, gn) in ((wq, wqn, qT[par], "qT%d" % par, qg, "qg"),
                                                                  (wk, wkn, kT[par], "kT%d" % par, kg, "kg")):
                                for tg in range(4):
                                    blocks.append((w_, wn_, dstT, dn, gain, gn, tg))

                            def S1(bi):
                                w_, wn_, dstT, dn, gain, gn, tg = blocks[bi]
                                pbk = bi % 2
                                for kc in range(8):
                                    T(lambda e, kc=kc: e.matmul(PS[pbk][:], lhsT=w_[:, kc, :], rhs=xT[:, kc, tg * 512:(tg + 1) * 512],
                                                                start=(kc == 0), stop=(kc == 7)), r=[wn_, "xT"], w=[psn[pbk]])

                            def S2(bi):
                                pbk = bi % 2
                                sq_, sqn = sqb[bi % 2], "sqb%d" % (bi % 2)
                                t_, tn = tt[bi % 2], "tt%d" % (bi % 2)
                                A(lambda e: e.activation(out=sq_[:], in_=PS[pbk][:], func=AF.Square), r=[psn[pbk]], w=[sqn])
                                T(lambda e: e.matmul(PS[2][:], lhsT=blk1[:], rhs=sq_[:], start=True, stop=True),
                                  r=["blk1", sqn], w=[psn[2]])
                                V(lambda e: e.tensor_scalar(out=t_[:], in0=PS[2][:], scalar1=1.0 / 64.0, scalar2=EPS, op0=ALU.mult,
                                                            op1=ALU.add), r=[psn[2]], w=[tn])

                            def S3(bi):
                                w_, wn_, dstT, dn, gain, gn, tg = blocks[bi]
                                pbk = bi % 2
                                t_, tn = tt[bi % 2], "tt%d" % (bi % 2)
                                A(lambda e: e.activation(out=t_[:], in_=t_[:], func=AF.Ln), r=[tn], w=[tn])
                                A(lambda e: e.activation(out=t_[:], in_=t_[:], func=AF.Exp, scale=-0.5), r=[tn], w=[tn])
                                V(lambda e: e.scalar_tensor_tensor(out=dstT[:, tg * 512:(tg + 1) * 512], in0=PS[pbk][:],
                                                                   scalar=gain[:, 0:1], in1=t_[:], op0=ALU.mult, op1=ALU.mult),
                                  r=[psn[pbk], gn, tn], w=[dn])

                            for t in range(8 + 2):
                                if 0 <= t - 2 < 8:
                                    S3(t - 2)
                                if 0 <= t - 1 < 8:
                                    S2(t - 1)
                                if t < 8:
                                    S1(t)
                                yield

                            def G1(tg):
                                pbk = tg % 2
                                for kc in range(8):
                                    T(lambda e, kc=kc: e.matmul(PS[pbk][:], lhsT=wa[:, kc, :], rhs=xT[:, kc, tg * 512:(tg + 1) * 512],
                                                                start=(kc == 0), stop=(kc == 7)), r=[wan, "xT"], w=[psn[pbk]])

                            def G2(tg):
                                pbk = tg % 2
                                t_, tn = tt[tg % 2], "tt%d" % (tg % 2)
                                A(lambda e: e.activation(out=t_[:], in_=PS[pbk][:], func=AF.Exp, scale=-1.0), r=[psn[pbk]], w=[tn])
                                A(lambda e: e.activation(out=t_[:], in_=t_[:], func=AF.Ln, bias=1.0), r=[tn], w=[tn])
                                A(lambda e: e.activation(out=t_[:], in_=t_[:], func=AF.Exp, scale=-1.0), r=[tn], w=[tn])

                            def G3(tg):
                                pbk = tg % 2
                                t_, tn = tt[tg % 2], "tt%d" % (tg % 2)
                                V(lambda e: e.tensor_tensor(out=gaT[par][:, tg * 512:(tg + 1) * 512], in0=PS[pbk][:], in1=t_[:],
                                                            op=ALU.mult), r=[tn, psn[pbk]], w=["gaT%d" % par])

                            for t in range(4 + 2):
                                if 0 <= t - 2 < 4:
                                    G3(t - 2)
                                if 0 <= t - 1 < 4:
                                    G2(t - 1)
                                if t < 4:
                                    G1(t)
                                yield

                            def V1(i4):
                                pbk = i4 % 2
                                for ii in range(4):
                                    i = i4 * 4 + ii
                                    for kc in range(8):
                                        T(lambda e, kc=kc, i=i, ii=ii: e.matmul(
                                            PS[pbk][:, ii * 128:(ii + 1) * 128], lhsT=xT[:, kc, i * 128:(i + 1) * 128], rhs=wv[:, kc, :],
                                            start=(kc == 0), stop=(kc == 7)), r=[wvn, "xT"], w=[psn[pbk]])

                            def V2(i4):
                                pbk = i4 % 2
                                pv = PS[pbk][:].rearrange("p (i c) -> p i c", i=4)
                                V(lambda e: e.tensor_copy(out=Vaug[par][:, i4 * 4:i4 * 4 + 4, 0, 0:64], in_=pv[:, :, 0:64]),
                                  r=[psn[pbk]], w=["Vaug%d" % par])
                                A(lambda e: e.copy(out=Vaug[par][:, i4 * 4:i4 * 4 + 4, 1, 64:128], in_=pv[:, :, 64:128]),
                                  r=[psn[pbk]], w=["Vaug%d" % par])

                            for t in range(4 + 1):
                                if 0 <= t - 1 < 4:
                                    V2(t - 1)
                                if t < 4:
                                    V1(t)
                                yield

                        def core_units(hp):
                            par = hp % 2
                            qn_, kn_, gn_, vn_ = "qT%d" % par, "kT%d" % par, "gaT%d" % par, "Vaug%d" % par
                            blocks = []
                            for h2 in range(2):
                                for Qg in range(4):
                                    nkb = 4 * Qg + 4
                                    for kb in range(nkb):
                                        blocks.append((h2, Qg, kb, nkb))
                            N = len(blocks)
                            NP = N // 2
                            for pi in range(NP + 1):
                                if pi < NP:
                                    qk_ops = []
                                    for idx in (2 * pi, 2 * pi + 1):
                                        h2, Qg, kb, nkb = blocks[idx]
                                        pb = 64 * h2
                                        sbk = SCB[idx % 4]
                                        off = Qg * 512 - kb * 128 + 384
                                        c0 = max(0, kb - 4 * Qg) * 128
                                        qk_ops.append((lambda e, kb=kb, Qg=Qg, sbk=sbk, pb=pb, c0=c0: e.matmul(
                                            PS[sbk][:, c0:512], lhsT=kT[par][pb:pb + 64, kb * 128:(kb + 1) * 128],
                                            rhs=qT[par][pb:pb + 64, Qg * 512 + c0:(Qg + 1) * 512], start=True, stop=False),
                                            [kn_, qn_], [psn[sbk]]))
                                        qk_ops.append((lambda e, sbk=sbk, off=off, c0=c0: e.matmul(
                                            PS[sbk][:, c0:512], lhsT=(identr[:].bitcast(F32R) if EXACT_MASK else identr[:]),
                                            rhs=(MT[:, off + c0:off + 512].bitcast(F32R) if EXACT_MASK else MT[:, off + c0:off + 512]),
                                            start=False, stop=True), ["identr", "MT"], [psn[sbk]]))
                                    S.group("pe", qk_ops)
                                if pi >= 1:
                                    pv_ops = []
                                    for j in (2 * pi - 2, 2 * pi - 1):
                                        h2, Qg, kb, nkb = blocks[j]
                                        eb = j % NE
                                        c0 = max(0, kb - 4 * Qg) * 128
                                        pv_ops.append((lambda e, kb=kb, h2=h2, eb=eb, nkb=nkb, c0=c0: e.matmul(
                                            PS[3][:, c0:512], lhsT=Vaug[par][:, kb, h2, :], rhs=Eb[eb][:, c0:512],
                                            start=(kb == 0), stop=(kb == nkb - 1)), [vn_, "Eb%d" % eb], [psn[3]]))
                                    pv_grp = pv_ops
                                else:
                                    pv_grp = None
                                if pi < NP:
                                    for idx in (2 * pi, 2 * pi + 1):
                                        h2, Qg, kb, nkb = blocks[idx]
                                        c0 = max(0, kb - 4 * Qg) * 128
                                        sbk = SCB[idx % 4]
                                        eb = idx % NE
                                        A(lambda e, sbk=sbk, eb=eb, c0=c0: e.activation(out=Eb[eb][:, c0:512], in_=PS[sbk][:, c0:512],
                                                                                        func=AF.Exp, scale=1.0 / ALPHA),
                                          r=[psn[sbk]], w=["Eb%d" % eb])
                                if pv_grp is not None:
                                    S.group("pe", pv_grp)
                                if pi >= 1:
                                    h2, Qg, kb, nkb = blocks[2 * pi - 1]
                                    ob = 3
                                    if kb == nkb - 1:
                                        fi = (h2 * 4 + Qg) % 2
                                        rd_, rdn = rd[fi], "rd%d" % fi
                                        ot_, otn = ot[fi], "ot%d" % fi
                                        oc_, ocn = oc[fi], "oc%d" % fi
                                        dlo, olo = (64, 0) if h2 == 0 else (0, 64)
                                        V(lambda e, oc_=oc_: e.tensor_copy(out=oc_[:], in_=PS[ob][:]), r=[psn[ob]], w=[ocn])
                                        A(lambda e, dlo=dlo, olo=olo, rd_=rd_, oc_=oc_: e.activation(
                                            out=rd_[olo:olo + 64, :], in_=oc_[dlo:dlo + 64, :], func=AF.Ln), r=[ocn], w=[rdn])
                                        A(lambda e, olo=olo, rd_=rd_: e.activation(out=rd_[olo:olo + 64, :], in_=rd_[olo:olo + 64, :],
                                                                                   func=AF.Exp, scale=-1.0), r=[rdn], w=[rdn])
                                        V(lambda e, olo=olo, rd_=rd_, ot_=ot_, oc_=oc_: e.tensor_tensor(
                                            out=ot_[olo:olo + 64, :], in0=oc_[olo:olo + 64, :], in1=rd_[olo:olo + 64, :],
                                            op=ALU.mult), r=[ocn, rdn], w=[otn])
                                        G(lambda e, olo=olo, Qg=Qg, ot_=ot_: e.tensor_tensor(
                                            out=attnT[olo:olo + 64, hp, Qg * 512:(Qg + 1) * 512], in0=ot_[olo:olo + 64, :],
                                            in1=gaT[par][olo:olo + 64, Qg * 512:(Qg + 1) * 512], op=ALU.mult),
                                          r=[otn, gn_], w=["attnT"])
                                yield

                        for _ in prep_units(0):
                            pass
                        for hp in range(4):
                            nxt = prep_units(hp + 1) if hp < 3 else None
                            for si, _ in enumerate(core_units(hp)):
                                if nxt is not None and (si % 2 == 1 or si in (0, 2, 4, 6)):
                                    try:
                                        next(nxt)
                                    except StopIteration:
                                        nxt = None
                            if nxt is not None:
                                for _ in nxt:
                                    pass
                        S.barrier()
                    ck("B")
                    with ExitStack() as cs_:
                        sbC0 = mk(cs_)
                        ygT = sbC0("ygT", [128, 4, SEQ], BF16)
                        c12 = ExitStack()
                        sbC = mk(c12)
                        Ug = sbC("Ug", [128, 32, 256], BF16)
                        Sb = sbC("Sb", [128, 16, 2, 16, 16])
                        sc_scope = ExitStack()
                        scA = [[mk(sc_scope)("scA%d_%d" % (c, i), [128, (10, 6)[c], 2, 16]) for i in range(2)] for c in range(2)]
                        u_scope = ExitStack()
                        sbU = mk(u_scope)
                        U = sbU("U", [128, 4, 8, 8, 16])
                        wst[0] = sbU("wstaC", [128, 8, 128])
                        wu_all = sbU("wu_all", [128, 8, 512], BF16)
                        for cb in range(4):
                            c0w = 2048 + cb * 128
                            D(lambda e, c0w=c0w: e.dma_start(out=wst[0][:], in_=win_d[:, c0w:c0w + 128].rearrange("(k p) c -> p k c", p=128)),
                              w=["wsta0"])
                            G(lambda e, cb=cb: e.tensor_tensor(out=wu_all[:, :, cb * 128:(cb + 1) * 128], in0=wst[0][:],
                                                               in1=mixg[:].to_broadcast([128, 8, 128]), op=ALU.mult),
                              r=["wsta0", "mixg"], w=["wu_all"])
                        for ct in range(2):
                            for sp_ in range(8):
                                bank = sp_ % 2
                                for kc in range(8):
                                    T(lambda e, kc=kc, sp_=sp_, bank=bank: e.matmul(
                                        PS[bank][:], lhsT=xT[:, kc, ct * 1024 + sp_:(ct + 1) * 1024:8], rhs=wu_all[:, kc, :],
                                        start=(kc == 0), stop=(kc == 7)), r=["wu_all", "xT"], w=[psn[bank]])
                                fev = (lambda e, sp_=sp_, bank=bank: e.copy(
                                    out=U[:, :, :, sp_, :], in_=PS[bank][:].rearrange("p (b g c) -> p b g c", b=4, g=8)))
                                A(fev, r=[psn[bank]], w=["U"])
                            for g4 in range(8):
                                bank = 2 + g4 % 2
                                for gi in range(4):
                                    g = g4 * 4 + gi
                                    T(lambda e, g=g, gi=gi, bank=bank: e.transpose(
                                        out=PS[bank][:, gi * 128:(gi + 1) * 128],
                                        in_=U[:, g // 8, g % 8, :, :].rearrange("p s c -> p (s c)"), identity=ident[:]),
                                      r=["U", "ident"], w=[psn[bank]])
                                V(lambda e, g4=g4, bank=bank: e.tensor_copy(out=Ug[:, g4 * 4:g4 * 4 + 4, ct * 128:(ct + 1) * 128],
                                                                            in_=PS[bank][:].rearrange("p (g k) -> p g k", g=4)),
                                  r=[psn[bank]], w=["Ug"])
                        for gp in range(16):
                            bank = 4 + gp % 2
                            for j in range(2):
                                g = 2 * gp + j
                                T(lambda e, g=g, j=j, bank=bank: e.matmul(PS[bank][64 * j:64 * j + 64, 0:256], lhsT=PGre[:, g, :],
                                                                          rhs=Ug[:, g, :], start=True, stop=True),
                                  r=["PGre", "Ug"], w=[psn[bank]])
                                T(lambda e, g=g, j=j, bank=bank: e.matmul(PS[bank][64 * j:64 * j + 64, 256:512], lhsT=PGim[:, g, :],
                                                                          rhs=Ug[:, g, :], start=True, stop=True),
                                  r=["PGim", "Ug"], w=[psn[bank]])
                            A(lambda e, gp=gp, bank=bank: e.copy(out=Sb[:, gp, :, :, :],
                                                                 in_=PS[bank][:].rearrange("p (c K j) -> p c j K", c=2, K=16)),
                              r=[psn[bank]], w=["Sb%d" % (0 if gp < 10 else 1)])
                        S.barrier()
                        u_scope.close()
                        ck("C1")
                        SPLIT = ((0, 10, "dve"), (10, 16, "pool"))
                        for ci, (g0, g1, eng) in enumerate(SPLIT):
                            ng = g1 - g0
                            ta, tb_ = scA[ci]
                            na, nb = "scA%d_0" % ci, "scA%d_1" % ci
                            rn = "Sb%d" % ci
                            def colv(j, k0, nk, rev=False, g0=g0, g1=g1):
                                cs_ = slice(None, None, -1) if rev else slice(None)
                                return Sb[:, g0:g1, cs_, j, k0:k0 + nk]
                            for j in range(1, 16):
                                prev, prev_sw, cur = colv(j - 1, 0, 16), colv(j - 1, 0, 16, True), colv(j, 0, 16)
                                ar = APR[:, 0, g0:g1, :].unsqueeze(3).to_broadcast([128, ng, 2, 16])
                                ai = API[:, 0, g0:g1, :].unsqueeze(3).to_broadcast([128, ng, 2, 16])
                                S.op(eng, lambda e, prev=prev, ar=ar: e.tensor_tensor(out=ta[:], in0=prev, in1=ar, op=ALU.mult),
                                     reads=[rn, "APR"], writes=[na])
                                S.op(eng, lambda e, prev_sw=prev_sw, ai=ai: e.tensor_tensor(out=tb_[:], in0=prev_sw, in1=ai, op=ALU.mult),
                                     reads=[rn, "API"], writes=[nb])
                                S.op(eng, lambda e: e.tensor_tensor(out=ta[:], in0=ta[:], in1=tb_[:], op=ALU.add), reads=[na, nb], writes=[na])
                                S.op(eng, lambda e, cur=cur: e.tensor_tensor(out=cur, in0=cur, in1=ta[:], op=ALU.add), reads=[na, rn], writes=[rn])
                            for K in range(1, 16):
                                prev, prev_sw, cur = colv(15, K - 1, 1), colv(15, K - 1, 1, True), colv(15, K, 1)
                                ar = APR[:, 15, g0:g1, :].unsqueeze(3)
                                ai = API[:, 15, g0:g1, :].unsqueeze(3)
                                S.op(eng, lambda e, prev=prev, ar=ar: e.tensor_tensor(out=ta[:, :, :, 0:1], in0=prev, in1=ar, op=ALU.mult),
                                     reads=[rn, "APR"], writes=[na])
                                S.op(eng, lambda e, prev_sw=prev_sw, ai=ai: e.tensor_tensor(out=tb_[:, :, :, 0:1], in0=prev_sw, in1=ai,
                                                                                         op=ALU.mult), reads=[rn, "API"], writes=[nb])
                                S.op(eng, lambda e: e.tensor_tensor(out=ta[:, :, :, 0:1], in0=ta[:, :, :, 0:1], in1=tb_[:, :, :, 0:1],
                                                                    op=ALU.add), reads=[na, nb], writes=[na])
                                S.op(eng, lambda e, cur=cur: e.tensor_tensor(out=cur, in0=cur, in1=ta[:, :, :, 0:1], op=ALU.add),
                                     reads=[na, rn], writes=[rn])
                            for j in range(15):
                                prev, prev_sw, cur = colv(15, 0, 15), colv(15, 0, 15, True), colv(j, 1, 15)
                                ar = APR[:, j, g0:g1, :].unsqueeze(3).to_broadcast([128, ng, 2, 15])
                                ai = API[:, j, g0:g1, :].unsqueeze(3).to_broadcast([128, ng, 2, 15])
                                S.op(eng, lambda e, prev=prev, ar=ar: e.tensor_tensor(out=ta[:, :, :, 0:15], in0=prev, in1=ar, op=ALU.mult),
                                     reads=[rn, "APR"], writes=[na])
                                S.op(eng, lambda e, prev_sw=prev_sw, ai=ai: e.tensor_tensor(out=tb_[:, :, :, 0:15], in0=prev_sw, in1=ai,
                                                                                         op=ALU.mult), reads=[rn, "API"], writes=[nb])
                                S.op(eng, lambda e: e.tensor_tensor(out=ta[:, :, :, 0:15], in0=ta[:, :, :, 0:15], in1=tb_[:, :, :, 0:15],
                                                                    op=ALU.add), reads=[na, nb], writes=[na])
                                S.op(eng, lambda e, cur=cur: e.tensor_tensor(out=cur, in0=cur, in1=ta[:, :, :, 0:15], op=ALU.add),
                                     reads=[na, rn], writes=[rn])
                        S.barrier()
                        sc_scope.close()
                        Sh = sbC("Sh", [128, 16, 2, 256], BF16)
                        for ci, (g0, g1, eng) in enumerate(SPLIT):
                            rn = "Sb%d" % ci
                            S.op(eng, lambda e, g0=g0, g1=g1: e.memset(Sh[:, g0:g1, :, 0:1], 0.0), writes=["Sh%d" % ci])
                            for c in range(2):
                                S.op(eng, lambda e, g0=g0, g1=g1, c=c: e.tensor_copy(
                                    out=Sh[:, g0:g1, c, 1:241].rearrange("p g (K j) -> p g K j", j=16),
                                    in_=Sb[:, g0:g1, c, :, 0:15].rearrange("p g j K -> p g K j")), reads=[rn], writes=["Sh%d" % ci])
                                S.op(eng, lambda e, g0=g0, g1=g1, c=c: e.tensor_copy(
                                    out=Sh[:, g0:g1, c, 241:256], in_=Sb[:, g0:g1, c, 0:15, 15]), reads=[rn], writes=["Sh%d" % ci])
                        ck("C2")
                        with ExitStack() as c3:
                            sb3 = mk(c3)
                            Ysbs = [sb3("Ysb%d" % i, [128, 2, 128]) for i in range(2)]
                            YG = sb3("YG", [128, 8, 512])
                            def mm3(ct, gp):
                                bank = gp % 2
                                for j in range(2):
                                    g = 2 * gp + j
                                    pb = 64 * j
                                    o_ = PS[bank][:, j * 128:(j + 1) * 128]
                                    T(lambda e, g=g, o_=o_: e.matmul(o_, lhsT=Tm[:, g, :], rhs=Ug[:, g, ct * 128:(ct + 1) * 128],
                                                                     start=True, stop=False), r=["Tm", "Ug"], w=[psn[bank]])
                                    T(lambda e, pb=pb, o_=o_: e.matmul(
                                        o_, lhsT=PCre[pb:pb + 64, gp, :], rhs=Sh[pb:pb + 64, gp, 0, ct * 128:(ct + 1) * 128],
                                        start=False, stop=False), r=["PCre", "Sh0", "Sh1"], w=[psn[bank]])
                                    T(lambda e, pb=pb, o_=o_: e.matmul(
                                        o_, lhsT=PCim[pb:pb + 64, gp, :], rhs=Sh[pb:pb + 64, gp, 1, ct * 128:(ct + 1) * 128],
                                        start=False, stop=True), r=["PCim", "Sh0", "Sh1"], w=[psn[bank]])

                            def ev3(ct, gp):
                                bank = gp % 2
                                Ysb, Ysn = Ysbs[gp % 2], "Ysb%d" % (gp % 2)
                                A(lambda e: e.copy(out=Ysb[:], in_=PS[bank][:, 0:256].rearrange("p (j k) -> p j k", j=2)),
                                  r=[psn[bank]], w=[Ysn])
                                tb = 2 + gp % 2
                                for j in range(2):
                                    T(lambda e, j=j: e.transpose(out=PS[tb][:, j * 128:(j + 1) * 128], in_=Ysb[:, j, :],
                                                                 identity=ident[:]), r=[Ysn, "ident"], w=[psn[tb]])
                                for j in range(2):
                                    g = 2 * gp + j
                                    A(lambda e, j=j, g=g: e.activation(
                                        out=YG[:, :, g * 16:(g + 1) * 16],
                                        in_=PS[tb][:, j * 128:(j + 1) * 128].rearrange("p (t c) -> p t c", t=8),
                                        func=AF.Gelu), r=[psn[tb]], w=["YG"])

                            for ct in range(2):
                                mm3(ct, 0)
                                for gp in range(16):
                                    if gp + 1 < 16:
                                        mm3(ct, gp + 1)
                                    ev3(ct, gp)
                                for tp in range(8):
                                    bank = 4 + tp % 2
                                    for cbi in range(4):
                                        T(lambda e, tp=tp, cbi=cbi, bank=bank: e.transpose(
                                            out=PS[bank][:, cbi * 128:(cbi + 1) * 128], in_=YG[:, tp, cbi * 128:(cbi + 1) * 128],
                                            identity=ident[:]), r=["YG", "ident"], w=[psn[bank]])
                                    col = (ct * 8 + tp) * 128
                                    V(lambda e, bank=bank, col=col: e.tensor_copy(out=ygT[:, :, col:col + 128],
                                                                                  in_=PS[bank][:].rearrange("p (a t) -> p a t", a=4)),
                                      r=[psn[bank]], w=["ygT"])
                            S.barrier()
                        S.barrier()
                        c12.close()
                        ck("C3")
                        with ExitStack() as c5:
                            sb5 = mk(c5)
                            wgl = sb5("wgl", [128, 4, 512], BF16)
                            wst[0] = sb5("wsta5", [128, 8, 128])
                            wbf[0] = sb5("wbf5a", [128, 8, 128], BF16)
                            wbf[1] = sb5("wbf5b", [128, 8, 128], BF16)
                            wgs_st = sb5("wgs_st", [128, 512])
                            sig = [sb5("sig%d" % i, [128, 512]) for i in range(2)]
                            gsl = [sb5("gsl%d" % i, [128, 512], BF16) for i in range(2)]
                            gth = [sb5("gth%d" % i, [128, 512]) for i in range(2)]
                            tm5 = [sb5("tm5%d" % i, [128, 512]) for i in range(2)]
                            for kc in range(4):
                                D(lambda e, kc=kc: e.dma_start(out=wgs_st[:], in_=wglu_d[kc * 128:(kc + 1) * 128, :]), w=["wgs_st"])
                                V(lambda e, kc=kc: e.tensor_copy(out=wgl[:, kc, :], in_=wgs_st[:]), r=["wgs_st"], w=["wgl"])
                            steps = [(cb, tg) for cb in range(4) for tg in range(4)]
                            wcur = {}

                            def pvw(ap, half):
                                return ap.rearrange("p (t q) -> p t q", t=8)[:, :, half * 64:(half + 1) * 64]

                            def mm5(si):
                                cb, tg = steps[si]
                                if tg == 0:
                                    wcur[cb] = load_w(2560 + cb * 128, cb % 2)
                                wgs, wgsn = wcur[cb]
                                ct, half = tg // 2, tg % 2
                                b0, b1 = 2 * (si % 4), 2 * (si % 4) + 1
                                for kc in range(4):
                                    T(lambda e, kc=kc: e.matmul(PS[b0][:], lhsT=wgl[:, kc, cb * 128:(cb + 1) * 128],
                                                                rhs=pvw(ygT[:, kc, ct * 1024:(ct + 1) * 1024], half),
                                                                start=(kc == 0), stop=(kc == 3)), r=["wgl", "ygT"], w=[psn[b0]])
                                for kc in range(8):
                                    T(lambda e, kc=kc: e.matmul(PS[b1][:], lhsT=wgs[:, kc, :], rhs=xT[:, kc, tg * 512:(tg + 1) * 512],
                                                                start=(kc == 0), stop=(kc == 7)), r=[wgsn, "xT"], w=[psn[b1]])

                            def ev5(si):
                                cb, tg = steps[si]
                                ct, half = tg // 2, tg % 2
                                i2 = si % 2
                                b0, b1 = 2 * (si % 4), 2 * (si % 4) + 1
                                sg, sgn = sig[i2], "sig%d" % i2
                                gh, ghn = gth[i2], "gth%d" % i2
                                gl, gln = gsl[i2], "gsl%d" % i2
                                t5, t5n = tm5[i2], "tm5%d" % i2
                                A(lambda e: e.activation(out=sg[:], in_=PS[b0][:], func=AF.Sigmoid, bias=bglu[:, cb, :]),
                                  r=[psn[b0], "bglu"], w=[sgn])
                                A(lambda e: e.activation(out=gh[:], in_=PS[b1][:], func=AF.Sigmoid), r=[psn[b1]], w=[ghn])
                                V(lambda e: e.tensor_tensor(out=gl[:].rearrange("p (t q) -> p q t", t=8),
                                                            in0=PS[b1][:].rearrange("p (q t) -> p q t", t=8),
                                                            in1=gh[:].rearrange("p (q t) -> p q t", t=8), op=ALU.mult),
                                  r=[ghn, psn[b1]], w=[gln])
                                G(lambda e: e.tensor_tensor(out=t5[:].rearrange("p (t q) -> p t q", t=8),
                                                            in0=pvw(ygT[:, cb, ct * 1024:(ct + 1) * 1024], half),
                                                            in1=sg[:].rearrange("p (t q) -> p t q", t=8), op=ALU.mult),
                                  r=["ygT", sgn], w=[t5n])
                                G(lambda e: e.tensor_tensor(out=pvw(ssmT[:, cb, ct * 1024:(ct + 1) * 1024], half),
                                                            in0=t5[:].rearrange("p (t q) -> p t q", t=8),
                                                            in1=gl[:].rearrange("p (t q) -> p t q", t=8), op=ALU.mult),
                                  r=[t5n, gln], w=["ssmT"])

                            mm5(0)
                            mm5(1)
                            mm5(2)
                            for si in range(16):
                                if si + 3 < 16:
                                    mm5(si + 3)
                                ev5(si)
                            if debug and s == 0:
                                with ExitStack() as dd:
                                    d32 = mk(dd)("d32y", [128, 4, SEQ])
                                    for nm, src in (("attnT", attnT), ("ssmT", ssmT), ("ygT", ygT)):
                                        S.barrier()
                                        V(lambda e, src=src: e.tensor_copy(out=d32[:], in_=src[:]), w=["d32y"])
                                        D(lambda e, nm=nm: e.dma_start(out=dbg[nm], in_=d32[:]), r=["d32y"])
                                    S.barrier()
                            S.barrier()
                    S.barrier()

                ck("C5")
                S.barrier()
                with ExitStack() as ds:
                    sbD = mk(ds)
                    Wout = sbD("Wout", [128, 8, DM], BF16)
                    Wg = sbD("Wg", [128, 8, DM], BF16)
                    Wp = sbD("Wp", [128, 2, DM], BF16)
                    wstd = [sbD("wstd%d" % i, [128, DM]) for i in range(2)]
                    cnt = 0
                    for (src, dst, dn_, nk, fold) in ((wout_d, Wout, "Wout", 8, None), (wg_d, Wg, "Wg", 8, pleg), (wp_d, Wp, "Wp", 2, None)):
                        for kc in range(nk):
                            st = wstd[cnt % 2]
                            rn = "wstd%d" % (cnt % 2)
                            D(lambda e, st=st, src=src, kc=kc: e.dma_start(out=st[:], in_=src[kc * 128:(kc + 1) * 128, :]), w=[rn])
                            if fold is None:
                                if cnt % 2 == 0:
                                    V(lambda e, st=st, dst=dst, kc=kc: e.tensor_copy(out=dst[:, kc, :], in_=st[:]), r=[rn], w=[dn_])
                                else:
                                    A(lambda e, st=st, dst=dst, kc=kc: e.copy(out=dst[:, kc, :], in_=st[:]), r=[rn], w=[dn_])
                            else:
                                V(lambda e, st=st, dst=dst, kc=kc, fold=fold: e.tensor_scalar(
                                    out=dst[:, kc, :], in0=st[:], scalar1=fold[:, kc, :], scalar2=None, op0=ALU.mult),
                                  r=[rn, "pleg"], w=[dn_])
                            cnt += 1
                    xd = [sbD("xd%d" % i, [128, DM]) for i in range(3)]
                    pd = [sbD("pd%d" % i, [128, 256]) for i in range(3)]
                    hh = [sbD("hh%d" % i, [128, DM]) for i in range(2)]
                    sq = sbD("sqd", [128, DM])
                    st4 = [sbD("st4d%d" % i, [128, 4]) for i in range(2)]
                    hT = [sbD("hT%d" % i, [128, 8, 128], BF16) for i in range(2)]
                    pT = [sbD("pT%d" % i, [128, 2, 128], BF16) for i in range(2)]
                    gate = [sbD("gate%d" % i, [128, DM]) for i in range(2)]
                    oo = [sbD("oo%d" % i, [128, DM]) for i in range(2)]

                    def XL(it):
                        ct, tp = it // 8, it % 8
                        xi, xn = xd[it % 3], "xd%d" % (it % 3)
                        pi_, pn = pd[it % 3], "pd%d" % (it % 3)
                        r0 = ct * 1024 + tp
                        D(lambda e: e.dma_start(out=xi[:], in_=x_d[s, r0:(ct + 1) * 1024:8, :]), w=[xn])
                        D(lambda e: e.dma_start(out=pi_[:], in_=p_d[s, r0:(ct + 1) * 1024:8, :]), w=[pn])

                    def X1(it):
                        ct, tp = it // 8, it % 8
                        b2 = it % 2
                        xi, xn = xd[it % 3], "xd%d" % (it % 3)
                        h_, hn = hh[b2], "hh%d" % b2
                        s_, sn_ = st4[b2], "st4d%d" % b2
                        r0 = ct * 1024 + tp
                        col = it * 128
                        for nh in range(2):
                            for kc in range(8):
                                if kc < 4:
                                    lt = attnT[:, kc, r0:(ct + 1) * 1024:8]
                                    rn = "attnT"
                                else:
                                    lt = ssmT[:, kc - 4, col:col + 128]
                                    rn = "ssmT"
                                T(lambda e, lt=lt, kc=kc, nh=nh: e.matmul(PS[nh][:], lhsT=lt, rhs=Wout[:, kc, nh * 512:(nh + 1) * 512],
                                                                          start=(kc == 0), stop=(kc == 7)), r=[rn, "Wout"], w=[psn[nh]])
                            V(lambda e, nh=nh: e.tensor_tensor(out=h_[:, nh * 512:(nh + 1) * 512], in0=PS[nh][:],
                                                               in1=xi[:, nh * 512:(nh + 1) * 512], op=ALU.add),
                              r=[psn[nh], xn], w=[hn])
                        A(lambda e: e.activation(out=sq[:], in_=h_[:], func=AF.Square), r=[hn], w=["sqd"])
                        V(lambda e: e.tensor_reduce(out=s_[:, 0:1], in_=sq[:], axis=AX.X, op=ALU.add), r=["sqd"], w=[sn_])
                        V(lambda e: e.tensor_scalar(out=s_[:, 1:2], in0=s_[:, 0:1], scalar1=1.0 / DM, scalar2=EPS, op0=ALU.mult,
                                                    op1=ALU.add), r=[sn_], w=[sn_])
                        G(lambda e: e.tensor_tensor(out=s_[:, 3:4], in0=s_[:, 1:2], in1=cneg[:, 0:1], op=ALU.pow),
                          r=[sn_, "cneg"], w=[sn_])

                    def X2(it):
                        b2 = it % 2
                        pi_, pn = pd[it % 3], "pd%d" % (it % 3)
                        h_, hn = hh[b2], "hh%d" % b2
                        hT_, hTn = hT[b2], "hT%d" % b2
                        pT_, pTn = pT[b2], "pT%d" % b2
                        for half in range(2):
                            for q4 in range(4):
                                kc = half * 4 + q4
                                T(lambda e, kc=kc, q4=q4, half=half: e.transpose(out=PS[2 + half][:, q4 * 128:(q4 + 1) * 128],
                                                                                 in_=h_[:, kc * 128:(kc + 1) * 128], identity=ident[:]),
                                  r=[hn, "ident"], w=[psn[2 + half]])
                            A(lambda e, half=half: e.copy(out=hT_[:, half * 4:half * 4 + 4, :],
                                                          in_=PS[2 + half][:].rearrange("p (k t) -> p k t", k=4)),
                              r=[psn[2 + half]], w=[hTn])
                        for q2 in range(2):
                            T(lambda e, q2=q2: e.transpose(out=PS[4][:, q2 * 128:(q2 + 1) * 128], in_=pi_[:, q2 * 128:(q2 + 1) * 128],
                                                           identity=ident[:]), r=[pn, "ident"], w=[psn[4]])
                        V(lambda e: e.tensor_copy(out=pT_[:], in_=PS[4][:, 0:256].rearrange("p (k t) -> p k t", k=2)),
                          r=[psn[4]], w=[pTn])

                    def Y(it):
                        ct, tp = it // 8, it % 8
                        b2 = it % 2
                        h_, hn = hh[b2], "hh%d" % b2
                        s_, sn_ = st4[b2], "st4d%d" % b2
                        hT_, hTn = hT[b2], "hT%d" % b2
                        pT_, pTn = pT[b2], "pT%d" % b2
                        g_, gn = gate[b2], "gate%d" % b2
                        oi, on = oo[b2], "oo%d" % b2
                        r0 = ct * 1024 + tp
                        for nh in range(2):
                            for kc in range(8):
                                T(lambda e, kc=kc, nh=nh: e.matmul(PS[5][:], lhsT=hT_[:, kc, :], rhs=Wg[:, kc, nh * 512:(nh + 1) * 512],
                                                                   start=(kc == 0), stop=(kc == 7)), r=[hTn, "Wg"], w=[psn[5]])
                            A(lambda e, nh=nh: e.activation(out=g_[:, nh * 512:(nh + 1) * 512], in_=PS[5][:], func=AF.Sigmoid,
                                                            scale=s_[:, 3:4]), r=[psn[5], sn_], w=[gn])
                            for kc in range(2):
                                T(lambda e, kc=kc, nh=nh: e.matmul(PS[6 + nh][:], lhsT=pT_[:, kc, :], rhs=Wp[:, kc, nh * 512:(nh + 1) * 512],
                                                                   start=(kc == 0), stop=(kc == 1)), r=[pTn, "Wp"], w=[psn[6 + nh]])
                            V(lambda e, nh=nh: e.tensor_tensor(out=oi[:, nh * 512:(nh + 1) * 512], in0=PS[6 + nh][:],
                                                               in1=g_[:, nh * 512:(nh + 1) * 512], op=ALU.mult),
                              r=[psn[6 + nh], gn], w=[on])
                        G(lambda e: e.tensor_tensor(out=oi[:], in0=oi[:], in1=h_[:], op=ALU.add), r=[on, hn], w=[on])
                        D(lambda e: e.dma_start(out=out_d[s, r0:(ct + 1) * 1024:8, :], in_=oi[:]), r=[on])

                    XL(0)
                    XL(1)
                    X1(0)
                    X2(0)
                    for it in range(16):
                        if it + 2 < 16:
                            XL(it + 2)
                        if it + 1 < 16:
                            X1(it + 1)
                        Y(it)
                        if it + 1 < 16:
                            X2(it + 1)
                    S.barrier()
                    ck("D")
        except _Stop:
            pass
        S.barrier()
    return nc


_NC_CACHE = {}


def kernel(**inputs):
    f = lambda a: np.ascontiguousarray(np.asarray(a, dtype=np.float32))
    x = f(inputs["x"])
    p = f(inputs["p"])[0]
    shared = {}
    for k in ("mix_norm", "w_in", "q_norm", "k_norm", "lambda_re", "lambda_im", "log_dt", "b_re", "b_im", "c_re", "c_im",
              "d_skip", "w_glu", "b_glu", "w_out", "ple_norm", "w_ple_gate", "w_ple_proj"):
        shared[k] = f(inputs[k])[0]
    if "nc" not in _NC_CACHE:
        _NC_CACHE["nc"] = build_nc()
    nc = _NC_CACHE["nc"]
    in_maps = []
    for c in range(NCORES):
        m = {"x": np.ascontiguousarray(x[2 * c:2 * c + 2]), "p": np.ascontiguousarray(p[2 * c:2 * c + 2])}
        m.update(shared)
        in_maps.append(m)
    res = run_bass_kernel_spmd(nc, in_maps, core_ids=list(range(NCORES)))
    out = np.concatenate([np.asarray(r["out"], dtype=np.float32) for r in res.results], axis=0)
    return out
```

```python
import math
from contextlib import ExitStack
import numpy as np
import concourse.bass as bass
import concourse.mybir as mybir
from concourse.bass_utils import run_bass_kernel_spmd

F32 = mybir.dt.float32
BF16 = mybir.dt.bfloat16
F32R = mybir.dt.float32r
I32 = mybir.dt.int32
ALU = mybir.AluOpType
AF = mybir.ActivationFunctionType
AX = mybir.AxisListType

NCORES = 8
SEQ = 2048
DM = 1024
EPS = 1e-6
MAGIC = 12582912.0
TWO_PI = 2.0 * math.pi
EXACT_MASK = False
ALPHA = 1.0 if EXACT_MASK else 38.2305


class Sched:
    ENGS = ("pe", "act", "dve", "pool", "sp")

    def __init__(self, nc, n_dma_sems=24):
        self.nc = nc
        self.cnt = {e: 0 for e in self.ENGS}
        self.sem = {}
        self.seen = {e: {} for e in self.ENGS}
        self.last_w = {}
        self.readers = {}
        self.dma_sems = []
        self.dma_uses = []
        self.n_dma_sems = n_dma_sems
        self.dma_rr = 0

    def alloc(self, stack):
        for e in self.ENGS:
            if e == "sp":
                continue
            self.sem[e] = stack.enter_context(self.nc.semaphore("s_" + e))
        for i in range(self.n_dma_sems):
            self.dma_sems.append(stack.enter_context(self.nc.semaphore("d%d" % i)))
            self.dma_uses.append(0)

    def _need(self, eng, tok, waits):
        key, sem, val, src = tok
        if self.seen[eng].get(key, 0) >= val:
            return
        self.seen[eng][key] = val
        waits.append((sem, val))

    dead = False

    def op(self, eng, fn, reads=(), writes=(), dma=False):
        if self.dead:
            return None
        waits = []
        for r in reads:
            t = self.last_w.get(r)
            if t is not None:
                self._need(eng, t, waits)
        for w in writes:
            t = self.last_w.get(w)
            if t is not None and (dma or t[3] != eng or eng != "pe"):
                self._need(eng, t, waits)
            for t in self.readers.get(w, {}).values():
                if dma or t[3] != eng or eng != "pe":
                    self._need(eng, t, waits)
        if dma:
            j = self.dma_rr
            self.dma_rr = (self.dma_rr + 1) % self.n_dma_sems
            k = self.dma_uses[j]
            if k > 0:
                self._need(eng, ("d%d" % j, self.dma_sems[j], 16 * k, None), waits)
            self.dma_uses[j] = k + 1
            tok = ("d%d" % j, self.dma_sems[j], 16 * (k + 1), None)
            inc = (self.dma_sems[j], 16)
        else:
            self.cnt[eng] += 1
            tok = (eng, self.sem[eng], self.cnt[eng], eng)
            inc = (self.sem[eng], 1)
        for w in writes:
            self.last_w[w] = tok
            self.readers[w] = {}
        for r in reads:
            self.readers.setdefault(r, {})[tok[0]] = tok
        self._emit(eng, waits, fn, inc)
        return tok

    def group(self, eng, ops):
        if self.dead:
            return
        waits = []
        for fn, reads, writes in ops:
            for r in reads:
                t = self.last_w.get(r)
                if t is not None:
                    self._need(eng, t, waits)
            for w in writes:
                t = self.last_w.get(w)
                if t is not None and (t[3] != eng or eng != "pe"):
                    self._need(eng, t, waits)
                for t in self.readers.get(w, {}).values():
                    if t[3] != eng or eng != "pe":
                        self._need(eng, t, waits)
        first = True
        for fn, reads, writes in ops:
            self.cnt[eng] += 1
            tok = (eng, self.sem[eng], self.cnt[eng], eng)
            for w in writes:
                self.last_w[w] = tok
                self.readers[w] = {}
            for r in reads:
                self.readers.setdefault(r, {})[tok[0]] = tok
            self._emit(eng, waits if first else [], fn, (self.sem[eng], 1))
            first = False

    def _eng(self, name):
        nc = self.nc
        return {"pe": nc.tensor, "act": nc.scalar, "dve": nc.vector, "pool": nc.gpsimd, "sp": nc.sync}[name]

    def _emit(self, eng, waits, fn, inc):
        e = self._eng(eng)
        for sem, val in waits:
            e.wait_ge(sem, val)
        if fn is not None:
            ins = fn(e)
            ins.then_inc(inc[0], inc[1])

    def barrier(self):
        if self.dead:
            return
        toks = []
        for e in self.ENGS:
            if e != "sp" and self.cnt[e] > 0:
                toks.append((e, self.sem[e], self.cnt[e], e))
        for j in range(self.n_dma_sems):
            if self.dma_uses[j] > 0:
                toks.append(("d%d" % j, self.dma_sems[j], 16 * self.dma_uses[j], None))
        for e in self.ENGS:
            waits = []
            for t in toks:
                self._need(e, t, waits)
            if waits:
                self._emit(e, waits, None, None)


class _Stop(Exception):
    pass


def build_nc(debug=False, stop_after=None):
    nc = bass.Bass("TRN2", target_bir_lowering=False)

    def din(name, shape):
        return nc.dram_tensor(name, shape, F32, kind="ExternalInput").ap()

    x_d = din("x", [2, SEQ, DM])
    p_d = din("p", [2, SEQ, 256])
    mixn_d = din("mix_norm", [DM])
    win_d = din("w_in", [DM, 3072])
    qn_d = din("q_norm", [64])
    kn_d = din("k_norm", [64])
    lre_d = din("lambda_re", [32, 64])
    lim_d = din("lambda_im", [32, 64])
    ldt_d = din("log_dt", [32])
    bre_d = din("b_re", [32, 64, 16])
    bim_d = din("b_im", [32, 64, 16])
    cre_d = din("c_re", [32, 16, 64])
    cim_d = din("c_im", [32, 16, 64])
    dsk_d = din("d_skip", [512])
    wglu_d = din("w_glu", [512, 512])
    bglu_d = din("b_glu", [512])
    wout_d = din("w_out", [DM, DM])
    plen_d = din("ple_norm", [DM])
    wg_d = din("w_ple_gate", [DM, DM])
    wp_d = din("w_ple_proj", [256, DM])
    out_d = nc.dram_tensor("out", [2, SEQ, DM], F32, kind="ExternalOutput").ap()
    dbg = {}
    if debug:
        dbg["xT"] = nc.dram_tensor("dbg_xT", [128, 8, SEQ], F32, kind="ExternalOutput").ap()
        dbg["attnT"] = nc.dram_tensor("dbg_attnT", [128, 4, SEQ], F32, kind="ExternalOutput").ap()
        dbg["ssmT"] = nc.dram_tensor("dbg_ssmT", [128, 4, SEQ], F32, kind="ExternalOutput").ap()
        dbg["ygT"] = nc.dram_tensor("dbg_ygT", [128, 4, SEQ], F32, kind="ExternalOutput").ap()
        dbg["qT"] = nc.dram_tensor("dbg_qT", [128, SEQ], F32, kind="ExternalOutput").ap()
        dbg["T"] = nc.dram_tensor("dbg_T", [128, 32, 128], F32, kind="ExternalOutput").ap()
        dbg["S"] = nc.dram_tensor("dbg_S", [128, 16, 2, 257], F32, kind="ExternalOutput").ap()

    with ExitStack() as top:
        S = Sched(nc)
        S.alloc(top)
        top.enter_context(nc.allow_non_contiguous_dma(reason="small strided parameter loads"))

        uid = [0]

        def mk(stack):
            def sb(name, shape, dt=F32):
                uid[0] += 1
                return stack.enter_context(nc.sbuf_tensor("%s_u%d" % (name, uid[0]), shape, dt))
            return sb

        sbP = mk(top)
        PS = [top.enter_context(nc.psum_tensor("ps%d" % i, [128, 512], F32)) for i in range(8)]
        psn = ["ps%d" % i for i in range(8)]

        def V(fn, r=(), w=()):
            S.op("dve", fn, reads=r, writes=w)

        def A(fn, r=(), w=()):
            S.op("act", fn, reads=r, writes=w)

        def G(fn, r=(), w=()):
            S.op("pool", fn, reads=r, writes=w)

        def T(fn, r=(), w=()):
            S.op("pe", fn, reads=r, writes=w)

        def D(fn, r=(), w=()):
            S.op("sp", fn, reads=r, writes=w, dma=True)

        dq = [0]

        def D2(fn, r=(), w=()):
            dq[0] += 1
            S.op("sp" if dq[0] % 2 else "act", fn, reads=r, writes=w, dma=True)

        def ck(name):
            if stop_after == name:
                S.barrier()
                S.dead = True

        try:
            ident = sbP("ident", [128, 128])
            ones_f = sbP("ones_f", [128, 128])
            identr = sbP("identr", [128, 128], F32 if EXACT_MASK else BF16)
            cneg = sbP("cneg", [128, 2])
            blk1 = sbP("blk1", [128, 128], BF16)
            MT = sbP("MT", [128, 2432], F32 if EXACT_MASK else BF16)
            mixg = sbP("mixg", [128, 8, 1])
            pleg = sbP("pleg", [128, 8, 1])
            qg = sbP("qg", [128, 1])
            kg = sbP("kg", [128, 1])
            bglu = sbP("bglu", [128, 4, 1])
            Tm = sbP("Tm", [128, 32, 128], BF16)
            PGre = sbP("PGre", [128, 32, 64], BF16)
            PGim = sbP("PGim", [128, 32, 64], BF16)
            PCre = sbP("PCre", [128, 16, 128], BF16)
            PCim = sbP("PCim", [128, 16, 128], BF16)
            APR = sbP("APR", [128, 16, 16, 2])
            API = sbP("API", [128, 16, 16, 2])
            attnT = sbP("attnT", [128, 4, SEQ], BF16)
            ssmT = sbP("ssmT", [128, 4, SEQ], BF16)

            G(lambda e: e.memset(ident[:], 1.0), w=["ident"])
            G(lambda e: e.affine_select(out=ident[:], in_=ident[:], pattern=[[-1, 128]], base=0, channel_multiplier=1,
                                        compare_op=ALU.is_equal, fill=0.0), r=["ident"], w=["ident"])
            V(lambda e: e.memset(ones_f[:], 1.0), w=["ones_f"])
            V(lambda e: e.tensor_copy(out=(identr[:].bitcast(F32R) if EXACT_MASK else identr[:]), in_=ident[:]), r=["ident"], w=["identr"])
            V(lambda e: e.memset(cneg[:, 0:1], -0.5), w=["cneg"])
            V(lambda e: e.memset(cneg[:, 1:2], -1.0), w=["cneg"])
            V(lambda e: e.memset(blk1[:], 0.0), w=["blk1"])
            V(lambda e: e.memset(blk1[0:64, 0:64], 1.0), w=["blk1"])
            V(lambda e: e.memset(blk1[64:128, 64:128], 1.0), w=["blk1"])
            ck("consts")

            with ExitStack() as su:
                sb = mk(su)
                NI = 2432
                sb_keep = sb
                msk_scope = ExitStack()
                sb = mk(msk_scope)
                di = sb("di", [128, NI], I32)
                df = sb("df", [128, NI])
                t1 = sb("mt1", [128, NI])
                t2 = sb("mt2", [128, NI])
                acc = sb("macc", [128, NI])
                G(lambda e: e.iota(di[:], pattern=[[1, NI]], base=-384, channel_multiplier=-1), w=["di"])
                V(lambda e: e.tensor_copy(out=df[:], in_=di[:]), r=["di"], w=["df"])
                V(lambda e: e.tensor_scalar(out=acc[:], in0=df[:], scalar1=128.0, scalar2=None, op0=ALU.is_le), r=["df"], w=["acc"])
                for (mod, lim) in ((4.0, 512.0), (16.0, None)):
                    V(lambda e, mod=mod: e.tensor_scalar(out=t1[:], in0=df[:], scalar1=1.0 / mod, scalar2=MAGIC, op0=ALU.mult,
                                                         op1=ALU.add), r=["df"], w=["t1"])
                    V(lambda e, mod=mod: e.tensor_scalar(out=t1[:], in0=t1[:], scalar1=MAGIC, scalar2=mod, op0=ALU.subtract,
                                                         op1=ALU.mult), r=["t1"], w=["t1"])
                    V(lambda e: e.tensor_tensor(out=t1[:], in0=t1[:], in1=df[:], op=ALU.is_equal), r=["t1", "df"], w=["t1"])
                    if lim is not None:
                        V(lambda e, lim=lim: e.tensor_scalar(out=t2[:], in0=df[:], scalar1=lim, scalar2=None, op0=ALU.is_le),
                          r=["df"], w=["t2"])
                        V(lambda e: e.tensor_tensor(out=t1[:], in0=t1[:], in1=t2[:], op=ALU.mult), r=["t1", "t2"], w=["t1"])
                    V(lambda e: e.tensor_tensor(out=acc[:], in0=acc[:], in1=t1[:], op=ALU.add), r=["acc", "t1"], w=["acc"])
                V(lambda e: e.tensor_scalar(out=t2[:], in0=df[:], scalar1=0.0, scalar2=None, op0=ALU.is_ge), r=["df"], w=["t2"])
                V(lambda e: e.tensor_tensor(out=acc[:], in0=acc[:], in1=t2[:], op=ALU.mult), r=["acc", "t2"], w=["acc"])
                if EXACT_MASK:
                    V(lambda e: e.tensor_scalar(out=t1[:], in0=acc[:], scalar1=1.0, scalar2=None, op0=ALU.max), r=["acc"], w=["t1"])
                    A(lambda e: e.activation(out=t1[:], in_=t1[:], func=AF.Ln), r=["t1"], w=["t1"])
                    V(lambda e: e.tensor_scalar(out=t2[:], in0=acc[:], scalar1=0.0, scalar2=-30000.0, op0=ALU.is_equal, op1=ALU.mult),
                      r=["acc"], w=["t2"])
                    V(lambda e: e.tensor_tensor(out=MT[:].bitcast(F32R), in0=t1[:], in1=t2[:], op=ALU.add), r=["t1", "t2"], w=["MT"])
                else:
                    V(lambda e: e.tensor_scalar(out=t1[:], in0=acc[:], scalar1=2.0, scalar2=26.5, op0=ALU.is_equal, op1=ALU.mult),
                      r=["acc"], w=["t1"])
                    V(lambda e: e.tensor_scalar(out=t2[:], in0=acc[:], scalar1=3.0, scalar2=42.0, op0=ALU.is_equal, op1=ALU.mult),
                      r=["acc"], w=["t2"])
                    V(lambda e: e.tensor_tensor(out=t1[:], in0=t1[:], in1=t2[:], op=ALU.add), r=["t1", "t2"], w=["t1"])
                    V(lambda e: e.tensor_scalar(out=t2[:], in0=acc[:], scalar1=0.0, scalar2=-1.0e6, op0=ALU.is_equal, op1=ALU.mult),
                      r=["acc"], w=["t2"])
                    V(lambda e: e.tensor_tensor(out=MT[:], in0=t1[:], in1=t2[:], op=ALU.add), r=["t1", "t2"], w=["MT"])
                S.barrier()
                msk_scope.close()
                sb = sb_keep
                ck("mask")

                LR = sb("LR", [128, 16, 1])
                LI = sb("LI", [128, 16, 1])
                DT = sb("DT", [128, 16, 1])
                dtrow = sb("dtrow", [1, 32])
                D(lambda e: e.dma_start(out=dtrow[:], in_=ldt_d.rearrange("(o g) -> o g", o=1)), w=["dtrow"])
                T(lambda e: e.matmul(PS[5][:, 0:32], lhsT=ones_f[0:1, :], rhs=dtrow[:], start=True, stop=True),
                  r=["ones_f", "dtrow"], w=[psn[5]])
                for j in range(2):
                    V(lambda e, j=j: e.tensor_copy(out=DT[64 * j:64 * j + 64, :, 0], in_=PS[5][64 * j:64 * j + 64, j:32:2]),
                      r=[psn[5]], w=["DT"])
                Ln_ = [sb("Lnat%d" % i, [32, 2, 64]) for i in range(2)]
                for li, (src, dst, dname) in enumerate(((lre_d, LR, "LR"), (lim_d, LI, "LI"))):
                    D(lambda e, li=li, src=src: e.dma_start(out=Ln_[li][:, 0, :], in_=src), w=["Lnat%d" % li])
                    G(lambda e, li=li: e.tensor_copy(out=Ln_[li][:, 1, :], in_=Ln_[li][:, 0, :]), r=["Lnat%d" % li], w=["Lnat%d" % li])
                    bank = 6 + li
                    T(lambda e, li=li, bank=bank: e.transpose(out=PS[bank][:, 0:32], in_=Ln_[li][:].rearrange("p d n -> p (d n)"),
                                                              identity=ident[0:32, 0:32]), r=["Lnat%d" % li, "ident"], w=[psn[bank]])
                    for j in range(2):
                        V(lambda e, dst=dst, j=j, bank=bank: e.tensor_copy(
                            out=dst[64 * j:64 * j + 64, :, 0], in_=PS[bank][64 * j:64 * j + 64, j:32:2]), r=[psn[bank]], w=[dname])
                BR = sb("BR", [128, 16, 16])
                BI = sb("BI", [128, 16, 16])
                for j in range(2):
                    D2(lambda e, j=j: e.dma_start(out=BR[64 * j:64 * j + 64], in_=bre_d.rearrange("(gp j) n c -> j n gp c", j=2)[j]), w=["BR"])
                    D2(lambda e, j=j: e.dma_start(out=BI[64 * j:64 * j + 64], in_=bim_d.rearrange("(gp j) n c -> j n gp c", j=2)[j]), w=["BI"])
                Cn = [sb("Cn%d" % i, [128, 4, 2, 64]) for i in range(2)]
                for ci, src in enumerate((cre_d, cim_d)):
                    D(lambda e, ci=ci, src=src: e.dma_start(out=Cn[ci][:, :, 0, :],
                                                            in_=src.rearrange("(t g) c n -> (g c) t n", t=4)), w=["Cn%d" % ci])
                    G(lambda e, ci=ci: e.tensor_copy(out=Cn[ci][:, :, 1, :], in_=Cn[ci][:, :, 0, :]), r=["Cn%d" % ci], w=["Cn%d" % ci])
                gnat = sb("gnat", [20, 128])
                D(lambda e: e.dma_start(out=gnat[0:8, :], in_=mixn_d.rearrange("(k p) -> k p", p=128)), w=["gnat"])
                D(lambda e: e.dma_start(out=gnat[8:16, :], in_=plen_d.rearrange("(k p) -> k p", p=128)), w=["gnat"])
                D(lambda e: e.dma_start(out=gnat[16:20, :], in_=bglu_d.rearrange("(k p) -> k p", p=128)), w=["gnat"])
                T(lambda e: e.transpose(out=PS[4][:, 0:20], in_=gnat[:], identity=ident[0:20, 0:20]), r=["gnat", "ident"], w=[psn[4]])
                V(lambda e: e.tensor_copy(out=mixg[:, :, 0], in_=PS[4][:, 0:8]), r=[psn[4]], w=["mixg"])
                V(lambda e: e.tensor_copy(out=pleg[:, :, 0], in_=PS[4][:, 8:16]), r=[psn[4]], w=["pleg"])
                V(lambda e: e.tensor_copy(out=bglu[:, :, 0], in_=PS[4][:, 16:20]), r=[psn[4]], w=["bglu"])
                qkrow = sb("qkrow", [2, 2, 64])
                for hh in range(2):
                    D(lambda e, hh=hh: e.dma_start(out=qkrow[0:1, hh, :], in_=qn_d.rearrange("(o n) -> o n", o=1)), w=["qkrow"])
                    D(lambda e, hh=hh: e.dma_start(out=qkrow[1:2, hh, :], in_=kn_d.rearrange("(o n) -> o n", o=1)), w=["qkrow"])
                T(lambda e: e.transpose(out=PS[4][:, 32:34], in_=qkrow[:].rearrange("r h n -> r (h n)"), identity=ident[0:2, 0:2]),
                  r=["qkrow", "ident"], w=[psn[4]])
                V(lambda e: e.tensor_scalar(out=qg[:], in0=PS[4][:, 32:33], scalar1=0.125 * ALPHA, scalar2=None, op0=ALU.mult),
                  r=[psn[4]], w=["qg"])
                V(lambda e: e.tensor_copy(out=kg[:], in_=PS[4][:, 33:34]), r=[psn[4]], w=["kg"])
                CR = sb("CR", [128, 16, 16])
                CI = sb("CI", [128, 16, 16])
                for ci, (dst, dname) in enumerate(((CR, "CR"), (CI, "CI"))):
                    for t in range(4):
                        bank = (ci * 4 + t) % 8
                        T(lambda e, ci=ci, t=t, bank=bank: e.transpose(
                            out=PS[bank][:, 0:128], in_=Cn[ci][:, t, :, :].rearrange("p d n -> p (d n)"), identity=ident[:]),
                          r=["Cn%d" % ci, "ident"], w=[psn[bank]])
                        for j in range(2):
                            V(lambda e, dst=dst, t=t, j=j, bank=bank: e.tensor_copy(
                                out=dst[64 * j:64 * j + 64, 4 * t:4 * t + 4, :],
                                in_=PS[bank][64 * j:64 * j + 64, 0:128].rearrange("p (g q c) -> p g q c", q=2, c=16)[:, :, j, :]),
                              r=[psn[bank]], w=[dname])
                dcol = sb("dcol", [128, 32, 1])
                dnat = sb("dnat", [32, 16])
                d16 = sb("d16", [16, 32])
                Rrep = sb("Rrep", [16, 128])
                D(lambda e: e.dma_start(out=dnat[:], in_=dsk_d.rearrange("(g c) -> g c", c=16)), w=["dnat"])
                T(lambda e: e.transpose(out=PS[4][0:16, 64:96], in_=dnat[:], identity=ident[0:32, 0:32]), r=["dnat", "ident"], w=[psn[4]])
                V(lambda e: e.tensor_copy(out=d16[:], in_=PS[4][0:16, 64:96]), r=[psn[4]], w=["d16"])
                for sp_ in range(8):
                    V(lambda e, sp_=sp_: e.tensor_copy(out=Rrep[:, 16 * sp_:16 * sp_ + 16], in_=ident[0:16, 0:16]), r=["ident"], w=["Rrep"])
                T(lambda e: e.matmul(PS[4][:, 128:160], lhsT=Rrep[:], rhs=d16[:], start=True, stop=True), r=["Rrep", "d16"], w=[psn[4]])
                V(lambda e: e.tensor_copy(out=dcol[:, :, 0], in_=PS[4][:, 128:160]), r=[psn[4]], w=["dcol"])

                def sm(name):
                    return sb("s_" + name, [128, 16, 1])

                dt_ = sm("dt_"); th = sm("th"); lrdt = sm("lrdt"); mag = sm("mag"); magi = sm("magi")
                rr = sm("rr"); sn = sm("sn"); cs = sm("cs"); tA = sm("tA"); tB = sm("tB")
                a_re = sm("a_re"); a_im = sm("a_im"); i_re = sm("i_re"); i_im = sm("i_im")
                c_re = sm("c_re"); c_im = sm("c_im"); den = sm("den")
                A(lambda e: e.activation(out=dt_[:], in_=DT[:], func=AF.Exp), r=["DT"], w=["dt_"])
                V(lambda e: e.tensor_tensor(out=th[:], in0=LI[:], in1=dt_[:], op=ALU.mult), r=["LI", "dt_"], w=["th"])
                V(lambda e: e.tensor_tensor(out=lrdt[:], in0=LR[:], in1=dt_[:], op=ALU.mult), r=["LR", "dt_"], w=["lrdt"])
                A(lambda e: e.activation(out=mag[:], in_=lrdt[:], func=AF.Exp), r=["lrdt"], w=["mag"])
                A(lambda e: e.activation(out=magi[:], in_=lrdt[:], func=AF.Exp, scale=-1.0), r=["lrdt"], w=["magi"])

                def sin_of(dst, shift, dname):
                    V(lambda e: e.tensor_scalar(out=tA[:], in0=th[:], scalar1=shift, scalar2=1.0 / TWO_PI, op0=ALU.add, op1=ALU.mult),
                      r=["th"], w=["tA"])
                    V(lambda e: e.tensor_scalar(out=tA[:], in0=tA[:], scalar1=MAGIC, scalar2=MAGIC, op0=ALU.add, op1=ALU.subtract),
                      r=["tA"], w=["tA"])
                    V(lambda e: e.tensor_scalar(out=tB[:], in0=th[:], scalar1=shift, scalar2=None, op0=ALU.add), r=["th"], w=["tB"])
                    V(lambda e: e.scalar_tensor_tensor(out=rr[:], in0=tA[:], scalar=-TWO_PI, in1=tB[:], op0=ALU.mult, op1=ALU.add),
                      r=["tA", "tB"], w=["rr"])
                    V(lambda e: e.tensor_scalar(out=rr[:], in0=rr[:], scalar1=math.pi, scalar2=-math.pi, op0=ALU.min, op1=ALU.max),
                      r=["rr"], w=["rr"])
                    A(lambda e: e.activation(out=dst[:], in_=rr[:], func=AF.Sin), r=["rr"], w=[dname])

                sin_of(sn, 0.0, "sn")
                sin_of(cs, math.pi / 2.0, "cs")
                V(lambda e: e.tensor_tensor(out=a_re[:], in0=mag[:], in1=cs[:], op=ALU.mult), r=["mag", "cs"], w=["a_re"])
                V(lambda e: e.tensor_tensor(out=a_im[:], in0=mag[:], in1=sn[:], op=ALU.mult), r=["mag", "sn"], w=["a_im"])
                V(lambda e: e.tensor_tensor(out=i_re[:], in0=magi[:], in1=cs[:], op=ALU.mult), r=["magi", "cs"], w=["i_re"])
                V(lambda e: e.scalar_tensor_tensor(out=i_im[:], in0=magi[:], scalar=-1.0, in1=sn[:], op0=ALU.mult, op1=ALU.mult),
                  r=["magi", "sn"], w=["i_im"])
                V(lambda e: e.tensor_tensor(out=den[:], in0=LR[:], in1=LR[:], op=ALU.mult), r=["LR"], w=["den"])
                V(lambda e: e.tensor_tensor(out=tA[:], in0=LI[:], in1=LI[:], op=ALU.mult), r=["LI"], w=["tA"])
                V(lambda e: e.tensor_tensor(out=den[:], in0=den[:], in1=tA[:], op=ALU.add), r=["den", "tA"], w=["den"])
                V(lambda e: e.reciprocal(out=den[:], in_=den[:]), r=["den"], w=["den"])
                V(lambda e: e.tensor_scalar(out=tB[:], in0=a_re[:], scalar1=-1.0, scalar2=None, op0=ALU.add), r=["a_re"], w=["tB"])
                V(lambda e: e.tensor_tensor(out=c_re[:], in0=tB[:], in1=LR[:], op=ALU.mult), r=["tB", "LR"], w=["c_re"])
                V(lambda e: e.tensor_tensor(out=tA[:], in0=a_im[:], in1=LI[:], op=ALU.mult), r=["a_im", "LI"], w=["tA"])
                V(lambda e: e.tensor_tensor(out=c_re[:], in0=c_re[:], in1=tA[:], op=ALU.add), r=["c_re", "tA"], w=["c_re"])
                V(lambda e: e.tensor_tensor(out=c_re[:], in0=c_re[:], in1=den[:], op=ALU.mult), r=["c_re", "den"], w=["c_re"])
                V(lambda e: e.tensor_tensor(out=c_im[:], in0=a_im[:], in1=LR[:], op=ALU.mult), r=["a_im", "LR"], w=["c_im"])
                V(lambda e: e.tensor_tensor(out=tA[:], in0=tB[:], in1=LI[:], op=ALU.mult), r=["tB", "LI"], w=["tA"])
                V(lambda e: e.tensor_tensor(out=c_im[:], in0=c_im[:], in1=tA[:], op=ALU.subtract), r=["c_im", "tA"], w=["c_im"])
                V(lambda e: e.tensor_tensor(out=c_im[:], in0=c_im[:], in1=den[:], op=ALU.mult), r=["c_im", "den"], w=["c_im"])

                tb1 = sb("tb1", [128, 16, 16])
                tb2 = sb("tb2", [128, 16, 16])

                lit = {a_re.name: "a_re", a_im.name: "a_im", i_re.name: "i_re", i_im.name: "i_im",
                       c_re.name: "c_re", c_im.name: "c_im"}

                def nm(t):
                    return lit.get(t.name, t.name)

                tmpE = {"dve": (tb1, tb2, "tb1", "tb2", tA, tB, "tA", "tB"),
                        "pool": (sb("tb1p", [128, 16, 16]), sb("tb2p", [128, 16, 16]), "tb1p", "tb2p",
                                 sm("tAp"), sm("tBp"), "tAp", "tBp")}

                def cmul_b(eng, o_re, o_im, on, x_re, x_im, xn, s_re, s_im, neg_im=False):
                    u1, u2, n1, n2 = tmpE[eng][0:4]
                    sr = s_re[:].to_broadcast([128, 16, 16])
                    si = s_im[:].to_broadcast([128, 16, 16])
                    O = lambda fn, r, w: S.op(eng, fn, reads=r, writes=w)
                    O(lambda e: e.tensor_tensor(out=u1[:], in0=x_re, in1=sr, op=ALU.mult), xn + [nm(s_re)], [n1])
                    O(lambda e: e.tensor_tensor(out=u2[:], in0=x_im, in1=si, op=ALU.mult), xn + [nm(s_im)], [n2])
                    O(lambda e: e.tensor_tensor(out=o_re, in0=u1[:], in1=u2[:], op=ALU.subtract), [n1, n2], [on])
                    O(lambda e: e.tensor_tensor(out=u1[:], in0=x_re, in1=si, op=ALU.mult), xn + [nm(s_im)], [n1])
                    O(lambda e: e.tensor_tensor(out=u2[:], in0=x_im, in1=sr, op=ALU.mult), xn + [nm(s_re)], [n2])
                    if neg_im:
                        O(lambda e: e.tensor_tensor(out=u1[:], in0=u1[:], in1=u2[:], op=ALU.add), [n1, n2], [n1])
                        O(lambda e: e.tensor_scalar(out=o_im, in0=u1[:], scalar1=-1.0, scalar2=None, op0=ALU.mult), [n1], [on])
                    else:
                        O(lambda e: e.tensor_tensor(out=o_im, in0=u1[:], in1=u2[:], op=ALU.add), [n1, n2], [on])

                def cmul_s(eng, o_re, o_im, x_re, x_im, s_re, s_im):
                    u1, u2, n1, n2 = tmpE[eng][4:8]
                    O = lambda fn, r, w: S.op(eng, fn, reads=r, writes=w)
                    O(lambda e: e.tensor_tensor(out=u1[:], in0=x_re[:], in1=s_re[:], op=ALU.mult), [nm(x_re), nm(s_re)], [n1])
                    O(lambda e: e.tensor_tensor(out=u2[:], in0=x_im[:], in1=s_im[:], op=ALU.mult), [nm(x_im), nm(s_im)], [n2])
                    O(lambda e: e.tensor_tensor(out=o_re[:], in0=u1[:], in1=u2[:], op=ALU.subtract), [n1, n2], [nm(o_re)])
                    O(lambda e: e.tensor_tensor(out=u1[:], in0=x_re[:], in1=s_im[:], op=ALU.mult), [nm(x_re), nm(s_im)], [n1])
                    O(lambda e: e.tensor_tensor(out=u2[:], in0=x_im[:], in1=s_re[:], op=ALU.mult), [nm(x_im), nm(s_re)], [n2])
                    O(lambda e: e.tensor_tensor(out=o_im[:], in0=u1[:], in1=u2[:], op=ALU.add), [n1, n2], [nm(o_im)])

                BBr = sb("BBr", [128, 16, 16])
                BBi = sb("BBi", [128, 16, 16])
                cmul_b("dve", BBr[:], BBi[:], "BB", BR[:], BI[:], ["BR", "BI"], c_re, c_im)
                pw_re = [sm("pwr%d" % i) for i in range(9)]
                pw_im = [sm("pwi%d" % i) for i in range(9)]
                iw_re = [sm("iwr%d" % i) for i in range(8)]
                iw_im = [sm("iwi%d" % i) for i in range(8)]
                CPr = sb("CPr", [128, 16, 9, 16])
                CPi = sb("CPi", [128, 16, 9, 16])
                for (eng, lst_re, lst_im) in (("dve", pw_re, pw_im), ("pool", iw_re, iw_im)):
                    S.op(eng, lambda e, t=lst_re[0]: e.memset(t[:], 1.0), writes=[lst_re[0].name])
                    S.op(eng, lambda e, t=lst_im[0]: e.memset(t[:], 0.0), writes=[lst_im[0].name])
                for i in range(1, 8):
                    cmul_s("pool", iw_re[i], iw_im[i], iw_re[i - 1], iw_im[i - 1], i_re, i_im)
                for tau in range(9):
                    if tau + 1 < 9:
                        cmul_s("dve", pw_re[tau + 1], pw_im[tau + 1], pw_re[tau], pw_im[tau], a_re, a_im)
                    cmul_b("dve", CPr[:, :, tau, :], CPi[:, :, tau, :], "CP%d" % tau, CR[:], CI[:], ["CR", "CI"],
                           pw_re[tau], pw_im[tau], neg_im=True)
                Qr = sb("Qr", [128, 16, 8, 16])
                Qi = sb("Qi", [128, 16, 8, 16])
                Hr = sb("Hr", [128, 16, 8, 16])
                Hi = sb("Hi", [128, 16, 8, 16])
                for s_ in range(8):
                    cmul_b("pool", Qr[:, :, s_, :], Qi[:, :, s_, :], "Q%d" % s_, BBr[:], BBi[:], ["BB"], iw_re[s_], iw_im[s_])
                    cmul_b("dve" if s_ < 3 else "pool", Hr[:, :, s_, :], Hi[:, :, s_, :], "H%d" % s_, BBr[:], BBi[:], ["BB"],
                           pw_re[7 - s_], pw_im[7 - s_])
                CPn = ["CP%d" % t for t in range(9)]
                Qn = ["Q%d" % t for t in range(8)]
                Hn = ["H%d" % t for t in range(8)]
                V(lambda e: e.tensor_copy(out=PCre[:].rearrange("p g (t c) -> p g t c", c=16), in_=CPr[:, :, 1:9, :]), r=CPn, w=["PCre"])
                V(lambda e: e.tensor_copy(out=PCim[:].rearrange("p g (t c) -> p g t c", c=16), in_=CPi[:, :, 1:9, :]), r=CPn, w=["PCim"])
                qw_re = [pw_re[8]] + [sm("qwr%d" % i) for i in range(1, 16)]
                qw_im = [pw_im[8]] + [sm("qwi%d" % i) for i in range(1, 16)]
                for jj in range(16):
                    if jj > 0:
                        cmul_s("pool", qw_re[jj], qw_im[jj], qw_re[jj - 1], qw_im[jj - 1], pw_re[8], pw_im[8])
                    G(lambda e, jj=jj: e.tensor_copy(out=APR[:, jj, :, 0:1], in_=qw_re[jj][:]), r=[qw_re[jj].name], w=["APR"])
                    G(lambda e, jj=jj: e.tensor_copy(out=APR[:, jj, :, 1:2], in_=qw_re[jj][:]), r=[qw_re[jj].name], w=["APR"])
                    G(lambda e, jj=jj: e.tensor_scalar(out=API[:, jj, :, 0:1], in0=qw_im[jj][:], scalar1=-1.0, scalar2=None,
                                                       op0=ALU.mult), r=[qw_im[jj].name], w=["API"])
                    G(lambda e, jj=jj: e.tensor_copy(out=API[:, jj, :, 1:2], in_=qw_im[jj][:]), r=[qw_im[jj].name], w=["API"])
                for gp in range(16):
                    for ci, (src, dst, dn_) in enumerate(((Hr, PGre, "PGre"), (Hi, PGim, "PGim"))):
                        bank = (2 * gp + ci) % 8
                        T(lambda e, src=src, gp=gp, bank=bank: e.transpose(
                            out=PS[bank][:, 0:128], in_=src[:, gp, :, :].rearrange("p s c -> p (s c)"), identity=ident[:]),
                          r=Hn + ["ident"], w=[psn[bank]])
                        fcp = (lambda e, dst=dst, gp=gp, bank=bank: e.tensor_copy(
                            out=dst[:, 2 * gp:2 * gp + 2, :], in_=PS[bank][:, 0:128].rearrange("p (j n) -> p j n", j=2)))
                        if ci == 0:
                            V(fcp, r=[psn[bank]], w=[dn_])
                        else:
                            A(lambda e, dst=dst, gp=gp, bank=bank: e.copy(
                                out=dst[:, 2 * gp:2 * gp + 2, :], in_=PS[bank][:, 0:128].rearrange("p (j n) -> p j n", j=2)),
                              r=[psn[bank]], w=[dn_])
                tmask = sb("tmask", [128, 128])
                ttmp = sb("ttmp", [128, 128])
                G(lambda e: e.memset(tmask[:], 1.0), w=["tmask"])
                G(lambda e: e.affine_select(out=tmask[:], in_=tmask[:], pattern=[[16, 8], [0, 16]], base=15, channel_multiplier=-1,
                                            compare_op=ALU.is_ge, fill=0.0), r=["tmask"], w=["tmask"])
                for g in range(32):
                    gp, j = g // 2, g % 2
                    bank = g % 8
                    pb = 64 * j
                    T(lambda e, gp=gp, pb=pb, bank=bank: e.matmul(PS[bank][:, 0:128], lhsT=Qr[pb:pb + 64, gp, :, :].rearrange("p s c -> p (s c)"),
                                                                   rhs=CPr[pb:pb + 64, gp, 0:8, :].rearrange("p s c -> p (s c)"), start=True, stop=False),
                      r=Qn + CPn, w=[psn[bank]])
                    T(lambda e, gp=gp, pb=pb, bank=bank: e.matmul(PS[bank][:, 0:128], lhsT=Qi[pb:pb + 64, gp, :, :].rearrange("p s c -> p (s c)"),
                                                                   rhs=CPi[pb:pb + 64, gp, 0:8, :].rearrange("p s c -> p (s c)"), start=False, stop=True),
                      r=Qn + CPn, w=[psn[bank]])
                    V(lambda e, bank=bank: e.tensor_tensor(out=ttmp[:], in0=PS[bank][:, 0:128], in1=tmask[:], op=ALU.mult),
                      r=[psn[bank], "tmask"], w=["ttmp"])
                    V(lambda e, g=g: e.scalar_tensor_tensor(out=Tm[:, g, :], in0=ident[:], scalar=dcol[:, g, :], in1=ttmp[:],
                                                            op0=ALU.mult, op1=ALU.add), r=["ident", "dcol", "ttmp"], w=["Tm"])
                if debug:
                    dT = sb("dT", [128, 32, 128])
                    V(lambda e: e.tensor_copy(out=dT[:], in_=Tm[:]), r=["Tm"], w=["dT"])
                    D(lambda e: e.dma_start(out=dbg["T"], in_=dT[:]), r=["dT"])
                S.barrier()

            for s in range(2):
                S.barrier()
                with ExitStack() as abc:
                    sb = mk(abc)
                    xT = sb("xT", [128, 8, SEQ], BF16)
                    wst = [None]
                    wbf = [None, None]
                    wctr = [0]

                    def load_w(c0, slot):
                        i = 0
                        D(lambda e: e.dma_start(out=wst[i][:], in_=win_d[:, c0:c0 + 128].rearrange("(k p) c -> p k c", p=128)),
                          w=["wsta%d" % i])
                        G(lambda e: e.tensor_tensor(out=wbf[slot][:], in0=wst[i][:], in1=mixg[:].to_broadcast([128, 8, 128]),
                                                    op=ALU.mult), r=["wsta%d" % i, "mixg"], w=["wbf%d" % slot])
                        return wbf[slot], "wbf%d" % slot

                    ck("ssmsetup")
                    with ExitStack() as a0:
                        sb0 = mk(a0)
                        xt = [sb0("xt%d" % i, [128, DM]) for i in range(4)]
                        sqs = [sb0("sq%d" % i, [128, DM]) for i in range(2)]
                        st4s = [sb0("st4%d" % i, [128, 4]) for i in range(2)]
                        dgs = [sb0("dg%d" % i, [128, 128]) for i in range(2)]
                        rbcs = [sb0("rbc%d" % i, [128, 128]) for i in range(2)]
                        def a0_load(i):
                            xi = xt[i % 4]
                            xn = "xt%d" % (i % 4)
                            sq, sqn = sqs[i % 2], "sq%d" % (i % 2)
                            D(lambda e: e.dma_start(out=xi[:], in_=x_d[s, i * 128:(i + 1) * 128, :]), w=[xn])
                            A(lambda e: e.activation(out=sq[:], in_=xi[:], func=AF.Square), r=[xn], w=[sqn])

                        def a0_stats(i):
                            p2 = i % 2
                            xi = xt[i % 4]
                            xn = "xt%d" % (i % 4)
                            sq, sqn = sqs[p2], "sq%d" % p2
                            st4, stn = st4s[p2], "st4%d" % p2
                            dg, dgn = dgs[p2], "dg%d" % p2
                            rbc, rbn = rbcs[p2], "rbc%d" % p2
                            rb = 2 + p2
                            V(lambda e: e.tensor_reduce(out=st4[:, 0:1], in_=sq[:], axis=AX.X, op=ALU.add), r=[sqn], w=[stn])
                            V(lambda e: e.tensor_scalar(out=st4[:, 1:2], in0=st4[:, 0:1], scalar1=1.0 / DM, scalar2=EPS,
                                                        op0=ALU.mult, op1=ALU.add), r=[stn], w=[stn])
                            G(lambda e: e.tensor_tensor(out=st4[:, 3:4], in0=st4[:, 1:2], in1=cneg[:, 0:1], op=ALU.pow),
                              r=[stn, "cneg"], w=[stn])

                        def a0_stats_b(i):
                            p2 = i % 2
                            st4, stn = st4s[p2], "st4%d" % p2
                            xi = xt[i % 4]
                            xn = "xt%d" % (i % 4)
                            A(lambda e: e.activation(out=xi[:], in_=xi[:], func=AF.Copy, scale=st4[:, 3:4]), r=[xn, stn], w=[xn])

                        def a0_xpose(i):
                            p2 = i % 2
                            xi = xt[i % 4]
                            xn = "xt%d" % (i % 4)
                            for half in range(2):
                                tbk = 4 * p2 + half
                                for q4 in range(4):
                                    kc = half * 4 + q4
                                    T(lambda e, kc=kc, q4=q4, tbk=tbk: e.transpose(
                                        out=PS[tbk][:, q4 * 128:(q4 + 1) * 128], in_=xi[:, kc * 128:(kc + 1) * 128], identity=ident[:]),
                                      r=[xn, "ident"], w=[psn[tbk]])
                                if half == 0:
                                    V(lambda e, half=half, tbk=tbk: e.tensor_copy(
                                        out=xT[:, half * 4:half * 4 + 4, i * 128:(i + 1) * 128],
                                        in_=PS[tbk][:].rearrange("p (k t) -> p k t", k=4)), r=[psn[tbk]], w=["xT"])
                                else:
                                    A(lambda e, half=half, tbk=tbk: e.copy(
                                        out=xT[:, half * 4:half * 4 + 4, i * 128:(i + 1) * 128],
                                        in_=PS[tbk][:].rearrange("p (k t) -> p k t", k=4)), r=[psn[tbk]], w=["xT"])

                        a0_load(0)
                        a0_load(1)
                        a0_stats(0)
                        a0_stats_b(0)
                        for i in range(16):
                            if i + 2 < 16:
                                a0_load(i + 2)
                            if i + 1 < 16:
                                a0_stats(i + 1)
                            a0_xpose(i)
                            if i + 1 < 16:
                                a0_stats_b(i + 1)
                        S.barrier()
                    if debug and s == 0:
                        with ExitStack() as dd:
                            d32 = mk(dd)("d32", [128, 8, SEQ])
                            V(lambda e: e.tensor_copy(out=d32[:], in_=xT[:]), r=["xT"], w=["d32"])
                            D(lambda e: e.dma_start(out=dbg["xT"], in_=d32[:]), r=["d32"])
                            S.barrier()

                    ck("A0")
                    with ExitStack() as bs:
                        sbB = mk(bs)
                        qT = [sbB("qT%d" % i, [128, SEQ], BF16) for i in range(2)]
                        kT = [sbB("kT%d" % i, [128, SEQ], BF16) for i in range(2)]
                        gaT = [sbB("gaT%d" % i, [128, SEQ], BF16) for i in range(2)]
                        Vaug = [sbB("Vaug%d" % i, [128, 16, 2, 128], BF16) for i in range(2)]
                        wB = [sbB("wB%d" % i, [128, 8, 128], BF16) for i in range(8)]
                        wst[0] = sbB("wstaB", [128, 8, 128])
                        sqb = [sbB("sqb%d" % i, [128, 512], BF16) for i in range(2)]
                        tt = [sbB("tt%d" % i, [128, 512]) for i in range(2)]
                        NE = 4
                        Eb = [sbB("Eb%d" % i, [128, 512], BF16) for i in range(NE)]
                        rd = [sbB("rd%d" % i, [128, 512]) for i in range(2)]
                        ot = [sbB("ot%d" % i, [128, 512]) for i in range(2)]
                        oc = [sbB("oc%d" % i, [128, 512]) for i in range(2)]
                        SCB = (4, 5, 6, 7)
                        for i in range(2):
                            V(lambda e, i=i: e.memset(Vaug[i][:], 1.0), w=["Vaug%d" % i])

                        def load_wB(c0, slot):
                            i = 0
                            D(lambda e: e.dma_start(out=wst[i][:], in_=win_d[:, c0:c0 + 128].rearrange("(k p) c -> p k c", p=128)),
                              w=["wsta%d" % i])
                            G(lambda e: e.tensor_tensor(out=wB[slot][:], in0=wst[i][:], in1=mixg[:].to_broadcast([128, 8, 128]),
                                                        op=ALU.mult), r=["wsta%d" % i, "mixg"], w=["wB%d" % slot])
                            return wB[slot], "wB%d" % slot

                        def prep_units(hp):
                            par = hp % 2
                            wq, wqn = load_wB(hp * 128, par * 4 + 0)
                            wk, wkn = load_wB(512 + hp * 128, par * 4 + 1)
                            yield
                            wv, wvn = load_wB(1024 + hp * 128, par * 4 + 2)
                            wa, wan = load_wB(1536 + hp * 128, par * 4 + 3)
                            yield
                            blocks = []
                            for (w_, wn_, dstT, dn, gain, gn) in ((wq, wqn, qT[par], "qT%d" % par, qg, "qg"),
                                                                  (wk, wkn, kT[par], "kT%d" % par, kg, "kg")):
                                for tg in range(4):
                                    blocks.append((w_, wn_, dstT, dn, gain, gn, tg))

                            def S1(bi):
                                w_, wn_, dstT, dn, gain, gn, tg = blocks[bi]
                                pbk = bi % 2
                                for kc in range(8):
                                    T(lambda e, kc=kc: e.matmul(PS[pbk][:], lhsT=w_[:, kc, :], rhs=xT[:, kc, tg * 512:(tg + 1) * 512],
                                                                start=(kc == 0), stop=(kc == 7)), r=[wn_, "xT"], w=[psn[pbk]])

                            def S2(bi):
                                pbk = bi % 2
                                sq_, sqn = sqb[bi % 2], "sqb%d" % (bi % 2)
                                t_, tn = tt[bi % 2], "tt%d" % (bi % 2)
                                A(lambda e: e.activation(out=sq_[:], in_=PS[pbk][:], func=AF.Square), r=[psn[pbk]], w=[sqn])
                                T(lambda e: e.matmul(PS[2][:], lhsT=blk1[:], rhs=sq_[:], start=True, stop=True),
                                  r=["blk1", sqn], w=[psn[2]])
                                V(lambda e: e.tensor_scalar(out=t_[:], in0=PS[2][:], scalar1=1.0 / 64.0, scalar2=EPS, op0=ALU.mult,
                                                            op1=ALU.add), r=[psn[2]], w=[tn])

                            def S3(bi):
                                w_, wn_, dstT, dn, gain, gn, tg = blocks[bi]
                                pbk = bi % 2
                                t_, tn = tt[bi % 2], "tt%d" % (bi % 2)
                                A(lambda e: e.activation(out=t_[:], in_=t_[:], func=AF.Ln), r=[tn], w=[tn])
                                A(lambda e: e.activation(out=t_[:], in_=t_[:], func=AF.Exp, scale=-0.5), r=[tn], w=[tn])
                                V(lambda e: e.scalar_tensor_tensor(out=dstT[:, tg * 512:(tg + 1) * 512], in0=PS[pbk][:],
                                                                   scalar=gain[:, 0:1], in1=t_[:], op0=ALU.mult, op1=ALU.mult),
                                  r=[psn[pbk], gn, tn], w=[dn])

                            for t in range(8 + 2):
                                if 0 <= t - 2 < 8:
                                    S3(t - 2)
                                if 0 <= t - 1 < 8:
                                    S2(t - 1)
                                if t < 8:
                                    S1(t)
                                yield

                            def G1(tg):
                                pbk = tg % 2
                                for kc in range(8):
                                    T(lambda e, kc=kc: e.matmul(PS[pbk][:], lhsT=wa[:, kc, :], rhs=xT[:, kc, tg * 512:(tg + 1) * 512],
                                                                start=(kc == 0), stop=(kc == 7)), r=[wan, "xT"], w=[psn[pbk]])

                            def G2(tg):
                                pbk = tg % 2
                                t_, tn = tt[tg % 2], "tt%d" % (tg % 2)
                                A(lambda e: e.activation(out=t_[:], in_=PS[pbk][:], func=AF.Exp, scale=-1.0), r=[psn[pbk]], w=[tn])
                                A(lambda e: e.activation(out=t_[:], in_=t_[:], func=AF.Ln, bias=1.0), r=[tn], w=[tn])
                                A(lambda e: e.activation(out=t_[:], in_=t_[:], func=AF.Exp, scale=-1.0), r=[tn], w=[tn])

                            def G3(tg):
                                pbk = tg % 2
                                t_, tn = tt[tg % 2], "tt%d" % (tg % 2)
                                V(lambda e: e.tensor_tensor(out=gaT[par][:, tg * 512:(tg + 1) * 512], in0=PS[pbk][:], in1=t_[:],
                                                            op=ALU.mult), r=[tn, psn[pbk]], w=["gaT%d" % par])

                            for t in range(4 + 2):
                                if 0 <= t - 2 < 4:
                                    G3(t - 2)
                                if 0 <= t - 1 < 4:
                                    G2(t - 1)
                                if t < 4:
                                    G1(t)
                                yield

                            def V1(i4):
                                pbk = i4 % 2
                                for ii in range(4):
                                    i = i4 * 4 + ii
                                    for kc in range(8):
                                        T(lambda e, kc=kc, i=i, ii=ii: e.matmul(
                                            PS[pbk][:, ii * 128:(ii + 1) * 128], lhsT=xT[:, kc, i * 128:(i + 1) * 128], rhs=wv[:, kc, :],
                                            start=(kc == 0), stop=(kc == 7)), r=[wvn, "xT"], w=[psn[pbk]])

                            def V2(i4):
                                pbk = i4 % 2
                                pv = PS[pbk][:].rearrange("p (i c) -> p i c", i=4)
                                V(lambda e: e.tensor_copy(out=Vaug[par][:, i4 * 4:i4 * 4 + 4, 0, 0:64], in_=pv[:, :, 0:64]),
                                  r=[psn[pbk]], w=["Vaug%d" % par])
                                A(lambda e: e.copy(out=Vaug[par][:, i4 * 4:i4 * 4 + 4, 1, 64:128], in_=pv[:, :, 64:128]),
                                  r=[psn[pbk]], w=["Vaug%d" % par])

                            for t in range(4 + 1):
                                if 0 <= t - 1 < 4:
                                    V2(t - 1)
                                if t < 4:
                                    V1(t)
                                yield

                        def core_units(hp):
                            par = hp % 2
                            qn_, kn_, gn_, vn_ = "qT%d" % par, "kT%d" % par, "gaT%d" % par, "Vaug%d" % par
                            blocks = []
                            for h2 in range(2):
                                for Qg in range(4):
                                    nkb = 4 * Qg + 4
                                    for kb in range(nkb):
                                        blocks.append((h2, Qg, kb, nkb))
                            N = len(blocks)
                            NP = N // 2
                            for pi in range(NP + 1):
                                if pi < NP:
                                    qk_ops = []
                                    for idx in (2 * pi, 2 * pi + 1):
                                        h2, Qg, kb, nkb = blocks[idx]
                                        pb = 64 * h2
                                        sbk = SCB[idx % 4]
                                        off = Qg * 512 - kb * 128 + 384
                                        c0 = max(0, kb - 4 * Qg) * 128
                                        qk_ops.append((lambda e, kb=kb, Qg=Qg, sbk=sbk, pb=pb, c0=c0: e.matmul(
                                            PS[sbk][:, c0:512], lhsT=kT[par][pb:pb + 64, kb * 128:(kb + 1) * 128],
                                            rhs=qT[par][pb:pb + 64, Qg * 512 + c0:(Qg + 1) * 512], start=True, stop=False),
                                            [kn_, qn_], [psn[sbk]]))
                                        qk_ops.append((lambda e, sbk=sbk, off=off, c0=c0: e.matmul(
                                            PS[sbk][:, c0:512], lhsT=(identr[:].bitcast(F32R) if EXACT_MASK else identr[:]),
                                            rhs=(MT[:, off + c0:off + 512].bitcast(F32R) if EXACT_MASK else MT[:, off + c0:off + 512]),
                                            start=False, stop=True), ["identr", "MT"], [psn[sbk]]))
                                    S.group("pe", qk_ops)
                                if pi >= 1:
                                    pv_ops = []
                                    for j in (2 * pi - 2, 2 * pi - 1):
                                        h2, Qg, kb, nkb = blocks[j]
                                        eb = j % NE
                                        c0 = max(0, kb - 4 * Qg) * 128
                                        pv_ops.append((lambda e, kb=kb, h2=h2, eb=eb, nkb=nkb, c0=c0: e.matmul(
                                            PS[3][:, c0:512], lhsT=Vaug[par][:, kb, h2, :], rhs=Eb[eb][:, c0:512],
                                            start=(kb == 0), stop=(kb == nkb - 1)), [vn_, "Eb%d" % eb], [psn[3]]))
                                    pv_grp = pv_ops
                                else:
                                    pv_grp = None
                                if pi < NP:
                                    for idx in (2 * pi, 2 * pi + 1):
                                        h2, Qg, kb, nkb = blocks[idx]
                                        c0 = max(0, kb - 4 * Qg) * 128
                                        sbk = SCB[idx % 4]
                                        eb = idx % NE
                                        A(lambda e, sbk=sbk, eb=eb, c0=c0: e.activation(out=Eb[eb][:, c0:512], in_=PS[sbk][:, c0:512],
                                                                                        func=AF.Exp, scale=1.0 / ALPHA),
                                          r=[psn[sbk]], w=["Eb%d" % eb])
                                if pv_grp is not None:
                                    S.group("pe", pv_grp)
                                if pi >= 1:
                                    h2, Qg, kb, nkb = blocks[2 * pi - 1]
                                    ob = 3
                                    if kb == nkb - 1:
                                        fi = (h2 * 4 + Qg) % 2
                                        rd_, rdn = rd[fi], "rd%d" % fi
                                        ot_, otn = ot[fi], "ot%d" % fi
                                        oc_, ocn = oc[fi], "oc%d" % fi
                                        dlo, olo = (64, 0) if h2 == 0 else (0, 64)
                                        V(lambda e, oc_=oc_: e.tensor_copy(out=oc_[:], in_=PS[ob][:]), r=[psn[ob]], w=[ocn])
                                        A(lambda e, dlo=dlo, olo=olo, rd_=rd_, oc_=oc_: e.activation(
                                            out=rd_[olo:olo + 64, :], in_=oc_[dlo:dlo + 64, :], func=AF.Ln), r=[ocn], w=[rdn])
                                        A(lambda e, olo=olo, rd_=rd_: e.activation(out=rd_[olo:olo + 64, :], in_=rd_[olo:olo + 64, :],
                                                                                   func=AF.Exp, scale=-1.0), r=[rdn], w=[rdn])
                                        V(lambda e, olo=olo, rd_=rd_, ot_=ot_, oc_=oc_: e.tensor_tensor(
                                            out=ot_[olo:olo + 64, :], in0=oc_[olo:olo + 64, :], in1=rd_[olo:olo + 64, :],
                                            op=ALU.mult), r=[ocn, rdn], w=[otn])
                                        G(lambda e, olo=olo, Qg=Qg, ot_=ot_: e.tensor_tensor(
                                            out=attnT[olo:olo + 64, hp, Qg * 512:(Qg + 1) * 512], in0=ot_[olo:olo + 64, :],
                                            in1=gaT[par][olo:olo + 64, Qg * 512:(Qg + 1) * 512], op=ALU.mult),
                                          r=[otn, gn_], w=["attnT"])
                                yield

                        for _ in prep_units(0):
                            pass
                        for hp in range(4):
                            nxt = prep_units(hp + 1) if hp < 3 else None
                            for si, _ in enumerate(core_units(hp)):
                                if nxt is not None and si % 4 == 1:
                                    try:
                                        next(nxt)
                                    except StopIteration:
                                        nxt = None
                            if nxt is not None:
                                for _ in nxt:
                                    pass
                        S.barrier()
                    ck("B")
                    with ExitStack() as cs_:
                        sbC0 = mk(cs_)
                        ygT = sbC0("ygT", [128, 4, SEQ], BF16)
                        c12 = ExitStack()
                        sbC = mk(c12)
                        Ug = sbC("Ug", [128, 32, 256], BF16)
                        Sb = sbC("Sb", [128, 16, 2, 16, 16])
                        sc_scope = ExitStack()
                        scA = [[mk(sc_scope)("scA%d_%d" % (c, i), [128, (10, 6)[c], 2, 16]) for i in range(2)] for c in range(2)]
                        u_scope = ExitStack()
                        sbU = mk(u_scope)
                        U = sbU("U", [128, 4, 8, 8, 16])
                        wst[0] = sbU("wstaC", [128, 8, 128])
                        wu_all = sbU("wu_all", [128, 8, 512], BF16)
                        for cb in range(4):
                            c0w = 2048 + cb * 128
                            D(lambda e, c0w=c0w: e.dma_start(out=wst[0][:], in_=win_d[:, c0w:c0w + 128].rearrange("(k p) c -> p k c", p=128)),
                              w=["wsta0"])
                            G(lambda e, cb=cb: e.tensor_tensor(out=wu_all[:, :, cb * 128:(cb + 1) * 128], in0=wst[0][:],
                                                               in1=mixg[:].to_broadcast([128, 8, 128]), op=ALU.mult),
                              r=["wsta0", "mixg"], w=["wu_all"])
                        for ct in range(2):
                            for sp_ in range(8):
                                bank = sp_ % 2
                                for kc in range(8):
                                    T(lambda e, kc=kc, sp_=sp_, bank=bank: e.matmul(
                                        PS[bank][:], lhsT=xT[:, kc, ct * 1024 + sp_:(ct + 1) * 1024:8], rhs=wu_all[:, kc, :],
                                        start=(kc == 0), stop=(kc == 7)), r=["wu_all", "xT"], w=[psn[bank]])
                                fev = (lambda e, sp_=sp_, bank=bank: e.copy(
                                    out=U[:, :, :, sp_, :], in_=PS[bank][:].rearrange("p (b g c) -> p b g c", b=4, g=8)))
                                A(fev, r=[psn[bank]], w=["U"])
                            for g4 in range(8):
                                bank = 2 + g4 % 2
                                for gi in range(4):
                                    g = g4 * 4 + gi
                                    T(lambda e, g=g, gi=gi, bank=bank: e.transpose(
                                        out=PS[bank][:, gi * 128:(gi + 1) * 128],
                                        in_=U[:, g // 8, g % 8, :, :].rearrange("p s c -> p (s c)"), identity=ident[:]),
                                      r=["U", "ident"], w=[psn[bank]])
                                V(lambda e, g4=g4, bank=bank: e.tensor_copy(out=Ug[:, g4 * 4:g4 * 4 + 4, ct * 128:(ct + 1) * 128],
                                                                            in_=PS[bank][:].rearrange("p (g k) -> p g k", g=4)),
                                  r=[psn[bank]], w=["Ug"])
                        for gp in range(16):
                            bank = 4 + gp % 2
                            for j in range(2):
                                g = 2 * gp + j
                                T(lambda e, g=g, j=j, bank=bank: e.matmul(PS[bank][64 * j:64 * j + 64, 0:256], lhsT=PGre[:, g, :],
                                                                          rhs=Ug[:, g, :], start=True, stop=True),
                                  r=["PGre", "Ug"], w=[psn[bank]])
                                T(lambda e, g=g, j=j, bank=bank: e.matmul(PS[bank][64 * j:64 * j + 64, 256:512], lhsT=PGim[:, g, :],
                                                                          rhs=Ug[:, g, :], start=True, stop=True),
                                  r=["PGim", "Ug"], w=[psn[bank]])
                            A(lambda e, gp=gp, bank=bank: e.copy(out=Sb[:, gp, :, :, :],
                                                                 in_=PS[bank][:].rearrange("p (c K j) -> p c j K", c=2, K=16)),
                              r=[psn[bank]], w=["Sb%d" % (0 if gp < 10 else 1)])
                        S.barrier()
                        u_scope.close()
                        ck("C1")
                        SPLIT = ((0, 10, "dve"), (10, 16, "pool"))
                        for ci, (g0, g1, eng) in enumerate(SPLIT):
                            ng = g1 - g0
                            ta, tb_ = scA[ci]
                            na, nb = "scA%d_0" % ci, "scA%d_1" % ci
                            rn = "Sb%d" % ci
                            def colv(j, k0, nk, rev=False, g0=g0, g1=g1):
                                cs_ = slice(None, None, -1) if rev else slice(None)
                                return Sb[:, g0:g1, cs_, j, k0:k0 + nk]
                            for j in range(1, 16):
                                prev, prev_sw, cur = colv(j - 1, 0, 16), colv(j - 1, 0, 16, True), colv(j, 0, 16)
                                ar = APR[:, 0, g0:g1, :].unsqueeze(3).to_broadcast([128, ng, 2, 16])
                                ai = API[:, 0, g0:g1, :].unsqueeze(3).to_broadcast([128, ng, 2, 16])
                                S.op(eng, lambda e, prev=prev, ar=ar: e.tensor_tensor(out=ta[:], in0=prev, in1=ar, op=ALU.mult),
                                     reads=[rn, "APR"], writes=[na])
                                S.op(eng, lambda e, prev_sw=prev_sw, ai=ai: e.tensor_tensor(out=tb_[:], in0=prev_sw, in1=ai, op=ALU.mult),
                                     reads=[rn, "API"], writes=[nb])
                                S.op(eng, lambda e: e.tensor_tensor(out=ta[:], in0=ta[:], in1=tb_[:], op=ALU.add), reads=[na, nb], writes=[na])
                                S.op(eng, lambda e, cur=cur: e.tensor_tensor(out=cur, in0=cur, in1=ta[:], op=ALU.add), reads=[na, rn], writes=[rn])
                            for K in range(1, 16):
                                prev, prev_sw, cur = colv(15, K - 1, 1), colv(15, K - 1, 1, True), colv(15, K, 1)
                                ar = APR[:, 15, g0:g1, :].unsqueeze(3)
                                ai = API[:, 15, g0:g1, :].unsqueeze(3)
                                S.op(eng, lambda e, prev=prev, ar=ar: e.tensor_tensor(out=ta[:, :, :, 0:1], in0=prev, in1=ar, op=ALU.mult),
                                     reads=[rn, "APR"], writes=[na])
                                S.op(eng, lambda e, prev_sw=prev_sw, ai=ai: e.tensor_tensor(out=tb_[:, :, :, 0:1], in0=prev_sw, in1=ai,
                                                                                         op=ALU.mult), reads=[rn, "API"], writes=[nb])
                                S.op(eng, lambda e: e.tensor_tensor(out=ta[:, :, :, 0:1], in0=ta[:, :, :, 0:1], in1=tb_[:, :, :, 0:1],
                                                                    op=ALU.add), reads=[na, nb], writes=[na])
                                S.op(eng, lambda e, cur=cur: e.tensor_tensor(out=cur, in0=cur, in1=ta[:, :, :, 0:1], op=ALU.add),
                                     reads=[na, rn], writes=[rn])
                            for j in range(15):
                                prev, prev_sw, cur = colv(15, 0, 15), colv(15, 0, 15, True), colv(j, 1, 15)
                                ar = APR[:, j, g0:g1, :].unsqueeze(3).to_broadcast([128, ng, 2, 15])
                                ai = API[:, j, g0:g1, :].unsqueeze(3).to_broadcast([128, ng, 2, 15])
                                S.op(eng, lambda e, prev=prev, ar=ar: e.tensor_tensor(out=ta[:, :, :, 0:15], in0=prev, in1=ar, op=ALU.mult),
                                     reads=[rn, "APR"], writes=[na])
                                S.op(eng, lambda e, prev_sw=prev_sw, ai=ai: e.tensor_tensor(out=tb_[:, :, :, 0:15], in0=prev_sw, in1=ai,
                                                                                         op=ALU.mult), reads=[rn, "API"], writes=[nb])
                                S.op(eng, lambda e: e.tensor_tensor(out=ta[:, :, :, 0:15], in0=ta[:, :, :, 0:15], in1=tb_[:, :, :, 0:15],
                                                                    op=ALU.add), reads=[na, nb], writes=[na])
                                S.op(eng, lambda e, cur=cur: e.tensor_tensor(out=cur, in0=cur, in1=ta[:, :, :, 0:15], op=ALU.add),
                                     reads=[na, rn], writes=[rn])
                        S.barrier()
                        sc_scope.close()
                        Sh = sbC("Sh", [128, 16, 2, 256], BF16)
                        for ci, (g0, g1, eng) in enumerate(SPLIT):
                            rn = "Sb%d" % ci
                            S.op(eng, lambda e, g0=g0, g1=g1: e.memset(Sh[:, g0:g1, :, 0:1], 0.0), writes=["Sh%d" % ci])
                            for c in range(2):
                                S.op(eng, lambda e, g0=g0, g1=g1, c=c: e.tensor_copy(
                                    out=Sh[:, g0:g1, c, 1:241].rearrange("p g (K j) -> p g K j", j=16),
                                    in_=Sb[:, g0:g1, c, :, 0:15].rearrange("p g j K -> p g K j")), reads=[rn], writes=["Sh%d" % ci])
                                S.op(eng, lambda e, g0=g0, g1=g1, c=c: e.tensor_copy(
                                    out=Sh[:, g0:g1, c, 241:256], in_=Sb[:, g0:g1, c, 0:15, 15]), reads=[rn], writes=["Sh%d" % ci])
                        ck("C2")
                        with ExitStack() as c3:
                            sb3 = mk(c3)
                            Ysbs = [sb3("Ysb%d" % i, [128, 2, 128]) for i in range(2)]
                            YG = sb3("YG", [128, 8, 512])
                            def mm3(ct, gp):
                                bank = gp % 2
                                for j in range(2):
                                    g = 2 * gp + j
                                    pb = 64 * j
                                    o_ = PS[bank][:, j * 128:(j + 1) * 128]
                                    T(lambda e, g=g, o_=o_: e.matmul(o_, lhsT=Tm[:, g, :], rhs=Ug[:, g, ct * 128:(ct + 1) * 128],
                                                                     start=True, stop=False), r=["Tm", "Ug"], w=[psn[bank]])
                                    T(lambda e, pb=pb, o_=o_: e.matmul(
                                        o_, lhsT=PCre[pb:pb + 64, gp, :], rhs=Sh[pb:pb + 64, gp, 0, ct * 128:(ct + 1) * 128],
                                        start=False, stop=False), r=["PCre", "Sh0", "Sh1"], w=[psn[bank]])
                                    T(lambda e, pb=pb, o_=o_: e.matmul(
                                        o_, lhsT=PCim[pb:pb + 64, gp, :], rhs=Sh[pb:pb + 64, gp, 1, ct * 128:(ct + 1) * 128],
                                        start=False, stop=True), r=["PCim", "Sh0", "Sh1"], w=[psn[bank]])

                            def ev3(ct, gp):
                                bank = gp % 2
                                Ysb, Ysn = Ysbs[gp % 2], "Ysb%d" % (gp % 2)
                                A(lambda e: e.copy(out=Ysb[:], in_=PS[bank][:, 0:256].rearrange("p (j k) -> p j k", j=2)),
                                  r=[psn[bank]], w=[Ysn])
                                tb = 2 + gp % 2
                                for j in range(2):
                                    T(lambda e, j=j: e.transpose(out=PS[tb][:, j * 128:(j + 1) * 128], in_=Ysb[:, j, :],
                                                                 identity=ident[:]), r=[Ysn, "ident"], w=[psn[tb]])
                                for j in range(2):
                                    g = 2 * gp + j
                                    A(lambda e, j=j, g=g: e.activation(
                                        out=YG[:, :, g * 16:(g + 1) * 16],
                                        in_=PS[tb][:, j * 128:(j + 1) * 128].rearrange("p (t c) -> p t c", t=8),
                                        func=AF.Gelu), r=[psn[tb]], w=["YG"])

                            for ct in range(2):
                                mm3(ct, 0)
                                for gp in range(16):
                                    if gp + 1 < 16:
                                        mm3(ct, gp + 1)
                                    ev3(ct, gp)
                                for tp in range(8):
                                    bank = 4 + tp % 2
                                    for cbi in range(4):
                                        T(lambda e, tp=tp, cbi=cbi, bank=bank: e.transpose(
                                            out=PS[bank][:, cbi * 128:(cbi + 1) * 128], in_=YG[:, tp, cbi * 128:(cbi + 1) * 128],
                                            identity=ident[:]), r=["YG", "ident"], w=[psn[bank]])
                                    col = (ct * 8 + tp) * 128
                                    V(lambda e, bank=bank, col=col: e.tensor_copy(out=ygT[:, :, col:col + 128],
                                                                                  in_=PS[bank][:].rearrange("p (a t) -> p a t", a=4)),
                                      r=[psn[bank]], w=["ygT"])
                            S.barrier()
                        S.barrier()
                        c12.close()
                        ck("C3")
                        with ExitStack() as c5:
                            sb5 = mk(c5)
                            wgl = sb5("wgl", [128, 4, 512], BF16)
                            wst[0] = sb5("wsta5", [128, 8, 128])
                            wbf[0] = sb5("wbf5a", [128, 8, 128], BF16)
                            wbf[1] = sb5("wbf5b", [128, 8, 128], BF16)
                            wgs_st = sb5("wgs_st", [128, 512])
                            sig = [sb5("sig%d" % i, [128, 512]) for i in range(2)]
                            gsl = [sb5("gsl%d" % i, [128, 512], BF16) for i in range(2)]
                            gth = [sb5("gth%d" % i, [128, 512]) for i in range(2)]
                            tm5 = [sb5("tm5%d" % i, [128, 512]) for i in range(2)]
                            for kc in range(4):
                                D(lambda e, kc=kc: e.dma_start(out=wgs_st[:], in_=wglu_d[kc * 128:(kc + 1) * 128, :]), w=["wgs_st"])
                                V(lambda e, kc=kc: e.tensor_copy(out=wgl[:, kc, :], in_=wgs_st[:]), r=["wgs_st"], w=["wgl"])
                            steps = [(cb, tg) for cb in range(4) for tg in range(4)]
                            wcur = {}

                            def pvw(ap, half):
                                return ap.rearrange("p (t q) -> p t q", t=8)[:, :, half * 64:(half + 1) * 64]

                            def mm5(si):
                                cb, tg = steps[si]
                                if tg == 0:
                                    wcur[cb] = load_w(2560 + cb * 128, cb % 2)
                                wgs, wgsn = wcur[cb]
                                ct, half = tg // 2, tg % 2
                                b0, b1 = 2 * (si % 4), 2 * (si % 4) + 1
                                for kc in range(4):
                                    T(lambda e, kc=kc: e.matmul(PS[b0][:], lhsT=wgl[:, kc, cb * 128:(cb + 1) * 128],
                                                                rhs=pvw(ygT[:, kc, ct * 1024:(ct + 1) * 1024], half),
                                                                start=(kc == 0), stop=(kc == 3)), r=["wgl", "ygT"], w=[psn[b0]])
                                for kc in range(8):
                                    T(lambda e, kc=kc: e.matmul(PS[b1][:], lhsT=wgs[:, kc, :], rhs=xT[:, kc, tg * 512:(tg + 1) * 512],
                                                                start=(kc == 0), stop=(kc == 7)), r=[wgsn, "xT"], w=[psn[b1]])

                            def ev5(si):
                                cb, tg = steps[si]
                                ct, half = tg // 2, tg % 2
                                i2 = si % 2
                                b0, b1 = 2 * (si % 4), 2 * (si % 4) + 1
                                sg, sgn = sig[i2], "sig%d" % i2
                                gh, ghn = gth[i2], "gth%d" % i2
                                gl, gln = gsl[i2], "gsl%d" % i2
                                t5, t5n = tm5[i2], "tm5%d" % i2
                                A(lambda e: e.activation(out=sg[:], in_=PS[b0][:], func=AF.Sigmoid, bias=bglu[:, cb, :]),
                                  r=[psn[b0], "bglu"], w=[sgn])
                                A(lambda e: e.activation(out=gh[:], in_=PS[b1][:], func=AF.Sigmoid), r=[psn[b1]], w=[ghn])
                                V(lambda e: e.tensor_tensor(out=gl[:].rearrange("p (t q) -> p q t", t=8),
                                                            in0=PS[b1][:].rearrange("p (q t) -> p q t", t=8),
                                                            in1=gh[:].rearrange("p (q t) -> p q t", t=8), op=ALU.mult),
                                  r=[ghn, psn[b1]], w=[gln])
                                G(lambda e: e.tensor_tensor(out=t5[:].rearrange("p (t q) -> p t q", t=8),
                                                            in0=pvw(ygT[:, cb, ct * 1024:(ct + 1) * 1024], half),
                                                            in1=sg[:].rearrange("p (t q) -> p t q", t=8), op=ALU.mult),
                                  r=["ygT", sgn], w=[t5n])
                                G(lambda e: e.tensor_tensor(out=pvw(ssmT[:, cb, ct * 1024:(ct + 1) * 1024], half),
                                                            in0=t5[:].rearrange("p (t q) -> p t q", t=8),
                                                            in1=gl[:].rearrange("p (t q) -> p t q", t=8), op=ALU.mult),
                                  r=[t5n, gln], w=["ssmT"])

                            mm5(0)
                            mm5(1)
                            mm5(2)
                            for si in range(16):
                                if si + 3 < 16:
                                    mm5(si + 3)
                                ev5(si)
                            if debug and s == 0:
                                with ExitStack() as dd:
                                    d32 = mk(dd)("d32y", [128, 4, SEQ])
                                    for nm, src in (("attnT", attnT), ("ssmT", ssmT), ("ygT", ygT)):
                                        S.barrier()
                                        V(lambda e, src=src: e.tensor_copy(out=d32[:], in_=src[:]), w=["d32y"])
                                        D(lambda e, nm=nm: e.dma_start(out=dbg[nm], in_=d32[:]), r=["d32y"])
                                    S.barrier()
                            S.barrier()
                    S.barrier()

                ck("C5")
                S.barrier()
                with ExitStack() as ds:
                    sbD = mk(ds)
                    Wout = sbD("Wout", [128, 8, DM], BF16)
                    Wg = sbD("Wg", [128, 8, DM], BF16)
                    Wp = sbD("Wp", [128, 2, DM], BF16)
                    wstd = [sbD("wstd%d" % i, [128, DM]) for i in range(2)]
                    cnt = 0
                    for (src, dst, dn_, nk, fold) in ((wout_d, Wout, "Wout", 8, None), (wg_d, Wg, "Wg", 8, pleg), (wp_d, Wp, "Wp", 2, None)):
                        for kc in range(nk):
                            st = wstd[cnt % 2]
                            rn = "wstd%d" % (cnt % 2)
                            D(lambda e, st=st, src=src, kc=kc: e.dma_start(out=st[:], in_=src[kc * 128:(kc + 1) * 128, :]), w=[rn])
                            if fold is None:
                                if cnt % 2 == 0:
                                    V(lambda e, st=st, dst=dst, kc=kc: e.tensor_copy(out=dst[:, kc, :], in_=st[:]), r=[rn], w=[dn_])
                                else:
                                    A(lambda e, st=st, dst=dst, kc=kc: e.copy(out=dst[:, kc, :], in_=st[:]), r=[rn], w=[dn_])
                            else:
                                V(lambda e, st=st, dst=dst, kc=kc, fold=fold: e.tensor_scalar(
                                    out=dst[:, kc, :], in0=st[:], scalar1=fold[:, kc, :], scalar2=None, op0=ALU.mult),
                                  r=[rn, "pleg"], w=[dn_])
                            cnt += 1
                    xd = [sbD("xd%d" % i, [128, DM]) for i in range(3)]
                    pd = [sbD("pd%d" % i, [128, 256]) for i in range(3)]
                    hh = [sbD("hh%d" % i, [128, DM]) for i in range(2)]
                    sq = sbD("sqd", [128, DM])
                    st4 = [sbD("st4d%d" % i, [128, 4]) for i in range(2)]
                    hT = [sbD("hT%d" % i, [128, 8, 128], BF16) for i in range(2)]
                    pT = [sbD("pT%d" % i, [128, 2, 128], BF16) for i in range(2)]
                    gate = [sbD("gate%d" % i, [128, DM]) for i in range(2)]
                    oo = [sbD("oo%d" % i, [128, DM]) for i in range(2)]

                    def XL(it):
                        ct, tp = it // 8, it % 8
                        xi, xn = xd[it % 3], "xd%d" % (it % 3)
                        pi_, pn = pd[it % 3], "pd%d" % (it % 3)
                        r0 = ct * 1024 + tp
                        D(lambda e: e.dma_start(out=xi[:], in_=x_d[s, r0:(ct + 1) * 1024:8, :]), w=[xn])
                        D(lambda e: e.dma_start(out=pi_[:], in_=p_d[s, r0:(ct + 1) * 1024:8, :]), w=[pn])

                    def X1(it):
                        ct, tp = it // 8, it % 8
                        b2 = it % 2
                        xi, xn = xd[it % 3], "xd%d" % (it % 3)
                        h_, hn = hh[b2], "hh%d" % b2
                        s_, sn_ = st4[b2], "st4d%d" % b2
                        r0 = ct * 1024 + tp
                        col = it * 128
                        for nh in range(2):
                            for kc in range(8):
                                if kc < 4:
                                    lt = attnT[:, kc, r0:(ct + 1) * 1024:8]
                                    rn = "attnT"
                                else:
                                    lt = ssmT[:, kc - 4, col:col + 128]
                                    rn = "ssmT"
                                T(lambda e, lt=lt, kc=kc, nh=nh: e.matmul(PS[nh][:], lhsT=lt, rhs=Wout[:, kc, nh * 512:(nh + 1) * 512],
                                                                          start=(kc == 0), stop=(kc == 7)), r=[rn, "Wout"], w=[psn[nh]])
                            V(lambda e, nh=nh: e.tensor_tensor(out=h_[:, nh * 512:(nh + 1) * 512], in0=PS[nh][:],
                                                               in1=xi[:, nh * 512:(nh + 1) * 512], op=ALU.add),
                              r=[psn[nh], xn], w=[hn])
                        A(lambda e: e.activation(out=sq[:], in_=h_[:], func=AF.Square), r=[hn], w=["sqd"])
                        V(lambda e: e.tensor_reduce(out=s_[:, 0:1], in_=sq[:], axis=AX.X, op=ALU.add), r=["sqd"], w=[sn_])
                        V(lambda e: e.tensor_scalar(out=s_[:, 1:2], in0=s_[:, 0:1], scalar1=1.0 / DM, scalar2=EPS, op0=ALU.mult,
                                                    op1=ALU.add), r=[sn_], w=[sn_])
                        G(lambda e: e.tensor_tensor(out=s_[:, 3:4], in0=s_[:, 1:2], in1=cneg[:, 0:1], op=ALU.pow),
                          r=[sn_, "cneg"], w=[sn_])

                    def X2(it):
                        b2 = it % 2
                        pi_, pn = pd[it % 3], "pd%d" % (it % 3)
                        h_, hn = hh[b2], "hh%d" % b2
                        hT_, hTn = hT[b2], "hT%d" % b2
                        pT_, pTn = pT[b2], "pT%d" % b2
                        for half in range(2):
                            for q4 in range(4):
                                kc = half * 4 + q4
                                T(lambda e, kc=kc, q4=q4, half=half: e.transpose(out=PS[2 + half][:, q4 * 128:(q4 + 1) * 128],
                                                                                 in_=h_[:, kc * 128:(kc + 1) * 128], identity=ident[:]),
                                  r=[hn, "ident"], w=[psn[2 + half]])
                            A(lambda e, half=half: e.copy(out=hT_[:, half * 4:half * 4 + 4, :],
                                                          in_=PS[2 + half][:].rearrange("p (k t) -> p k t", k=4)),
                              r=[psn[2 + half]], w=[hTn])
                        for q2 in range(2):
                            T(lambda e, q2=q2: e.transpose(out=PS[4][:, q2 * 128:(q2 + 1) * 128], in_=pi_[:, q2 * 128:(q2 + 1) * 128],
                                                           identity=ident[:]), r=[pn, "ident"], w=[psn[4]])
                        V(lambda e: e.tensor_copy(out=pT_[:], in_=PS[4][:, 0:256].rearrange("p (k t) -> p k t", k=2)),
                          r=[psn[4]], w=[pTn])

                    def Y(it):
                        ct, tp = it // 8, it % 8
                        b2 = it % 2
                        h_, hn = hh[b2], "hh%d" % b2
                        s_, sn_ = st4[b2], "st4d%d" % b2
                        hT_, hTn = hT[b2], "hT%d" % b2
                        pT_, pTn = pT[b2], "pT%d" % b2
                        g_, gn = gate[b2], "gate%d" % b2
                        oi, on = oo[b2], "oo%d" % b2
                        r0 = ct * 1024 + tp
                        for nh in range(2):
                            for kc in range(8):
                                T(lambda e, kc=kc, nh=nh: e.matmul(PS[5][:], lhsT=hT_[:, kc, :], rhs=Wg[:, kc, nh * 512:(nh + 1) * 512],
                                                                   start=(kc == 0), stop=(kc == 7)), r=[hTn, "Wg"], w=[psn[5]])
                            A(lambda e, nh=nh: e.activation(out=g_[:, nh * 512:(nh + 1) * 512], in_=PS[5][:], func=AF.Sigmoid,
                                                            scale=s_[:, 3:4]), r=[psn[5], sn_], w=[gn])
                            for kc in range(2):
                                T(lambda e, kc=kc, nh=nh: e.matmul(PS[6 + nh][:], lhsT=pT_[:, kc, :], rhs=Wp[:, kc, nh * 512:(nh + 1) * 512],
                                                                   start=(kc == 0), stop=(kc == 1)), r=[pTn, "Wp"], w=[psn[6 + nh]])
                            V(lambda e, nh=nh: e.tensor_tensor(out=oi[:, nh * 512:(nh + 1) * 512], in0=PS[6 + nh][:],
                                                               in1=g_[:, nh * 512:(nh + 1) * 512], op=ALU.mult),
                              r=[psn[6 + nh], gn], w=[on])
                        G(lambda e: e.tensor_tensor(out=oi[:], in0=oi[:], in1=h_[:], op=ALU.add), r=[on, hn], w=[on])
                        D(lambda e: e.dma_start(out=out_d[s, r0:(ct + 1) * 1024:8, :], in_=oi[:]), r=[on])

                    XL(0)
                    XL(1)
                    X1(0)
                    X2(0)
                    for it in range(16):
                        if it + 2 < 16:
                            XL(it + 2)
                        if it + 1 < 16:
                            X1(it + 1)
                        Y(it)
                        if it + 1 < 16:
                            X2(it + 1)
                    S.barrier()
                    ck("D")
        except _Stop:
            pass
        S.barrier()
    return nc


_NC_CACHE = {}


def kernel(**inputs):
    f = lambda a: np.ascontiguousarray(np.asarray(a, dtype=np.float32))
    x = f(inputs["x"])
    p = f(inputs["p"])[0]
    shared = {}
    for k in ("mix_norm", "w_in", "q_norm", "k_norm", "lambda_re", "lambda_im", "log_dt", "b_re", "b_im", "c_re", "c_im",
              "d_skip", "w_glu", "b_glu", "w_out", "ple_norm", "w_ple_gate", "w_ple_proj"):
        shared[k] = f(inputs[k])[0]
    if "nc" not in _NC_CACHE:
        _NC_CACHE["nc"] = build_nc()
    nc = _NC_CACHE["nc"]
    in_maps = []
    for c in range(NCORES):
        m = {"x": np.ascontiguousarray(x[2 * c:2 * c + 2]), "p": np.ascontiguousarray(p[2 * c:2 * c + 2])}
        m.update(shared)
        in_maps.append(m)
    res = run_bass_kernel_spmd(nc, in_maps, core_ids=list(range(NCORES)))
    out = np.concatenate([np.asarray(r["out"], dtype=np.float32) for r in res.results], axis=0)
    return out
```

```python
import math
from contextlib import ExitStack
import numpy as np
import concourse.bass as bass
import concourse.mybir as mybir
from concourse.bass_utils import run_bass_kernel_spmd

F32 = mybir.dt.float32
BF16 = mybir.dt.bfloat16
F32R = mybir.dt.float32r
I32 = mybir.dt.int32
ALU = mybir.AluOpType
AF = mybir.ActivationFunctionType
AX = mybir.AxisListType

NCORES = 8
SEQ = 2048
DM = 1024
EPS = 1e-6
MAGIC = 12582912.0
TWO_PI = 2.0 * math.pi
EXACT_MASK = False
ALPHA = 1.0 if EXACT_MASK else 38.2305


class Sched:
    ENGS = ("pe", "act", "dve", "pool", "sp")

    def __init__(self, nc, n_dma_sems=24):
        self.nc = nc
        self.cnt = {e: 0 for e in self.ENGS}
        self.sem = {}
        self.seen = {e: {} for e in self.ENGS}
        self.last_w = {}
        self.readers = {}
        self.dma_sems = []
        self.dma_uses = []
        self.n_dma_sems = n_dma_sems
        self.dma_rr = 0

    def alloc(self, stack):
        for e in self.ENGS:
            if e == "sp":
                continue
            self.sem[e] = stack.enter_context(self.nc.semaphore("s_" + e))
        for i in range(self.n_dma_sems):
            self.dma_sems.append(stack.enter_context(self.nc.semaphore("d%d" % i)))
            self.dma_uses.append(0)

    def _need(self, eng, tok, waits):
        key, sem, val, src = tok
        if self.seen[eng].get(key, 0) >= val:
            return
        self.seen[eng][key] = val
        waits.append((sem, val))

    dead = False

    def op(self, eng, fn, reads=(), writes=(), dma=False):
        if self.dead:
            return None
        waits = []
        for r in reads:
            t = self.last_w.get(r)
            if t is not None:
                self._need(eng, t, waits)
        for w in writes:
            t = self.last_w.get(w)
            if t is not None and (dma or t[3] != eng or eng != "pe"):
                self._need(eng, t, waits)
            for t in self.readers.get(w, {}).values():
                if dma or t[3] != eng or eng != "pe":
                    self._need(eng, t, waits)
        if dma:
            j = self.dma_rr
            self.dma_rr = (self.dma_rr + 1) % self.n_dma_sems
            k = self.dma_uses[j]
            if k > 0:
                self._need(eng, ("d%d" % j, self.dma_sems[j], 16 * k, None), waits)
            self.dma_uses[j] = k + 1
            tok = ("d%d" % j, self.dma_sems[j], 16 * (k + 1), None)
            inc = (self.dma_sems[j], 16)
        else:
            self.cnt[eng] += 1
            tok = (eng, self.sem[eng], self.cnt[eng], eng)
            inc = (self.sem[eng], 1)
        for w in writes:
            self.last_w[w] = tok
            self.readers[w] = {}
        for r in reads:
            self.readers.setdefault(r, {})[tok[0]] = tok
        self._emit(eng, waits, fn, inc)
        return tok

    def group(self, eng, ops):
        if self.dead:
            return
        waits = []
        for fn, reads, writes in ops:
            for r in reads:
                t = self.last_w.get(r)
                if t is not None:
                    self._need(eng, t, waits)
            for w in writes:
                t = self.last_w.get(w)
                if t is not None and (t[3] != eng or eng != "pe"):
                    self._need(eng, t, waits)
                for t in self.readers.get(w, {}).values():
                    if t[3] != eng or eng != "pe":
                        self._need(eng, t, waits)
        first = True
        for fn, reads, writes in ops:
            self.cnt[eng] += 1
            tok = (eng, self.sem[eng], self.cnt[eng], eng)
            for w in writes:
                self.last_w[w] = tok
                self.readers[w] = {}
            for r in reads:
                self.readers.setdefault(r, {})[tok[0]] = tok
            self._emit(eng, waits if first else [], fn, (self.sem[eng], 1))
            first = False

    def _eng(self, name):
        nc = self.nc
        return {"pe": nc.tensor, "act": nc.scalar, "dve": nc.vector, "pool": nc.gpsimd, "sp": nc.sync}[name]

    def _emit(self, eng, waits, fn, inc):
        e = self._eng(eng)
        for sem, val in waits:
            e.wait_ge(sem, val)
        if fn is not None:
            ins = fn(e)
            ins.then_inc(inc[0], inc[1])

    def barrier(self):
        if self.dead:
            return
        toks = []
        for e in self.ENGS:
            if e != "sp" and self.cnt[e] > 0:
                toks.append((e, self.sem[e], self.cnt[e], e))
        for j in range(self.n_dma_sems):
            if self.dma_uses[j] > 0:
                toks.append(("d%d" % j, self.dma_sems[j], 16 * self.dma_uses[j], None))
        for e in self.ENGS:
            waits = []
            for t in toks:
                self._need(e, t, waits)
            if waits:
                self._emit(e, waits, None, None)


class _Stop(Exception):
    pass


def build_nc(debug=False, stop_after=None):
    nc = bass.Bass("TRN2", target_bir_lowering=False)

    def din(name, shape):
        return nc.dram_tensor(name, shape, F32, kind="ExternalInput").ap()

    x_d = din("x", [2, SEQ, DM])
    p_d = din("p", [2, SEQ, 256])
    mixn_d = din("mix_norm", [DM])
    win_d = din("w_in", [DM, 3072])
    qn_d = din("q_norm", [64])
    kn_d = din("k_norm", [64])
    lre_d = din("lambda_re", [32, 64])
    lim_d = din("lambda_im", [32, 64])
    ldt_d = din("log_dt", [32])
    bre_d = din("b_re", [32, 64, 16])
    bim_d = din("b_im", [32, 64, 16])
    cre_d = din("c_re", [32, 16, 64])
    cim_d = din("c_im", [32, 16, 64])
    dsk_d = din("d_skip", [512])
    wglu_d = din("w_glu", [512, 512])
    bglu_d = din("b_glu", [512])
    wout_d = din("w_out", [DM, DM])
    plen_d = din("ple_norm", [DM])
    wg_d = din("w_ple_gate", [DM, DM])
    wp_d = din("w_ple_proj", [256, DM])
    out_d = nc.dram_tensor("out", [2, SEQ, DM], F32, kind="ExternalOutput").ap()
    dbg = {}
    if debug:
        dbg["xT"] = nc.dram_tensor("dbg_xT", [128, 8, SEQ], F32, kind="ExternalOutput").ap()
        dbg["attnT"] = nc.dram_tensor("dbg_attnT", [128, 4, SEQ], F32, kind="ExternalOutput").ap()
        dbg["ssmT"] = nc.dram_tensor("dbg_ssmT", [128, 4, SEQ], F32, kind="ExternalOutput").ap()
        dbg["ygT"] = nc.dram_tensor("dbg_ygT", [128, 4, SEQ], F32, kind="ExternalOutput").ap()
        dbg["qT"] = nc.dram_tensor("dbg_qT", [128, SEQ], F32, kind="ExternalOutput").ap()
        dbg["T"] = nc.dram_tensor("dbg_T", [128, 32, 128], F32, kind="ExternalOutput").ap()
        dbg["S"] = nc.dram_tensor("dbg_S", [128, 16, 2, 257], F32, kind="ExternalOutput").ap()

    with ExitStack() as top:
        S = Sched(nc)
        S.alloc(top)
        top.enter_context(nc.allow_non_contiguous_dma(reason="small strided parameter loads"))

        uid = [0]

        def mk(stack):
            def sb(name, shape, dt=F32):
                uid[0] += 1
                return stack.enter_context(nc.sbuf_tensor("%s_u%d" % (name, uid[0]), shape, dt))
            return sb

        sbP = mk(top)
        PS = [top.enter_context(nc.psum_tensor("ps%d" % i, [128, 512], F32)) for i in range(8)]
        psn = ["ps%d" % i for i in range(8)]

        def V(fn, r=(), w=()):
            S.op("dve", fn, reads=r, writes=w)

        def A(fn, r=(), w=()):
            S.op("act", fn, reads=r, writes=w)

        def G(fn, r=(), w=()):
            S.op("pool", fn, reads=r, writes=w)

        def T(fn, r=(), w=()):
            S.op("pe", fn, reads=r, writes=w)

        def D(fn, r=(), w=()):
            S.op("sp", fn, reads=r, writes=w, dma=True)

        dq = [0]

        def D2(fn, r=(), w=()):
            dq[0] += 1
            S.op("sp" if dq[0] % 2 else "act", fn, reads=r, writes=w, dma=True)

        def ck(name):
            if stop_after == name:
                S.barrier()
                S.dead = True

        try:
            ident = sbP("ident", [128, 128])
            ones_f = sbP("ones_f", [128, 128])
            identr = sbP("identr", [128, 128], F32 if EXACT_MASK else BF16)
            cneg = sbP("cneg", [128, 2])
            blk1 = sbP("blk1", [128, 128], BF16)
            MT = sbP("MT", [128, 2432], F32 if EXACT_MASK else BF16)
            mixg = sbP("mixg", [128, 8, 1])
            pleg = sbP("pleg", [128, 8, 1])
            qg = sbP("qg", [128, 1])
            kg = sbP("kg", [128, 1])
            bglu = sbP("bglu", [128, 4, 1])
            Tm = sbP("Tm", [128, 32, 128], BF16)
            PGre = sbP("PGre", [128, 32, 64], BF16)
            PGim = sbP("PGim", [128, 32, 64], BF16)
            PCre = sbP("PCre", [128, 16, 128], BF16)
            PCim = sbP("PCim", [128, 16, 128], BF16)
            APR = sbP("APR", [128, 16, 16, 2])
            API = sbP("API", [128, 16, 16, 2])
            attnT = sbP("attnT", [128, 4, SEQ], BF16)
            ssmT = sbP("ssmT", [128, 4, SEQ], BF16)

            G(lambda e: e.memset(ident[:], 1.0), w=["ident"])
            G(lambda e: e.affine_select(out=ident[:], in_=ident[:], pattern=[[-1, 128]], base=0, channel_multiplier=1,
                                        compare_op=ALU.is_equal, fill=0.0), r=["ident"], w=["ident"])
            V(lambda e: e.memset(ones_f[:], 1.0), w=["ones_f"])
            V(lambda e: e.tensor_copy(out=(identr[:].bitcast(F32R) if EXACT_MASK else identr[:]), in_=ident[:]), r=["ident"], w=["identr"])
            V(lambda e: e.memset(cneg[:, 0:1], -0.5), w=["cneg"])
            V(lambda e: e.memset(cneg[:, 1:2], -1.0), w=["cneg"])
            V(lambda e: e.memset(blk1[:], 0.0), w=["blk1"])
            V(lambda e: e.memset(blk1[0:64, 0:64], 1.0), w=["blk1"])
            V(lambda e: e.memset(blk1[64:128, 64:128], 1.0), w=["blk1"])
            ck("consts")

            with ExitStack() as su:
                sb = mk(su)
                NI = 2432
                sb_keep = sb
                msk_scope = ExitStack()
                sb = mk(msk_scope)
                di = sb("di", [128, NI], I32)
                df = sb("df", [128, NI])
                t1 = sb("mt1", [128, NI])
                t2 = sb("mt2", [128, NI])
                acc = sb("macc", [128, NI])
                G(lambda e: e.iota(di[:], pattern=[[1, NI]], base=-384, channel_multiplier=-1), w=["di"])
                V(lambda e: e.tensor_copy(out=df[:], in_=di[:]), r=["di"], w=["df"])
                V(lambda e: e.tensor_scalar(out=acc[:], in0=df[:], scalar1=128.0, scalar2=None, op0=ALU.is_le), r=["df"], w=["acc"])
                for (mod, lim) in ((4.0, 512.0), (16.0, None)):
                    V(lambda e, mod=mod: e.tensor_scalar(out=t1[:], in0=df[:], scalar1=1.0 / mod, scalar2=MAGIC, op0=ALU.mult,
                                                         op1=ALU.add), r=["df"], w=["t1"])
                    V(lambda e, mod=mod: e.tensor_scalar(out=t1[:], in0=t1[:], scalar1=MAGIC, scalar2=mod, op0=ALU.subtract,
                                                         op1=ALU.mult), r=["t1"], w=["t1"])
                    V(lambda e: e.tensor_tensor(out=t1[:], in0=t1[:], in1=df[:], op=ALU.is_equal), r=["t1", "df"], w=["t1"])
                    if lim is not None:
                        V(lambda e, lim=lim: e.tensor_scalar(out=t2[:], in0=df[:], scalar1=lim, scalar2=None, op0=ALU.is_le),
                          r=["df"], w=["t2"])
                        V(lambda e: e.tensor_tensor(out=t1[:], in0=t1[:], in1=t2[:], op=ALU.mult), r=["t1", "t2"], w=["t1"])
                    V(lambda e: e.tensor_tensor(out=acc[:], in0=acc[:], in1=t1[:], op=ALU.add), r=["acc", "t1"], w=["acc"])
                V(lambda e: e.tensor_scalar(out=t2[:], in0=df[:], scalar1=0.0, scalar2=None, op0=ALU.is_ge), r=["df"], w=["t2"])
                V(lambda e: e.tensor_tensor(out=acc[:], in0=acc[:], in1=t2[:], op=ALU.mult), r=["acc", "t2"], w=["acc"])
                if EXACT_MASK:
                    V(lambda e: e.tensor_scalar(out=t1[:], in0=acc[:], scalar1=1.0, scalar2=None, op0=ALU.max), r=["acc"], w=["t1"])
                    A(lambda e: e.activation(out=t1[:], in_=t1[:], func=AF.Ln), r=["t1"], w=["t1"])
                    V(lambda e: e.tensor_scalar(out=t2[:], in0=acc[:], scalar1=0.0, scalar2=-30000.0, op0=ALU.is_equal, op1=ALU.mult),
                      r=["acc"], w=["t2"])
                    V(lambda e: e.tensor_tensor(out=MT[:].bitcast(F32R), in0=t1[:], in1=t2[:], op=ALU.add), r=["t1", "t2"], w=["MT"])
                else:
                    V(lambda e: e.tensor_scalar(out=t1[:], in0=acc[:], scalar1=2.0, scalar2=26.5, op0=ALU.is_equal, op1=ALU.mult),
                      r=["acc"], w=["t1"])
                    V(lambda e: e.tensor_scalar(out=t2[:], in0=acc[:], scalar1=3.0, scalar2=42.0, op0=ALU.is_equal, op1=ALU.mult),
                      r=["acc"], w=["t2"])
                    V(lambda e: e.tensor_tensor(out=t1[:], in0=t1[:], in1=t2[:], op=ALU.add), r=["t1", "t2"], w=["t1"])
                    V(lambda e: e.tensor_scalar(out=t2[:], in0=acc[:], scalar1=0.0, scalar2=-1.0e6, op0=ALU.is_equal, op1=ALU.mult),
                      r=["acc"], w=["t2"])
                    V(lambda e: e.tensor_tensor(out=MT[:], in0=t1[:], in1=t2[:], op=ALU.add), r=["t1", "t2"], w=["MT"])
                S.barrier()
                msk_scope.close()
                sb = sb_keep
                ck("mask")

                LR = sb("LR", [128, 16, 1])
                LI = sb("LI", [128, 16, 1])
                DT = sb("DT", [128, 16, 1])
                dtrow = sb("dtrow", [1, 32])
                D(lambda e: e.dma_start(out=dtrow[:], in_=ldt_d.rearrange("(o g) -> o g", o=1)), w=["dtrow"])
                T(lambda e: e.matmul(PS[5][:, 0:32], lhsT=ones_f[0:1, :], rhs=dtrow[:], start=True, stop=True),
                  r=["ones_f", "dtrow"], w=[psn[5]])
                for j in range(2):
                    V(lambda e, j=j: e.tensor_copy(out=DT[64 * j:64 * j + 64, :, 0], in_=PS[5][64 * j:64 * j + 64, j:32:2]),
                      r=[psn[5]], w=["DT"])
                Ln_ = [sb("Lnat%d" % i, [32, 2, 64]) for i in range(2)]
                for li, (src, dst, dname) in enumerate(((lre_d, LR, "LR"), (lim_d, LI, "LI"))):
                    D(lambda e, li=li, src=src: e.dma_start(out=Ln_[li][:, 0, :], in_=src), w=["Lnat%d" % li])
                    G(lambda e, li=li: e.tensor_copy(out=Ln_[li][:, 1, :], in_=Ln_[li][:, 0, :]), r=["Lnat%d" % li], w=["Lnat%d" % li])
                    bank = 6 + li
                    T(lambda e, li=li, bank=bank: e.transpose(out=PS[bank][:, 0:32], in_=Ln_[li][:].rearrange("p d n -> p (d n)"),
                                                              identity=ident[0:32, 0:32]), r=["Lnat%d" % li, "ident"], w=[psn[bank]])
                    for j in range(2):
                        V(lambda e, dst=dst, j=j, bank=bank: e.tensor_copy(
                            out=dst[64 * j:64 * j + 64, :, 0], in_=PS[bank][64 * j:64 * j + 64, j:32:2]), r=[psn[bank]], w=[dname])
                BR = sb("BR", [128, 16, 16])
                BI = sb("BI", [128, 16, 16])
                for j in range(2):
                    D2(lambda e, j=j: e.dma_start(out=BR[64 * j:64 * j + 64], in_=bre_d.rearrange("(gp j) n c -> j n gp c", j=2)[j]), w=["BR"])
                    D2(lambda e, j=j: e.dma_start(out=BI[64 * j:64 * j + 64], in_=bim_d.rearrange("(gp j) n c -> j n gp c", j=2)[j]), w=["BI"])
                Cn = [sb("Cn%d" % i, [128, 4, 2, 64]) for i in range(2)]
                for ci, src in enumerate((cre_d, cim_d)):
                    D(lambda e, ci=ci, src=src: e.dma_start(out=Cn[ci][:, :, 0, :],
                                                            in_=src.rearrange("(t g) c n -> (g c) t n", t=4)), w=["Cn%d" % ci])
                    G(lambda e, ci=ci: e.tensor_copy(out=Cn[ci][:, :, 1, :], in_=Cn[ci][:, :, 0, :]), r=["Cn%d" % ci], w=["Cn%d" % ci])
                gnat = sb("gnat", [20, 128])
                D(lambda e: e.dma_start(out=gnat[0:8, :], in_=mixn_d.rearrange("(k p) -> k p", p=128)), w=["gnat"])
                D(lambda e: e.dma_start(out=gnat[8:16, :], in_=plen_d.rearrange("(k p) -> k p", p=128)), w=["gnat"])
                D(lambda e: e.dma_start(out=gnat[16:20, :], in_=bglu_d.rearrange("(k p) -> k p", p=128)), w=["gnat"])
                T(lambda e: e.transpose(out=PS[4][:, 0:20], in_=gnat[:], identity=ident[0:20, 0:20]), r=["gnat", "ident"], w=[psn[4]])
                V(lambda e: e.tensor_copy(out=mixg[:, :, 0], in_=PS[4][:, 0:8]), r=[psn[4]], w=["mixg"])
                V(lambda e: e.tensor_copy(out=pleg[:, :, 0], in_=PS[4][:, 8:16]), r=[psn[4]], w=["pleg"])
                V(lambda e: e.tensor_copy(out=bglu[:, :, 0], in_=PS[4][:, 16:20]), r=[psn[4]], w=["bglu"])
                qkrow = sb("qkrow", [2, 2, 64])
                for hh in range(2):
                    D(lambda e, hh=hh: e.dma_start(out=qkrow[0:1, hh, :], in_=qn_d.rearrange("(o n) -> o n", o=1)), w=["qkrow"])
                    D(lambda e, hh=hh: e.dma_start(out=qkrow[1:2, hh, :], in_=kn_d.rearrange("(o n) -> o n", o=1)), w=["qkrow"])
                T(lambda e: e.transpose(out=PS[4][:, 32:34], in_=qkrow[:].rearrange("r h n -> r (h n)"), identity=ident[0:2, 0:2]),
                  r=["qkrow", "ident"], w=[psn[4]])
                V(lambda e: e.tensor_scalar(out=qg[:], in0=PS[4][:, 32:33], scalar1=0.125 * ALPHA, scalar2=None, op0=ALU.mult),
                  r=[psn[4]], w=["qg"])
                V(lambda e: e.tensor_copy(out=kg[:], in_=PS[4][:, 33:34]), r=[psn[4]], w=["kg"])
                CR = sb("CR", [128, 16, 16])
                CI = sb("CI", [128, 16, 16])
                for ci, (dst, dname) in enumerate(((CR, "CR"), (CI, "CI"))):
                    for t in range(4):
                        bank = (ci * 4 + t) % 8
                        T(lambda e, ci=ci, t=t, bank=bank: e.transpose(
                            out=PS[bank][:, 0:128], in_=Cn[ci][:, t, :, :].rearrange("p d n -> p (d n)"), identity=ident[:]),
                          r=["Cn%d" % ci, "ident"], w=[psn[bank]])
                        for j in range(2):
                            V(lambda e, dst=dst, t=t, j=j, bank=bank: e.tensor_copy(
                                out=dst[64 * j:64 * j + 64, 4 * t:4 * t + 4, :],
                                in_=PS[bank][64 * j:64 * j + 64, 0:128].rearrange("p (g q c) -> p g q c", q=2, c=16)[:, :, j, :]),
                              r=[psn[bank]], w=[dname])
                dcol = sb("dcol", [128, 32, 1])
                dnat = sb("dnat", [32, 16])
                d16 = sb("d16", [16, 32])
                Rrep = sb("Rrep", [16, 128])
                D(lambda e: e.dma_start(out=dnat[:], in_=dsk_d.rearrange("(g c) -> g c", c=16)), w=["dnat"])
                T(lambda e: e.transpose(out=PS[4][0:16, 64:96], in_=dnat[:], identity=ident[0:32, 0:32]), r=["dnat", "ident"], w=[psn[4]])
                V(lambda e: e.tensor_copy(out=d16[:], in_=PS[4][0:16, 64:96]), r=[psn[4]], w=["d16"])
                for sp_ in range(8):
                    V(lambda e, sp_=sp_: e.tensor_copy(out=Rrep[:, 16 * sp_:16 * sp_ + 16], in_=ident[0:16, 0:16]), r=["ident"], w=["Rrep"])
                T(lambda e: e.matmul(PS[4][:, 128:160], lhsT=Rrep[:], rhs=d16[:], start=True, stop=True), r=["Rrep", "d16"], w=[psn[4]])
                V(lambda e: e.tensor_copy(out=dcol[:, :, 0], in_=PS[4][:, 128:160]), r=[psn[4]], w=["dcol"])

                def sm(name):
                    return sb("s_" + name, [128, 16, 1])

                dt_ = sm("dt_"); th = sm("th"); lrdt = sm("lrdt"); mag = sm("mag"); magi = sm("magi")
                rr = sm("rr"); sn = sm("sn"); cs = sm("cs"); tA = sm("tA"); tB = sm("tB")
                a_re = sm("a_re"); a_im = sm("a_im"); i_re = sm("i_re"); i_im = sm("i_im")
                c_re = sm("c_re"); c_im = sm("c_im"); den = sm("den")
                A(lambda e: e.activation(out=dt_[:], in_=DT[:], func=AF.Exp), r=["DT"], w=["dt_"])
                V(lambda e: e.tensor_tensor(out=th[:], in0=LI[:], in1=dt_[:], op=ALU.mult), r=["LI", "dt_"], w=["th"])
                V(lambda e: e.tensor_tensor(out=lrdt[:], in0=LR[:], in1=dt_[:], op=ALU.mult), r=["LR", "dt_"], w=["lrdt"])
                A(lambda e: e.activation(out=mag[:], in_=lrdt[:], func=AF.Exp), r=["lrdt"], w=["mag"])
                A(lambda e: e.activation(out=magi[:], in_=lrdt[:], func=AF.Exp, scale=-1.0), r=["lrdt"], w=["magi"])

                def sin_of(dst, shift, dname):
                    V(lambda e: e.tensor_scalar(out=tA[:], in0=th[:], scalar1=shift, scalar2=1.0 / TWO_PI, op0=ALU.add, op1=ALU.mult),
                      r=["th"], w=["tA"])
                    V(lambda e: e.tensor_scalar(out=tA[:], in0=tA[:], scalar1=MAGIC, scalar2=MAGIC, op0=ALU.add, op1=ALU.subtract),
                      r=["tA"], w=["tA"])
                    V(lambda e: e.tensor_scalar(out=tB[:], in0=th[:], scalar1=shift, scalar2=None, op0=ALU.add), r=["th"], w=["tB"])
                    V(lambda e: e.scalar_tensor_tensor(out=rr[:], in0=tA[:], scalar=-TWO_PI, in1=tB[:], op0=ALU.mult, op1=ALU.add),
                      r=["tA", "tB"], w=["rr"])
                    V(lambda e: e.tensor_scalar(out=rr[:], in0=rr[:], scalar1=math.pi, scalar2=-math.pi, op0=ALU.min, op1=ALU.max),
                      r=["rr"], w=["rr"])
                    A(lambda e: e.activation(out=dst[:], in_=rr[:], func=AF.Sin), r=["rr"], w=[dname])

                sin_of(sn, 0.0, "sn")
                sin_of(cs, math.pi / 2.0, "cs")
                V(lambda e: e.tensor_tensor(out=a_re[:], in0=mag[:], in1=cs[:], op=ALU.mult), r=["mag", "cs"], w=["a_re"])
                V(lambda e: e.tensor_tensor(out=a_im[:], in0=mag[:], in1=sn[:], op=ALU.mult), r=["mag", "sn"], w=["a_im"])
                V(lambda e: e.tensor_tensor(out=i_re[:], in0=magi[:], in1=cs[:], op=ALU.mult), r=["magi", "cs"], w=["i_re"])
                V(lambda e: e.scalar_tensor_tensor(out=i_im[:], in0=magi[:], scalar=-1.0, in1=sn[:], op0=ALU.mult, op1=ALU.mult),
                  r=["magi", "sn"], w=["i_im"])
                V(lambda e: e.tensor_tensor(out=den[:], in0=LR[:], in1=LR[:], op=ALU.mult), r=["LR"], w=["den"])
                V(lambda e: e.tensor_tensor(out=tA[:], in0=LI[:], in1=LI[:], op=ALU.mult), r=["LI"], w=["tA"])
                V(lambda e: e.tensor_tensor(out=den[:], in0=den[:], in1=tA[:], op=ALU.add), r=["den", "tA"], w=["den"])
                V(lambda e: e.reciprocal(out=den[:], in_=den[:]), r=["den"], w=["den"])
                V(lambda e: e.tensor_scalar(out=tB[:], in0=a_re[:], scalar1=-1.0, scalar2=None, op0=ALU.add), r=["a_re"], w=["tB"])
                V(lambda e: e.tensor_tensor(out=c_re[:], in0=tB[:], in1=LR[:], op=ALU.mult), r=["tB", "LR"], w=["c_re"])
                V(lambda e: e.tensor_tensor(out=tA[:], in0=a_im[:], in1=LI[:], op=ALU.mult), r=["a_im", "LI"], w=["tA"])
                V(lambda e: e.tensor_tensor(out=c_re[:], in0=c_re[:], in1=tA[:], op=ALU.add), r=["c_re", "tA"], w=["c_re"])
                V(lambda e: e.tensor_tensor(out=c_re[:], in0=c_re[:], in1=den[:], op=ALU.mult), r=["c_re", "den"], w=["c_re"])
                V(lambda e: e.tensor_tensor(out=c_im[:], in0=a_im[:], in1=LR[:], op=ALU.mult), r=["a_im", "LR"], w=["c_im"])
                V(lambda e: e.tensor_tensor(out=tA[:], in0=tB[:], in1=LI[:], op=ALU.mult), r=["tB", "LI"], w=["tA"])
                V(lambda e: e.tensor_tensor(out=c_im[:], in0=c_im[:], in1=tA[:], op=ALU.subtract), r=["c_im", "tA"], w=["c_im"])
                V(lambda e: e.tensor_tensor(out=c_im[:], in0=c_im[:], in1=den[:], op=ALU.mult), r=["c_im", "den"], w=["c_im"])

                tb1 = sb("tb1", [128, 16, 16])
                tb2 = sb("tb2", [128, 16, 16])

                lit = {a_re.name: "a_re", a_im.name: "a_im", i_re.name: "i_re", i_im.name: "i_im",
                       c_re.name: "c_re", c_im.name: "c_im"}

                def nm(t):
                    return lit.get(t.name, t.name)

                tmpE = {"dve": (tb1, tb2, "tb1", "tb2", tA, tB, "tA", "tB"),
                        "pool": (sb("tb1p", [128, 16, 16]), sb("tb2p", [128, 16, 16]), "tb1p", "tb2p",
                                 sm("tAp"), sm("tBp"), "tAp", "tBp")}

                def cmul_b(eng, o_re, o_im, on, x_re, x_im, xn, s_re, s_im, neg_im=False):
                    u1, u2, n1, n2 = tmpE[eng][0:4]
                    sr = s_re[:].to_broadcast([128, 16, 16])
                    si = s_im[:].to_broadcast([128, 16, 16])
                    O = lambda fn, r, w: S.op(eng, fn, reads=r, writes=w)
                    O(lambda e: e.tensor_tensor(out=u1[:], in0=x_re, in1=sr, op=ALU.mult), xn + [nm(s_re)], [n1])
                    O(lambda e: e.tensor_tensor(out=u2[:], in0=x_im, in1=si, op=ALU.mult), xn + [nm(s_im)], [n2])
                    O(lambda e: e.tensor_tensor(out=o_re, in0=u1[:], in1=u2[:], op=ALU.subtract), [n1, n2], [on])
                    O(lambda e: e.tensor_tensor(out=u1[:], in0=x_re, in1=si, op=ALU.mult), xn + [nm(s_im)], [n1])
                    O(lambda e: e.tensor_tensor(out=u2[:], in0=x_im, in1=sr, op=ALU.mult), xn + [nm(s_re)], [n2])
                    if neg_im:
                        O(lambda e: e.tensor_tensor(out=u1[:], in0=u1[:], in1=u2[:], op=ALU.add), [n1, n2], [n1])
                        O(lambda e: e.tensor_scalar(out=o_im, in0=u1[:], scalar1=-1.0, scalar2=None, op0=ALU.mult), [n1], [on])
                    else:
                        O(lambda e: e.tensor_tensor(out=o_im, in0=u1[:], in1=u2[:], op=ALU.add), [n1, n2], [on])

                def cmul_s(eng, o_re, o_im, x_re, x_im, s_re, s_im):
                    u1, u2, n1, n2 = tmpE[eng][4:8]
                    O = lambda fn, r, w: S.op(eng, fn, reads=r, writes=w)
                    O(lambda e: e.tensor_tensor(out=u1[:], in0=x_re[:], in1=s_re[:], op=ALU.mult), [nm(x_re), nm(s_re)], [n1])
                    O(lambda e: e.tensor_tensor(out=u2[:], in0=x_im[:], in1=s_im[:], op=ALU.mult), [nm(x_im), nm(s_im)], [n2])
                    O(lambda e: e.tensor_tensor(out=o_re[:], in0=u1[:], in1=u2[:], op=ALU.subtract), [n1, n2], [nm(o_re)])
                    O(lambda e: e.tensor_tensor(out=u1[:], in0=x_re[:], in1=s_im[:], op=ALU.mult), [nm(x_re), nm(s_im)], [n1])
                    O(lambda e: e.tensor_tensor(out=u2[:], in0=x_im[:], in1=s_re[:], op=ALU.mult), [nm(x_im), nm(s_re)], [n2])
                    O(lambda e: e.tensor_tensor(out=o_im[:], in0=u1[:], in1=u2[:], op=ALU.add), [n1, n2], [nm(o_im)])

                BBr = sb("BBr", [128, 16, 16])
                BBi = sb("BBi", [128, 16, 16])
                cmul_b("dve", BBr[:], BBi[:], "BB", BR[:], BI[:], ["BR", "BI"], c_re, c_im)
                pw_re = [sm("pwr%d" % i) for i in range(9)]
                pw_im = [sm("pwi%d" % i) for i in range(9)]
                iw_re = [sm("iwr%d" % i) for i in range(8)]
                iw_im = [sm("iwi%d" % i) for i in range(8)]
                CPr = sb("CPr", [128, 16, 9, 16])
                CPi = sb("CPi", [128, 16, 9, 16])
                for (eng, lst_re, lst_im) in (("dve", pw_re, pw_im), ("pool", iw_re, iw_im)):
                    S.op(eng, lambda e, t=lst_re[0]: e.memset(t[:], 1.0), writes=[lst_re[0].name])
                    S.op(eng, lambda e, t=lst_im[0]: e.memset(t[:], 0.0), writes=[lst_im[0].name])
                for i in range(1, 8):
                    cmul_s("pool", iw_re[i], iw_im[i], iw_re[i - 1], iw_im[i - 1], i_re, i_im)
                for tau in range(9):
                    if tau + 1 < 9:
                        cmul_s("dve", pw_re[tau + 1], pw_im[tau + 1], pw_re[tau], pw_im[tau], a_re, a_im)
                    cmul_b("dve", CPr[:, :, tau, :], CPi[:, :, tau, :], "CP%d" % tau, CR[:], CI[:], ["CR", "CI"],
                           pw_re[tau], pw_im[tau], neg_im=True)
                Qr = sb("Qr", [128, 16, 8, 16])
                Qi = sb("Qi", [128, 16, 8, 16])
                Hr = sb("Hr", [128, 16, 8, 16])
                Hi = sb("Hi", [128, 16, 8, 16])
                for s_ in range(8):
                    cmul_b("pool", Qr[:, :, s_, :], Qi[:, :, s_, :], "Q%d" % s_, BBr[:], BBi[:], ["BB"], iw_re[s_], iw_im[s_])
                    cmul_b("dve" if s_ < 3 else "pool", Hr[:, :, s_, :], Hi[:, :, s_, :], "H%d" % s_, BBr[:], BBi[:], ["BB"],
                           pw_re[7 - s_], pw_im[7 - s_])
                CPn = ["CP%d" % t for t in range(9)]
                Qn = ["Q%d" % t for t in range(8)]
                Hn = ["H%d" % t for t in range(8)]
                V(lambda e: e.tensor_copy(out=PCre[:].rearrange("p g (t c) -> p g t c", c=16), in_=CPr[:, :, 1:9, :]), r=CPn, w=["PCre"])
                V(lambda e: e.tensor_copy(out=PCim[:].rearrange("p g (t c) -> p g t c", c=16), in_=CPi[:, :, 1:9, :]), r=CPn, w=["PCim"])
                qw_re = [pw_re[8]] + [sm("qwr%d" % i) for i in range(1, 16)]
                qw_im = [pw_im[8]] + [sm("qwi%d" % i) for i in range(1, 16)]
                for jj in range(16):
                    if jj > 0:
                        cmul_s("pool", qw_re[jj], qw_im[jj], qw_re[jj - 1], qw_im[jj - 1], pw_re[8], pw_im[8])
                    G(lambda e, jj=jj: e.tensor_copy(out=APR[:, jj, :, 0:1], in_=qw_re[jj][:]), r=[qw_re[jj].name], w=["APR"])
                    G(lambda e, jj=jj: e.tensor_copy(out=APR[:, jj, :, 1:2], in_=qw_re[jj][:]), r=[qw_re[jj].name], w=["APR"])
                    G(lambda e, jj=jj: e.tensor_scalar(out=API[:, jj, :, 0:1], in0=qw_im[jj][:], scalar1=-1.0, scalar2=None,
                                                       op0=ALU.mult), r=[qw_im[jj].name], w=["API"])
                    G(lambda e, jj=jj: e.tensor_copy(out=API[:, jj, :, 1:2], in_=qw_im[jj][:]), r=[qw_im[jj].name], w=["API"])
                for gp in range(16):
                    for ci, (src, dst, dn_) in enumerate(((Hr, PGre, "PGre"), (Hi, PGim, "PGim"))):
                        bank = (2 * gp + ci) % 8
                        T(lambda e, src=src, gp=gp, bank=bank: e.transpose(
                            out=PS[bank][:, 0:128], in_=src[:, gp, :, :].rearrange("p s c -> p (s c)"), identity=ident[:]),
                          r=Hn + ["ident"], w=[psn[bank]])
                        fcp = (lambda e, dst=dst, gp=gp, bank=bank: e.tensor_copy(
                            out=dst[:, 2 * gp:2 * gp + 2, :], in_=PS[bank][:, 0:128].rearrange("p (j n) -> p j n", j=2)))
                        if ci == 0:
                            V(fcp, r=[psn[bank]], w=[dn_])
                        else:
                            A(lambda e, dst=dst, gp=gp, bank=bank: e.copy(
                                out=dst[:, 2 * gp:2 * gp + 2, :], in_=PS[bank][:, 0:128].rearrange("p (j n) -> p j n", j=2)),
                              r=[psn[bank]], w=[dn_])
                tmask = sb("tmask", [128, 128])
                ttmp = sb("ttmp", [128, 128])
                G(lambda e: e.memset(tmask[:], 1.0), w=["tmask"])
                G(lambda e: e.affine_select(out=tmask[:], in_=tmask[:], pattern=[[16, 8], [0, 16]], base=15, channel_multiplier=-1,
                                            compare_op=ALU.is_ge, fill=0.0), r=["tmask"], w=["tmask"])
                for g in range(32):
                    gp, j = g // 2, g % 2
                    bank = g % 8
                    pb = 64 * j
                    T(lambda e, gp=gp, pb=pb, bank=bank: e.matmul(PS[bank][:, 0:128], lhsT=Qr[pb:pb + 64, gp, :, :].rearrange("p s c -> p (s c)"),
                                                                   rhs=CPr[pb:pb + 64, gp, 0:8, :].rearrange("p s c -> p (s c)"), start=True, stop=False),
                      r=Qn + CPn, w=[psn[bank]])
                    T(lambda e, gp=gp, pb=pb, bank=bank: e.matmul(PS[bank][:, 0:128], lhsT=Qi[pb:pb + 64, gp, :, :].rearrange("p s c -> p (s c)"),
                                                                   rhs=CPi[pb:pb + 64, gp, 0:8, :].rearrange("p s c -> p (s c)"), start=False, stop=True),
                      r=Qn + CPn, w=[psn[bank]])
                    V(lambda e, bank=bank: e.tensor_tensor(out=ttmp[:], in0=PS[bank][:, 0:128], in1=tmask[:], op=ALU.mult),
                      r=[psn[bank], "tmask"], w=["ttmp"])
                    V(lambda e, g=g: e.scalar_tensor_tensor(out=Tm[:, g, :], in0=ident[:], scalar=dcol[:, g, :], in1=ttmp[:],
                                                            op0=ALU.mult, op1=ALU.add), r=["ident", "dcol", "ttmp"], w=["Tm"])
                if debug:
                    dT = sb("dT", [128, 32, 128])
                    V(lambda e: e.tensor_copy(out=dT[:], in_=Tm[:]), r=["Tm"], w=["dT"])
                    D(lambda e: e.dma_start(out=dbg["T"], in_=dT[:]), r=["dT"])
                S.barrier()

            for s in range(2):
                S.barrier()
                with ExitStack() as abc:
                    sb = mk(abc)
                    xT = sb("xT", [128, 8, SEQ], BF16)
                    wst = [None]
                    wbf = [None, None]
                    wctr = [0]

                    def load_w(c0, slot):
                        i = 0
                        D(lambda e: e.dma_start(out=wst[i][:], in_=win_d[:, c0:c0 + 128].rearrange("(k p) c -> p k c", p=128)),
                          w=["wsta%d" % i])
                        G(lambda e: e.tensor_tensor(out=wbf[slot][:], in0=wst[i][:], in1=mixg[:].to_broadcast([128, 8, 128]),
                                                    op=ALU.mult), r=["wsta%d" % i, "mixg"], w=["wbf%d" % slot])
                        return wbf[slot], "wbf%d" % slot

                    ck("ssmsetup")
                    with ExitStack() as a0:
                        sb0 = mk(a0)
                        xt = [sb0("xt%d" % i, [128, DM]) for i in range(4)]
                        sqs = [sb0("sq%d" % i, [128, DM]) for i in range(2)]
                        st4s = [sb0("st4%d" % i, [128, 4]) for i in range(2)]
                        dgs = [sb0("dg%d" % i, [128, 128]) for i in range(2)]
                        rbcs = [sb0("rbc%d" % i, [128, 128]) for i in range(2)]
                        def a0_load(i):
                            xi = xt[i % 4]
                            xn = "xt%d" % (i % 4)
                            sq, sqn = sqs[i % 2], "sq%d" % (i % 2)
                            D(lambda e: e.dma_start(out=xi[:], in_=x_d[s, i * 128:(i + 1) * 128, :]), w=[xn])
                            A(lambda e: e.activation(out=sq[:], in_=xi[:], func=AF.Square), r=[xn], w=[sqn])

                        def a0_stats(i):
                            p2 = i % 2
                            xi = xt[i % 4]
                            xn = "xt%d" % (i % 4)
                            sq, sqn = sqs[p2], "sq%d" % p2
                            st4, stn = st4s[p2], "st4%d" % p2
                            dg, dgn = dgs[p2], "dg%d" % p2
                            rbc, rbn = rbcs[p2], "rbc%d" % p2
                            rb = 2 + p2
                            V(lambda e: e.tensor_reduce(out=st4[:, 0:1], in_=sq[:], axis=AX.X, op=ALU.add), r=[sqn], w=[stn])
                            V(lambda e: e.tensor_scalar(out=st4[:, 1:2], in0=st4[:, 0:1], scalar1=1.0 / DM, scalar2=EPS,
                                                        op0=ALU.mult, op1=ALU.add), r=[stn], w=[stn])
                            G(lambda e: e.tensor_tensor(out=st4[:, 3:4], in0=st4[:, 1:2], in1=cneg[:, 0:1], op=ALU.pow),
                              r=[stn, "cneg"], w=[stn])

                        def a0_stats_b(i):
                            p2 = i % 2
                            st4, stn = st4s[p2], "st4%d" % p2
                            xi = xt[i % 4]
                            xn = "xt%d" % (i % 4)
                            A(lambda e: e.activation(out=xi[:], in_=xi[:], func=AF.Copy, scale=st4[:, 3:4]), r=[xn, stn], w=[xn])

                        def a0_xpose(i):
                            p2 = i % 2
                            xi = xt[i % 4]
                            xn = "xt%d" % (i % 4)
                            for half in range(2):
                                tbk = 4 * p2 + half
                                for q4 in range(4):
                                    kc = half * 4 + q4
                                    T(lambda e, kc=kc, q4=q4, tbk=tbk: e.transpose(
                                        out=PS[tbk][:, q4 * 128:(q4 + 1) * 128], in_=xi[:, kc * 128:(kc + 1) * 128], identity=ident[:]),
                                      r=[xn, "ident"], w=[psn[tbk]])
                                if half == 0:
                                    V(lambda e, half=half, tbk=tbk: e.tensor_copy(
                                        out=xT[:, half * 4:half * 4 + 4, i * 128:(i + 1) * 128],
                                        in_=PS[tbk][:].rearrange("p (k t) -> p k t", k=4)), r=[psn[tbk]], w=["xT"])
                                else:
                                    A(lambda e, half=half, tbk=tbk: e.copy(
                                        out=xT[:, half * 4:half * 4 + 4, i * 128:(i + 1) * 128],
                                        in_=PS[tbk][:].rearrange("p (k t) -> p k t", k=4)), r=[psn[tbk]], w=["xT"])

                        a0_load(0)
                        a0_load(1)
                        a0_stats(0)
                        a0_stats_b(0)
                        for i in range(16):
                            if i + 2 < 16:
                                a0_load(i + 2)
                            if i + 1 < 16:
                                a0_stats(i + 1)
                            a0_xpose(i)
                            if i + 1 < 16:
                                a0_stats_b(i + 1)
                        S.barrier()
                    if debug and s == 0:
                        with ExitStack() as dd:
                            d32 = mk(dd)("d32", [128, 8, SEQ])
                            V(lambda e: e.tensor_copy(out=d32[:], in_=xT[:]), r=["xT"], w=["d32"])
                            D(lambda e: e.dma_start(out=dbg["xT"], in_=d32[:]), r=["d32"])
                            S.barrier()

                    ck("A0")
                    with ExitStack() as bs:
                        sbB = mk(bs)
                        qT = [sbB("qT%d" % i, [128, SEQ], BF16) for i in range(2)]
                        kT = [sbB("kT%d" % i, [128, SEQ], BF16) for i in range(2)]
                        gaT = [sbB("gaT%d" % i, [128, SEQ], BF16) for i in range(2)]
                        Vaug = [sbB("Vaug%d" % i, [128, 16, 2, 128], BF16) for i in range(2)]
                        wB = [sbB("wB%d" % i, [128, 8, 128], BF16) for i in range(8)]
                        wst[0] = sbB("wstaB", [128, 8, 128])
                        sqb = [sbB("sqb%d" % i, [128, 512], BF16) for i in range(2)]
                        tt = [sbB("tt%d" % i, [128, 512]) for i in range(2)]
                        NE = 4
                        Eb = [sbB("Eb%d" % i, [128, 512], BF16) for i in range(NE)]
                        rd = [sbB("rd%d" % i, [128, 512]) for i in range(2)]
                        ot = [sbB("ot%d" % i, [128, 512]) for i in range(2)]
                        oc = [sbB("oc%d" % i, [128, 512]) for i in range(2)]
                        SCB = (4, 5, 6, 7)
                        for i in range(2):
                            V(lambda e, i=i: e.memset(Vaug[i][:], 1.0), w=["Vaug%d" % i])

                        def load_wB(c0, slot):
                            i = 0
                            D(lambda e: e.dma_start(out=wst[i][:], in_=win_d[:, c0:c0 + 128].rearrange("(k p) c -> p k c", p=128)),
                              w=["wsta%d" % i])
                            G(lambda e: e.tensor_tensor(out=wB[slot][:], in0=wst[i][:], in1=mixg[:].to_broadcast([128, 8, 128]),
                                                        op=ALU.mult), r=["wsta%d" % i, "mixg"], w=["wB%d" % slot])
                            return wB[slot], "wB%d" % slot

                        def prep_units(hp):
                            par = hp % 2
                            wq, wqn = load_wB(hp * 128, par * 4 + 0)
                            wk, wkn = load_wB(512 + hp * 128, par * 4 + 1)
                            yield
                            wv, wvn = load_wB(1024 + hp * 128, par * 4 + 2)
                            wa, wan = load_wB(1536 + hp * 128, par * 4 + 3)
                            yield
                            blocks = []
                            for (w_, wn_, dstT, dn, gain, gn) in ((wq, wqn, qT[par], "qT%d" % par, qg, "qg"),
                                                                  (wk, wkn, kT[par], "kT%d" % par, kg, "kg")):
                                for tg in range(4):
                                    blocks.append((w_, wn_, dstT, dn, gain, gn, tg))

                            def S1(bi):
                                w_, wn_, dstT, dn, gain, gn, tg = blocks[bi]
                                pbk = bi % 2
                                for kc in range(8):
                                    T(lambda e, kc=kc: e.matmul(PS[pbk][:], lhsT=w_[:, kc, :], rhs=xT[:, kc, tg * 512:(tg + 1) * 512],
                                                                start=(kc == 0), stop=(kc == 7)), r=[wn_, "xT"], w=[psn[pbk]])

                            def S2(bi):
                                pbk = bi % 2
                                sq_, sqn = sqb[bi % 2], "sqb%d" % (bi % 2)
                                t_, tn = tt[bi % 2], "tt%d" % (bi % 2)
                                A(lambda e: e.activation(out=sq_[:], in_=PS[pbk][:], func=AF.Square), r=[psn[pbk]], w=[sqn])
                                T(lambda e: e.matmul(PS[2][:], lhsT=blk1[:], rhs=sq_[:], start=True, stop=True),
                                  r=["blk1", sqn], w=[psn[2]])
                                V(lambda e: e.tensor_scalar(out=t_[:], in0=PS[2][:], scalar1=1.0 / 64.0, scalar2=EPS, op0=ALU.mult,
                                                            op1=ALU.add), r=[psn[2]], w=[tn])

                            def S3(bi):
                                w_, wn_, dstT, dn, gain, gn, tg = blocks[bi]
                                pbk = bi % 2
                                t_, tn = tt[bi % 2], "tt%d" % (bi % 2)
                                A(lambda e: e.activation(out=t_[:], in_=t_[:], func=AF.Ln), r=[tn], w=[tn])
                                A(lambda e: e.activation(out=t_[:], in_=t_[:], func=AF.Exp, scale=-0.5), r=[tn], w=[tn])
                                V(lambda e: e.scalar_tensor_tensor(out=dstT[:, tg * 512:(tg + 1) * 512], in0=PS[pbk][:],
                                                                   scalar=gain[:, 0:1], in1=t_[:], op0=ALU.mult, op1=ALU.mult),
                                  r=[psn[pbk], gn, tn], w=[dn])

                            for t in range(8 + 2):
                                if 0 <= t - 2 < 8:
                                    S3(t - 2)
                                if 0 <= t - 1 < 8:
                                    S2(t - 1)
                                if t < 8:
                                    S1(t)
                                yield

                            def G1(tg):
                                pbk = tg % 2
                                for kc in range(8):
                                    T(lambda e, kc=kc: e.matmul(PS[pbk][:], lhsT=wa[:, kc, :], rhs=xT[:, kc, tg * 512:(tg + 1) * 512],
                                                                start=(kc == 0), stop=(kc == 7)), r=[wan, "xT"], w=[psn[pbk]])

                            def G2(tg):
                                pbk = tg % 2
                                t_, tn = tt[tg % 2], "tt%d" % (tg % 2)
                                A(lambda e: e.activation(out=t_[:], in_=PS[pbk][:], func=AF.Exp, scale=-1.0), r=[psn[pbk]], w=[tn])
                                A(lambda e: e.activation(out=t_[:], in_=t_[:], func=AF.Ln, bias=1.0), r=[tn], w=[tn])
                                A(lambda e: e.activation(out=t_[:], in_=t_[:], func=AF.Exp, scale=-1.0), r=[tn], w=[tn])

                            def G3(tg):
                                pbk = tg % 2
                                t_, tn = tt[tg % 2], "tt%d" % (tg % 2)
                                V(lambda e: e.tensor_tensor(out=gaT[par][:, tg * 512:(tg + 1) * 512], in0=PS[pbk][:], in1=t_[:],
                                                            op=ALU.mult), r=[tn, psn[pbk]], w=["gaT%d" % par])

                            for t in range(4 + 2):
                                if 0 <= t - 2 < 4:
                                    G3(t - 2)
                                if 0 <= t - 1 < 4:
                                    G2(t - 1)
                                if t < 4:
                                    G1(t)
                                yield

                            def V1(i4):
                                pbk = i4 % 2
                                for ii in range(4):
                                    i = i4 * 4 + ii
                                    for kc in range(8):
                                        T(lambda e, kc=kc, i=i, ii=ii: e.matmul(
                                            PS[pbk][:, ii * 128:(ii + 1) * 128], lhsT=xT[:, kc, i * 128:(i + 1) * 128], rhs=wv[:, kc, :],
                                            start=(kc == 0), stop=(kc == 7)), r=[wvn, "xT"], w=[psn[pbk]])

                            def V2(i4):
                                pbk = i4 % 2
                                pv = PS[pbk][:].rearrange("p (i c) -> p i c", i=4)
                                V(lambda e: e.tensor_copy(out=Vaug[par][:, i4 * 4:i4 * 4 + 4, 0, 0:64], in_=pv[:, :, 0:64]),
                                  r=[psn[pbk]], w=["Vaug%d" % par])
                                A(lambda e: e.copy(out=Vaug[par][:, i4 * 4:i4 * 4 + 4, 1, 64:128], in_=pv[:, :, 64:128]),
                                  r=[psn[pbk]], w=["Vaug%d" % par])

                            for t in range(4 + 1):
                                if 0 <= t - 1 < 4:
                                    V2(t - 1)
                                if t < 4:
                                    V1(t)
                                yield

                        def core_units(hp):
                            par = hp % 2
                            qn_, kn_, gn_, vn_ = "qT%d" % par, "kT%d" % par, "gaT%d" % par, "Vaug%d" % par
                            blocks = []
                            for h2 in range(2):
                                for Qg in range(4):
                                    nkb = 4 * Qg + 4
                                    for kb in range(nkb):
                                        blocks.append((h2, Qg, kb, nkb))
                            N = len(blocks)
                            NP = N // 2
                            for pi in range(NP + 1):
                                if pi < NP:
                                    qk_ops = []
                                    for idx in (2 * pi, 2 * pi + 1):
                                        h2, Qg, kb, nkb = blocks[idx]
                                        pb = 64 * h2
                                        sbk = SCB[idx % 4]
                                        off = Qg * 512 - kb * 128 + 384
                                        c0 = max(0, kb - 4 * Qg) * 128
                                        qk_ops.append((lambda e, kb=kb, Qg=Qg, sbk=sbk, pb=pb, c0=c0: e.matmul(
                                            PS[sbk][:, c0:512], lhsT=kT[par][pb:pb + 64, kb * 128:(kb + 1) * 128],
                                            rhs=qT[par][pb:pb + 64, Qg * 512 + c0:(Qg + 1) * 512], start=True, stop=False),
                                            [kn_, qn_], [psn[sbk]]))
                                        qk_ops.append((lambda e, sbk=sbk, off=off, c0=c0: e.matmul(
                                            PS[sbk][:, c0:512], lhsT=(identr[:].bitcast(F32R) if EXACT_MASK else identr[:]),
                                            rhs=(MT[:, off + c0:off + 512].bitcast(F32R) if EXACT_MASK else MT[:, off + c0:off + 512]),
                                            start=False, stop=True), ["identr", "MT"], [psn[sbk]]))
                                    S.group("pe", qk_ops)
                                if pi >= 1:
                                    pv_ops = []
                                    for j in (2 * pi - 2, 2 * pi - 1):
                                        h2, Qg, kb, nkb = blocks[j]
                                        eb = j % NE
                                        c0 = max(0, kb - 4 * Qg) * 128
                                        pv_ops.append((lambda e, kb=kb, h2=h2, eb=eb, nkb=nkb, c0=c0: e.matmul(
                                            PS[3][:, c0:512], lhsT=Vaug[par][:, kb, h2, :], rhs=Eb[eb][:, c0:512],
                                            start=(kb == 0), stop=(kb == nkb - 1)), [vn_, "Eb%d" % eb], [psn[3]]))
                                    pv_grp = pv_ops
                                else:
                                    pv_grp = None
                                if pi < NP:
                                    for idx in (2 * pi, 2 * pi + 1):
                                        h2, Qg, kb, nkb = blocks[idx]
                                        c0 = max(0, kb - 4 * Qg) * 128
                                        sbk = SCB[idx % 4]
                                        eb = idx % NE
                                        A(lambda e, sbk=sbk, eb=eb, c0=c0: e.activation(out=Eb[eb][:, c0:512], in_=PS[sbk][:, c0:512],
                                                                                        func=AF.Exp, scale=1.0 / ALPHA),
                                          r=[psn[sbk]], w=["Eb%d" % eb])
                                if pv_grp is not None:
                                    S.group("pe", pv_grp)
                                if pi >= 1:
                                    h2, Qg, kb, nkb = blocks[2 * pi - 1]
                                    ob = 3
                                    if kb == nkb - 1:
                                        fi = (h2 * 4 + Qg) % 2
                                        rd_, rdn = rd[fi], "rd%d" % fi
                                        ot_, otn = ot[fi], "ot%d" % fi
                                        oc_, ocn = oc[fi], "oc%d" % fi
                                        dlo, olo = (64, 0) if h2 == 0 else (0, 64)
                                        V(lambda e, oc_=oc_: e.tensor_copy(out=oc_[:], in_=PS[ob][:]), r=[psn[ob]], w=[ocn])
                                        A(lambda e, dlo=dlo, olo=olo, rd_=rd_, oc_=oc_: e.activation(
                                            out=rd_[olo:olo + 64, :], in_=oc_[dlo:dlo + 64, :], func=AF.Ln), r=[ocn], w=[rdn])
                                        A(lambda e, olo=olo, rd_=rd_: e.activation(out=rd_[olo:olo + 64, :], in_=rd_[olo:olo + 64, :],
                                                                                   func=AF.Exp, scale=-1.0), r=[rdn], w=[rdn])
                                        V(lambda e, olo=olo, rd_=rd_, ot_=ot_, oc_=oc_: e.tensor_tensor(
                                            out=ot_[olo:olo + 64, :], in0=oc_[olo:olo + 64, :], in1=rd_[olo:olo + 64, :],
                                            op=ALU.mult), r=[ocn, rdn], w=[otn])
                                        G(lambda e, olo=olo, Qg=Qg, ot_=ot_: e.tensor_tensor(
                                            out=attnT[olo:olo + 64, hp, Qg * 512:(Qg + 1) * 512], in0=ot_[olo:olo + 64, :],
                                            in1=gaT[par][olo:olo + 64, Qg * 512:(Qg + 1) * 512], op=ALU.mult),
                                          r=[otn, gn_], w=["attnT"])
                                yield

                        for _ in prep_units(0):
                            pass
                        for hp in range(4):
                            nxt = prep_units(hp + 1) if hp < 3 else None
                            for si, _ in enumerate(core_units(hp)):
                                if nxt is not None and si % 2 == 1:
                                    try:
                                        next(nxt)
                                    except StopIteration:
                                        nxt = None
                            if nxt is not None:
                                for _ in nxt:
                                    pass
                        S.barrier()
                    ck("B")
                    with ExitStack() as cs_:
                        sbC0 = mk(cs_)
                        ygT = sbC0("ygT", [128, 4, SEQ], BF16)
                        c12 = ExitStack()
                        sbC = mk(c12)
                        Ug = sbC("Ug", [128, 32, 256], BF16)
                        Sb = sbC("Sb", [128, 16, 2, 16, 16])
                        sc_scope = ExitStack()
                        scA = [[mk(sc_scope)("scA%d_%d" % (c, i), [128, (11, 5)[c], 2, 16]) for i in range(2)] for c in range(2)]
                        u_scope = ExitStack()
                        sbU = mk(u_scope)
                        U = sbU("U", [128, 4, 8, 8, 16])
                        wst[0] = sbU("wstaC", [128, 8, 128])
                        wu_all = sbU("wu_all", [128, 8, 512], BF16)
                        for cb in range(4):
                            c0w = 2048 + cb * 128
                            D(lambda e, c0w=c0w: e.dma_start(out=wst[0][:], in_=win_d[:, c0w:c0w + 128].rearrange("(k p) c -> p k c", p=128)),
                              w=["wsta0"])
                            G(lambda e, cb=cb: e.tensor_tensor(out=wu_all[:, :, cb * 128:(cb + 1) * 128], in0=wst[0][:],
                                                               in1=mixg[:].to_broadcast([128, 8, 128]), op=ALU.mult),
                              r=["wsta0", "mixg"], w=["wu_all"])
                        for ct in range(2):
                            for sp_ in range(8):
                                bank = sp_ % 2
                                for kc in range(8):
                                    T(lambda e, kc=kc, sp_=sp_, bank=bank: e.matmul(
                                        PS[bank][:], lhsT=xT[:, kc, ct * 1024 + sp_:(ct + 1) * 1024:8], rhs=wu_all[:, kc, :],
                                        start=(kc == 0), stop=(kc == 7)), r=["wu_all", "xT"], w=[psn[bank]])
                                fev = (lambda e, sp_=sp_, bank=bank: e.copy(
                                    out=U[:, :, :, sp_, :], in_=PS[bank][:].rearrange("p (b g c) -> p b g c", b=4, g=8)))
                                A(fev, r=[psn[bank]], w=["U"])
                            for g4 in range(8):
                                bank = 2 + g4 % 2
                                for gi in range(4):
                                    g = g4 * 4 + gi
                                    T(lambda e, g=g, gi=gi, bank=bank: e.transpose(
                                        out=PS[bank][:, gi * 128:(gi + 1) * 128],
                                        in_=U[:, g // 8, g % 8, :, :].rearrange("p s c -> p (s c)"), identity=ident[:]),
                                      r=["U", "ident"], w=[psn[bank]])
                                V(lambda e, g4=g4, bank=bank: e.tensor_copy(out=Ug[:, g4 * 4:g4 * 4 + 4, ct * 128:(ct + 1) * 128],
                                                                            in_=PS[bank][:].rearrange("p (g k) -> p g k", g=4)),
                                  r=[psn[bank]], w=["Ug"])
                        for gp in range(16):
                            bank = 4 + gp % 2
                            for j in range(2):
                                g = 2 * gp + j
                                T(lambda e, g=g, j=j, bank=bank: e.matmul(PS[bank][64 * j:64 * j + 64, 0:256], lhsT=PGre[:, g, :],
                                                                          rhs=Ug[:, g, :], start=True, stop=True),
                                  r=["PGre", "Ug"], w=[psn[bank]])
                                T(lambda e, g=g, j=j, bank=bank: e.matmul(PS[bank][64 * j:64 * j + 64, 256:512], lhsT=PGim[:, g, :],
                                                                          rhs=Ug[:, g, :], start=True, stop=True),
                                  r=["PGim", "Ug"], w=[psn[bank]])
                            A(lambda e, gp=gp, bank=bank: e.copy(out=Sb[:, gp, :, :, :],
                                                                 in_=PS[bank][:].rearrange("p (c K j) -> p c j K", c=2, K=16)),
                              r=[psn[bank]], w=["Sb%d" % (0 if gp < 11 else 1)])
                        S.barrier()
                        u_scope.close()
                        ck("C1")
                        SPLIT = ((0, 11, "dve"), (11, 16, "pool"))
                        for ci, (g0, g1, eng) in enumerate(SPLIT):
                            ng = g1 - g0
                            ta, tb_ = scA[ci]
                            na, nb = "scA%d_0" % ci, "scA%d_1" % ci
                            rn = "Sb%d" % ci
                            def colv(j, k0, nk, rev=False, g0=g0, g1=g1):
                                cs_ = slice(None, None, -1) if rev else slice(None)
                                return Sb[:, g0:g1, cs_, j, k0:k0 + nk]
                            for j in range(1, 16):
                                prev, prev_sw, cur = colv(j - 1, 0, 16), colv(j - 1, 0, 16, True), colv(j, 0, 16)
                                ar = APR[:, 0, g0:g1, :].unsqueeze(3).to_broadcast([128, ng, 2, 16])
                                ai = API[:, 0, g0:g1, :].unsqueeze(3).to_broadcast([128, ng, 2, 16])
                                S.op(eng, lambda e, prev=prev, ar=ar: e.tensor_tensor(out=ta[:], in0=prev, in1=ar, op=ALU.mult),
                                     reads=[rn, "APR"], writes=[na])
                                S.op(eng, lambda e, prev_sw=prev_sw, ai=ai: e.tensor_tensor(out=tb_[:], in0=prev_sw, in1=ai, op=ALU.mult),
                                     reads=[rn, "API"], writes=[nb])
                                S.op(eng, lambda e: e.tensor_tensor(out=ta[:], in0=ta[:], in1=tb_[:], op=ALU.add), reads=[na, nb], writes=[na])
                                S.op(eng, lambda e, cur=cur: e.tensor_tensor(out=cur, in0=cur, in1=ta[:], op=ALU.add), reads=[na, rn], writes=[rn])
                            for K in range(1, 16):
                                prev, prev_sw, cur = colv(15, K - 1, 1), colv(15, K - 1, 1, True), colv(15, K, 1)
                                ar = APR[:, 15, g0:g1, :].unsqueeze(3)
                                ai = API[:, 15, g0:g1, :].unsqueeze(3)
                                S.op(eng, lambda e, prev=prev, ar=ar: e.tensor_tensor(out=ta[:, :, :, 0:1], in0=prev, in1=ar, op=ALU.mult),
                                     reads=[rn, "APR"], writes=[na])
                                S.op(eng, lambda e, prev_sw=prev_sw, ai=ai: e.tensor_tensor(out=tb_[:, :, :, 0:1], in0=prev_sw, in1=ai,
                                                                                         op=ALU.mult), reads=[rn, "API"], writes=[nb])
                                S.op(eng, lambda e: e.tensor_tensor(out=ta[:, :, :, 0:1], in0=ta[:, :, :, 0:1], in1=tb_[:, :, :, 0:1],
                                                                    op=ALU.add), reads=[na, nb], writes=[na])
                                S.op(eng, lambda e, cur=cur: e.tensor_tensor(out=cur, in0=cur, in1=ta[:, :, :, 0:1], op=ALU.add),
                                     reads=[na, rn], writes=[rn])
                            for j in range(15):
                                prev, prev_sw, cur = colv(15, 0, 15), colv(15, 0, 15, True), colv(j, 1, 15)
                                ar = APR[:, j, g0:g1, :].unsqueeze(3).to_broadcast([128, ng, 2, 15])
                                ai = API[:, j, g0:g1, :].unsqueeze(3).to_broadcast([128, ng, 2, 15])
                                S.op(eng, lambda e, prev=prev, ar=ar: e.tensor_tensor(out=ta[:, :, :, 0:15], in0=prev, in1=ar, op=ALU.mult),
                                     reads=[rn, "APR"], writes=[na])
                                S.op(eng, lambda e, prev_sw=prev_sw, ai=ai: e.tensor_tensor(out=tb_[:, :, :, 0:15], in0=prev_sw, in1=ai,
                                                                                         op=ALU.mult), reads=[rn, "API"], writes=[nb])
                                S.op(eng, lambda e: e.tensor_tensor(out=ta[:, :, :, 0:15], in0=ta[:, :, :, 0:15], in1=tb_[:, :, :, 0:15],
                                                                    op=ALU.add), reads=[na, nb], writes=[na])
                                S.op(eng, lambda e, cur=cur: e.tensor_tensor(out=cur, in0=cur, in1=ta[:, :, :, 0:15], op=ALU.add),
                                     reads=[na, rn], writes=[rn])
                        S.barrier()
                        sc_scope.close()
                        Sh = sbC("Sh", [128, 16, 2, 256], BF16)
                        for ci, (g0, g1, eng) in enumerate(SPLIT):
                            rn = "Sb%d" % ci
                            S.op(eng, lambda e, g0=g0, g1=g1: e.memset(Sh[:, g0:g1, :, 0:1], 0.0), writes=["Sh%d" % ci])
                            for c in range(2):
                                S.op(eng, lambda e, g0=g0, g1=g1, c=c: e.tensor_copy(
                                    out=Sh[:, g0:g1, c, 1:241].rearrange("p g (K j) -> p g K j", j=16),
                                    in_=Sb[:, g0:g1, c, :, 0:15].rearrange("p g j K -> p g K j")), reads=[rn], writes=["Sh%d" % ci])
                                S.op(eng, lambda e, g0=g0, g1=g1, c=c: e.tensor_copy(
                                    out=Sh[:, g0:g1, c, 241:256], in_=Sb[:, g0:g1, c, 0:15, 15]), reads=[rn], writes=["Sh%d" % ci])
                        ck("C2")
                        with ExitStack() as c3:
                            sb3 = mk(c3)
                            Ysbs = [sb3("Ysb%d" % i, [128, 2, 128]) for i in range(2)]
                            YG = sb3("YG", [128, 8, 512])
                            def mm3(ct, gp):
                                bank = gp % 2
                                for j in range(2):
                                    g = 2 * gp + j
                                    pb = 64 * j
                                    o_ = PS[bank][:, j * 128:(j + 1) * 128]
                                    T(lambda e, g=g, o_=o_: e.matmul(o_, lhsT=Tm[:, g, :], rhs=Ug[:, g, ct * 128:(ct + 1) * 128],
                                                                     start=True, stop=False), r=["Tm", "Ug"], w=[psn[bank]])
                                    T(lambda e, pb=pb, o_=o_: e.matmul(
                                        o_, lhsT=PCre[pb:pb + 64, gp, :], rhs=Sh[pb:pb + 64, gp, 0, ct * 128:(ct + 1) * 128],
                                        start=False, stop=False), r=["PCre", "Sh0", "Sh1"], w=[psn[bank]])
                                    T(lambda e, pb=pb, o_=o_: e.matmul(
                                        o_, lhsT=PCim[pb:pb + 64, gp, :], rhs=Sh[pb:pb + 64, gp, 1, ct * 128:(ct + 1) * 128],
                                        start=False, stop=True), r=["PCim", "Sh0", "Sh1"], w=[psn[bank]])

                            def ev3(ct, gp):
                                bank = gp % 2
                                Ysb, Ysn = Ysbs[gp % 2], "Ysb%d" % (gp % 2)
                                A(lambda e: e.copy(out=Ysb[:], in_=PS[bank][:, 0:256].rearrange("p (j k) -> p j k", j=2)),
                                  r=[psn[bank]], w=[Ysn])
                                tb = 2 + gp % 2
                                for j in range(2):
                                    T(lambda e, j=j: e.transpose(out=PS[tb][:, j * 128:(j + 1) * 128], in_=Ysb[:, j, :],
                                                                 identity=ident[:]), r=[Ysn, "ident"], w=[psn[tb]])
                                for j in range(2):
                                    g = 2 * gp + j
                                    A(lambda e, j=j, g=g: e.activation(
                                        out=YG[:, :, g * 16:(g + 1) * 16],
                                        in_=PS[tb][:, j * 128:(j + 1) * 128].rearrange("p (t c) -> p t c", t=8),
                                        func=AF.Gelu), r=[psn[tb]], w=["YG"])

                            for ct in range(2):
                                mm3(ct, 0)
                                for gp in range(16):
                                    if gp + 1 < 16:
                                        mm3(ct, gp + 1)
                                    ev3(ct, gp)
                                for tp in range(8):
                                    bank = 4 + tp % 2
                                    for cbi in range(4):
                                        T(lambda e, tp=tp, cbi=cbi, bank=bank: e.transpose(
                                            out=PS[bank][:, cbi * 128:(cbi + 1) * 128], in_=YG[:, tp, cbi * 128:(cbi + 1) * 128],
                                            identity=ident[:]), r=["YG", "ident"], w=[psn[bank]])
                                    col = (ct * 8 + tp) * 128
                                    V(lambda e, bank=bank, col=col: e.tensor_copy(out=ygT[:, :, col:col + 128],
                                                                                  in_=PS[bank][:].rearrange("p (a t) -> p a t", a=4)),
                                      r=[psn[bank]], w=["ygT"])
                            S.barrier()
                        S.barrier()
                        c12.close()
                        ck("C3")
                        with ExitStack() as c5:
                            sb5 = mk(c5)
                            wgl = sb5("wgl", [128, 4, 512], BF16)
                            wst[0] = sb5("wsta5", [128, 8, 128])
                            wbf[0] = sb5("wbf5a", [128, 8, 128], BF16)
                            wbf[1] = sb5("wbf5b", [128, 8, 128], BF16)
                            wgs_st = sb5("wgs_st", [128, 512])
                            sig = [sb5("sig%d" % i, [128, 512]) for i in range(2)]
                            gsl = [sb5("gsl%d" % i, [128, 512], BF16) for i in range(2)]
                            gth = [sb5("gth%d" % i, [128, 512]) for i in range(2)]
                            tm5 = [sb5("tm5%d" % i, [128, 512]) for i in range(2)]
                            for kc in range(4):
                                D(lambda e, kc=kc: e.dma_start(out=wgs_st[:], in_=wglu_d[kc * 128:(kc + 1) * 128, :]), w=["wgs_st"])
                                V(lambda e, kc=kc: e.tensor_copy(out=wgl[:, kc, :], in_=wgs_st[:]), r=["wgs_st"], w=["wgl"])
                            steps = [(cb, tg) for cb in range(4) for tg in range(4)]
                            wcur = {}

                            def pvw(ap, half):
                                return ap.rearrange("p (t q) -> p t q", t=8)[:, :, half * 64:(half + 1) * 64]

                            def mm5(si):
                                cb, tg = steps[si]
                                if tg == 0:
                                    wcur[cb] = load_w(2560 + cb * 128, cb % 2)
                                wgs, wgsn = wcur[cb]
                                ct, half = tg // 2, tg % 2
                                b0, b1 = 2 * (si % 4), 2 * (si % 4) + 1
                                for kc in range(4):
                                    T(lambda e, kc=kc: e.matmul(PS[b0][:], lhsT=wgl[:, kc, cb * 128:(cb + 1) * 128],
                                                                rhs=pvw(ygT[:, kc, ct * 1024:(ct + 1) * 1024], half),
                                                                start=(kc == 0), stop=(kc == 3)), r=["wgl", "ygT"], w=[psn[b0]])
                                for kc in range(8):
                                    T(lambda e, kc=kc: e.matmul(PS[b1][:], lhsT=wgs[:, kc, :], rhs=xT[:, kc, tg * 512:(tg + 1) * 512],
                                                                start=(kc == 0), stop=(kc == 7)), r=[wgsn, "xT"], w=[psn[b1]])

                            def ev5(si):
                                cb, tg = steps[si]
                                ct, half = tg // 2, tg % 2
                                i2 = si % 2
                                b0, b1 = 2 * (si % 4), 2 * (si % 4) + 1
                                sg, sgn = sig[i2], "sig%d" % i2
                                gh, ghn = gth[i2], "gth%d" % i2
                                gl, gln = gsl[i2], "gsl%d" % i2
                                t5, t5n = tm5[i2], "tm5%d" % i2
                                A(lambda e: e.activation(out=sg[:], in_=PS[b0][:], func=AF.Sigmoid, bias=bglu[:, cb, :]),
                                  r=[psn[b0], "bglu"], w=[sgn])
                                A(lambda e: e.activation(out=gh[:], in_=PS[b1][:], func=AF.Sigmoid), r=[psn[b1]], w=[ghn])
                                V(lambda e: e.tensor_tensor(out=gl[:].rearrange("p (t q) -> p q t", t=8),
                                                            in0=PS[b1][:].rearrange("p (q t) -> p q t", t=8),
                                                            in1=gh[:].rearrange("p (q t) -> p q t", t=8), op=ALU.mult),
                                  r=[ghn, psn[b1]], w=[gln])
                                G(lambda e: e.tensor_tensor(out=t5[:].rearrange("p (t q) -> p t q", t=8),
                                                            in0=pvw(ygT[:, cb, ct * 1024:(ct + 1) * 1024], half),
                                                            in1=sg[:].rearrange("p (t q) -> p t q", t=8), op=ALU.mult),
                                  r=["ygT", sgn], w=[t5n])
                                G(lambda e: e.tensor_tensor(out=pvw(ssmT[:, cb, ct * 1024:(ct + 1) * 1024], half),
                                                            in0=t5[:].rearrange("p (t q) -> p t q", t=8),
                                                            in1=gl[:].rearrange("p (t q) -> p t q", t=8), op=ALU.mult),
                                  r=[t5n, gln], w=["ssmT"])

                            mm5(0)
                            mm5(1)
                            mm5(2)
                            for si in range(16):
                                if si + 3 < 16:
                                    mm5(si + 3)
                                ev5(si)
                            if debug and s == 0:
                                with ExitStack() as dd:
                                    d32 = mk(dd)("d32y", [128, 4, SEQ])
                                    for nm, src in (("attnT", attnT), ("ssmT", ssmT), ("ygT", ygT)):
                                        S.barrier()
                                        V(lambda e, src=src: e.tensor_copy(out=d32[:], in_=src[:]), w=["d32y"])
                                        D(lambda e, nm=nm: e.dma_start(out=dbg[nm], in_=d32[:]), r=["d32y"])
                                    S.barrier()
                            S.barrier()
                    S.barrier()

                ck("C5")
                S.barrier()
                with ExitStack() as ds:
                    sbD = mk(ds)
                    Wout = sbD("Wout", [128, 8, DM], BF16)
                    Wg = sbD("Wg", [128, 8, DM], BF16)
                    Wp = sbD("Wp", [128, 2, DM], BF16)
                    wstd = [sbD("wstd%d" % i, [128, DM]) for i in range(2)]
                    cnt = 0
                    for (src, dst, dn_, nk, fold) in ((wout_d, Wout, "Wout", 8, None), (wg_d, Wg, "Wg", 8, pleg), (wp_d, Wp, "Wp", 2, None)):
                        for kc in range(nk):
                            st = wstd[cnt % 2]
                            rn = "wstd%d" % (cnt % 2)
                            D(lambda e, st=st, src=src, kc=kc: e.dma_start(out=st[:], in_=src[kc * 128:(kc + 1) * 128, :]), w=[rn])
                            if fold is None:
                                if cnt % 2 == 0:
                                    V(lambda e, st=st, dst=dst, kc=kc: e.tensor_copy(out=dst[:, kc, :], in_=st[:]), r=[rn], w=[dn_])
                                else:
                                    A(lambda e, st=st, dst=dst, kc=kc: e.copy(out=dst[:, kc, :], in_=st[:]), r=[rn], w=[dn_])
                            else:
                                V(lambda e, st=st, dst=dst, kc=kc, fold=fold: e.tensor_scalar(
                                    out=dst[:, kc, :], in0=st[:], scalar1=fold[:, kc, :], scalar2=None, op0=ALU.mult),
                                  r=[rn, "pleg"], w=[dn_])
                            cnt += 1
                    xd = [sbD("xd%d" % i, [128, DM]) for i in range(3)]
                    pd = [sbD("pd%d" % i, [128, 256]) for i in range(3)]
                    hh = [sbD("hh%d" % i, [128, DM]) for i in range(2)]
                    sq = sbD("sqd", [128, DM])
                    st4 = [sbD("st4d%d" % i, [128, 4]) for i in range(2)]
                    hT = [sbD("hT%d" % i, [128, 8, 128], BF16) for i in range(2)]
                    pT = [sbD("pT%d" % i, [128, 2, 128], BF16) for i in range(2)]
                    gate = [sbD("gate%d" % i, [128, DM]) for i in range(2)]
                    oo = [sbD("oo%d" % i, [128, DM]) for i in range(2)]

                    def XL(it):
                        ct, tp = it // 8, it % 8
                        xi, xn = xd[it % 3], "xd%d" % (it % 3)
                        pi_, pn = pd[it % 3], "pd%d" % (it % 3)
                        r0 = ct * 1024 + tp
                        D(lambda e: e.dma_start(out=xi[:], in_=x_d[s, r0:(ct + 1) * 1024:8, :]), w=[xn])
                        D(lambda e: e.dma_start(out=pi_[:], in_=p_d[s, r0:(ct + 1) * 1024:8, :]), w=[pn])

                    def X1(it):
                        ct, tp = it // 8, it % 8
                        b2 = it % 2
                        xi, xn = xd[it % 3], "xd%d" % (it % 3)
                        h_, hn = hh[b2], "hh%d" % b2
                        s_, sn_ = st4[b2], "st4d%d" % b2
                        r0 = ct * 1024 + tp
                        col = it * 128
                        for nh in range(2):
                            for kc in range(8):
                                if kc < 4:
                                    lt = attnT[:, kc, r0:(ct + 1) * 1024:8]
                                    rn = "attnT"
                                else:
                                    lt = ssmT[:, kc - 4, col:col + 128]
                                    rn = "ssmT"
                                T(lambda e, lt=lt, kc=kc, nh=nh: e.matmul(PS[nh][:], lhsT=lt, rhs=Wout[:, kc, nh * 512:(nh + 1) * 512],
                                                                          start=(kc == 0), stop=(kc == 7)), r=[rn, "Wout"], w=[psn[nh]])
                            V(lambda e, nh=nh: e.tensor_tensor(out=h_[:, nh * 512:(nh + 1) * 512], in0=PS[nh][:],
                                                               in1=xi[:, nh * 512:(nh + 1) * 512], op=ALU.add),
                              r=[psn[nh], xn], w=[hn])
                        A(lambda e: e.activation(out=sq[:], in_=h_[:], func=AF.Square), r=[hn], w=["sqd"])
                        V(lambda e: e.tensor_reduce(out=s_[:, 0:1], in_=sq[:], axis=AX.X, op=ALU.add), r=["sqd"], w=[sn_])
                        V(lambda e: e.tensor_scalar(out=s_[:, 1:2], in0=s_[:, 0:1], scalar1=1.0 / DM, scalar2=EPS, op0=ALU.mult,
                                                    op1=ALU.add), r=[sn_], w=[sn_])
                        G(lambda e: e.tensor_tensor(out=s_[:, 3:4], in0=s_[:, 1:2], in1=cneg[:, 0:1], op=ALU.pow),
                          r=[sn_, "cneg"], w=[sn_])

                    def X2(it):
                        b2 = it % 2
                        pi_, pn = pd[it % 3], "pd%d" % (it % 3)
                        h_, hn = hh[b2], "hh%d" % b2
                        hT_, hTn = hT[b2], "hT%d" % b2
                        pT_, pTn = pT[b2], "pT%d" % b2
                        for half in range(2):
                            for q4 in range(4):
                                kc = half * 4 + q4
                                T(lambda e, kc=kc, q4=q4, half=half: e.transpose(out=PS[2 + half][:, q4 * 128:(q4 + 1) * 128],
                                                                                 in_=h_[:, kc * 128:(kc + 1) * 128], identity=ident[:]),
                                  r=[hn, "ident"], w=[psn[2 + half]])
                            A(lambda e, half=half: e.copy(out=hT_[:, half * 4:half * 4 + 4, :],
                                                          in_=PS[2 + half][:].rearrange("p (k t) -> p k t", k=4)),
                              r=[psn[2 + half]], w=[hTn])
                        for q2 in range(2):
                            T(lambda e, q2=q2: e.transpose(out=PS[4][:, q2 * 128:(q2 + 1) * 128], in_=pi_[:, q2 * 128:(q2 + 1) * 128],
                                                           identity=ident[:]), r=[pn, "ident"], w=[psn[4]])
                        V(lambda e: e.tensor_copy(out=pT_[:], in_=PS[4][:, 0:256].rearrange("p (k t) -> p k t", k=2)),
                          r=[psn[4]], w=[pTn])

                    def Y(it):
                        ct, tp = it // 8, it % 8
                        b2 = it % 2
                        h_, hn = hh[b2], "hh%d" % b2
                        s_, sn_ = st4[b2], "st4d%d" % b2
                        hT_, hTn = hT[b2], "hT%d" % b2
                        pT_, pTn = pT[b2], "pT%d" % b2
                        g_, gn = gate[b2], "gate%d" % b2
                        oi, on = oo[b2], "oo%d" % b2
                        r0 = ct * 1024 + tp
                        for nh in range(2):
                            for kc in range(8):
                                T(lambda e, kc=kc, nh=nh: e.matmul(PS[5][:], lhsT=hT_[:, kc, :], rhs=Wg[:, kc, nh * 512:(nh + 1) * 512],
                                                                   start=(kc == 0), stop=(kc == 7)), r=[hTn, "Wg"], w=[psn[5]])
                            A(lambda e, nh=nh: e.activation(out=g_[:, nh * 512:(nh + 1) * 512], in_=PS[5][:], func=AF.Sigmoid,
                                                            scale=s_[:, 3:4]), r=[psn[5], sn_], w=[gn])
                            for kc in range(2):
                                T(lambda e, kc=kc, nh=nh: e.matmul(PS[6 + nh][:], lhsT=pT_[:, kc, :], rhs=Wp[:, kc, nh * 512:(nh + 1) * 512],
                                                                   start=(kc == 0), stop=(kc == 1)), r=[pTn, "Wp"], w=[psn[6 + nh]])
                            V(lambda e, nh=nh: e.tensor_tensor(out=oi[:, nh * 512:(nh + 1) * 512], in0=PS[6 + nh][:],
                                                               in1=g_[:, nh * 512:(nh + 1) * 512], op=ALU.mult),
                              r=[psn[6 + nh], gn], w=[on])
                        G(lambda e: e.tensor_tensor(out=oi[:], in0=oi[:], in1=h_[:], op=ALU.add), r=[on, hn], w=[on])
                        D(lambda e: e.dma_start(out=out_d[s, r0:(ct + 1) * 1024:8, :], in_=oi[:]), r=[on])

                    XL(0)
                    XL(1)
                    X1(0)
                    X2(0)
                    for it in range(16):
                        if it + 2 < 16:
                            XL(it + 2)
                        if it + 1 < 16:
                            X1(it + 1)
                        Y(it)
                        if it + 1 < 16:
                            X2(it + 1)
                    S.barrier()
                    ck("D")
        except _Stop:
            pass
        S.barrier()
    return nc


_NC_CACHE = {}


def kernel(**inputs):
    f = lambda a: np.ascontiguousarray(np.asarray(a, dtype=np.float32))
    x = f(inputs["x"])
    p = f(inputs["p"])[0]
    shared = {}
    for k in ("mix_norm", "w_in", "q_norm", "k_norm", "lambda_re", "lambda_im", "log_dt", "b_re", "b_im", "c_re", "c_im",
              "d_skip", "w_glu", "b_glu", "w_out", "ple_norm", "w_ple_gate", "w_ple_proj"):
        shared[k] = f(inputs[k])[0]
    if "nc" not in _NC_CACHE:
        _NC_CACHE["nc"] = build_nc()
    nc = _NC_CACHE["nc"]
    in_maps = []
    for c in range(NCORES):
        m = {"x": np.ascontiguousarray(x[2 * c:2 * c + 2]), "p": np.ascontiguousarray(p[2 * c:2 * c + 2])}
        m.update(shared)
        in_maps.append(m)
    res = run_bass_kernel_spmd(nc, in_maps, core_ids=list(range(NCORES)))
    out = np.concatenate([np.asarray(r["out"], dtype=np.float32) for r in res.results], axis=0)
    return out
```

```python
import math
from contextlib import ExitStack
import numpy as np
import concourse.bass as bass
import concourse.mybir as mybir
from concourse.bass_utils import run_bass_kernel_spmd

F32 = mybir.dt.float32
BF16 = mybir.dt.bfloat16
F32R = mybir.dt.float32r
I32 = mybir.dt.int32
ALU = mybir.AluOpType
AF = mybir.ActivationFunctionType
AX = mybir.AxisListType

NCORES = 8
SEQ = 2048
DM = 1024
EPS = 1e-6
MAGIC = 12582912.0
TWO_PI = 2.0 * math.pi
EXACT_MASK = False
ALPHA = 1.0 if EXACT_MASK else 38.2305


class Sched:
    ENGS = ("pe", "act", "dve", "pool", "sp")

    def __init__(self, nc, n_dma_sems=24):
        self.nc = nc
        self.cnt = {e: 0 for e in self.ENGS}
        self.sem = {}
        self.seen = {e: {} for e in self.ENGS}
        self.last_w = {}
        self.readers = {}
        self.dma_sems = []
        self.dma_uses = []
        self.n_dma_sems = n_dma_sems
        self.dma_rr = 0

    def alloc(self, stack):
        for e in self.ENGS:
            if e == "sp":
                continue
            self.sem[e] = stack.enter_context(self.nc.semaphore("s_" + e))
        for i in range(self.n_dma_sems):
            self.dma_sems.append(stack.enter_context(self.nc.semaphore("d%d" % i)))
            self.dma_uses.append(0)

    def _need(self, eng, tok, waits):
        key, sem, val, src = tok
        if self.seen[eng].get(key, 0) >= val:
            return
        self.seen[eng][key] = val
        waits.append((sem, val))

    dead = False

    def op(self, eng, fn, reads=(), writes=(), dma=False):
        if self.dead:
            return None
        waits = []
        for r in reads:
            t = self.last_w.get(r)
            if t is not None:
                self._need(eng, t, waits)
        for w in writes:
            t = self.last_w.get(w)
            if t is not None and (dma or t[3] != eng or eng != "pe"):
                self._need(eng, t, waits)
            for t in self.readers.get(w, {}).values():
                if dma or t[3] != eng or eng != "pe":
                    self._need(eng, t, waits)
        if dma:
            j = self.dma_rr
            self.dma_rr = (self.dma_rr + 1) % self.n_dma_sems
            k = self.dma_uses[j]
            if k > 0:
                self._need(eng, ("d%d" % j, self.dma_sems[j], 16 * k, None), waits)
            self.dma_uses[j] = k + 1
            tok = ("d%d" % j, self.dma_sems[j], 16 * (k + 1), None)
            inc = (self.dma_sems[j], 16)
        else:
            self.cnt[eng] += 1
            tok = (eng, self.sem[eng], self.cnt[eng], eng)
            inc = (self.sem[eng], 1)
        for w in writes:
            self.last_w[w] = tok
            self.readers[w] = {}
        for r in reads:
            self.readers.setdefault(r, {})[tok[0]] = tok
        self._emit(eng, waits, fn, inc)
        return tok

    def group(self, eng, ops):
        if self.dead:
            return
        waits = []
        for fn, reads, writes in ops:
            for r in reads:
                t = self.last_w.get(r)
                if t is not None:
                    self._need(eng, t, waits)
            for w in writes:
                t = self.last_w.get(w)
                if t is not None and (t[3] != eng or eng != "pe"):
                    self._need(eng, t, waits)
                for t in self.readers.get(w, {}).values():
                    if t[3] != eng or eng != "pe":
                        self._need(eng, t, waits)
        first = True
        for fn, reads, writes in ops:
            self.cnt[eng] += 1
            tok = (eng, self.sem[eng], self.cnt[eng], eng)
            for w in writes:
                self.last_w[w] = tok
                self.readers[w] = {}
            for r in reads:
                self.readers.setdefault(r, {})[tok[0]] = tok
            self._emit(eng, waits if first else [], fn, (self.sem[eng], 1))
            first = False

    def _eng(self, name):
        nc = self.nc
        return {"pe": nc.tensor, "act": nc.scalar, "dve": nc.vector, "pool": nc.gpsimd, "sp": nc.sync}[name]

    def _emit(self, eng, waits, fn, inc):
        e = self._eng(eng)
        for sem, val in waits:
            e.wait_ge(sem, val)
        if fn is not None:
            ins = fn(e)
            ins.then_inc(inc[0], inc[1])

    def barrier(self):
        if self.dead:
            return
        toks = []
        for e in self.ENGS:
            if e != "sp" and self.cnt[e] > 0:
                toks.append((e, self.sem[e], self.cnt[e], e))
        for j in range(self.n_dma_sems):
            if self.dma_uses[j] > 0:
                toks.append(("d%d" % j, self.dma_sems[j], 16 * self.dma_uses[j], None))
        for e in self.ENGS:
            waits = []
            for t in toks:
                self._need(e, t, waits)
            if waits:
                self._emit(e, waits, None, None)


class _Stop(Exception):
    pass


def build_nc(debug=False, stop_after=None):
    nc = bass.Bass("TRN2", target_bir_lowering=False)

    def din(name, shape):
        return nc.dram_tensor(name, shape, F32, kind="ExternalInput").ap()

    x_d = din("x", [2, SEQ, DM])
    p_d = din("p", [2, SEQ, 256])
    mixn_d = din("mix_norm", [DM])
    win_d = din("w_in", [DM, 3072])
    qn_d = din("q_norm", [64])
    kn_d = din("k_norm", [64])
    lre_d = din("lambda_re", [32, 64])
    lim_d = din("lambda_im", [32, 64])
    ldt_d = din("log_dt", [32])
    bre_d = din("b_re", [32, 64, 16])
    bim_d = din("b_im", [32, 64, 16])
    cre_d = din("c_re", [32, 16, 64])
    cim_d = din("c_im", [32, 16, 64])
    dsk_d = din("d_skip", [512])
    wglu_d = din("w_glu", [512, 512])
    bglu_d = din("b_glu", [512])
    wout_d = din("w_out", [DM, DM])
    plen_d = din("ple_norm", [DM])
    wg_d = din("w_ple_gate", [DM, DM])
    wp_d = din("w_ple_proj", [256, DM])
    out_d = nc.dram_tensor("out", [2, SEQ, DM], F32, kind="ExternalOutput").ap()
    dbg = {}
    if debug:
        dbg["xT"] = nc.dram_tensor("dbg_xT", [128, 8, SEQ], F32, kind="ExternalOutput").ap()
        dbg["attnT"] = nc.dram_tensor("dbg_attnT", [128, 4, SEQ], F32, kind="ExternalOutput").ap()
        dbg["ssmT"] = nc.dram_tensor("dbg_ssmT", [128, 4, SEQ], F32, kind="ExternalOutput").ap()
        dbg["ygT"] = nc.dram_tensor("dbg_ygT", [128, 4, SEQ], F32, kind="ExternalOutput").ap()
        dbg["qT"] = nc.dram_tensor("dbg_qT", [128, SEQ], F32, kind="ExternalOutput").ap()
        dbg["T"] = nc.dram_tensor("dbg_T", [128, 32, 128], F32, kind="ExternalOutput").ap()
        dbg["S"] = nc.dram_tensor("dbg_S", [128, 16, 2, 257], F32, kind="ExternalOutput").ap()

    with ExitStack() as top:
        S = Sched(nc)
        S.alloc(top)
        top.enter_context(nc.allow_non_contiguous_dma(reason="small strided parameter loads"))

        uid = [0]

        def mk(stack):
            def sb(name, shape, dt=F32):
                uid[0] += 1
                return stack.enter_context(nc.sbuf_tensor("%s_u%d" % (name, uid[0]), shape, dt))
            return sb

        sbP = mk(top)
        PS = [top.enter_context(nc.psum_tensor("ps%d" % i, [128, 512], F32)) for i in range(8)]
        psn = ["ps%d" % i for i in range(8)]

        def V(fn, r=(), w=()):
            S.op("dve", fn, reads=r, writes=w)

        def A(fn, r=(), w=()):
            S.op("act", fn, reads=r, writes=w)

        def G(fn, r=(), w=()):
            S.op("pool", fn, reads=r, writes=w)

        def T(fn, r=(), w=()):
            S.op("pe", fn, reads=r, writes=w)

        def D(fn, r=(), w=()):
            S.op("sp", fn, reads=r, writes=w, dma=True)

        dq = [0]

        def D2(fn, r=(), w=()):
            dq[0] += 1
            S.op("sp" if dq[0] % 2 else "act", fn, reads=r, writes=w, dma=True)

        def ck(name):
            if stop_after == name:
                S.barrier()
                S.dead = True

        try:
            ident = sbP("ident", [128, 128])
            ones_f = sbP("ones_f", [128, 128])
            identr = sbP("identr", [128, 128], F32 if EXACT_MASK else BF16)
            cneg = sbP("cneg", [128, 2])
            blk1 = sbP("blk1", [128, 128], BF16)
            MT = sbP("MT", [128, 2432], F32 if EXACT_MASK else BF16)
            mixg = sbP("mixg", [128, 8, 1])
            pleg = sbP("pleg", [128, 8, 1])
            qg = sbP("qg", [128, 1])
            kg = sbP("kg", [128, 1])
            bglu = sbP("bglu", [128, 4, 1])
            Tm = sbP("Tm", [128, 32, 128], BF16)
            PGre = sbP("PGre", [128, 32, 64], BF16)
            PGim = sbP("PGim", [128, 32, 64], BF16)
            PCre = sbP("PCre", [128, 16, 128], BF16)
            PCim = sbP("PCim", [128, 16, 128], BF16)
            APR = sbP("APR", [128, 16, 16, 2])
            API = sbP("API", [128, 16, 16, 2])
            attnT = sbP("attnT", [128, 4, SEQ], BF16)
            ssmT = sbP("ssmT", [128, 4, SEQ], BF16)

            G(lambda e: e.memset(ident[:], 1.0), w=["ident"])
            G(lambda e: e.affine_select(out=ident[:], in_=ident[:], pattern=[[-1, 128]], base=0, channel_multiplier=1,
                                        compare_op=ALU.is_equal, fill=0.0), r=["ident"], w=["ident"])
            V(lambda e: e.memset(ones_f[:], 1.0), w=["ones_f"])
            V(lambda e: e.tensor_copy(out=(identr[:].bitcast(F32R) if EXACT_MASK else identr[:]), in_=ident[:]), r=["ident"], w=["identr"])
            V(lambda e: e.memset(cneg[:, 0:1], -0.5), w=["cneg"])
            V(lambda e: e.memset(cneg[:, 1:2], -1.0), w=["cneg"])
            V(lambda e: e.memset(blk1[:], 0.0), w=["blk1"])
            V(lambda e: e.memset(blk1[0:64, 0:64], 1.0), w=["blk1"])
            V(lambda e: e.memset(blk1[64:128, 64:128], 1.0), w=["blk1"])
            ck("consts")

            with ExitStack() as su:
                sb = mk(su)
                NI = 2432
                sb_keep = sb
                msk_scope = ExitStack()
                sb = mk(msk_scope)
                di = sb("di", [128, NI], I32)
                df = sb("df", [128, NI])
                t1 = sb("mt1", [128, NI])
                t2 = sb("mt2", [128, NI])
                acc = sb("macc", [128, NI])
                G(lambda e: e.iota(di[:], pattern=[[1, NI]], base=-384, channel_multiplier=-1), w=["di"])
                V(lambda e: e.tensor_copy(out=df[:], in_=di[:]), r=["di"], w=["df"])
                V(lambda e: e.tensor_scalar(out=acc[:], in0=df[:], scalar1=128.0, scalar2=None, op0=ALU.is_le), r=["df"], w=["acc"])
                for (mod, lim) in ((4.0, 512.0), (16.0, None)):
                    V(lambda e, mod=mod: e.tensor_scalar(out=t1[:], in0=df[:], scalar1=1.0 / mod, scalar2=MAGIC, op0=ALU.mult,
                                                         op1=ALU.add), r=["df"], w=["t1"])
                    V(lambda e, mod=mod: e.tensor_scalar(out=t1[:], in0=t1[:], scalar1=MAGIC, scalar2=mod, op0=ALU.subtract,
                                                         op1=ALU.mult), r=["t1"], w=["t1"])
                    V(lambda e: e.tensor_tensor(out=t1[:], in0=t1[:], in1=df[:], op=ALU.is_equal), r=["t1", "df"], w=["t1"])
                    if lim is not None:
                        V(lambda e, lim=lim: e.tensor_scalar(out=t2[:], in0=df[:], scalar1=lim, scalar2=None, op0=ALU.is_le),
                          r=["df"], w=["t2"])
                        V(lambda e: e.tensor_tensor(out=t1[:], in0=t1[:], in1=t2[:], op=ALU.mult), r=["t1", "t2"], w=["t1"])
                    V(lambda e: e.tensor_tensor(out=acc[:], in0=acc[:], in1=t1[:], op=ALU.add), r=["acc", "t1"], w=["acc"])
                V(lambda e: e.tensor_scalar(out=t2[:], in0=df[:], scalar1=0.0, scalar2=None, op0=ALU.is_ge), r=["df"], w=["t2"])
                V(lambda e: e.tensor_tensor(out=acc[:], in0=acc[:], in1=t2[:], op=ALU.mult), r=["acc", "t2"], w=["acc"])
                if EXACT_MASK:
                    V(lambda e: e.tensor_scalar(out=t1[:], in0=acc[:], scalar1=1.0, scalar2=None, op0=ALU.max), r=["acc"], w=["t1"])
                    A(lambda e: e.activation(out=t1[:], in_=t1[:], func=AF.Ln), r=["t1"], w=["t1"])
                    V(lambda e: e.tensor_scalar(out=t2[:], in0=acc[:], scalar1=0.0, scalar2=-30000.0, op0=ALU.is_equal, op1=ALU.mult),
                      r=["acc"], w=["t2"])
                    V(lambda e: e.tensor_tensor(out=MT[:].bitcast(F32R), in0=t1[:], in1=t2[:], op=ALU.add), r=["t1", "t2"], w=["MT"])
                else:
                    V(lambda e: e.tensor_scalar(out=t1[:], in0=acc[:], scalar1=2.0, scalar2=26.5, op0=ALU.is_equal, op1=ALU.mult),
                      r=["acc"], w=["t1"])
                    V(lambda e: e.tensor_scalar(out=t2[:], in0=acc[:], scalar1=3.0, scalar2=42.0, op0=ALU.is_equal, op1=ALU.mult),
                      r=["acc"], w=["t2"])
                    V(lambda e: e.tensor_tensor(out=t1[:], in0=t1[:], in1=t2[:], op=ALU.add), r=["t1", "t2"], w=["t1"])
                    V(lambda e: e.tensor_scalar(out=t2[:], in0=acc[:], scalar1=0.0, scalar2=-1.0e6, op0=ALU.is_equal, op1=ALU.mult),
                      r=["acc"], w=["t2"])
                    V(lambda e: e.tensor_tensor(out=MT[:], in0=t1[:], in1=t2[:], op=ALU.add), r=["t1", "t2"], w=["MT"])
                S.barrier()
                msk_scope.close()
                sb = sb_keep
                ck("mask")

                LR = sb("LR", [128, 16, 1])
                LI = sb("LI", [128, 16, 1])
                DT = sb("DT", [128, 16, 1])
                dtrow = sb("dtrow", [1, 32])
                D(lambda e: e.dma_start(out=dtrow[:], in_=ldt_d.rearrange("(o g) -> o g", o=1)), w=["dtrow"])
                T(lambda e: e.matmul(PS[5][:, 0:32], lhsT=ones_f[0:1, :], rhs=dtrow[:], start=True, stop=True),
                  r=["ones_f", "dtrow"], w=[psn[5]])
                for j in range(2):
                    V(lambda e, j=j: e.tensor_copy(out=DT[64 * j:64 * j + 64, :, 0], in_=PS[5][64 * j:64 * j + 64, j:32:2]),
                      r=[psn[5]], w=["DT"])
                Ln_ = [sb("Lnat%d" % i, [32, 2, 64]) for i in range(2)]
                for li, (src, dst, dname) in enumerate(((lre_d, LR, "LR"), (lim_d, LI, "LI"))):
                    D(lambda e, li=li, src=src: e.dma_start(out=Ln_[li][:, 0, :], in_=src), w=["Lnat%d" % li])
                    G(lambda e, li=li: e.tensor_copy(out=Ln_[li][:, 1, :], in_=Ln_[li][:, 0, :]), r=["Lnat%d" % li], w=["Lnat%d" % li])
                    bank = 6 + li
                    T(lambda e, li=li, bank=bank: e.transpose(out=PS[bank][:, 0:32], in_=Ln_[li][:].rearrange("p d n -> p (d n)"),
                                                              identity=ident[0:32, 0:32]), r=["Lnat%d" % li, "ident"], w=[psn[bank]])
                    for j in range(2):
                        V(lambda e, dst=dst, j=j, bank=bank: e.tensor_copy(
                            out=dst[64 * j:64 * j + 64, :, 0], in_=PS[bank][64 * j:64 * j + 64, j:32:2]), r=[psn[bank]], w=[dname])
                BR = sb("BR", [128, 16, 16])
                BI = sb("BI", [128, 16, 16])
                for j in range(2):
                    D2(lambda e, j=j: e.dma_start(out=BR[64 * j:64 * j + 64], in_=bre_d.rearrange("(gp j) n c -> j n gp c", j=2)[j]), w=["BR"])
                    D2(lambda e, j=j: e.dma_start(out=BI[64 * j:64 * j + 64], in_=bim_d.rearrange("(gp j) n c -> j n gp c", j=2)[j]), w=["BI"])
                Cn = [sb("Cn%d" % i, [128, 4, 2, 64]) for i in range(2)]
                for ci, src in enumerate((cre_d, cim_d)):
                    D(lambda e, ci=ci, src=src: e.dma_start(out=Cn[ci][:, :, 0, :],
                                                            in_=src.rearrange("(t g) c n -> (g c) t n", t=4)), w=["Cn%d" % ci])
                    G(lambda e, ci=ci: e.tensor_copy(out=Cn[ci][:, :, 1, :], in_=Cn[ci][:, :, 0, :]), r=["Cn%d" % ci], w=["Cn%d" % ci])
                gnat = sb("gnat", [20, 128])
                D(lambda e: e.dma_start(out=gnat[0:8, :], in_=mixn_d.rearrange("(k p) -> k p", p=128)), w=["gnat"])
                D(lambda e: e.dma_start(out=gnat[8:16, :], in_=plen_d.rearrange("(k p) -> k p", p=128)), w=["gnat"])
                D(lambda e: e.dma_start(out=gnat[16:20, :], in_=bglu_d.rearrange("(k p) -> k p", p=128)), w=["gnat"])
                T(lambda e: e.transpose(out=PS[4][:, 0:20], in_=gnat[:], identity=ident[0:20, 0:20]), r=["gnat", "ident"], w=[psn[4]])
                V(lambda e: e.tensor_copy(out=mixg[:, :, 0], in_=PS[4][:, 0:8]), r=[psn[4]], w=["mixg"])
                V(lambda e: e.tensor_copy(out=pleg[:, :, 0], in_=PS[4][:, 8:16]), r=[psn[4]], w=["pleg"])
                V(lambda e: e.tensor_copy(out=bglu[:, :, 0], in_=PS[4][:, 16:20]), r=[psn[4]], w=["bglu"])
                qkrow = sb("qkrow", [2, 2, 64])
                for hh in range(2):
                    D(lambda e, hh=hh: e.dma_start(out=qkrow[0:1, hh, :], in_=qn_d.rearrange("(o n) -> o n", o=1)), w=["qkrow"])
                    D(lambda e, hh=hh: e.dma_start(out=qkrow[1:2, hh, :], in_=kn_d.rearrange("(o n) -> o n", o=1)), w=["qkrow"])
                T(lambda e: e.transpose(out=PS[4][:, 32:34], in_=qkrow[:].rearrange("r h n -> r (h n)"), identity=ident[0:2, 0:2]),
                  r=["qkrow", "ident"], w=[psn[4]])
                V(lambda e: e.tensor_scalar(out=qg[:], in0=PS[4][:, 32:33], scalar1=0.125 * ALPHA, scalar2=None, op0=ALU.mult),
                  r=[psn[4]], w=["qg"])
                V(lambda e: e.tensor_copy(out=kg[:], in_=PS[4][:, 33:34]), r=[psn[4]], w=["kg"])
                CR = sb("CR", [128, 16, 16])
                CI = sb("CI", [128, 16, 16])
                for ci, (dst, dname) in enumerate(((CR, "CR"), (CI, "CI"))):
                    for t in range(4):
                        bank = (ci * 4 + t) % 8
                        T(lambda e, ci=ci, t=t, bank=bank: e.transpose(
                            out=PS[bank][:, 0:128], in_=Cn[ci][:, t, :, :].rearrange("p d n -> p (d n)"), identity=ident[:]),
                          r=["Cn%d" % ci, "ident"], w=[psn[bank]])
                        for j in range(2):
                            V(lambda e, dst=dst, t=t, j=j, bank=bank: e.tensor_copy(
                                out=dst[64 * j:64 * j + 64, 4 * t:4 * t + 4, :],
                                in_=PS[bank][64 * j:64 * j + 64, 0:128].rearrange("p (g q c) -> p g q c", q=2, c=16)[:, :, j, :]),
                              r=[psn[bank]], w=[dname])
                dcol = sb("dcol", [128, 32, 1])
                dnat = sb("dnat", [32, 16])
                d16 = sb("d16", [16, 32])
                Rrep = sb("Rrep", [16, 128])
                D(lambda e: e.dma_start(out=dnat[:], in_=dsk_d.rearrange("(g c) -> g c", c=16)), w=["dnat"])
                T(lambda e: e.transpose(out=PS[4][0:16, 64:96], in_=dnat[:], identity=ident[0:32, 0:32]), r=["dnat", "ident"], w=[psn[4]])
                V(lambda e: e.tensor_copy(out=d16[:], in_=PS[4][0:16, 64:96]), r=[psn[4]], w=["d16"])
                for sp_ in range(8):
                    V(lambda e, sp_=sp_: e.tensor_copy(out=Rrep[:, 16 * sp_:16 * sp_ + 16], in_=ident[0:16, 0:16]), r=["ident"], w=["Rrep"])
                T(lambda e: e.matmul(PS[4][:, 128:160], lhsT=Rrep[:], rhs=d16[:], start=True, stop=True), r=["Rrep", "d16"], w=[psn[4]])
                V(lambda e: e.tensor_copy(out=dcol[:, :, 0], in_=PS[4][:, 128:160]), r=[psn[4]], w=["dcol"])

                def sm(name):
                    return sb("s_" + name, [128, 16, 1])

                dt_ = sm("dt_"); th = sm("th"); lrdt = sm("lrdt"); mag = sm("mag"); magi = sm("magi")
                rr = sm("rr"); sn = sm("sn"); cs = sm("cs"); tA = sm("tA"); tB = sm("tB")
                a_re = sm("a_re"); a_im = sm("a_im"); i_re = sm("i_re"); i_im = sm("i_im")
                c_re = sm("c_re"); c_im = sm("c_im"); den = sm("den")
                A(lambda e: e.activation(out=dt_[:], in_=DT[:], func=AF.Exp), r=["DT"], w=["dt_"])
                V(lambda e: e.tensor_tensor(out=th[:], in0=LI[:], in1=dt_[:], op=ALU.mult), r=["LI", "dt_"], w=["th"])
                V(lambda e: e.tensor_tensor(out=lrdt[:], in0=LR[:], in1=dt_[:], op=ALU.mult), r=["LR", "dt_"], w=["lrdt"])
                A(lambda e: e.activation(out=mag[:], in_=lrdt[:], func=AF.Exp), r=["lrdt"], w=["mag"])
                A(lambda e: e.activation(out=magi[:], in_=lrdt[:], func=AF.Exp, scale=-1.0), r=["lrdt"], w=["magi"])

                def sin_of(dst, shift, dname):
                    V(lambda e: e.tensor_scalar(out=tA[:], in0=th[:], scalar1=shift, scalar2=1.0 / TWO_PI, op0=ALU.add, op1=ALU.mult),
                      r=["th"], w=["tA"])
                    V(lambda e: e.tensor_scalar(out=tA[:], in0=tA[:], scalar1=MAGIC, scalar2=MAGIC, op0=ALU.add, op1=ALU.subtract),
                      r=["tA"], w=["tA"])
                    V(lambda e: e.tensor_scalar(out=tB[:], in0=th[:], scalar1=shift, scalar2=None, op0=ALU.add), r=["th"], w=["tB"])
                    V(lambda e: e.scalar_tensor_tensor(out=rr[:], in0=tA[:], scalar=-TWO_PI, in1=tB[:], op0=ALU.mult, op1=ALU.add),
                      r=["tA", "tB"], w=["rr"])
                    V(lambda e: e.tensor_scalar(out=rr[:], in0=rr[:], scalar1=math.pi, scalar2=-math.pi, op0=ALU.min, op1=ALU.max),
                      r=["rr"], w=["rr"])
                    A(lambda e: e.activation(out=dst[:], in_=rr[:], func=AF.Sin), r=["rr"], w=[dname])

                sin_of(sn, 0.0, "sn")
                sin_of(cs, math.pi / 2.0, "cs")
                V(lambda e: e.tensor_tensor(out=a_re[:], in0=mag[:], in1=cs[:], op=ALU.mult), r=["mag", "cs"], w=["a_re"])
                V(lambda e: e.tensor_tensor(out=a_im[:], in0=mag[:], in1=sn[:], op=ALU.mult), r=["mag", "sn"], w=["a_im"])
                V(lambda e: e.tensor_tensor(out=i_re[:], in0=magi[:], in1=cs[:], op=ALU.mult), r=["magi", "cs"], w=["i_re"])
                V(lambda e: e.scalar_tensor_tensor(out=i_im[:], in0=magi[:], scalar=-1.0, in1=sn[:], op0=ALU.mult, op1=ALU.mult),
                  r=["magi", "sn"], w=["i_im"])
                V(lambda e: e.tensor_tensor(out=den[:], in0=LR[:], in1=LR[:], op=ALU.mult), r=["LR"], w=["den"])
                V(lambda e: e.tensor_tensor(out=tA[:], in0=LI[:], in1=LI[:], op=ALU.mult), r=["LI"], w=["tA"])
                V(lambda e: e.tensor_tensor(out=den[:], in0=den[:], in1=tA[:], op=ALU.add), r=["den", "tA"], w=["den"])
                V(lambda e: e.reciprocal(out=den[:], in_=den[:]), r=["den"], w=["den"])
                V(lambda e: e.tensor_scalar(out=tB[:], in0=a_re[:], scalar1=-1.0, scalar2=None, op0=ALU.add), r=["a_re"], w=["tB"])
                V(lambda e: e.tensor_tensor(out=c_re[:], in0=tB[:], in1=LR[:], op=ALU.mult), r=["tB", "LR"], w=["c_re"])
                V(lambda e: e.tensor_tensor(out=tA[:], in0=a_im[:], in1=LI[:], op=ALU.mult), r=["a_im", "LI"], w=["tA"])
                V(lambda e: e.tensor_tensor(out=c_re[:], in0=c_re[:], in1=tA[:], op=ALU.add), r=["c_re", "tA"], w=["c_re"])
                V(lambda e: e.tensor_tensor(out=c_re[:], in0=c_re[:], in1=den[:], op=ALU.mult), r=["c_re", "den"], w=["c_re"])
                V(lambda e: e.tensor_tensor(out=c_im[:], in0=a_im[:], in1=LR[:], op=ALU.mult), r=["a_im", "LR"], w=["c_im"])
                V(lambda e: e.tensor_tensor(out=tA[:], in0=tB[:], in1=LI[:], op=ALU.mult), r=["tB", "LI"], w=["tA"])
                V(lambda e: e.tensor_tensor(out=c_im[:], in0=c_im[:], in1=tA[:], op=ALU.subtract), r=["c_im", "tA"], w=["c_im"])
                V(lambda e: e.tensor_tensor(out=c_im[:], in0=c_im[:], in1=den[:], op=ALU.mult), r=["c_im", "den"], w=["c_im"])

                tb1 = sb("tb1", [128, 16, 16])
                tb2 = sb("tb2", [128, 16, 16])

                lit = {a_re.name: "a_re", a_im.name: "a_im", i_re.name: "i_re", i_im.name: "i_im",
                       c_re.name: "c_re", c_im.name: "c_im"}

                def nm(t):
                    return lit.get(t.name, t.name)

                tmpE = {"dve": (tb1, tb2, "tb1", "tb2", tA, tB, "tA", "tB"),
                        "pool": (sb("tb1p", [128, 16, 16]), sb("tb2p", [128, 16, 16]), "tb1p", "tb2p",
                                 sm("tAp"), sm("tBp"), "tAp", "tBp")}

                def cmul_b(eng, o_re, o_im, on, x_re, x_im, xn, s_re, s_im, neg_im=False):
                    u1, u2, n1, n2 = tmpE[eng][0:4]
                    sr = s_re[:].to_broadcast([128, 16, 16])
                    si = s_im[:].to_broadcast([128, 16, 16])
                    O = lambda fn, r, w: S.op(eng, fn, reads=r, writes=w)
                    O(lambda e: e.tensor_tensor(out=u1[:], in0=x_re, in1=sr, op=ALU.mult), xn + [nm(s_re)], [n1])
                    O(lambda e: e.tensor_tensor(out=u2[:], in0=x_im, in1=si, op=ALU.mult), xn + [nm(s_im)], [n2])
                    O(lambda e: e.tensor_tensor(out=o_re, in0=u1[:], in1=u2[:], op=ALU.subtract), [n1, n2], [on])
                    O(lambda e: e.tensor_tensor(out=u1[:], in0=x_re, in1=si, op=ALU.mult), xn + [nm(s_im)], [n1])
                    O(lambda e: e.tensor_tensor(out=u2[:], in0=x_im, in1=sr, op=ALU.mult), xn + [nm(s_re)], [n2])
                    if neg_im:
                        O(lambda e: e.tensor_tensor(out=u1[:], in0=u1[:], in1=u2[:], op=ALU.add), [n1, n2], [n1])
                        O(lambda e: e.tensor_scalar(out=o_im, in0=u1[:], scalar1=-1.0, scalar2=None, op0=ALU.mult), [n1], [on])
                    else:
                        O(lambda e: e.tensor_tensor(out=o_im, in0=u1[:], in1=u2[:], op=ALU.add), [n1, n2], [on])

                def cmul_s(eng, o_re, o_im, x_re, x_im, s_re, s_im):
                    u1, u2, n1, n2 = tmpE[eng][4:8]
                    O = lambda fn, r, w: S.op(eng, fn, reads=r, writes=w)
                    O(lambda e: e.tensor_tensor(out=u1[:], in0=x_re[:], in1=s_re[:], op=ALU.mult), [nm(x_re), nm(s_re)], [n1])
                    O(lambda e: e.tensor_tensor(out=u2[:], in0=x_im[:], in1=s_im[:], op=ALU.mult), [nm(x_im), nm(s_im)], [n2])
                    O(lambda e: e.tensor_tensor(out=o_re[:], in0=u1[:], in1=u2[:], op=ALU.subtract), [n1, n2], [nm(o_re)])
                    O(lambda e: e.tensor_tensor(out=u1[:], in0=x_re[:], in1=s_im[:], op=ALU.mult), [nm(x_re), nm(s_im)], [n1])
                    O(lambda e: e.tensor_tensor(out=u2[:], in0=x_im[:], in1=s_re[:], op=ALU.mult), [nm(x_im), nm(s_re)], [n2])
                    O(lambda e: e.tensor_tensor(out=o_im[:], in0=u1[:], in1=u2[:], op=ALU.add), [n1, n2], [nm(o_im)])

                BBr = sb("BBr", [128, 16, 16])
                BBi = sb("BBi", [128, 16, 16])
                cmul_b("dve", BBr[:], BBi[:], "BB", BR[:], BI[:], ["BR", "BI"], c_re, c_im)
                pw_re = [sm("pwr%d" % i) for i in range(9)]
                pw_im = [sm("pwi%d" % i) for i in range(9)]
                iw_re = [sm("iwr%d" % i) for i in range(8)]
                iw_im = [sm("iwi%d" % i) for i in range(8)]
                CPr = sb("CPr", [128, 16, 9, 16])
                CPi = sb("CPi", [128, 16, 9, 16])
                for (eng, lst_re, lst_im) in (("dve", pw_re, pw_im), ("pool", iw_re, iw_im)):
                    S.op(eng, lambda e, t=lst_re[0]: e.memset(t[:], 1.0), writes=[lst_re[0].name])
                    S.op(eng, lambda e, t=lst_im[0]: e.memset(t[:], 0.0), writes=[lst_im[0].name])
                for i in range(1, 8):
                    cmul_s("pool", iw_re[i], iw_im[i], iw_re[i - 1], iw_im[i - 1], i_re, i_im)
                for tau in range(9):
                    if tau + 1 < 9:
                        cmul_s("dve", pw_re[tau + 1], pw_im[tau + 1], pw_re[tau], pw_im[tau], a_re, a_im)
                    cmul_b("dve", CPr[:, :, tau, :], CPi[:, :, tau, :], "CP%d" % tau, CR[:], CI[:], ["CR", "CI"],
                           pw_re[tau], pw_im[tau], neg_im=True)
                Qr = sb("Qr", [128, 16, 8, 16])
                Qi = sb("Qi", [128, 16, 8, 16])
                Hr = sb("Hr", [128, 16, 8, 16])
                Hi = sb("Hi", [128, 16, 8, 16])
                for s_ in range(8):
                    cmul_b("pool", Qr[:, :, s_, :], Qi[:, :, s_, :], "Q%d" % s_, BBr[:], BBi[:], ["BB"], iw_re[s_], iw_im[s_])
                    cmul_b("dve" if s_ < 3 else "pool", Hr[:, :, s_, :], Hi[:, :, s_, :], "H%d" % s_, BBr[:], BBi[:], ["BB"],
                           pw_re[7 - s_], pw_im[7 - s_])
                CPn = ["CP%d" % t for t in range(9)]
                Qn = ["Q%d" % t for t in range(8)]
                Hn = ["H%d" % t for t in range(8)]
                V(lambda e: e.tensor_copy(out=PCre[:].rearrange("p g (t c) -> p g t c", c=16), in_=CPr[:, :, 1:9, :]), r=CPn, w=["PCre"])
                V(lambda e: e.tensor_copy(out=PCim[:].rearrange("p g (t c) -> p g t c", c=16), in_=CPi[:, :, 1:9, :]), r=CPn, w=["PCim"])
                qw_re = [pw_re[8]] + [sm("qwr%d" % i) for i in range(1, 16)]
                qw_im = [pw_im[8]] + [sm("qwi%d" % i) for i in range(1, 16)]
                for jj in range(16):
                    if jj > 0:
                        cmul_s("pool", qw_re[jj], qw_im[jj], qw_re[jj - 1], qw_im[jj - 1], pw_re[8], pw_im[8])
                    G(lambda e, jj=jj: e.tensor_copy(out=APR[:, jj, :, 0:1], in_=qw_re[jj][:]), r=[qw_re[jj].name], w=["APR"])
                    G(lambda e, jj=jj: e.tensor_copy(out=APR[:, jj, :, 1:2], in_=qw_re[jj][:]), r=[qw_re[jj].name], w=["APR"])
                    G(lambda e, jj=jj: e.tensor_scalar(out=API[:, jj, :, 0:1], in0=qw_im[jj][:], scalar1=-1.0, scalar2=None,
                                                       op0=ALU.mult), r=[qw_im[jj].name], w=["API"])
                    G(lambda e, jj=jj: e.tensor_copy(out=API[:, jj, :, 1:2], in_=qw_im[jj][:]), r=[qw_im[jj].name], w=["API"])
                for gp in range(16):
                    for ci, (src, dst, dn_) in enumerate(((Hr, PGre, "PGre"), (Hi, PGim, "PGim"))):
                        bank = (2 * gp + ci) % 8
                        T(lambda e, src=src, gp=gp, bank=bank: e.transpose(
                            out=PS[bank][:, 0:128], in_=src[:, gp, :, :].rearrange("p s c -> p (s c)"), identity=ident[:]),
                          r=Hn + ["ident"], w=[psn[bank]])
                        fcp = (lambda e, dst=dst, gp=gp, bank=bank: e.tensor_copy(
                            out=dst[:, 2 * gp:2 * gp + 2, :], in_=PS[bank][:, 0:128].rearrange("p (j n) -> p j n", j=2)))
                        if ci == 0:
                            V(fcp, r=[psn[bank]], w=[dn_])
                        else:
                            A(lambda e, dst=dst, gp=gp, bank=bank: e.copy(
                                out=dst[:, 2 * gp:2 * gp + 2, :], in_=PS[bank][:, 0:128].rearrange("p (j n) -> p j n", j=2)),
                              r=[psn[bank]], w=[dn_])
                tmask = sb("tmask", [128, 128])
                ttmp = sb("ttmp", [128, 128])
                G(lambda e: e.memset(tmask[:], 1.0), w=["tmask"])
                G(lambda e: e.affine_select(out=tmask[:], in_=tmask[:], pattern=[[16, 8], [0, 16]], base=15, channel_multiplier=-1,
                                            compare_op=ALU.is_ge, fill=0.0), r=["tmask"], w=["tmask"])
                for g in range(32):
                    gp, j = g // 2, g % 2
                    bank = g % 8
                    pb = 64 * j
                    T(lambda e, gp=gp, pb=pb, bank=bank: e.matmul(PS[bank][:, 0:128], lhsT=Qr[pb:pb + 64, gp, :, :].rearrange("p s c -> p (s c)"),
                                                                   rhs=CPr[pb:pb + 64, gp, 0:8, :].rearrange("p s c -> p (s c)"), start=True, stop=False),
                      r=Qn + CPn, w=[psn[bank]])
                    T(lambda e, gp=gp, pb=pb, bank=bank: e.matmul(PS[bank][:, 0:128], lhsT=Qi[pb:pb + 64, gp, :, :].rearrange("p s c -> p (s c)"),
                                                                   rhs=CPi[pb:pb + 64, gp, 0:8, :].rearrange("p s c -> p (s c)"), start=False, stop=True),
                      r=Qn + CPn, w=[psn[bank]])
                    V(lambda e, bank=bank: e.tensor_tensor(out=ttmp[:], in0=PS[bank][:, 0:128], in1=tmask[:], op=ALU.mult),
                      r=[psn[bank], "tmask"], w=["ttmp"])
                    V(lambda e, g=g: e.scalar_tensor_tensor(out=Tm[:, g, :], in0=ident[:], scalar=dcol[:, g, :], in1=ttmp[:],
                                                            op0=ALU.mult, op1=ALU.add), r=["ident", "dcol", "ttmp"], w=["Tm"])
                if debug:
                    dT = sb("dT", [128, 32, 128])
                    V(lambda e: e.tensor_copy(out=dT[:], in_=Tm[:]), r=["Tm"], w=["dT"])
                    D(lambda e: e.dma_start(out=dbg["T"], in_=dT[:]), r=["dT"])
                S.barrier()

            for s in range(2):
                S.barrier()
                with ExitStack() as abc:
                    sb = mk(abc)
                    xT = sb("xT", [128, 8, SEQ], BF16)
                    wst = [None]
                    wbf = [None, None]
                    wctr = [0]

                    def load_w(c0, slot):
                        i = 0
                        D(lambda e: e.dma_start(out=wst[i][:], in_=win_d[:, c0:c0 + 128].rearrange("(k p) c -> p k c", p=128)),
                          w=["wsta%d" % i])
                        G(lambda e: e.tensor_tensor(out=wbf[slot][:], in0=wst[i][:], in1=mixg[:].to_broadcast([128, 8, 128]),
                                                    op=ALU.mult), r=["wsta%d" % i, "mixg"], w=["wbf%d" % slot])
                        return wbf[slot], "wbf%d" % slot

                    ck("ssmsetup")
                    with ExitStack() as a0:
                        sb0 = mk(a0)
                        xt = [sb0("xt%d" % i, [128, DM]) for i in range(4)]
                        sqs = [sb0("sq%d" % i, [128, DM]) for i in range(2)]
                        st4s = [sb0("st4%d" % i, [128, 4]) for i in range(2)]
                        dgs = [sb0("dg%d" % i, [128, 128]) for i in range(2)]
                        rbcs = [sb0("rbc%d" % i, [128, 128]) for i in range(2)]
                        def a0_load(i):
                            xi = xt[i % 4]
                            xn = "xt%d" % (i % 4)
                            sq, sqn = sqs[i % 2], "sq%d" % (i % 2)
                            D(lambda e: e.dma_start(out=xi[:], in_=x_d[s, i * 128:(i + 1) * 128, :]), w=[xn])
                            A(lambda e: e.activation(out=sq[:], in_=xi[:], func=AF.Square), r=[xn], w=[sqn])

                        def a0_stats(i):
                            p2 = i % 2
                            xi = xt[i % 4]
                            xn = "xt%d" % (i % 4)
                            sq, sqn = sqs[p2], "sq%d" % p2
                            st4, stn = st4s[p2], "st4%d" % p2
                            dg, dgn = dgs[p2], "dg%d" % p2
                            rbc, rbn = rbcs[p2], "rbc%d" % p2
                            rb = 2 + p2
                            V(lambda e: e.tensor_reduce(out=st4[:, 0:1], in_=sq[:], axis=AX.X, op=ALU.add), r=[sqn], w=[stn])
                            V(lambda e: e.tensor_scalar(out=st4[:, 1:2], in0=st4[:, 0:1], scalar1=1.0 / DM, scalar2=EPS,
                                                        op0=ALU.mult, op1=ALU.add), r=[stn], w=[stn])
                            G(lambda e: e.tensor_tensor(out=st4[:, 3:4], in0=st4[:, 1:2], in1=cneg[:, 0:1], op=ALU.pow),
                              r=[stn, "cneg"], w=[stn])

                        def a0_stats_b(i):
                            p2 = i % 2
                            st4, stn = st4s[p2], "st4%d" % p2
                            xi = xt[i % 4]
                            xn = "xt%d" % (i % 4)
                            A(lambda e: e.activation(out=xi[:], in_=xi[:], func=AF.Copy, scale=st4[:, 3:4]), r=[xn, stn], w=[xn])

                        def a0_xpose(i):
                            p2 = i % 2
                            xi = xt[i % 4]
                            xn = "xt%d" % (i % 4)
                            for half in range(2):
                                tbk = 4 * p2 + half
                                for q4 in range(4):
                                    kc = half * 4 + q4
                                    T(lambda e, kc=kc, q4=q4, tbk=tbk: e.transpose(
                                        out=PS[tbk][:, q4 * 128:(q4 + 1) * 128], in_=xi[:, kc * 128:(kc + 1) * 128], identity=ident[:]),
                                      r=[xn, "ident"], w=[psn[tbk]])
                                if half == 0:
                                    V(lambda e, half=half, tbk=tbk: e.tensor_copy(
                                        out=xT[:, half * 4:half * 4 + 4, i * 128:(i + 1) * 128],
                                        in_=PS[tbk][:].rearrange("p (k t) -> p k t", k=4)), r=[psn[tbk]], w=["xT"])
                                else:
                                    A(lambda e, half=half, tbk=tbk: e.copy(
                                        out=xT[:, half * 4:half * 4 + 4, i * 128:(i + 1) * 128],
                                        in_=PS[tbk][:].rearrange("p (k t) -> p k t", k=4)), r=[psn[tbk]], w=["xT"])

                        a0_load(0)
                        a0_load(1)
                        a0_stats(0)
                        a0_stats_b(0)
                        for i in range(16):
                            if i + 2 < 16:
                                a0_load(i + 2)
                            if i + 1 < 16:
                                a0_stats(i + 1)
                            a0_xpose(i)
                            if i + 1 < 16:
                                a0_stats_b(i + 1)
                        S.barrier()
                    if debug and s == 0:
                        with ExitStack() as dd:
                            d32 = mk(dd)("d32", [128, 8, SEQ])
                            V(lambda e: e.tensor_copy(out=d32[:], in_=xT[:]), r=["xT"], w=["d32"])
                            D(lambda e: e.dma_start(out=dbg["xT"], in_=d32[:]), r=["d32"])
                            S.barrier()

                    ck("A0")
                    with ExitStack() as bs:
                        sbB = mk(bs)
                        qT = [sbB("qT%d" % i, [128, SEQ], BF16) for i in range(2)]
                        kT = [sbB("kT%d" % i, [128, SEQ], BF16) for i in range(2)]
                        gaT = [sbB("gaT%d" % i, [128, SEQ], BF16) for i in range(2)]
                        Vaug = [sbB("Vaug%d" % i, [128, 16, 2, 128], BF16) for i in range(2)]
                        wB = [sbB("wB%d" % i, [128, 8, 128], BF16) for i in range(8)]
                        wst[0] = sbB("wstaB", [128, 8, 128])
                        sqb = [sbB("sqb%d" % i, [128, 512], BF16) for i in range(2)]
                        tt = [sbB("tt%d" % i, [128, 512]) for i in range(2)]
                        NE = 4
                        Eb = [sbB("Eb%d" % i, [128, 512], BF16) for i in range(NE)]
                        rd = [sbB("rd%d" % i, [128, 512]) for i in range(2)]
                        ot = [sbB("ot%d" % i, [128, 512]) for i in range(2)]
                        oc = [sbB("oc%d" % i, [128, 512]) for i in range(2)]
                        SCB = (4, 5, 6, 7)
                        for i in range(2):
                            V(lambda e, i=i: e.memset(Vaug[i][:], 1.0), w=["Vaug%d" % i])

                        def load_wB(c0, slot):
                            i = 0
                            D(lambda e: e.dma_start(out=wst[i][:], in_=win_d[:, c0:c0 + 128].rearrange("(k p) c -> p k c", p=128)),
                              w=["wsta%d" % i])
                            G(lambda e: e.tensor_tensor(out=wB[slot][:], in0=wst[i][:], in1=mixg[:].to_broadcast([128, 8, 128]),
                                                        op=ALU.mult), r=["wsta%d" % i, "mixg"], w=["wB%d" % slot])
                            return wB[slot], "wB%d" % slot

                        def prep_units(hp):
                            par = hp % 2
                            wq, wqn = load_wB(hp * 128, par * 4 + 0)
                            wk, wkn = load_wB(512 + hp * 128, par * 4 + 1)
                            yield
                            wv, wvn = load_wB(1024 + hp * 128, par * 4 + 2)
                            wa, wan = load_wB(1536 + hp * 128, par * 4 + 3)
                            yield
                            blocks = []
                            for (w_, wn_, dstT, dn, gain, gn) in ((wq, wqn, qT[par], "qT%d" % par, qg, "qg"),
                                                                  (wk, wkn, kT[par], "kT%d" % par, kg, "kg")):
                                for tg in range(4):
                                    blocks.append((w_, wn_, dstT, dn, gain, gn, tg))

                            def S1(bi):
                                w_, wn_, dstT, dn, gain, gn, tg = blocks[bi]
                                pbk = bi % 2
                                for kc in range(8):
                                    T(lambda e, kc=kc: e.matmul(PS[pbk][:], lhsT=w_[:, kc, :], rhs=xT[:, kc, tg * 512:(tg + 1) * 512],
                                                                start=(kc == 0), stop=(kc == 7)), r=[wn_, "xT"], w=[psn[pbk]])

                            def S2(bi):
                                pbk = bi % 2
                                sq_, sqn = sqb[bi % 2], "sqb%d" % (bi % 2)
                                t_, tn = tt[bi % 2], "tt%d" % (bi % 2)
                                A(lambda e: e.activation(out=sq_[:], in_=PS[pbk][:], func=AF.Square), r=[psn[pbk]], w=[sqn])
                                T(lambda e: e.matmul(PS[2][:], lhsT=blk1[:], rhs=sq_[:], start=True, stop=True),
                                  r=["blk1", sqn], w=[psn[2]])
                                V(lambda e: e.tensor_scalar(out=t_[:], in0=PS[2][:], scalar1=1.0 / 64.0, scalar2=EPS, op0=ALU.mult,
                                                            op1=ALU.add), r=[psn[2]], w=[tn])

                            def S3(bi):
                                w_, wn_, dstT, dn, gain, gn, tg = blocks[bi]
                                pbk = bi % 2
                                t_, tn = tt[bi % 2], "tt%d" % (bi % 2)
                                A(lambda e: e.activation(out=t_[:], in_=t_[:], func=AF.Ln), r=[tn], w=[tn])
                                A(lambda e: e.activation(out=t_[:], in_=t_[:], func=AF.Exp, scale=-0.5), r=[tn], w=[tn])
                                V(lambda e: e.scalar_tensor_tensor(out=dstT[:, tg * 512:(tg + 1) * 512], in0=PS[pbk][:],
                                                                   scalar=gain[:, 0:1], in1=t_[:], op0=ALU.mult, op1=ALU.mult),
                                  r=[psn[pbk], gn, tn], w=[dn])

                            for t in range(8 + 2):
                                if 0 <= t - 2 < 8:
                                    S3(t - 2)
                                if 0 <= t - 1 < 8:
                                    S2(t - 1)
                                if t < 8:
                                    S1(t)
                                yield

                            def G1(tg):
                                pbk = tg % 2
                                for kc in range(8):
                                    T(lambda e, kc=kc: e.matmul(PS[pbk][:], lhsT=wa[:, kc, :], rhs=xT[:, kc, tg * 512:(tg + 1) * 512],
                                                                start=(kc == 0), stop=(kc == 7)), r=[wan, "xT"], w=[psn[pbk]])

                            def G2(tg):
                                pbk = tg % 2
                                t_, tn = tt[tg % 2], "tt%d" % (tg % 2)
                                A(lambda e: e.activation(out=t_[:], in_=PS[pbk][:], func=AF.Exp, scale=-1.0), r=[psn[pbk]], w=[tn])
                                A(lambda e: e.activation(out=t_[:], in_=t_[:], func=AF.Ln, bias=1.0), r=[tn], w=[tn])
                                A(lambda e: e.activation(out=t_[:], in_=t_[:], func=AF.Exp, scale=-1.0), r=[tn], w=[tn])

                            def G3(tg):
                                pbk = tg % 2
                                t_, tn = tt[tg % 2], "tt%d" % (tg % 2)
                                V(lambda e: e.tensor_tensor(out=gaT[par][:, tg * 512:(tg + 1) * 512], in0=PS[pbk][:], in1=t_[:],
                                                            op=ALU.mult), r=[tn, psn[pbk]], w=["gaT%d" % par])

                            for t in range(4 + 2):
                                if 0 <= t - 2 < 4:
                                    G3(t - 2)
                                if 0 <= t - 1 < 4:
                                    G2(t - 1)
                                if t < 4:
                                    G1(t)
                                yield

                            def V1(i4):
                                pbk = i4 % 2
                                for ii in range(4):
                                    i = i4 * 4 + ii
                                    for kc in range(8):
                                        T(lambda e, kc=kc, i=i, ii=ii: e.matmul(
                                            PS[pbk][:, ii * 128:(ii + 1) * 128], lhsT=xT[:, kc, i * 128:(i + 1) * 128], rhs=wv[:, kc, :],
                                            start=(kc == 0), stop=(kc == 7)), r=[wvn, "xT"], w=[psn[pbk]])

                            def V2(i4):
                                pbk = i4 % 2
                                pv = PS[pbk][:].rearrange("p (i c) -> p i c", i=4)
                                V(lambda e: e.tensor_copy(out=Vaug[par][:, i4 * 4:i4 * 4 + 4, 0, 0:64], in_=pv[:, :, 0:64]),
                                  r=[psn[pbk]], w=["Vaug%d" % par])
                                A(lambda e: e.copy(out=Vaug[par][:, i4 * 4:i4 * 4 + 4, 1, 64:128], in_=pv[:, :, 64:128]),
                                  r=[psn[pbk]], w=["Vaug%d" % par])

                            for t in range(4 + 1):
                                if 0 <= t - 1 < 4:
                                    V2(t - 1)
                                if t < 4:
                                    V1(t)
                                yield

                        def core_units(hp):
                            par = hp % 2
                            qn_, kn_, gn_, vn_ = "qT%d" % par, "kT%d" % par, "gaT%d" % par, "Vaug%d" % par
                            blocks = []
                            for h2 in range(2):
                                for Qg in range(4):
                                    nkb = 4 * Qg + 4
                                    for kb in range(nkb):
                                        blocks.append((h2, Qg, kb, nkb))
                            N = len(blocks)
                            NP = N // 2
                            for pi in range(NP + 1):
                                if pi < NP:
                                    qk_ops = []
                                    for idx in (2 * pi, 2 * pi + 1):
                                        h2, Qg, kb, nkb = blocks[idx]
                                        pb = 64 * h2
                                        sbk = SCB[idx % 4]
                                        off = Qg * 512 - kb * 128 + 384
                                        c0 = max(0, kb - 4 * Qg) * 128
                                        qk_ops.append((lambda e, kb=kb, Qg=Qg, sbk=sbk, pb=pb, c0=c0: e.matmul(
                                            PS[sbk][:, c0:512], lhsT=kT[par][pb:pb + 64, kb * 128:(kb + 1) * 128],
                                            rhs=qT[par][pb:pb + 64, Qg * 512 + c0:(Qg + 1) * 512], start=True, stop=False),
                                            [kn_, qn_], [psn[sbk]]))
                                        qk_ops.append((lambda e, sbk=sbk, off=off, c0=c0: e.matmul(
                                            PS[sbk][:, c0:512], lhsT=(identr[:].bitcast(F32R) if EXACT_MASK else identr[:]),
                                            rhs=(MT[:, off + c0:off + 512].bitcast(F32R) if EXACT_MASK else MT[:, off + c0:off + 512]),
                                            start=False, stop=True), ["identr", "MT"], [psn[sbk]]))
                                    S.group("pe", qk_ops)
                                if pi >= 1:
                                    pv_ops = []
                                    for j in (2 * pi - 2, 2 * pi - 1):
                                        h2, Qg, kb, nkb = blocks[j]
                                        eb = j % NE
                                        c0 = max(0, kb - 4 * Qg) * 128
                                        pv_ops.append((lambda e, kb=kb, h2=h2, eb=eb, nkb=nkb, c0=c0: e.matmul(
                                            PS[3][:, c0:512], lhsT=Vaug[par][:, kb, h2, :], rhs=Eb[eb][:, c0:512],
                                            start=(kb == 0), stop=(kb == nkb - 1)), [vn_, "Eb%d" % eb], [psn[3]]))
                                    pv_grp = pv_ops
                                else:
                                    pv_grp = None
                                if pi < NP:
                                    for idx in (2 * pi, 2 * pi + 1):
                                        h2, Qg, kb, nkb = blocks[idx]
                                        c0 = max(0, kb - 4 * Qg) * 128
                                        sbk = SCB[idx % 4]
                                        eb = idx % NE
                                        A(lambda e, sbk=sbk, eb=eb, c0=c0: e.activation(out=Eb[eb][:, c0:512], in_=PS[sbk][:, c0:512],
                                                                                        func=AF.Exp, scale=1.0 / ALPHA),
                                          r=[psn[sbk]], w=["Eb%d" % eb])
                                if pv_grp is not None:
                                    S.group("pe", pv_grp)
                                if pi >= 1:
                                    h2, Qg, kb, nkb = blocks[2 * pi - 1]
                                    ob = 3
                                    if kb == nkb - 1:
                                        fi = (h2 * 4 + Qg) % 2
                                        rd_, rdn = rd[fi], "rd%d" % fi
                                        ot_, otn = ot[fi], "ot%d" % fi
                                        oc_, ocn = oc[fi], "oc%d" % fi
                                        dlo, olo = (64, 0) if h2 == 0 else (0, 64)
                                        V(lambda e, oc_=oc_: e.tensor_copy(out=oc_[:], in_=PS[ob][:]), r=[psn[ob]], w=[ocn])
                                        A(lambda e, dlo=dlo, olo=olo, rd_=rd_, oc_=oc_: e.activation(
                                            out=rd_[olo:olo + 64, :], in_=oc_[dlo:dlo + 64, :], func=AF.Ln), r=[ocn], w=[rdn])
                                        A(lambda e, olo=olo, rd_=rd_: e.activation(out=rd_[olo:olo + 64, :], in_=rd_[olo:olo + 64, :],
                                                                                   func=AF.Exp, scale=-1.0), r=[rdn], w=[rdn])
                                        V(lambda e, olo=olo, rd_=rd_, ot_=ot_, oc_=oc_: e.tensor_tensor(
                                            out=ot_[olo:olo + 64, :], in0=oc_[olo:olo + 64, :], in1=rd_[olo:olo + 64, :],
                                            op=ALU.mult), r=[ocn, rdn], w=[otn])
                                        G(lambda e, olo=olo, Qg=Qg, ot_=ot_: e.tensor_tensor(
                                            out=attnT[olo:olo + 64, hp, Qg * 512:(Qg + 1) * 512], in0=ot_[olo:olo + 64, :],
                                            in1=gaT[par][olo:olo + 64, Qg * 512:(Qg + 1) * 512], op=ALU.mult),
                                          r=[otn, gn_], w=["attnT"])
                                yield

                        for _ in prep_units(0):
                            pass
                        for hp in range(4):
                            nxt = prep_units(hp + 1) if hp < 3 else None
                            for si, _ in enumerate(core_units(hp)):
                                if nxt is not None and si % 2 == 1:
                                    try:
                                        next(nxt)
                                    except StopIteration:
                                        nxt = None
                            if nxt is not None:
                                for _ in nxt:
                                    pass
                        S.barrier()
                    ck("B")
                    with ExitStack() as cs_:
                        sbC0 = mk(cs_)
                        ygT = sbC0("ygT", [128, 4, SEQ], BF16)
                        c12 = ExitStack()
                        sbC = mk(c12)
                        Ug = sbC("Ug", [128, 32, 256], BF16)
                        Sb = sbC("Sb", [128, 16, 2, 16, 16])
                        sc_scope = ExitStack()
                        scA = [[mk(sc_scope)("scA%d_%d" % (c, i), [128, (12, 4)[c], 2, 16]) for i in range(2)] for c in range(2)]
                        u_scope = ExitStack()
                        sbU = mk(u_scope)
                        U = sbU("U", [128, 4, 8, 8, 16])
                        wst[0] = sbU("wstaC", [128, 8, 128])
                        wu_all = sbU("wu_all", [128, 8, 512], BF16)
                        for cb in range(4):
                            c0w = 2048 + cb * 128
                            D(lambda e, c0w=c0w: e.dma_start(out=wst[0][:], in_=win_d[:, c0w:c0w + 128].rearrange("(k p) c -> p k c", p=128)),
                              w=["wsta0"])
                            G(lambda e, cb=cb: e.tensor_tensor(out=wu_all[:, :, cb * 128:(cb + 1) * 128], in0=wst[0][:],
                                                               in1=mixg[:].to_broadcast([128, 8, 128]), op=ALU.mult),
                              r=["wsta0", "mixg"], w=["wu_all"])
                        for ct in range(2):
                            for sp_ in range(8):
                                bank = sp_ % 2
                                for kc in range(8):
                                    T(lambda e, kc=kc, sp_=sp_, bank=bank: e.matmul(
                                        PS[bank][:], lhsT=xT[:, kc, ct * 1024 + sp_:(ct + 1) * 1024:8], rhs=wu_all[:, kc, :],
                                        start=(kc == 0), stop=(kc == 7)), r=["wu_all", "xT"], w=[psn[bank]])
                                fev = (lambda e, sp_=sp_, bank=bank: e.copy(
                                    out=U[:, :, :, sp_, :], in_=PS[bank][:].rearrange("p (b g c) -> p b g c", b=4, g=8)))
                                A(fev, r=[psn[bank]], w=["U"])
                            for g4 in range(8):
                                bank = 2 + g4 % 2
                                for gi in range(4):
                                    g = g4 * 4 + gi
                                    T(lambda e, g=g, gi=gi, bank=bank: e.transpose(
                                        out=PS[bank][:, gi * 128:(gi + 1) * 128],
                                        in_=U[:, g // 8, g % 8, :, :].rearrange("p s c -> p (s c)"), identity=ident[:]),
                                      r=["U", "ident"], w=[psn[bank]])
                                V(lambda e, g4=g4, bank=bank: e.tensor_copy(out=Ug[:, g4 * 4:g4 * 4 + 4, ct * 128:(ct + 1) * 128],
                                                                            in_=PS[bank][:].rearrange("p (g k) -> p g k", g=4)),
                                  r=[psn[bank]], w=["Ug"])
                        for gp in range(16):
                            bank = 4 + gp % 2
                            for j in range(2):
                                g = 2 * gp + j
                                T(lambda e, g=g, j=j, bank=bank: e.matmul(PS[bank][64 * j:64 * j + 64, 0:256], lhsT=PGre[:, g, :],
                                                                          rhs=Ug[:, g, :], start=True, stop=True),
                                  r=["PGre", "Ug"], w=[psn[bank]])
                                T(lambda e, g=g, j=j, bank=bank: e.matmul(PS[bank][64 * j:64 * j + 64, 256:512], lhsT=PGim[:, g, :],
                                                                          rhs=Ug[:, g, :], start=True, stop=True),
                                  r=["PGim", "Ug"], w=[psn[bank]])
                            A(lambda e, gp=gp, bank=bank: e.copy(out=Sb[:, gp, :, :, :],
                                                                 in_=PS[bank][:].rearrange("p (c K j) -> p c j K", c=2, K=16)),
                              r=[psn[bank]], w=["Sb%d" % (0 if gp < 12 else 1)])
                        S.barrier()
                        u_scope.close()
                        ck("C1")
                        SPLIT = ((0, 12, "dve"), (12, 16, "pool"))
                        for ci, (g0, g1, eng) in enumerate(SPLIT):
                            ng = g1 - g0
                            ta, tb_ = scA[ci]
                            na, nb = "scA%d_0" % ci, "scA%d_1" % ci
                            rn = "Sb%d" % ci
                            def colv(j, k0, nk, rev=False, g0=g0, g1=g1):
                                cs_ = slice(None, None, -1) if rev else slice(None)
                                return Sb[:, g0:g1, cs_, j, k0:k0 + nk]
                            for j in range(1, 16):
                                prev, prev_sw, cur = colv(j - 1, 0, 16), colv(j - 1, 0, 16, True), colv(j, 0, 16)
                                ar = APR[:, 0, g0:g1, :].unsqueeze(3).to_broadcast([128, ng, 2, 16])
                                ai = API[:, 0, g0:g1, :].unsqueeze(3).to_broadcast([128, ng, 2, 16])
                                S.op(eng, lambda e, prev=prev, ar=ar: e.tensor_tensor(out=ta[:], in0=prev, in1=ar, op=ALU.mult),
                                     reads=[rn, "APR"], writes=[na])
                                S.op(eng, lambda e, prev_sw=prev_sw, ai=ai: e.tensor_tensor(out=tb_[:], in0=prev_sw, in1=ai, op=ALU.mult),
                                     reads=[rn, "API"], writes=[nb])
                                S.op(eng, lambda e: e.tensor_tensor(out=ta[:], in0=ta[:], in1=tb_[:], op=ALU.add), reads=[na, nb], writes=[na])
                                S.op(eng, lambda e, cur=cur: e.tensor_tensor(out=cur, in0=cur, in1=ta[:], op=ALU.add), reads=[na, rn], writes=[rn])
                            for K in range(1, 16):
                                prev, prev_sw, cur = colv(15, K - 1, 1), colv(15, K - 1, 1, True), colv(15, K, 1)
                                ar = APR[:, 15, g0:g1, :].unsqueeze(3)
                                ai = API[:, 15, g0:g1, :].unsqueeze(3)
                                S.op(eng, lambda e, prev=prev, ar=ar: e.tensor_tensor(out=ta[:, :, :, 0:1], in0=prev, in1=ar, op=ALU.mult),
                                     reads=[rn, "APR"], writes=[na])
                                S.op(eng, lambda e, prev_sw=prev_sw, ai=ai: e.tensor_tensor(out=tb_[:, :, :, 0:1], in0=prev_sw, in1=ai,
                                                                                         op=ALU.mult), reads=[rn, "API"], writes=[nb])
                                S.op(eng, lambda e: e.tensor_tensor(out=ta[:, :, :, 0:1], in0=ta[:, :, :, 0:1], in1=tb_[:, :, :, 0:1],
                                                                    op=ALU.add), reads=[na, nb], writes=[na])
                                S.op(eng, lambda e, cur=cur: e.tensor_tensor(out=cur, in0=cur, in1=ta[:, :, :, 0:1], op=ALU.add),
                                     reads=[na, rn], writes=[rn])
                            for j in range(15):
                                prev, prev_sw, cur = colv(15, 0, 15), colv(15, 0, 15, True), colv(j, 1, 15)
                                ar = APR[:, j, g0:g1, :].unsqueeze(3).to_broadcast([128, ng, 2, 15])
                                ai = API[:, j, g0:g1, :].unsqueeze(3).to_broadcast([128, ng, 2, 15])
                                S.op(eng, lambda e, prev=prev, ar=ar: e.tensor_tensor(out=ta[:, :, :, 0:15], in0=prev, in1=ar, op=ALU.mult),
                                     reads=[rn, "APR"], writes=[na])
                                S.op(eng, lambda e, prev_sw=prev_sw, ai=ai: e.tensor_tensor(out=tb_[:, :, :, 0:15], in0=prev_sw, in1=ai,
                                                                                         op=ALU.mult), reads=[rn, "API"], writes=[nb])
                                S.op(eng, lambda e: e.tensor_tensor(out=ta[:, :, :, 0:15], in0=ta[:, :, :, 0:15], in1=tb_[:, :, :, 0:15],
                                                                    op=ALU.add), reads=[na, nb], writes=[na])
                                S.op(eng, lambda e, cur=cur: e.tensor_tensor(out=cur, in0=cur, in1=ta[:, :, :, 0:15], op=ALU.add),
                                     reads=[na, rn], writes=[rn])
                        S.barrier()
                        sc_scope.close()
                        Sh = sbC("Sh", [128, 16, 2, 256], BF16)
                        for ci, (g0, g1, eng) in enumerate(SPLIT):
                            rn = "Sb%d" % ci
                            S.op(eng, lambda e, g0=g0, g1=g1: e.memset(Sh[:, g0:g1, :, 0:1], 0.0), writes=["Sh%d" % ci])
                            for c in range(2):
                                S.op(eng, lambda e, g0=g0, g1=g1, c=c: e.tensor_copy(
                                    out=Sh[:, g0:g1, c, 1:241].rearrange("p g (K j) -> p g K j", j=16),
                                    in_=Sb[:, g0:g1, c, :, 0:15].rearrange("p g j K -> p g K j")), reads=[rn], writes=["Sh%d" % ci])
                                S.op(eng, lambda e, g0=g0, g1=g1, c=c: e.tensor_copy(
                                    out=Sh[:, g0:g1, c, 241:256], in_=Sb[:, g0:g1, c, 0:15, 15]), reads=[rn], writes=["Sh%d" % ci])
                        ck("C2")
                        with ExitStack() as c3:
                            sb3 = mk(c3)
                            Ysbs = [sb3("Ysb%d" % i, [128, 2, 128]) for i in range(2)]
                            YG = sb3("YG", [128, 8, 512])
                            def mm3(ct, gp):
                                bank = gp % 2
                                for j in range(2):
                                    g = 2 * gp + j
                                    pb = 64 * j
                                    o_ = PS[bank][:, j * 128:(j + 1) * 128]
                                    T(lambda e, g=g, o_=o_: e.matmul(o_, lhsT=Tm[:, g, :], rhs=Ug[:, g, ct * 128:(ct + 1) * 128],
                                                                     start=True, stop=False), r=["Tm", "Ug"], w=[psn[bank]])
                                    T(lambda e, pb=pb, o_=o_: e.matmul(
                                        o_, lhsT=PCre[pb:pb + 64, gp, :], rhs=Sh[pb:pb + 64, gp, 0, ct * 128:(ct + 1) * 128],
                                        start=False, stop=False), r=["PCre", "Sh0", "Sh1"], w=[psn[bank]])
                                    T(lambda e, pb=pb, o_=o_: e.matmul(
                                        o_, lhsT=PCim[pb:pb + 64, gp, :], rhs=Sh[pb:pb + 64, gp, 1, ct * 128:(ct + 1) * 128],
                                        start=False, stop=True), r=["PCim", "Sh0", "Sh1"], w=[psn[bank]])

                            def ev3(ct, gp):
                                bank = gp % 2
                                Ysb, Ysn = Ysbs[gp % 2], "Ysb%d" % (gp % 2)
                                A(lambda e: e.copy(out=Ysb[:], in_=PS[bank][:, 0:256].rearrange("p (j k) -> p j k", j=2)),
                                  r=[psn[bank]], w=[Ysn])
                                tb = 2 + gp % 2
                                for j in range(2):
                                    T(lambda e, j=j: e.transpose(out=PS[tb][:, j * 128:(j + 1) * 128], in_=Ysb[:, j, :],
                                                                 identity=ident[:]), r=[Ysn, "ident"], w=[psn[tb]])
                                for j in range(2):
                                    g = 2 * gp + j
                                    A(lambda e, j=j, g=g: e.activation(
                                        out=YG[:, :, g * 16:(g + 1) * 16],
                                        in_=PS[tb][:, j * 128:(j + 1) * 128].rearrange("p (t c) -> p t c", t=8),
                                        func=AF.Gelu), r=[psn[tb]], w=["YG"])

                            for ct in range(2):
                                mm3(ct, 0)
                                for gp in range(16):
                                    if gp + 1 < 16:
                                        mm3(ct, gp + 1)
                                    ev3(ct, gp)
                                for tp in range(8):
                                    bank = 4 + tp % 2
                                    for cbi in range(4):
                                        T(lambda e, tp=tp, cbi=cbi, bank=bank: e.transpose(
                                            out=PS[bank][:, cbi * 128:(cbi + 1) * 128], in_=YG[:, tp, cbi * 128:(cbi + 1) * 128],
                                            identity=ident[:]), r=["YG", "ident"], w=[psn[bank]])
                                    col = (ct * 8 + tp) * 128
                                    V(lambda e, bank=bank, col=col: e.tensor_copy(out=ygT[:, :, col:col + 128],
                                                                                  in_=PS[bank][:].rearrange("p (a t) -> p a t", a=4)),
                                      r=[psn[bank]], w=["ygT"])
                            S.barrier()
                        S.barrier()
                        c12.close()
                        ck("C3")
                        with ExitStack() as c5:
                            sb5 = mk(c5)
                            wgl = sb5("wgl", [128, 4, 512], BF16)
                            wst[0] = sb5("wsta5", [128, 8, 128])
                            wbf[0] = sb5("wbf5a", [128, 8, 128], BF16)
                            wbf[1] = sb5("wbf5b", [128, 8, 128], BF16)
                            wgs_st = sb5("wgs_st", [128, 512])
                            sig = [sb5("sig%d" % i, [128, 512]) for i in range(2)]
                            gsl = [sb5("gsl%d" % i, [128, 512], BF16) for i in range(2)]
                            gth = [sb5("gth%d" % i, [128, 512]) for i in range(2)]
                            tm5 = [sb5("tm5%d" % i, [128, 512]) for i in range(2)]
                            for kc in range(4):
                                D(lambda e, kc=kc: e.dma_start(out=wgs_st[:], in_=wglu_d[kc * 128:(kc + 1) * 128, :]), w=["wgs_st"])
                                V(lambda e, kc=kc: e.tensor_copy(out=wgl[:, kc, :], in_=wgs_st[:]), r=["wgs_st"], w=["wgl"])
                            steps = [(cb, tg) for cb in range(4) for tg in range(4)]
                            wcur = {}

                            def pvw(ap, half):
                                return ap.rearrange("p (t q) -> p t q", t=8)[:, :, half * 64:(half + 1) * 64]

                            def mm5(si):
                                cb, tg = steps[si]
                                if tg == 0:
                                    wcur[cb] = load_w(2560 + cb * 128, cb % 2)
                                wgs, wgsn = wcur[cb]
                                ct, half = tg // 2, tg % 2
                                b0, b1 = 2 * (si % 4), 2 * (si % 4) + 1
                                for kc in range(4):
                                    T(lambda e, kc=kc: e.matmul(PS[b0][:], lhsT=wgl[:, kc, cb * 128:(cb + 1) * 128],
                                                                rhs=pvw(ygT[:, kc, ct * 1024:(ct + 1) * 1024], half),
                                                                start=(kc == 0), stop=(kc == 3)), r=["wgl", "ygT"], w=[psn[b0]])
                                for kc in range(8):
                                    T(lambda e, kc=kc: e.matmul(PS[b1][:], lhsT=wgs[:, kc, :], rhs=xT[:, kc, tg * 512:(tg + 1) * 512],
                                                                start=(kc == 0), stop=(kc == 7)), r=[wgsn, "xT"], w=[psn[b1]])

                            def ev5(si):
                                cb, tg = steps[si]
                                ct, half = tg // 2, tg % 2
                                i2 = si % 2
                                b0, b1 = 2 * (si % 4), 2 * (si % 4) + 1
                                sg, sgn = sig[i2], "sig%d" % i2
                                gh, ghn = gth[i2], "gth%d" % i2
                                gl, gln = gsl[i2], "gsl%d" % i2
                                t5, t5n = tm5[i2], "tm5%d" % i2
                                A(lambda e: e.activation(out=sg[:], in_=PS[b0][:], func=AF.Sigmoid, bias=bglu[:, cb, :]),
                                  r=[psn[b0], "bglu"], w=[sgn])
                                A(lambda e: e.activation(out=gh[:], in_=PS[b1][:], func=AF.Sigmoid), r=[psn[b1]], w=[ghn])
                                V(lambda e: e.tensor_tensor(out=gl[:].rearrange("p (t q) -> p q t", t=8),
                                                            in0=PS[b1][:].rearrange("p (q t) -> p q t", t=8),
                                                            in1=gh[:].rearrange("p (q t) -> p q t", t=8), op=ALU.mult),
                                  r=[ghn, psn[b1]], w=[gln])
                                G(lambda e: e.tensor_tensor(out=t5[:].rearrange("p (t q) -> p t q", t=8),
                                                            in0=pvw(ygT[:, cb, ct * 1024:(ct + 1) * 1024], half),
                                                            in1=sg[:].rearrange("p (t q) -> p t q", t=8), op=ALU.mult),
                                  r=["ygT", sgn], w=[t5n])
                                G(lambda e: e.tensor_tensor(out=pvw(ssmT[:, cb, ct * 1024:(ct + 1) * 1024], half),
                                                            in0=t5[:].rearrange("p (t q) -> p t q", t=8),
                                                            in1=gl[:].rearrange("p (t q) -> p t q", t=8), op=ALU.mult),
                                  r=[t5n, gln], w=["ssmT"])

                            mm5(0)
                            mm5(1)
                            mm5(2)
                            for si in range(16):
                                if si + 3 < 16:
                                    mm5(si + 3)
                                ev5(si)
                            if debug and s == 0:
                                with ExitStack() as dd:
                                    d32 = mk(dd)("d32y", [128, 4, SEQ])
                                    for nm, src in (("attnT", attnT), ("ssmT", ssmT), ("ygT", ygT)):
                                        S.barrier()
                                        V(lambda e, src=src: e.tensor_copy(out=d32[:], in_=src[:]), w=["d32y"])
                                        D(lambda e, nm=nm: e.dma_start(out=dbg[nm], in_=d32[:]), r=["d32y"])
                                    S.barrier()
                            S.barrier()
                    S.barrier()

                ck("C5")
                S.barrier()
                with ExitStack() as ds:
                    sbD = mk(ds)
                    Wout = sbD("Wout", [128, 8, DM], BF16)
                    Wg = sbD("Wg", [128, 8, DM], BF16)
                    Wp = sbD("Wp", [128, 2, DM], BF16)
                    wstd = [sbD("wstd%d" % i, [128, DM]) for i in range(2)]
                    cnt = 0
                    for (src, dst, dn_, nk, fold) in ((wout_d, Wout, "Wout", 8, None), (wg_d, Wg, "Wg", 8, pleg), (wp_d, Wp, "Wp", 2, None)):
                        for kc in range(nk):
                            st = wstd[cnt % 2]
                            rn = "wstd%d" % (cnt % 2)
                            D(lambda e, st=st, src=src, kc=kc: e.dma_start(out=st[:], in_=src[kc * 128:(kc + 1) * 128, :]), w=[rn])
                            if fold is None:
                                if cnt % 2 == 0:
                                    V(lambda e, st=st, dst=dst, kc=kc: e.tensor_copy(out=dst[:, kc, :], in_=st[:]), r=[rn], w=[dn_])
                                else:
                                    A(lambda e, st=st, dst=dst, kc=kc: e.copy(out=dst[:, kc, :], in_=st[:]), r=[rn], w=[dn_])
                            else:
                                V(lambda e, st=st, dst=dst, kc=kc, fold=fold: e.tensor_scalar(
                                    out=dst[:, kc, :], in0=st[:], scalar1=fold[:, kc, :], scalar2=None, op0=ALU.mult),
                                  r=[rn, "pleg"], w=[dn_])
                            cnt += 1
                    xd = [sbD("xd%d" % i, [128, DM]) for i in range(3)]
                    pd = [sbD("pd%d" % i, [128, 256]) for i in range(3)]
                    hh = [sbD("hh%d" % i, [128, DM]) for i in range(2)]
                    sq = sbD("sqd", [128, DM])
                    st4 = [sbD("st4d%d" % i, [128, 4]) for i in range(2)]
                    hT = [sbD("hT%d" % i, [128, 8, 128], BF16) for i in range(2)]
                    pT = [sbD("pT%d" % i, [128, 2, 128], BF16) for i in range(2)]
                    gate = [sbD("gate%d" % i, [128, DM]) for i in range(2)]
                    oo = [sbD("oo%d" % i, [128, DM]) for i in range(2)]

                    def XL(it):
                        ct, tp = it // 8, it % 8
                        xi, xn = xd[it % 3], "xd%d" % (it % 3)
                        pi_, pn = pd[it % 3], "pd%d" % (it % 3)
                        r0 = ct * 1024 + tp
                        D(lambda e: e.dma_start(out=xi[:], in_=x_d[s, r0:(ct + 1) * 1024:8, :]), w=[xn])
                        D(lambda e: e.dma_start(out=pi_[:], in_=p_d[s, r0:(ct + 1) * 1024:8, :]), w=[pn])

                    def X1(it):
                        ct, tp = it // 8, it % 8
                        b2 = it % 2
                        xi, xn = xd[it % 3], "xd%d" % (it % 3)
                        h_, hn = hh[b2], "hh%d" % b2
                        s_, sn_ = st4[b2], "st4d%d" % b2
                        r0 = ct * 1024 + tp
                        col = it * 128
                        for nh in range(2):
                            for kc in range(8):
                                if kc < 4:
                                    lt = attnT[:, kc, r0:(ct + 1) * 1024:8]
                                    rn = "attnT"
                                else:
                                    lt = ssmT[:, kc - 4, col:col + 128]
                                    rn = "ssmT"
                                T(lambda e, lt=lt, kc=kc, nh=nh: e.matmul(PS[nh][:], lhsT=lt, rhs=Wout[:, kc, nh * 512:(nh + 1) * 512],
                                                                          start=(kc == 0), stop=(kc == 7)), r=[rn, "Wout"], w=[psn[nh]])
                            V(lambda e, nh=nh: e.tensor_tensor(out=h_[:, nh * 512:(nh + 1) * 512], in0=PS[nh][:],
                                                               in1=xi[:, nh * 512:(nh + 1) * 512], op=ALU.add),
                              r=[psn[nh], xn], w=[hn])
                        A(lambda e: e.activation(out=sq[:], in_=h_[:], func=AF.Square), r=[hn], w=["sqd"])
                        V(lambda e: e.tensor_reduce(out=s_[:, 0:1], in_=sq[:], axis=AX.X, op=ALU.add), r=["sqd"], w=[sn_])
                        V(lambda e: e.tensor_scalar(out=s_[:, 1:2], in0=s_[:, 0:1], scalar1=1.0 / DM, scalar2=EPS, op0=ALU.mult,
                                                    op1=ALU.add), r=[sn_], w=[sn_])
                        G(lambda e: e.tensor_tensor(out=s_[:, 3:4], in0=s_[:, 1:2], in1=cneg[:, 0:1], op=ALU.pow),
                          r=[sn_, "cneg"], w=[sn_])

                    def X2(it):
                        b2 = it % 2
                        pi_, pn = pd[it % 3], "pd%d" % (it % 3)
                        h_, hn = hh[b2], "hh%d" % b2
                        hT_, hTn = hT[b2], "hT%d" % b2
                        pT_, pTn = pT[b2], "pT%d" % b2
                        for half in range(2):
                            for q4 in range(4):
                                kc = half * 4 + q4
                                T(lambda e, kc=kc, q4=q4, half=half: e.transpose(out=PS[2 + half][:, q4 * 128:(q4 + 1) * 128],
                                                                                 in_=h_[:, kc * 128:(kc + 1) * 128], identity=ident[:]),
                                  r=[hn, "ident"], w=[psn[2 + half]])
                            A(lambda e, half=half: e.copy(out=hT_[:, half * 4:half * 4 + 4, :],
                                                          in_=PS[2 + half][:].rearrange("p (k t) -> p k t", k=4)),
                              r=[psn[2 + half]], w=[hTn])
                        for q2 in range(2):
                            T(lambda e, q2=q2: e.transpose(out=PS[4][:, q2 * 128:(q2 + 1) * 128], in_=pi_[:, q2 * 128:(q2 + 1) * 128],
                                                           identity=ident[:]), r=[pn, "ident"], w=[psn[4]])
                        V(lambda e: e.tensor_copy(out=pT_[:], in_=PS[4][:, 0:256].rearrange("p (k t) -> p k t", k=2)),
                          r=[psn[4]], w=[pTn])

                    def Y(it):
                        ct, tp = it // 8, it % 8
                        b2 = it % 2
                        h_, hn = hh[b2], "hh%d" % b2
                        s_, sn_ = st4[b2], "st4d%d" % b2
                        hT_, hTn = hT[b2], "hT%d" % b2
                        pT_, pTn = pT[b2], "pT%d" % b2
                        g_, gn = gate[b2], "gate%d" % b2
                        oi, on = oo[b2], "oo%d" % b2
                        r0 = ct * 1024 + tp
                        for nh in range(2):
                            for kc in range(8):
                                T(lambda e, kc=kc, nh=nh: e.matmul(PS[5][:], lhsT=hT_[:, kc, :], rhs=Wg[:, kc, nh * 512:(nh + 1) * 512],
                                                                   start=(kc == 0), stop=(kc == 7)), r=[hTn, "Wg"], w=[psn[5]])
                            A(lambda e, nh=nh: e.activation(out=g_[:, nh * 512:(nh + 1) * 512], in_=PS[5][:], func=AF.Sigmoid,
                                                            scale=s_[:, 3:4]), r=[psn[5], sn_], w=[gn])
                            for kc in range(2):
                                T(lambda e, kc=kc, nh=nh: e.matmul(PS[6 + nh][:], lhsT=pT_[:, kc, :], rhs=Wp[:, kc, nh * 512:(nh + 1) * 512],
                                                                   start=(kc == 0), stop=(kc == 1)), r=[pTn, "Wp"], w=[psn[6 + nh]])
                            V(lambda e, nh=nh: e.tensor_tensor(out=oi[:, nh * 512:(nh + 1) * 512], in0=PS[6 + nh][:],
                                                               in1=g_[:, nh * 512:(nh + 1) * 512], op=ALU.mult),
                              r=[psn[6 + nh], gn], w=[on])
                        G(lambda e: e.tensor_tensor(out=oi[:], in0=oi[:], in1=h_[:], op=ALU.add), r=[on, hn], w=[on])
                        D(lambda e: e.dma_start(out=out_d[s, r0:(ct + 1) * 1024:8, :], in_=oi[:]), r=[on])

                    XL(0)
                    XL(1)
                    X1(0)
                    X2(0)
                    for it in range(16):
                        if it + 2 < 16:
                            XL(it + 2)
                        if it + 1 < 16:
                            X1(it + 1)
                        Y(it)
                        if it + 1 < 16:
                            X2(it + 1)
                    S.barrier()
                    ck("D")
        except _Stop:
            pass
        S.barrier()
    return nc


_NC_CACHE = {}


def kernel(**inputs):
    f = lambda a: np.ascontiguousarray(np.asarray(a, dtype=np.float32))
    x = f(inputs["x"])
    p = f(inputs["p"])[0]
    shared = {}
    for k in ("mix_norm", "w_in", "q_norm", "k_norm", "lambda_re", "lambda_im", "log_dt", "b_re", "b_im", "c_re", "c_im",
              "d_skip", "w_glu", "b_glu", "w_out", "ple_norm", "w_ple_gate", "w_ple_proj"):
        shared[k] = f(inputs[k])[0]
    if "nc" not in _NC_CACHE:
        _NC_CACHE["nc"] = build_nc()
    nc = _NC_CACHE["nc"]
    in_maps = []
    for c in range(NCORES):
        m = {"x": np.ascontiguousarray(x[2 * c:2 * c + 2]), "p": np.ascontiguousarray(p[2 * c:2 * c + 2])}
        m.update(shared)
        in_maps.append(m)
    res = run_bass_kernel_spmd(nc, in_maps, core_ids=list(range(NCORES)))
    out = np.concatenate([np.asarray(r["out"], dtype=np.float32) for r in res.results], axis=0)
    return out
```

```python
import math
from contextlib import ExitStack
import numpy as np
import concourse.bass as bass
import concourse.mybir as mybir
from concourse.bass_utils import run_bass_kernel_spmd

F32 = mybir.dt.float32
BF16 = mybir.dt.bfloat16
F32R = mybir.dt.float32r
I32 = mybir.dt.int32
ALU = mybir.AluOpType
AF = mybir.ActivationFunctionType
AX = mybir.AxisListType

NCORES = 8
SEQ = 2048
DM = 1024
EPS = 1e-6
MAGIC = 12582912.0
TWO_PI = 2.0 * math.pi
EXACT_MASK = False
ALPHA = 1.0 if EXACT_MASK else 38.2305


class Sched:
    ENGS = ("pe", "act", "dve", "pool", "sp")

    def __init__(self, nc, n_dma_sems=24):
        self.nc = nc
        self.cnt = {e: 0 for e in self.ENGS}
        self.sem = {}
        self.seen = {e: {} for e in self.ENGS}
        self.last_w = {}
        self.readers = {}
        self.dma_sems = []
        self.dma_uses = []
        self.n_dma_sems = n_dma_sems
        self.dma_rr = 0

    def alloc(self, stack):
        for e in self.ENGS:
            if e == "sp":
                continue
            self.sem[e] = stack.enter_context(self.nc.semaphore("s_" + e))
        for i in range(self.n_dma_sems):
            self.dma_sems.append(stack.enter_context(self.nc.semaphore("d%d" % i)))
            self.dma_uses.append(0)

    def _need(self, eng, tok, waits):
        key, sem, val, src = tok
        if self.seen[eng].get(key, 0) >= val:
            return
        self.seen[eng][key] = val
        waits.append((sem, val))

    dead = False

    def op(self, eng, fn, reads=(), writes=(), dma=False):
        if self.dead:
            return None
        waits = []
        for r in reads:
            t = self.last_w.get(r)
            if t is not None:
                self._need(eng, t, waits)
        for w in writes:
            t = self.last_w.get(w)
            if t is not None and (dma or t[3] != eng or eng != "pe"):
                self._need(eng, t, waits)
            for t in self.readers.get(w, {}).values():
                if dma or t[3] != eng or eng != "pe":
                    self._need(eng, t, waits)
        if dma:
            j = self.dma_rr
            self.dma_rr = (self.dma_rr + 1) % self.n_dma_sems
            k = self.dma_uses[j]
            if k > 0:
                self._need(eng, ("d%d" % j, self.dma_sems[j], 16 * k, None), waits)
            self.dma_uses[j] = k + 1
            tok = ("d%d" % j, self.dma_sems[j], 16 * (k + 1), None)
            inc = (self.dma_sems[j], 16)
        else:
            self.cnt[eng] += 1
            tok = (eng, self.sem[eng], self.cnt[eng], eng)
            inc = (self.sem[eng], 1)
        for w in writes:
            self.last_w[w] = tok
            self.readers[w] = {}
        for r in reads:
            self.readers.setdefault(r, {})[tok[0]] = tok
        self._emit(eng, waits, fn, inc)
        return tok

    def group(self, eng, ops):
        if self.dead:
            return
        waits = []
        for fn, reads, writes in ops:
            for r in reads:
                t = self.last_w.get(r)
                if t is not None:
                    self._need(eng, t, waits)
            for w in writes:
                t = self.last_w.get(w)
                if t is not None and (t[3] != eng or eng != "pe"):
                    self._need(eng, t, waits)
                for t in self.readers.get(w, {}).values():
                    if t[3] != eng or eng != "pe":
                        self._need(eng, t, waits)
        first = True
        for fn, reads, writes in ops:
            self.cnt[eng] += 1
            tok = (eng, self.sem[eng], self.cnt[eng], eng)
            for w in writes:
                self.last_w[w] = tok
                self.readers[w] = {}
            for r in reads:
                self.readers.setdefault(r, {})[tok[0]] = tok
            self._emit(eng, waits if first else [], fn, (self.sem[eng], 1))
            first = False

    def _eng(self, name):
        nc = self.nc
        return {"pe": nc.tensor, "act": nc.scalar, "dve": nc.vector, "pool": nc.gpsimd, "sp": nc.sync}[name]

    def _emit(self, eng, waits, fn, inc):
        e = self._eng(eng)
        for sem, val in waits:
            e.wait_ge(sem, val)
        if fn is not None:
            ins = fn(e)
            ins.then_inc(inc[0], inc[1])

    def barrier(self):
        if self.dead:
            return
        toks = []
        for e in self.ENGS:
            if e != "sp" and self.cnt[e] > 0:
                toks.append((e, self.sem[e], self.cnt[e], e))
        for j in range(self.n_dma_sems):
            if self.dma_uses[j] > 0:
                toks.append(("d%d" % j, self.dma_sems[j], 16 * self.dma_uses[j], None))
        for e in self.ENGS:
            waits = []
            for t in toks:
                self._need(e, t, waits)
            if waits:
                self._emit(e, waits, None, None)


class _Stop(Exception):
    pass


def build_nc(debug=False, stop_after=None):
    nc = bass.Bass("TRN2", target_bir_lowering=False)

    def din(name, shape):
        return nc.dram_tensor(name, shape, F32, kind="ExternalInput").ap()

    x_d = din("x", [2, SEQ, DM])
    p_d = din("p", [2, SEQ, 256])
    mixn_d = din("mix_norm", [DM])
    win_d = din("w_in", [DM, 3072])
    qn_d = din("q_norm", [64])
    kn_d = din("k_norm", [64])
    lre_d = din("lambda_re", [32, 64])
    lim_d = din("lambda_im", [32, 64])
    ldt_d = din("log_dt", [32])
    bre_d = din("b_re", [32, 64, 16])
    bim_d = din("b_im", [32, 64, 16])
    cre_d = din("c_re", [32, 16, 64])
    cim_d = din("c_im", [32, 16, 64])
    dsk_d = din("d_skip", [512])
    wglu_d = din("w_glu", [512, 512])
    bglu_d = din("b_glu", [512])
    wout_d = din("w_out", [DM, DM])
    plen_d = din("ple_norm", [DM])
    wg_d = din("w_ple_gate", [DM, DM])
    wp_d = din("w_ple_proj", [256, DM])
    out_d = nc.dram_tensor("out", [2, SEQ, DM], F32, kind="ExternalOutput").ap()
    dbg = {}
    if debug:
        dbg["xT"] = nc.dram_tensor("dbg_xT", [128, 8, SEQ], F32, kind="ExternalOutput").ap()
        dbg["attnT"] = nc.dram_tensor("dbg_attnT", [128, 4, SEQ], F32, kind="ExternalOutput").ap()
        dbg["ssmT"] = nc.dram_tensor("dbg_ssmT", [128, 4, SEQ], F32, kind="ExternalOutput").ap()
        dbg["ygT"] = nc.dram_tensor("dbg_ygT", [128, 4, SEQ], F32, kind="ExternalOutput").ap()
        dbg["qT"] = nc.dram_tensor("dbg_qT", [128, SEQ], F32, kind="ExternalOutput").ap()
        dbg["T"] = nc.dram_tensor("dbg_T", [128, 32, 128], F32, kind="ExternalOutput").ap()
        dbg["S"] = nc.dram_tensor("dbg_S", [128, 16, 2, 257], F32, kind="ExternalOutput").ap()

    with ExitStack() as top:
        S = Sched(nc)
        S.alloc(top)
        top.enter_context(nc.allow_non_contiguous_dma(reason="small strided parameter loads"))

        uid = [0]

        def mk(stack):
            def sb(name, shape, dt=F32):
                uid[0] += 1
                return stack.enter_context(nc.sbuf_tensor("%s_u%d" % (name, uid[0]), shape, dt))
            return sb

        sbP = mk(top)
        PS = [top.enter_context(nc.psum_tensor("ps%d" % i, [128, 512], F32)) for i in range(8)]
        psn = ["ps%d" % i for i in range(8)]

        def V(fn, r=(), w=()):
            S.op("dve", fn, reads=r, writes=w)

        def A(fn, r=(), w=()):
            S.op("act", fn, reads=r, writes=w)

        def G(fn, r=(), w=()):
            S.op("pool", fn, reads=r, writes=w)

        def T(fn, r=(), w=()):
            S.op("pe", fn, reads=r, writes=w)

        def D(fn, r=(), w=()):
            S.op("sp", fn, reads=r, writes=w, dma=True)

        dq = [0]

        def D2(fn, r=(), w=()):
            dq[0] += 1
            S.op("sp" if dq[0] % 2 else "act", fn, reads=r, writes=w, dma=True)

        def ck(name):
            if stop_after == name:
                S.barrier()
                S.dead = True

        try:
            ident = sbP("ident", [128, 128])
            ones_f = sbP("ones_f", [128, 128])
            identr = sbP("identr", [128, 128], F32 if EXACT_MASK else BF16)
            cneg = sbP("cneg", [128, 2])
            blk1 = sbP("blk1", [128, 128], BF16)
            MT = sbP("MT", [128, 2432], F32 if EXACT_MASK else BF16)
            mixg = sbP("mixg", [128, 8, 1])
            pleg = sbP("pleg", [128, 8, 1])
            qg = sbP("qg", [128, 1])
            kg = sbP("kg", [128, 1])
            bglu = sbP("bglu", [128, 4, 1])
            Tm = sbP("Tm", [128, 32, 128], BF16)
            PGre = sbP("PGre", [128, 32, 64], BF16)
            PGim = sbP("PGim", [128, 32, 64], BF16)
            PCre = sbP("PCre", [128, 16, 128], BF16)
            PCim = sbP("PCim", [128, 16, 128], BF16)
            APR = sbP("APR", [128, 16, 16, 2])
            API = sbP("API", [128, 16, 16, 2])
            attnT = sbP("attnT", [128, 4, SEQ], BF16)
            ssmT = sbP("ssmT", [128, 4, SEQ], BF16)

            G(lambda e: e.memset(ident[:], 1.0), w=["ident"])
            G(lambda e: e.affine_select(out=ident[:], in_=ident[:], pattern=[[-1, 128]], base=0, channel_multiplier=1,
                                        compare_op=ALU.is_equal, fill=0.0), r=["ident"], w=["ident"])
            V(lambda e: e.memset(ones_f[:], 1.0), w=["ones_f"])
            V(lambda e: e.tensor_copy(out=(identr[:].bitcast(F32R) if EXACT_MASK else identr[:]), in_=ident[:]), r=["ident"], w=["identr"])
            V(lambda e: e.memset(cneg[:, 0:1], -0.5), w=["cneg"])
            V(lambda e: e.memset(cneg[:, 1:2], -1.0), w=["cneg"])
            V(lambda e: e.memset(blk1[:], 0.0), w=["blk1"])
            V(lambda e: e.memset(blk1[0:64, 0:64], 1.0), w=["blk1"])
            V(lambda e: e.memset(blk1[64:128, 64:128], 1.0), w=["blk1"])
            ck("consts")

            with ExitStack() as su:
                sb = mk(su)
                NI = 2432
                sb_keep = sb
                msk_scope = ExitStack()
                sb = mk(msk_scope)
                di = sb("di", [128, NI], I32)
                df = sb("df", [128, NI])
                t1 = sb("mt1", [128, NI])
                t2 = sb("mt2", [128, NI])
                acc = sb("macc", [128, NI])
                G(lambda e: e.iota(di[:], pattern=[[1, NI]], base=-384, channel_multiplier=-1), w=["di"])
                V(lambda e: e.tensor_copy(out=df[:], in_=di[:]), r=["di"], w=["df"])
                V(lambda e: e.tensor_scalar(out=acc[:], in0=df[:], scalar1=128.0, scalar2=None, op0=ALU.is_le), r=["df"], w=["acc"])
                for (mod, lim) in ((4.0, 512.0), (16.0, None)):
                    V(lambda e, mod=mod: e.tensor_scalar(out=t1[:], in0=df[:], scalar1=1.0 / mod, scalar2=MAGIC, op0=ALU.mult,
                                                         op1=ALU.add), r=["df"], w=["t1"])
                    V(lambda e, mod=mod: e.tensor_scalar(out=t1[:], in0=t1[:], scalar1=MAGIC, scalar2=mod, op0=ALU.subtract,
                                                         op1=ALU.mult), r=["t1"], w=["t1"])
                    V(lambda e: e.tensor_tensor(out=t1[:], in0=t1[:], in1=df[:], op=ALU.is_equal), r=["t1", "df"], w=["t1"])
                    if lim is not None:
                        V(lambda e, lim=lim: e.tensor_scalar(out=t2[:], in0=df[:], scalar1=lim, scalar2=None, op0=ALU.is_le),
                          r=["df"], w=["t2"])
                        V(lambda e: e.tensor_tensor(out=t1[:], in0=t1[:], in1=t2[:], op=ALU.mult), r=["t1", "t2"], w=["t1"])
                    V(lambda e: e.tensor_tensor(out=acc[:], in0=acc[:], in1=t1[:], op=ALU.add), r=["acc", "t1"], w=["acc"])
                V(lambda e: e.tensor_scalar(out=t2[:], in0=df[:], scalar1=0.0, scalar2=None, op0=ALU.is_ge), r=["df"], w=["t2"])
                V(lambda e: e.tensor_tensor(out=acc[:], in0=acc[:], in1=t2[:], op=ALU.mult), r=["acc", "t2"], w=["acc"])
                if EXACT_MASK:
                    V(lambda e: e.tensor_scalar(out=t1[:], in0=acc[:], scalar1=1.0, scalar2=None, op0=ALU.max), r=["acc"], w=["t1"])
                    A(lambda e: e.activation(out=t1[:], in_=t1[:], func=AF.Ln), r=["t1"], w=["t1"])
                    V(lambda e: e.tensor_scalar(out=t2[:], in0=acc[:], scalar1=0.0, scalar2=-30000.0, op0=ALU.is_equal, op1=ALU.mult),
                      r=["acc"], w=["t2"])
                    V(lambda e: e.tensor_tensor(out=MT[:].bitcast(F32R), in0=t1[:], in1=t2[:], op=ALU.add), r=["t1", "t2"], w=["MT"])
                else:
                    V(lambda e: e.tensor_scalar(out=t1[:], in0=acc[:], scalar1=2.0, scalar2=26.5, op0=ALU.is_equal, op1=ALU.mult),
                      r=["acc"], w=["t1"])
                    V(lambda e: e.tensor_scalar(out=t2[:], in0=acc[:], scalar1=3.0, scalar2=42.0, op0=ALU.is_equal, op1=ALU.mult),
                      r=["acc"], w=["t2"])
                    V(lambda e: e.tensor_tensor(out=t1[:], in0=t1[:], in1=t2[:], op=ALU.add), r=["t1", "t2"], w=["t1"])
                    V(lambda e: e.tensor_scalar(out=t2[:], in0=acc[:], scalar1=0.0, scalar2=-1.0e6, op0=ALU.is_equal, op1=ALU.mult),
                      r=["acc"], w=["t2"])
                    V(lambda e: e.tensor_tensor(out=MT[:], in0=t1[:], in1=t2[:], op=ALU.add), r=["t1", "t2"], w=["MT"])
                S.barrier()
                msk_scope.close()
                sb = sb_keep
                ck("mask")

                LR = sb("LR", [128, 16, 1])
                LI = sb("LI", [128, 16, 1])
                DT = sb("DT", [128, 16, 1])
                dtrow = sb("dtrow", [1, 32])
                D(lambda e: e.dma_start(out=dtrow[:], in_=ldt_d.rearrange("(o g) -> o g", o=1)), w=["dtrow"])
                T(lambda e: e.matmul(PS[5][:, 0:32], lhsT=ones_f[0:1, :], rhs=dtrow[:], start=True, stop=True),
                  r=["ones_f", "dtrow"], w=[psn[5]])
                for j in range(2):
                    V(lambda e, j=j: e.tensor_copy(out=DT[64 * j:64 * j + 64, :, 0], in_=PS[5][64 * j:64 * j + 64, j:32:2]),
                      r=[psn[5]], w=["DT"])
                Ln_ = [sb("Lnat%d" % i, [32, 2, 64]) for i in range(2)]
                for li, (src, dst, dname) in enumerate(((lre_d, LR, "LR"), (lim_d, LI, "LI"))):
                    D(lambda e, li=li, src=src: e.dma_start(out=Ln_[li][:, 0, :], in_=src), w=["Lnat%d" % li])
                    G(lambda e, li=li: e.tensor_copy(out=Ln_[li][:, 1, :], in_=Ln_[li][:, 0, :]), r=["Lnat%d" % li], w=["Lnat%d" % li])
                    bank = 6 + li
                    T(lambda e, li=li, bank=bank: e.transpose(out=PS[bank][:, 0:32], in_=Ln_[li][:].rearrange("p d n -> p (d n)"),
                                                              identity=ident[0:32, 0:32]), r=["Lnat%d" % li, "ident"], w=[psn[bank]])
                    for j in range(2):
                        V(lambda e, dst=dst, j=j, bank=bank: e.tensor_copy(
                            out=dst[64 * j:64 * j + 64, :, 0], in_=PS[bank][64 * j:64 * j + 64, j:32:2]), r=[psn[bank]], w=[dname])
                BR = sb("BR", [128, 16, 16])
                BI = sb("BI", [128, 16, 16])
                for j in range(2):
                    D2(lambda e, j=j: e.dma_start(out=BR[64 * j:64 * j + 64], in_=bre_d.rearrange("(gp j) n c -> j n gp c", j=2)[j]), w=["BR"])
                    D2(lambda e, j=j: e.dma_start(out=BI[64 * j:64 * j + 64], in_=bim_d.rearrange("(gp j) n c -> j n gp c", j=2)[j]), w=["BI"])
                Cn = [sb("Cn%d" % i, [128, 4, 2, 64]) for i in range(2)]
                for ci, src in enumerate((cre_d, cim_d)):
                    D(lambda e, ci=ci, src=src: e.dma_start(out=Cn[ci][:, :, 0, :],
                                                            in_=src.rearrange("(t g) c n -> (g c) t n", t=4)), w=["Cn%d" % ci])
                    G(lambda e, ci=ci: e.tensor_copy(out=Cn[ci][:, :, 1, :], in_=Cn[ci][:, :, 0, :]), r=["Cn%d" % ci], w=["Cn%d" % ci])
                gnat = sb("gnat", [20, 128])
                D(lambda e: e.dma_start(out=gnat[0:8, :], in_=mixn_d.rearrange("(k p) -> k p", p=128)), w=["gnat"])
                D(lambda e: e.dma_start(out=gnat[8:16, :], in_=plen_d.rearrange("(k p) -> k p", p=128)), w=["gnat"])
                D(lambda e: e.dma_start(out=gnat[16:20, :], in_=bglu_d.rearrange("(k p) -> k p", p=128)), w=["gnat"])
                T(lambda e: e.transpose(out=PS[4][:, 0:20], in_=gnat[:], identity=ident[0:20, 0:20]), r=["gnat", "ident"], w=[psn[4]])
                V(lambda e: e.tensor_copy(out=mixg[:, :, 0], in_=PS[4][:, 0:8]), r=[psn[4]], w=["mixg"])
                V(lambda e: e.tensor_copy(out=pleg[:, :, 0], in_=PS[4][:, 8:16]), r=[psn[4]], w=["pleg"])
                V(lambda e: e.tensor_copy(out=bglu[:, :, 0], in_=PS[4][:, 16:20]), r=[psn[4]], w=["bglu"])
                qkrow = sb("qkrow", [2, 2, 64])
                for hh in range(2):
                    D(lambda e, hh=hh: e.dma_start(out=qkrow[0:1, hh, :], in_=qn_d.rearrange("(o n) -> o n", o=1)), w=["qkrow"])
                    D(lambda e, hh=hh: e.dma_start(out=qkrow[1:2, hh, :], in_=kn_d.rearrange("(o n) -> o n", o=1)), w=["qkrow"])
                T(lambda e: e.transpose(out=PS[4][:, 32:34], in_=qkrow[:].rearrange("r h n -> r (h n)"), identity=ident[0:2, 0:2]),
                  r=["qkrow", "ident"], w=[psn[4]])
                V(lambda e: e.tensor_scalar(out=qg[:], in0=PS[4][:, 32:33], scalar1=0.125 * ALPHA, scalar2=None, op0=ALU.mult),
                  r=[psn[4]], w=["qg"])
                V(lambda e: e.tensor_copy(out=kg[:], in_=PS[4][:, 33:34]), r=[psn[4]], w=["kg"])
                CR = sb("CR", [128, 16, 16])
                CI = sb("CI", [128, 16, 16])
                for ci, (dst, dname) in enumerate(((CR, "CR"), (CI, "CI"))):
                    for t in range(4):
                        bank = (ci * 4 + t) % 8
                        T(lambda e, ci=ci, t=t, bank=bank: e.transpose(
                            out=PS[bank][:, 0:128], in_=Cn[ci][:, t, :, :].rearrange("p d n -> p (d n)"), identity=ident[:]),
                          r=["Cn%d" % ci, "ident"], w=[psn[bank]])
                        for j in range(2):
                            V(lambda e, dst=dst, t=t, j=j, bank=bank: e.tensor_copy(
                                out=dst[64 * j:64 * j + 64, 4 * t:4 * t + 4, :],
                                in_=PS[bank][64 * j:64 * j + 64, 0:128].rearrange("p (g q c) -> p g q c", q=2, c=16)[:, :, j, :]),
                              r=[psn[bank]], w=[dname])
                dcol = sb("dcol", [128, 32, 1])
                dnat = sb("dnat", [32, 16])
                d16 = sb("d16", [16, 32])
                Rrep = sb("Rrep", [16, 128])
                D(lambda e: e.dma_start(out=dnat[:], in_=dsk_d.rearrange("(g c) -> g c", c=16)), w=["dnat"])
                T(lambda e: e.transpose(out=PS[4][0:16, 64:96], in_=dnat[:], identity=ident[0:32, 0:32]), r=["dnat", "ident"], w=[psn[4]])
                V(lambda e: e.tensor_copy(out=d16[:], in_=PS[4][0:16, 64:96]), r=[psn[4]], w=["d16"])
                for sp_ in range(8):
                    V(lambda e, sp_=sp_: e.tensor_copy(out=Rrep[:, 16 * sp_:16 * sp_ + 16], in_=ident[0:16, 0:16]), r=["ident"], w=["Rrep"])
                T(lambda e: e.matmul(PS[4][:, 128:160], lhsT=Rrep[:], rhs=d16[:], start=True, stop=True), r=["Rrep", "d16"], w=[psn[4]])
                V(lambda e: e.tensor_copy(out=dcol[:, :, 0], in_=PS[4][:, 128:160]), r=[psn[4]], w=["dcol"])

                def sm(name):
                    return sb("s_" + name, [128, 16, 1])

                dt_ = sm("dt_"); th = sm("th"); lrdt = sm("lrdt"); mag = sm("mag"); magi = sm("magi")
                rr = sm("rr"); sn = sm("sn"); cs = sm("cs"); tA = sm("tA"); tB = sm("tB")
                a_re = sm("a_re"); a_im = sm("a_im"); i_re = sm("i_re"); i_im = sm("i_im")
                c_re = sm("c_re"); c_im = sm("c_im"); den = sm("den")
                A(lambda e: e.activation(out=dt_[:], in_=DT[:], func=AF.Exp), r=["DT"], w=["dt_"])
                V(lambda e: e.tensor_tensor(out=th[:], in0=LI[:], in1=dt_[:], op=ALU.mult), r=["LI", "dt_"], w=["th"])
                V(lambda e: e.tensor_tensor(out=lrdt[:], in0=LR[:], in1=dt_[:], op=ALU.mult), r=["LR", "dt_"], w=["lrdt"])
                A(lambda e: e.activation(out=mag[:], in_=lrdt[:], func=AF.Exp), r=["lrdt"], w=["mag"])
                A(lambda e: e.activation(out=magi[:], in_=lrdt[:], func=AF.Exp, scale=-1.0), r=["lrdt"], w=["magi"])

                def sin_of(dst, shift, dname):
                    V(lambda e: e.tensor_scalar(out=tA[:], in0=th[:], scalar1=shift, scalar2=1.0 / TWO_PI, op0=ALU.add, op1=ALU.mult),
                      r=["th"], w=["tA"])
                    V(lambda e: e.tensor_scalar(out=tA[:], in0=tA[:], scalar1=MAGIC, scalar2=MAGIC, op0=ALU.add, op1=ALU.subtract),
                      r=["tA"], w=["tA"])
                    V(lambda e: e.tensor_scalar(out=tB[:], in0=th[:], scalar1=shift, scalar2=None, op0=ALU.add), r=["th"], w=["tB"])
                    V(lambda e: e.scalar_tensor_tensor(out=rr[:], in0=tA[:], scalar=-TWO_PI, in1=tB[:], op0=ALU.mult, op1=ALU.add),
                      r=["tA", "tB"], w=["rr"])
                    V(lambda e: e.tensor_scalar(out=rr[:], in0=rr[:], scalar1=math.pi, scalar2=-math.pi, op0=ALU.min, op1=ALU.max),
                      r=["rr"], w=["rr"])
                    A(lambda e: e.activation(out=dst[:], in_=rr[:], func=AF.Sin), r=["rr"], w=[dname])

                sin_of(sn, 0.0, "sn")
                sin_of(cs, math.pi / 2.0, "cs")
                V(lambda e: e.tensor_tensor(out=a_re[:], in0=mag[:], in1=cs[:], op=ALU.mult), r=["mag", "cs"], w=["a_re"])
                V(lambda e: e.tensor_tensor(out=a_im[:], in0=mag[:], in1=sn[:], op=ALU.mult), r=["mag", "sn"], w=["a_im"])
                V(lambda e: e.tensor_tensor(out=i_re[:], in0=magi[:], in1=cs[:], op=ALU.mult), r=["magi", "cs"], w=["i_re"])
                V(lambda e: e.scalar_tensor_tensor(out=i_im[:], in0=magi[:], scalar=-1.0, in1=sn[:], op0=ALU.mult, op1=ALU.mult),
                  r=["magi", "sn"], w=["i_im"])
                V(lambda e: e.tensor_tensor(out=den[:], in0=LR[:], in1=LR[:], op=ALU.mult), r=["LR"], w=["den"])
                V(lambda e: e.tensor_tensor(out=tA[:], in0=LI[:], in1=LI[:], op=ALU.mult), r=["LI"], w=["tA"])
                V(lambda e: e.tensor_tensor(out=den[:], in0=den[:], in1=tA[:], op=ALU.add), r=["den", "tA"], w=["den"])
                V(lambda e: e.reciprocal(out=den[:], in_=den[:]), r=["den"], w=["den"])
                V(lambda e: e.tensor_scalar(out=tB[:], in0=a_re[:], scalar1=-1.0, scalar2=None, op0=ALU.add), r=["a_re"], w=["tB"])
                V(lambda e: e.tensor_tensor(out=c_re[:], in0=tB[:], in1=LR[:], op=ALU.mult), r=["tB", "LR"], w=["c_re"])
                V(lambda e: e.tensor_tensor(out=tA[:], in0=a_im[:], in1=LI[:], op=ALU.mult), r=["a_im", "LI"], w=["tA"])
                V(lambda e: e.tensor_tensor(out=c_re[:], in0=c_re[:], in1=tA[:], op=ALU.add), r=["c_re", "tA"], w=["c_re"])
                V(lambda e: e.tensor_tensor(out=c_re[:], in0=c_re[:], in1=den[:], op=ALU.mult), r=["c_re", "den"], w=["c_re"])
                V(lambda e: e.tensor_tensor(out=c_im[:], in0=a_im[:], in1=LR[:], op=ALU.mult), r=["a_im", "LR"], w=["c_im"])
                V(lambda e: e.tensor_tensor(out=tA[:], in0=tB[:], in1=LI[:], op=ALU.mult), r=["tB", "LI"], w=["tA"])
                V(lambda e: e.tensor_tensor(out=c_im[:], in0=c_im[:], in1=tA[:], op=ALU.subtract), r=["c_im", "tA"], w=["c_im"])
                V(lambda e: e.tensor_tensor(out=c_im[:], in0=c_im[:], in1=den[:], op=ALU.mult), r=["c_im", "den"], w=["c_im"])

                tb1 = sb("tb1", [128, 16, 16])
                tb2 = sb("tb2", [128, 16, 16])

                lit = {a_re.name: "a_re", a_im.name: "a_im", i_re.name: "i_re", i_im.name: "i_im",
                       c_re.name: "c_re", c_im.name: "c_im"}

                def nm(t):
                    return lit.get(t.name, t.name)

                tmpE = {"dve": (tb1, tb2, "tb1", "tb2", tA, tB, "tA", "tB"),
                        "pool": (sb("tb1p", [128, 16, 16]), sb("tb2p", [128, 16, 16]), "tb1p", "tb2p",
                                 sm("tAp"), sm("tBp"), "tAp", "tBp")}

                def cmul_b(eng, o_re, o_im, on, x_re, x_im, xn, s_re, s_im, neg_im=False):
                    u1, u2, n1, n2 = tmpE[eng][0:4]
                    sr = s_re[:].to_broadcast([128, 16, 16])
                    si = s_im[:].to_broadcast([128, 16, 16])
                    O = lambda fn, r, w: S.op(eng, fn, reads=r, writes=w)
                    O(lambda e: e.tensor_tensor(out=u1[:], in0=x_re, in1=sr, op=ALU.mult), xn + [nm(s_re)], [n1])
                    O(lambda e: e.tensor_tensor(out=u2[:], in0=x_im, in1=si, op=ALU.mult), xn + [nm(s_im)], [n2])
                    O(lambda e: e.tensor_tensor(out=o_re, in0=u1[:], in1=u2[:], op=ALU.subtract), [n1, n2], [on])
                    O(lambda e: e.tensor_tensor(out=u1[:], in0=x_re, in1=si, op=ALU.mult), xn + [nm(s_im)], [n1])
                    O(lambda e: e.tensor_tensor(out=u2[:], in0=x_im, in1=sr, op=ALU.mult), xn + [nm(s_re)], [n2])
                    if neg_im:
                        O(lambda e: e.tensor_tensor(out=u1[:], in0=u1[:], in1=u2[:], op=ALU.add), [n1, n2], [n1])
                        O(lambda e: e.tensor_scalar(out=o_im, in0=u1[:], scalar1=-1.0, scalar2=None, op0=ALU.mult), [n1], [on])
                    else:
                        O(lambda e: e.tensor_tensor(out=o_im, in0=u1[:], in1=u2[:], op=ALU.add), [n1, n2], [on])

                def cmul_s(eng, o_re, o_im, x_re, x_im, s_re, s_im):
                    u1, u2, n1, n2 = tmpE[eng][4:8]
                    O = lambda fn, r, w: S.op(eng, fn, reads=r, writes=w)
                    O(lambda e: e.tensor_tensor(out=u1[:], in0=x_re[:], in1=s_re[:], op=ALU.mult), [nm(x_re), nm(s_re)], [n1])
                    O(lambda e: e.tensor_tensor(out=u2[:], in0=x_im[:], in1=s_im[:], op=ALU.mult), [nm(x_im), nm(s_im)], [n2])
                    O(lambda e: e.tensor_tensor(out=o_re[:], in0=u1[:], in1=u2[:], op=ALU.subtract), [n1, n2], [nm(o_re)])
                    O(lambda e: e.tensor_tensor(out=u1[:], in0=x_re[:], in1=s_im[:], op=ALU.mult), [nm(x_re), nm(s_im)], [n1])
                    O(lambda e: e.tensor_tensor(out=u2[:], in0=x_im[:], in1=s_re[:], op=ALU.mult), [nm(x_im), nm(s_re)], [n2])
                    O(lambda e: e.tensor_tensor(out=o_im[:], in0=u1[:], in1=u2[:], op=ALU.add), [n1, n2], [nm(o_im)])

                BBr = sb("BBr", [128, 16, 16])
                BBi = sb("BBi", [128, 16, 16])
                cmul_b("dve", BBr[:], BBi[:], "BB", BR[:], BI[:], ["BR", "BI"], c_re, c_im)
                pw_re = [sm("pwr%d" % i) for i in range(9)]
                pw_im = [sm("pwi%d" % i) for i in range(9)]
                iw_re = [sm("iwr%d" % i) for i in range(8)]
                iw_im = [sm("iwi%d" % i) for i in range(8)]
                CPr = sb("CPr", [128, 16, 9, 16])
                CPi = sb("CPi", [128, 16, 9, 16])
                for (eng, lst_re, lst_im) in (("dve", pw_re, pw_im), ("pool", iw_re, iw_im)):
                    S.op(eng, lambda e, t=lst_re[0]: e.memset(t[:], 1.0), writes=[lst_re[0].name])
                    S.op(eng, lambda e, t=lst_im[0]: e.memset(t[:], 0.0), writes=[lst_im[0].name])
                for i in range(1, 8):
                    cmul_s("pool", iw_re[i], iw_im[i], iw_re[i - 1], iw_im[i - 1], i_re, i_im)
                for tau in range(9):
                    if tau + 1 < 9:
                        cmul_s("dve", pw_re[tau + 1], pw_im[tau + 1], pw_re[tau], pw_im[tau], a_re, a_im)
                    cmul_b("dve", CPr[:, :, tau, :], CPi[:, :, tau, :], "CP%d" % tau, CR[:], CI[:], ["CR", "CI"],
                           pw_re[tau], pw_im[tau], neg_im=True)
                Qr = sb("Qr", [128, 16, 8, 16])
                Qi = sb("Qi", [128, 16, 8, 16])
                Hr = sb("Hr", [128, 16, 8, 16])
                Hi = sb("Hi", [128, 16, 8, 16])
                for s_ in range(8):
                    cmul_b("pool", Qr[:, :, s_, :], Qi[:, :, s_, :], "Q%d" % s_, BBr[:], BBi[:], ["BB"], iw_re[s_], iw_im[s_])
                    cmul_b("dve" if s_ < 3 else "pool", Hr[:, :, s_, :], Hi[:, :, s_, :], "H%d" % s_, BBr[:], BBi[:], ["BB"],
                           pw_re[7 - s_], pw_im[7 - s_])
                CPn = ["CP%d" % t for t in range(9)]
                Qn = ["Q%d" % t for t in range(8)]
                Hn = ["H%d" % t for t in range(8)]
                V(lambda e: e.tensor_copy(out=PCre[:].rearrange("p g (t c) -> p g t c", c=16), in_=CPr[:, :, 1:9, :]), r=CPn, w=["PCre"])
                V(lambda e: e.tensor_copy(out=PCim[:].rearrange("p g (t c) -> p g t c", c=16), in_=CPi[:, :, 1:9, :]), r=CPn, w=["PCim"])
                qw_re = [pw_re[8]] + [sm("qwr%d" % i) for i in range(1, 16)]
                qw_im = [pw_im[8]] + [sm("qwi%d" % i) for i in range(1, 16)]
                for jj in range(16):
                    if jj > 0:
                        cmul_s("pool", qw_re[jj], qw_im[jj], qw_re[jj - 1], qw_im[jj - 1], pw_re[8], pw_im[8])
                    G(lambda e, jj=jj: e.tensor_copy(out=APR[:, jj, :, 0:1], in_=qw_re[jj][:]), r=[qw_re[jj].name], w=["APR"])
                    G(lambda e, jj=jj: e.tensor_copy(out=APR[:, jj, :, 1:2], in_=qw_re[jj][:]), r=[qw_re[jj].name], w=["APR"])
                    G(lambda e, jj=jj: e.tensor_scalar(out=API[:, jj, :, 0:1], in0=qw_im[jj][:], scalar1=-1.0, scalar2=None,
                                                       op0=ALU.mult), r=[qw_im[jj].name], w=["API"])
                    G(lambda e, jj=jj: e.tensor_copy(out=API[:, jj, :, 1:2], in_=qw_im[jj][:]), r=[qw_im[jj].name], w=["API"])
                for gp in range(16):
                    for ci, (src, dst, dn_) in enumerate(((Hr, PGre, "PGre"), (Hi, PGim, "PGim"))):
                        bank = (2 * gp + ci) % 8
                        T(lambda e, src=src, gp=gp, bank=bank: e.transpose(
                            out=PS[bank][:, 0:128], in_=src[:, gp, :, :].rearrange("p s c -> p (s c)"), identity=ident[:]),
                          r=Hn + ["ident"], w=[psn[bank]])
                        fcp = (lambda e, dst=dst, gp=gp, bank=bank: e.tensor_copy(
                            out=dst[:, 2 * gp:2 * gp + 2, :], in_=PS[bank][:, 0:128].rearrange("p (j n) -> p j n", j=2)))
                        if ci == 0:
                            V(fcp, r=[psn[bank]], w=[dn_])
                        else:
                            A(lambda e, dst=dst, gp=gp, bank=bank: e.copy(
                                out=dst[:, 2 * gp:2 * gp + 2, :], in_=PS[bank][:, 0:128].rearrange("p (j n) -> p j n", j=2)),
                              r=[psn[bank]], w=[dn_])
                tmask = sb("tmask", [128, 128])
                ttmp = sb("ttmp", [128, 128])
                G(lambda e: e.memset(tmask[:], 1.0), w=["tmask"])
                G(lambda e: e.affine_select(out=tmask[:], in_=tmask[:], pattern=[[16, 8], [0, 16]], base=15, channel_multiplier=-1,
                                            compare_op=ALU.is_ge, fill=0.0), r=["tmask"], w=["tmask"])
                for g in range(32):
                    gp, j = g // 2, g % 2
                    bank = g % 8
                    pb = 64 * j
                    T(lambda e, gp=gp, pb=pb, bank=bank: e.matmul(PS[bank][:, 0:128], lhsT=Qr[pb:pb + 64, gp, :, :].rearrange("p s c -> p (s c)"),
                                                                   rhs=CPr[pb:pb + 64, gp, 0:8, :].rearrange("p s c -> p (s c)"), start=True, stop=False),
                      r=Qn + CPn, w=[psn[bank]])
                    T(lambda e, gp=gp, pb=pb, bank=bank: e.matmul(PS[bank][:, 0:128], lhsT=Qi[pb:pb + 64, gp, :, :].rearrange("p s c -> p (s c)"),
                                                                   rhs=CPi[pb:pb + 64, gp, 0:8, :].rearrange("p s c -> p (s c)"), start=False, stop=True),
                      r=Qn + CPn, w=[psn[bank]])
                    V(lambda e, bank=bank: e.tensor_tensor(out=ttmp[:], in0=PS[bank][:, 0:128], in1=tmask[:], op=ALU.mult),
                      r=[psn[bank], "tmask"], w=["ttmp"])
                    V(lambda e, g=g: e.scalar_tensor_tensor(out=Tm[:, g, :], in0=ident[:], scalar=dcol[:, g, :], in1=ttmp[:],
                                                            op0=ALU.mult, op1=ALU.add), r=["ident", "dcol", "ttmp"], w=["Tm"])
                if debug:
                    dT = sb("dT", [128, 32, 128])
                    V(lambda e: e.tensor_copy(out=dT[:], in_=Tm[:]), r=["Tm"], w=["dT"])
                    D(lambda e: e.dma_start(out=dbg["T"], in_=dT[:]), r=["dT"])
                S.barrier()

            for s in range(2):
                S.barrier()
                with ExitStack() as abc:
                    sb = mk(abc)
                    xT = sb("xT", [128, 8, SEQ], BF16)
                    wst = [None]
                    wbf = [None, None]
                    wctr = [0]

                    def load_w(c0, slot):
                        i = 0
                        D(lambda e: e.dma_start(out=wst[i][:], in_=win_d[:, c0:c0 + 128].rearrange("(k p) c -> p k c", p=128)),
                          w=["wsta%d" % i])
                        G(lambda e: e.tensor_tensor(out=wbf[slot][:], in0=wst[i][:], in1=mixg[:].to_broadcast([128, 8, 128]),
                                                    op=ALU.mult), r=["wsta%d" % i, "mixg"], w=["wbf%d" % slot])
                        return wbf[slot], "wbf%d" % slot

                    ck("ssmsetup")
                    with ExitStack() as a0:
                        sb0 = mk(a0)
                        xt = [sb0("xt%d" % i, [128, DM]) for i in range(4)]
                        sqs = [sb0("sq%d" % i, [128, DM]) for i in range(2)]
                        st4s = [sb0("st4%d" % i, [128, 4]) for i in range(2)]
                        dgs = [sb0("dg%d" % i, [128, 128]) for i in range(2)]
                        rbcs = [sb0("rbc%d" % i, [128, 128]) for i in range(2)]
                        def a0_load(i):
                            xi = xt[i % 4]
                            xn = "xt%d" % (i % 4)
                            sq, sqn = sqs[i % 2], "sq%d" % (i % 2)
                            D(lambda e: e.dma_start(out=xi[:], in_=x_d[s, i * 128:(i + 1) * 128, :]), w=[xn])
                            A(lambda e: e.activation(out=sq[:], in_=xi[:], func=AF.Square), r=[xn], w=[sqn])

                        def a0_stats(i):
                            p2 = i % 2
                            xi = xt[i % 4]
                            xn = "xt%d" % (i % 4)
                            sq, sqn = sqs[p2], "sq%d" % p2
                            st4, stn = st4s[p2], "st4%d" % p2
                            dg, dgn = dgs[p2], "dg%d" % p2
                            rbc, rbn = rbcs[p2], "rbc%d" % p2
                            rb = 2 + p2
                            V(lambda e: e.tensor_reduce(out=st4[:, 0:1], in_=sq[:], axis=AX.X, op=ALU.add), r=[sqn], w=[stn])
                            V(lambda e: e.tensor_scalar(out=st4[:, 1:2], in0=st4[:, 0:1], scalar1=1.0 / DM, scalar2=EPS,
                                                        op0=ALU.mult, op1=ALU.add), r=[stn], w=[stn])
                            G(lambda e: e.tensor_tensor(out=st4[:, 3:4], in0=st4[:, 1:2], in1=cneg[:, 0:1], op=ALU.pow),
                              r=[stn, "cneg"], w=[stn])

                        def a0_stats_b(i):
                            p2 = i % 2
                            st4, stn = st4s[p2], "st4%d" % p2
                            xi = xt[i % 4]
                            xn = "xt%d" % (i % 4)
                            A(lambda e: e.activation(out=xi[:], in_=xi[:], func=AF.Copy, scale=st4[:, 3:4]), r=[xn, stn], w=[xn])

                        def a0_xpose(i):
                            p2 = i % 2
                            xi = xt[i % 4]
                            xn = "xt%d" % (i % 4)
                            for half in range(2):
                                tbk = 4 * p2 + half
                                for q4 in range(4):
                                    kc = half * 4 + q4
                                    T(lambda e, kc=kc, q4=q4, tbk=tbk: e.transpose(
                                        out=PS[tbk][:, q4 * 128:(q4 + 1) * 128], in_=xi[:, kc * 128:(kc + 1) * 128], identity=ident[:]),
                                      r=[xn, "ident"], w=[psn[tbk]])
                                if half == 0:
                                    V(lambda e, half=half, tbk=tbk: e.tensor_copy(
                                        out=xT[:, half * 4:half * 4 + 4, i * 128:(i + 1) * 128],
                                        in_=PS[tbk][:].rearrange("p (k t) -> p k t", k=4)), r=[psn[tbk]], w=["xT"])
                                else:
                                    A(lambda e, half=half, tbk=tbk: e.copy(
                                        out=xT[:, half * 4:half * 4 + 4, i * 128:(i + 1) * 128],
                                        in_=PS[tbk][:].rearrange("p (k t) -> p k t", k=4)), r=[psn[tbk]], w=["xT"])

                        a0_load(0)
                        a0_load(1)
                        a0_stats(0)
                        a0_stats_b(0)
                        for i in range(16):
                            if i + 2 < 16:
                                a0_load(i + 2)
                            if i + 1 < 16:
                                a0_stats(i + 1)
                            a0_xpose(i)
                            if i + 1 < 16:
                                a0_stats_b(i + 1)
                        S.barrier()
                    if debug and s == 0:
                        with ExitStack() as dd:
                            d32 = mk(dd)("d32", [128, 8, SEQ])
                            V(lambda e: e.tensor_copy(out=d32[:], in_=xT[:]), r=["xT"], w=["d32"])
                            D(lambda e: e.dma_start(out=dbg["xT"], in_=d32[:]), r=["d32"])
                            S.barrier()

                    ck("A0")
                    with ExitStack() as bs:
                        sbB = mk(bs)
                        qT = [sbB("qT%d" % i, [128, SEQ], BF16) for i in range(2)]
                        kT = [sbB("kT%d" % i, [128, SEQ], BF16) for i in range(2)]
                        gaT = [sbB("gaT%d" % i, [128, SEQ], BF16) for i in range(2)]
                        Vaug = [sbB("Vaug%d" % i, [128, 16, 2, 128], BF16) for i in range(2)]
                        wB = [sbB("wB%d" % i, [128, 8, 128], BF16) for i in range(8)]
                        wst[0] = sbB("wstaB", [128, 8, 128])
                        sqb = [sbB("sqb%d" % i, [128, 512], BF16) for i in range(2)]
                        tt = [sbB("tt%d" % i, [128, 512]) for i in range(2)]
                        NE = 4
                        Eb = [sbB("Eb%d" % i, [128, 512], BF16) for i in range(NE)]
                        rd = [sbB("rd%d" % i, [128, 512]) for i in range(2)]
                        ot = [sbB("ot%d" % i, [128, 512]) for i in range(2)]
                        oc = [sbB("oc%d" % i, [128, 512]) for i in range(2)]
                        SCB = (4, 5, 6, 7)
                        for i in range(2):
                            V(lambda e, i=i: e.memset(Vaug[i][:], 1.0), w=["Vaug%d" % i])

                        def load_wB(c0, slot):
                            i = 0
                            D(lambda e: e.dma_start(out=wst[i][:], in_=win_d[:, c0:c0 + 128].rearrange("(k p) c -> p k c", p=128)),
                              w=["wsta%d" % i])
                            G(lambda e: e.tensor_tensor(out=wB[slot][:], in0=wst[i][:], in1=mixg[:].to_broadcast([128, 8, 128]),
                                                        op=ALU.mult), r=["wsta%d" % i, "mixg"], w=["wB%d" % slot])
                            return wB[slot], "wB%d" % slot

                        def prep_units(hp):
                            par = hp % 2
                            wq, wqn = load_wB(hp * 128, par * 4 + 0)
                            wk, wkn = load_wB(512 + hp * 128, par * 4 + 1)
                            yield
                            wv, wvn = load_wB(1024 + hp * 128, par * 4 + 2)
                            wa, wan = load_wB(1536 + hp * 128, par * 4 + 3)
                            yield
                            blocks = []
                            for (w_, wn_, dstT, dn, gain, gn) in ((wq, wqn, qT[par], "qT%d" % par, qg, "qg"),
                                                                  (wk, wkn, kT[par], "kT%d" % par, kg, "kg")):
                                for tg in range(4):
                                    blocks.append((w_, wn_, dstT, dn, gain, gn, tg))

                            def S1(bi):
                                w_, wn_, dstT, dn, gain, gn, tg = blocks[bi]
                                pbk = bi % 2
                                for kc in range(8):
                                    T(lambda e, kc=kc: e.matmul(PS[pbk][:], lhsT=w_[:, kc, :], rhs=xT[:, kc, tg * 512:(tg + 1) * 512],
                                                                start=(kc == 0), stop=(kc == 7)), r=[wn_, "xT"], w=[psn[pbk]])

                            def S2(bi):
                                pbk = bi % 2
                                sq_, sqn = sqb[bi % 2], "sqb%d" % (bi % 2)
                                t_, tn = tt[bi % 2], "tt%d" % (bi % 2)
                                A(lambda e: e.activation(out=sq_[:], in_=PS[pbk][:], func=AF.Square), r=[psn[pbk]], w=[sqn])
                                T(lambda e: e.matmul(PS[2][:], lhsT=blk1[:], rhs=sq_[:], start=True, stop=True),
                                  r=["blk1", sqn], w=[psn[2]])
                                V(lambda e: e.tensor_scalar(out=t_[:], in0=PS[2][:], scalar1=1.0 / 64.0, scalar2=EPS, op0=ALU.mult,
                                                            op1=ALU.add), r=[psn[2]], w=[tn])

                            def S3(bi):
                                w_, wn_, dstT, dn, gain, gn, tg = blocks[bi]
                                pbk = bi % 2
                                t_, tn = tt[bi % 2], "tt%d" % (bi % 2)
                                A(lambda e: e.activation(out=t_[:], in_=t_[:], func=AF.Ln), r=[tn], w=[tn])
                                A(lambda e: e.activation(out=t_[:], in_=t_[:], func=AF.Exp, scale=-0.5), r=[tn], w=[tn])
                                V(lambda e: e.scalar_tensor_tensor(out=dstT[:, tg * 512:(tg + 1) * 512], in0=PS[pbk][:],
                                                                   scalar=gain[:, 0:1], in1=t_[:], op0=ALU.mult, op1=ALU.mult),
                                  r=[psn[pbk], gn, tn], w=[dn])

                            for t in range(8 + 2):
                                if 0 <= t - 2 < 8:
                                    S3(t - 2)
                                if 0 <= t - 1 < 8:
                                    S2(t - 1)
                                if t < 8:
                                    S1(t)
                                yield

                            def G1(tg):
                                pbk = tg % 2
                                for kc in range(8):
                                    T(lambda e, kc=kc: e.matmul(PS[pbk][:], lhsT=wa[:, kc, :], rhs=xT[:, kc, tg * 512:(tg + 1) * 512],
                                                                start=(kc == 0), stop=(kc == 7)), r=[wan, "xT"], w=[psn[pbk]])

                            def G2(tg):
                                pbk = tg % 2
                                t_, tn = tt[tg % 2], "tt%d" % (tg % 2)
                                A(lambda e: e.activation(out=t_[:], in_=PS[pbk][:], func=AF.Exp, scale=-1.0), r=[psn[pbk]], w=[tn])
                                A(lambda e: e.activation(out=t_[:], in_=t_[:], func=AF.Ln, bias=1.0), r=[tn], w=[tn])
                                A(lambda e: e.activation(out=t_[:], in_=t_[:], func=AF.Exp, scale=-1.0), r=[tn], w=[tn])

                            def G3(tg):
                                pbk = tg % 2
                                t_, tn = tt[tg % 2], "tt%d" % (tg % 2)
                                V(lambda e: e.tensor_tensor(out=gaT[par][:, tg * 512:(tg + 1) * 512], in0=PS[pbk][:], in1=t_[:],
                                                            op=ALU.mult), r=[tn, psn[pbk]], w=["gaT%d" % par])

                            for t in range(4 + 2):
                                if 0 <= t - 2 < 4:
                                    G3(t - 2)
                                if 0 <= t - 1 < 4:
                                    G2(t - 1)
                                if t < 4:
                                    G1(t)
                                yield

                            def V1(i4):
                                pbk = i4 % 2
                                for ii in range(4):
                                    i = i4 * 4 + ii
                                    for kc in range(8):
                                        T(lambda e, kc=kc, i=i, ii=ii: e.matmul(
                                            PS[pbk][:, ii * 128:(ii + 1) * 128], lhsT=xT[:, kc, i * 128:(i + 1) * 128], rhs=wv[:, kc, :],
                                            start=(kc == 0), stop=(kc == 7)), r=[wvn, "xT"], w=[psn[pbk]])

                            def V2(i4):
                                pbk = i4 % 2
                                pv = PS[pbk][:].rearrange("p (i c) -> p i c", i=4)
                                V(lambda e: e.tensor_copy(out=Vaug[par][:, i4 * 4:i4 * 4 + 4, 0, 0:64], in_=pv[:, :, 0:64]),
                                  r=[psn[pbk]], w=["Vaug%d" % par])
                                A(lambda e: e.copy(out=Vaug[par][:, i4 * 4:i4 * 4 + 4, 1, 64:128], in_=pv[:, :, 64:128]),
                                  r=[psn[pbk]], w=["Vaug%d" % par])

                            for t in range(4 + 1):
                                if 0 <= t - 1 < 4:
                                    V2(t - 1)
                                if t < 4:
                                    V1(t)
                                yield

                        def core_units(hp):
                            par = hp % 2
                            qn_, kn_, gn_, vn_ = "qT%d" % par, "kT%d" % par, "gaT%d" % par, "Vaug%d" % par
                            blocks = []
                            for h2 in range(2):
                                for Qg in range(4):
                                    nkb = 4 * Qg + 4
                                    for kb in range(nkb):
                                        blocks.append((h2, Qg, kb, nkb))
                            N = len(blocks)
                            NP = N // 2
                            for pi in range(NP + 1):
                                if pi < NP:
                                    qk_ops = []
                                    for idx in (2 * pi, 2 * pi + 1):
                                        h2, Qg, kb, nkb = blocks[idx]
                                        pb = 64 * h2
                                        sbk = SCB[idx % 4]
                                        off = Qg * 512 - kb * 128 + 384
                                        c0 = max(0, kb - 4 * Qg) * 128
                                        qk_ops.append((lambda e, kb=kb, Qg=Qg, sbk=sbk, pb=pb, c0=c0: e.matmul(
                                            PS[sbk][:, c0:512], lhsT=kT[par][pb:pb + 64, kb * 128:(kb + 1) * 128],
                                            rhs=qT[par][pb:pb + 64, Qg * 512 + c0:(Qg + 1) * 512], start=True, stop=False),
                                            [kn_, qn_], [psn[sbk]]))
                                        qk_ops.append((lambda e, sbk=sbk, off=off, c0=c0: e.matmul(
                                            PS[sbk][:, c0:512], lhsT=(identr[:].bitcast(F32R) if EXACT_MASK else identr[:]),
                                            rhs=(MT[:, off + c0:off + 512].bitcast(F32R) if EXACT_MASK else MT[:, off + c0:off + 512]),
                                            start=False, stop=True), ["identr", "MT"], [psn[sbk]]))
                                    S.group("pe", qk_ops)
                                if pi >= 1:
                                    pv_ops = []
                                    for j in (2 * pi - 2, 2 * pi - 1):
                                        h2, Qg, kb, nkb = blocks[j]
                                        eb = j % NE
                                        c0 = max(0, kb - 4 * Qg) * 128
                                        pv_ops.append((lambda e, kb=kb, h2=h2, eb=eb, nkb=nkb, c0=c0: e.matmul(
                                            PS[3][:, c0:512], lhsT=Vaug[par][:, kb, h2, :], rhs=Eb[eb][:, c0:512],
                                            start=(kb == 0), stop=(kb == nkb - 1)), [vn_, "Eb%d" % eb], [psn[3]]))
                                    pv_grp = pv_ops
                                else:
                                    pv_grp = None
                                if pi < NP:
                                    for idx in (2 * pi, 2 * pi + 1):
                                        h2, Qg, kb, nkb = blocks[idx]
                                        c0 = max(0, kb - 4 * Qg) * 128
                                        sbk = SCB[idx % 4]
                                        eb = idx % NE
                                        A(lambda e, sbk=sbk, eb=eb, c0=c0: e.activation(out=Eb[eb][:, c0:512], in_=PS[sbk][:, c0:512],
                                                                                        func=AF.Exp, scale=1.0 / ALPHA),
                                          r=[psn[sbk]], w=["Eb%d" % eb])
                                if pv_grp is not None:
                                    S.group("pe", pv_grp)
                                if pi >= 1:
                                    h2, Qg, kb, nkb = blocks[2 * pi - 1]
                                    ob = 3
                                    if kb == nkb - 1:
                                        fi = (h2 * 4 + Qg) % 2
                                        rd_, rdn = rd[fi], "rd%d" % fi
                                        ot_, otn = ot[fi], "ot%d" % fi
                                        oc_, ocn = oc[fi], "oc%d" % fi
                                        dlo, olo = (64, 0) if h2 == 0 else (0, 64)
                                        V(lambda e, oc_=oc_: e.tensor_copy(out=oc_[:], in_=PS[ob][:]), r=[psn[ob]], w=[ocn])
                                        A(lambda e, dlo=dlo, olo=olo, rd_=rd_, oc_=oc_: e.activation(
                                            out=rd_[olo:olo + 64, :], in_=oc_[dlo:dlo + 64, :], func=AF.Ln), r=[ocn], w=[rdn])
                                        A(lambda e, olo=olo, rd_=rd_: e.activation(out=rd_[olo:olo + 64, :], in_=rd_[olo:olo + 64, :],
                                                                                   func=AF.Exp, scale=-1.0), r=[rdn], w=[rdn])
                                        V(lambda e, olo=olo, rd_=rd_, ot_=ot_, oc_=oc_: e.tensor_tensor(
                                            out=ot_[olo:olo + 64, :], in0=oc_[olo:olo + 64, :], in1=rd_[olo:olo + 64, :],
                                            op=ALU.mult), r=[ocn, rdn], w=[otn])
                                        G(lambda e, olo=olo, Qg=Qg, ot_=ot_: e.tensor_tensor(
                                            out=attnT[olo:olo + 64, hp, Qg * 512:(Qg + 1) * 512], in0=ot_[olo:olo + 64, :],
                                            in1=gaT[par][olo:olo + 64, Qg * 512:(Qg + 1) * 512], op=ALU.mult),
                                          r=[otn, gn_], w=["attnT"])
                                yield

                        for _ in prep_units(0):
                            pass
                        for hp in range(4):
                            nxt = prep_units(hp + 1) if hp < 3 else None
                            for si, _ in enumerate(core_units(hp)):
                                if nxt is not None and si % 2 == 1:
                                    try:
                                        next(nxt)
                                    except StopIteration:
                                        nxt = None
                            if nxt is not None:
                                for _ in nxt:
                                    pass
                        S.barrier()
                    ck("B")
                    with ExitStack() as cs_:
                        sbC0 = mk(cs_)
                        ygT = sbC0("ygT", [128, 4, SEQ], BF16)
                        c12 = ExitStack()
                        sbC = mk(c12)
                        Ug = sbC("Ug", [128, 32, 256], BF16)
                        Sb = sbC("Sb", [128, 16, 2, 16, 16])
                        sc_scope = ExitStack()
                        scA = [[mk(sc_scope)("scA%d_%d" % (c, i), [128, (13, 3)[c], 2, 16]) for i in range(2)] for c in range(2)]
                        u_scope = ExitStack()
                        sbU = mk(u_scope)
                        U = sbU("U", [128, 4, 8, 8, 16])
                        wst[0] = sbU("wstaC", [128, 8, 128])
                        wu_all = sbU("wu_all", [128, 8, 512], BF16)
                        for cb in range(4):
                            c0w = 2048 + cb * 128
                            D(lambda e, c0w=c0w: e.dma_start(out=wst[0][:], in_=win_d[:, c0w:c0w + 128].rearrange("(k p) c -> p k c", p=128)),
                              w=["wsta0"])
                            G(lambda e, cb=cb: e.tensor_tensor(out=wu_all[:, :, cb * 128:(cb + 1) * 128], in0=wst[0][:],
                                                               in1=mixg[:].to_broadcast([128, 8, 128]), op=ALU.mult),
                              r=["wsta0", "mixg"], w=["wu_all"])
                        for ct in range(2):
                            for sp_ in range(8):
                                bank = sp_ % 2
                                for kc in range(8):
                                    T(lambda e, kc=kc, sp_=sp_, bank=bank: e.matmul(
                                        PS[bank][:], lhsT=xT[:, kc, ct * 1024 + sp_:(ct + 1) * 1024:8], rhs=wu_all[:, kc, :],
                                        start=(kc == 0), stop=(kc == 7)), r=["wu_all", "xT"], w=[psn[bank]])
                                fev = (lambda e, sp_=sp_, bank=bank: e.copy(
                                    out=U[:, :, :, sp_, :], in_=PS[bank][:].rearrange("p (b g c) -> p b g c", b=4, g=8)))
                                A(fev, r=[psn[bank]], w=["U"])
                            for g4 in range(8):
                                bank = 2 + g4 % 2
                                for gi in range(4):
                                    g = g4 * 4 + gi
                                    T(lambda e, g=g, gi=gi, bank=bank: e.transpose(
                                        out=PS[bank][:, gi * 128:(gi + 1) * 128],
                                        in_=U[:, g // 8, g % 8, :, :].rearrange("p s c -> p (s c)"), identity=ident[:]),
                                      r=["U", "ident"], w=[psn[bank]])
                                V(lambda e, g4=g4, bank=bank: e.tensor_copy(out=Ug[:, g4 * 4:g4 * 4 + 4, ct * 128:(ct + 1) * 128],
                                                                            in_=PS[bank][:].rearrange("p (g k) -> p g k", g=4)),
                                  r=[psn[bank]], w=["Ug"])
                        for gp in range(16):
                            bank = 4 + gp % 2
                            for j in range(2):
                                g = 2 * gp + j
                                T(lambda e, g=g, j=j, bank=bank: e.matmul(PS[bank][64 * j:64 * j + 64, 0:256], lhsT=PGre[:, g, :],
                                                                          rhs=Ug[:, g, :], start=True, stop=True),
                                  r=["PGre", "Ug"], w=[psn[bank]])
                                T(lambda e, g=g, j=j, bank=bank: e.matmul(PS[bank][64 * j:64 * j + 64, 256:512], lhsT=PGim[:, g, :],
                                                                          rhs=Ug[:, g, :], start=True, stop=True),
                                  r=["PGim", "Ug"], w=[psn[bank]])
                            A(lambda e, gp=gp, bank=bank: e.copy(out=Sb[:, gp, :, :, :],
                                                                 in_=PS[bank][:].rearrange("p (c K j) -> p c j K", c=2, K=16)),
                              r=[psn[bank]], w=["Sb%d" % (0 if gp < 13 else 1)])
                        S.barrier()
                        u_scope.close()
                        ck("C1")
                        SPLIT = ((0, 13, "dve"), (13, 16, "pool"))
                        for ci, (g0, g1, eng) in enumerate(SPLIT):
                            ng = g1 - g0
                            ta, tb_ = scA[ci]
                            na, nb = "scA%d_0" % ci, "scA%d_1" % ci
                            rn = "Sb%d" % ci
                            def colv(j, k0, nk, rev=False, g0=g0, g1=g1):
                                cs_ = slice(None, None, -1) if rev else slice(None)
                                return Sb[:, g0:g1, cs_, j, k0:k0 + nk]
                            for j in range(1, 16):
                                prev, prev_sw, cur = colv(j - 1, 0, 16), colv(j - 1, 0, 16, True), colv(j, 0, 16)
                                ar = APR[:, 0, g0:g1, :].unsqueeze(3).to_broadcast([128, ng, 2, 16])
                                ai = API[:, 0, g0:g1, :].unsqueeze(3).to_broadcast([128, ng, 2, 16])
                                S.op(eng, lambda e, prev=prev, ar=ar: e.tensor_tensor(out=ta[:], in0=prev, in1=ar, op=ALU.mult),
                                     reads=[rn, "APR"], writes=[na])
                                S.op(eng, lambda e, prev_sw=prev_sw, ai=ai: e.tensor_tensor(out=tb_[:], in0=prev_sw, in1=ai, op=ALU.mult),
                                     reads=[rn, "API"], writes=[nb])
                                S.op(eng, lambda e: e.tensor_tensor(out=ta[:], in0=ta[:], in1=tb_[:], op=ALU.add), reads=[na, nb], writes=[na])
                                S.op(eng, lambda e, cur=cur: e.tensor_tensor(out=cur, in0=cur, in1=ta[:], op=ALU.add), reads=[na, rn], writes=[rn])
                            for K in range(1, 16):
                                prev, prev_sw, cur = colv(15, K - 1, 1), colv(15, K - 1, 1, True), colv(15, K, 1)
                                ar = APR[:, 15, g0:g1, :].unsqueeze(3)
                                ai = API[:, 15, g0:g1, :].unsqueeze(3)
                                S.op(eng, lambda e, prev=prev, ar=ar: e.tensor_tensor(out=ta[:, :, :, 0:1], in0=prev, in1=ar, op=ALU.mult),
                                     reads=[rn, "APR"], writes=[na])
                                S.op(eng, lambda e, prev_sw=prev_sw, ai=ai: e.tensor_tensor(out=tb_[:, :, :, 0:1], in0=prev_sw, in1=ai,
                                                                                         op=ALU.mult), reads=[rn, "API"], writes=[nb])
                                S.op(eng, lambda e: e.tensor_tensor(out=ta[:, :, :, 0:1], in0=ta[:, :, :, 0:1], in1=tb_[:, :, :, 0:1],
                                                                    op=ALU.add), reads=[na, nb], writes=[na])
                                S.op(eng, lambda e, cur=cur: e.tensor_tensor(out=cur, in0=cur, in1=ta[:, :, :, 0:1], op=ALU.add),
                                     reads=[na, rn], writes=[rn])
                            for j in range(15):
                                prev, prev_sw, cur = colv(15, 0, 15), colv(15, 0, 15, True), colv(j, 1, 15)
                                ar = APR[:, j, g0:g1, :].unsqueeze(3).to_broadcast([128, ng, 2, 15])
                                ai = API[:, j, g0:g1, :].unsqueeze(3).to_broadcast([128, ng, 2, 15])
                                S.op(eng, lambda e, prev=prev, ar=ar: e.tensor_tensor(out=ta[:, :, :, 0:15], in0=prev, in1=ar, op=ALU.mult),
                                     reads=[rn, "APR"], writes=[na])
                                S.op(eng, lambda e, prev_sw=prev_sw, ai=ai: e.tensor_tensor(out=tb_[:, :, :, 0:15], in0=prev_sw, in1=ai,
                                                                                         op=ALU.mult), reads=[rn, "API"], writes=[nb])
                                S.op(eng, lambda e: e.tensor_tensor(out=ta[:, :, :, 0:15], in0=ta[:, :, :, 0:15], in1=tb_[:, :, :, 0:15],
                                                                    op=ALU.add), reads=[na, nb], writes=[na])
                                S.op(eng, lambda e, cur=cur: e.tensor_tensor(out=cur, in0=cur, in1=ta[:, :, :, 0:15], op=ALU.add),
                                     reads=[na, rn], writes=[rn])
                        S.barrier()
                        sc_scope.close()
                        Sh = sbC("Sh", [128, 16, 2, 256], BF16)
                        for ci, (g0, g1, eng) in enumerate(SPLIT):
                            rn = "Sb%d" % ci
                            S.op(eng, lambda e, g0=g0, g1=g1: e.memset(Sh[:, g0:g1, :, 0:1], 0.0), writes=["Sh%d" % ci])
                            for c in range(2):
                                S.op(eng, lambda e, g0=g0, g1=g1, c=c: e.tensor_copy(
                                    out=Sh[:, g0:g1, c, 1:241].rearrange("p g (K j) -> p g K j", j=16),
                                    in_=Sb[:, g0:g1, c, :, 0:15].rearrange("p g j K -> p g K j")), reads=[rn], writes=["Sh%d" % ci])
                                S.op(eng, lambda e, g0=g0, g1=g1, c=c: e.tensor_copy(
                                    out=Sh[:, g0:g1, c, 241:256], in_=Sb[:, g0:g1, c, 0:15, 15]), reads=[rn], writes=["Sh%d" % ci])
                        ck("C2")
                        with ExitStack() as c3:
                            sb3 = mk(c3)
                            Ysbs = [sb3("Ysb%d" % i, [128, 2, 128]) for i in range(2)]
                            YG = sb3("YG", [128, 8, 512])
                            def mm3(ct, gp):
                                bank = gp % 2
                                for j in range(2):
                                    g = 2 * gp + j
                                    pb = 64 * j
                                    o_ = PS[bank][:, j * 128:(j + 1) * 128]
                                    T(lambda e, g=g, o_=o_: e.matmul(o_, lhsT=Tm[:, g, :], rhs=Ug[:, g, ct * 128:(ct + 1) * 128],
                                                                     start=True, stop=False), r=["Tm", "Ug"], w=[psn[bank]])
                                    T(lambda e, pb=pb, o_=o_: e.matmul(
                                        o_, lhsT=PCre[pb:pb + 64, gp, :], rhs=Sh[pb:pb + 64, gp, 0, ct * 128:(ct + 1) * 128],
                                        start=False, stop=False), r=["PCre", "Sh0", "Sh1"], w=[psn[bank]])
                                    T(lambda e, pb=pb, o_=o_: e.matmul(
                                        o_, lhsT=PCim[pb:pb + 64, gp, :], rhs=Sh[pb:pb + 64, gp, 1, ct * 128:(ct + 1) * 128],
                                        start=False, stop=True), r=["PCim", "Sh0", "Sh1"], w=[psn[bank]])

                            def ev3(ct, gp):
                                bank = gp % 2
                                Ysb, Ysn = Ysbs[gp % 2], "Ysb%d" % (gp % 2)
                                A(lambda e: e.copy(out=Ysb[:], in_=PS[bank][:, 0:256].rearrange("p (j k) -> p j k", j=2)),
                                  r=[psn[bank]], w=[Ysn])
                                tb = 2 + gp % 2
                                for j in range(2):
                                    T(lambda e, j=j: e.transpose(out=PS[tb][:, j * 128:(j + 1) * 128], in_=Ysb[:, j, :],
                                                                 identity=ident[:]), r=[Ysn, "ident"], w=[psn[tb]])
                                for j in range(2):
                                    g = 2 * gp + j
                                    A(lambda e, j=j, g=g: e.activation(
                                        out=YG[:, :, g * 16:(g + 1) * 16],
                                        in_=PS[tb][:, j * 128:(j + 1) * 128].rearrange("p (t c) -> p t c", t=8),
                                        func=AF.Gelu), r=[psn[tb]], w=["YG"])

                            for ct in range(2):
                                mm3(ct, 0)
                                for gp in range(16):
                                    if gp + 1 < 16:
                                        mm3(ct, gp + 1)
                                    ev3(ct, gp)
                                for tp in range(8):
                                    bank = 4 + tp % 2
                                    for cbi in range(4):
                                        T(lambda e, tp=tp, cbi=cbi, bank=bank: e.transpose(
                                            out=PS[bank][:, cbi * 128:(cbi + 1) * 128], in_=YG[:, tp, cbi * 128:(cbi + 1) * 128],
                                            identity=ident[:]), r=["YG", "ident"], w=[psn[bank]])
                                    col = (ct * 8 + tp) * 128
                                    V(lambda e, bank=bank, col=col: e.tensor_copy(out=ygT[:, :, col:col + 128],
                                                                                  in_=PS[bank][:].rearrange("p (a t) -> p a t", a=4)),
                                      r=[psn[bank]], w=["ygT"])
                            S.barrier()
                        S.barrier()
                        c12.close()
                        ck("C3")
                        with ExitStack() as c5:
                            sb5 = mk(c5)
                            wgl = sb5("wgl", [128, 4, 512], BF16)
                            wst[0] = sb5("wsta5", [128, 8, 128])
                            wbf[0] = sb5("wbf5a", [128, 8, 128], BF16)
                            wbf[1] = sb5("wbf5b", [128, 8, 128], BF16)
                            wgs_st = sb5("wgs_st", [128, 512])
                            sig = [sb5("sig%d" % i, [128, 512]) for i in range(2)]
                            gsl = [sb5("gsl%d" % i, [128, 512], BF16) for i in range(2)]
                            gth = [sb5("gth%d" % i, [128, 512]) for i in range(2)]
                            tm5 = [sb5("tm5%d" % i, [128, 512]) for i in range(2)]
                            for kc in range(4):
                                D(lambda e, kc=kc: e.dma_start(out=wgs_st[:], in_=wglu_d[kc * 128:(kc + 1) * 128, :]), w=["wgs_st"])
                                V(lambda e, kc=kc: e.tensor_copy(out=wgl[:, kc, :], in_=wgs_st[:]), r=["wgs_st"], w=["wgl"])
                            steps = [(cb, tg) for cb in range(4) for tg in range(4)]
                            wcur = {}

                            def pvw(ap, half):
                                return ap.rearrange("p (t q) -> p t q", t=8)[:, :, half * 64:(half + 1) * 64]

                            def mm5(si):
                                cb, tg = steps[si]
                                if tg == 0:
                                    wcur[cb] = load_w(2560 + cb * 128, cb % 2)
                                wgs, wgsn = wcur[cb]
                                ct, half = tg // 2, tg % 2
                                b0, b1 = 2 * (si % 4), 2 * (si % 4) + 1
                                for kc in range(4):
                                    T(lambda e, kc=kc: e.matmul(PS[b0][:], lhsT=wgl[:, kc, cb * 128:(cb + 1) * 128],
                                                                rhs=pvw(ygT[:, kc, ct * 1024:(ct + 1) * 1024], half),
                                                                start=(kc == 0), stop=(kc == 3)), r=["wgl", "ygT"], w=[psn[b0]])
                                for kc in range(8):
                                    T(lambda e, kc=kc: e.matmul(PS[b1][:], lhsT=wgs[:, kc, :], rhs=xT[:, kc, tg * 512:(tg + 1) * 512],
                                                                start=(kc == 0), stop=(kc == 7)), r=[wgsn, "xT"], w=[psn[b1]])

                            def ev5(si):
                                cb, tg = steps[si]
                                ct, half = tg // 2, tg % 2
                                i2 = si % 2
                                b0, b1 = 2 * (si % 4), 2 * (si % 4) + 1
                                sg, sgn = sig[i2], "sig%d" % i2
                                gh, ghn = gth[i2], "gth%d" % i2
                                gl, gln = gsl[i2], "gsl%d" % i2
                                t5, t5n = tm5[i2], "tm5%d" % i2
                                A(lambda e: e.activation(out=sg[:], in_=PS[b0][:], func=AF.Sigmoid, bias=bglu[:, cb, :]),
                                  r=[psn[b0], "bglu"], w=[sgn])
                                A(lambda e: e.activation(out=gh[:], in_=PS[b1][:], func=AF.Sigmoid), r=[psn[b1]], w=[ghn])
                                V(lambda e: e.tensor_tensor(out=gl[:].rearrange("p (t q) -> p q t", t=8),
                                                            in0=PS[b1][:].rearrange("p (q t) -> p q t", t=8),
                                                            in1=gh[:].rearrange("p (q t) -> p q t", t=8), op=ALU.mult),
                                  r=[ghn, psn[b1]], w=[gln])
                                G(lambda e: e.tensor_tensor(out=t5[:].rearrange("p (t q) -> p t q", t=8),
                                                            in0=pvw(ygT[:, cb, ct * 1024:(ct + 1) * 1024], half),
                                                            in1=sg[:].rearrange("p (t q) -> p t q", t=8), op=ALU.mult),
                                  r=["ygT", sgn], w=[t5n])
                                G(lambda e: e.tensor_tensor(out=pvw(ssmT[:, cb, ct * 1024:(ct + 1) * 1024], half),
                                                            in0=t5[:].rearrange("p (t q) -> p t q", t=8),
                                                            in1=gl[:].rearrange("p (t q) -> p t q", t=8), op=ALU.mult),
                                  r=[t5n, gln], w=["ssmT"])

                            mm5(0)
                            mm5(1)
                            mm5(2)
                            for si in range(16):
                                if si + 3 < 16:
                                    mm5(si + 3)
                                ev5(si)
                            if debug and s == 0:
                                with ExitStack() as dd:
                                    d32 = mk(dd)("d32y", [128, 4, SEQ])
                                    for nm, src in (("attnT", attnT), ("ssmT", ssmT), ("ygT", ygT)):
                                        S.barrier()
                                        V(lambda e, src=src: e.tensor_copy(out=d32[:], in_=src[:]), w=["d32y"])
                                        D(lambda e, nm=nm: e.dma_start(out=dbg[nm], in_=d32[:]), r=["d32y"])
                                    S.barrier()
                            S.barrier()
                    S.barrier()

                ck("C5")
                S.barrier()
                with ExitStack() as ds:
                    sbD = mk(ds)
                    Wout = sbD("Wout", [128, 8, DM], BF16)
                    Wg = sbD("Wg", [128, 8, DM], BF16)
                    Wp = sbD("Wp", [128, 2, DM], BF16)
                    wstd = [sbD("wstd%d" % i, [128, DM]) for i in range(2)]
                    cnt = 0
                    for (src, dst, dn_, nk, fold) in ((wout_d, Wout, "Wout", 8, None), (wg_d, Wg, "Wg", 8, pleg), (wp_d, Wp, "Wp", 2, None)):
                        for kc in range(nk):
                            st = wstd[cnt % 2]
                            rn = "wstd%d" % (cnt % 2)
                            D(lambda e, st=st, src=src, kc=kc: e.dma_start(out=st[:], in_=src[kc * 128:(kc + 1) * 128, :]), w=[rn])
                            if fold is None:
                                if cnt % 2 == 0:
                                    V(lambda e, st=st, dst=dst, kc=kc: e.tensor_copy(out=dst[:, kc, :], in_=st[:]), r=[rn], w=[dn_])
                                else:
                                    A(lambda e, st=st, dst=dst, kc=kc: e.copy(out=dst[:, kc, :], in_=st[:]), r=[rn], w=[dn_])
                            else:
                                V(lambda e, st=st, dst=dst, kc=kc, fold=fold: e.tensor_scalar(
                                    out=dst[:, kc, :], in0=st[:], scalar1=fold[:, kc, :], scalar2=None, op0=ALU.mult),
                                  r=[rn, "pleg"], w=[dn_])
                            cnt += 1
                    xd = [sbD("xd%d" % i, [128, DM]) for i in range(3)]
                    pd = [sbD("pd%d" % i, [128, 256]) for i in range(3)]
                    hh = [sbD("hh%d" % i, [128, DM]) for i in range(2)]
                    sq = sbD("sqd", [128, DM])
                    st4 = [sbD("st4d%d" % i, [128, 4]) for i in range(2)]
                    hT = [sbD("hT%d" % i, [128, 8, 128], BF16) for i in range(2)]
                    pT = [sbD("pT%d" % i, [128, 2, 128], BF16) for i in range(2)]
                    gate = [sbD("gate%d" % i, [128, DM]) for i in range(2)]
                    oo = [sbD("oo%d" % i, [128, DM]) for i in range(2)]

                    def XL(it):
                        ct, tp = it // 8, it % 8
                        xi, xn = xd[it % 3], "xd%d" % (it % 3)
                        pi_, pn = pd[it % 3], "pd%d" % (it % 3)
                        r0 = ct * 1024 + tp
                        D(lambda e: e.dma_start(out=xi[:], in_=x_d[s, r0:(ct + 1) * 1024:8, :]), w=[xn])
                        D(lambda e: e.dma_start(out=pi_[:], in_=p_d[s, r0:(ct + 1) * 1024:8, :]), w=[pn])

                    def X1(it):
                        ct, tp = it // 8, it % 8
                        b2 = it % 2
                        xi, xn = xd[it % 3], "xd%d" % (it % 3)
                        h_, hn = hh[b2], "hh%d" % b2
                        s_, sn_ = st4[b2], "st4d%d" % b2
                        r0 = ct * 1024 + tp
                        col = it * 128
                        for nh in range(2):
                            for kc in range(8):
                                if kc < 4:
                                    lt = attnT[:, kc, r0:(ct + 1) * 1024:8]
                                    rn = "attnT"
                                else:
                                    lt = ssmT[:, kc - 4, col:col + 128]
                                    rn = "ssmT"
                                T(lambda e, lt=lt, kc=kc, nh=nh: e.matmul(PS[nh][:], lhsT=lt, rhs=Wout[:, kc, nh * 512:(nh + 1) * 512],
                                                                          start=(kc == 0), stop=(kc == 7)), r=[rn, "Wout"], w=[psn[nh]])
                            V(lambda e, nh=nh: e.tensor_tensor(out=h_[:, nh * 512:(nh + 1) * 512], in0=PS[nh][:],
                                                               in1=xi[:, nh * 512:(nh + 1) * 512], op=ALU.add),
                              r=[psn[nh], xn], w=[hn])
                        A(lambda e: e.activation(out=sq[:], in_=h_[:], func=AF.Square), r=[hn], w=["sqd"])
                        V(lambda e: e.tensor_reduce(out=s_[:, 0:1], in_=sq[:], axis=AX.X, op=ALU.add), r=["sqd"], w=[sn_])
                        V(lambda e: e.tensor_scalar(out=s_[:, 1:2], in0=s_[:, 0:1], scalar1=1.0 / DM, scalar2=EPS, op0=ALU.mult,
                                                    op1=ALU.add), r=[sn_], w=[sn_])
                        G(lambda e: e.tensor_tensor(out=s_[:, 3:4], in0=s_[:, 1:2], in1=cneg[:, 0:1], op=ALU.pow),
                          r=[sn_, "cneg"], w=[sn_])

                    def X2(it):
                        b2 = it % 2
                        pi_, pn = pd[it % 3], "pd%d" % (it % 3)
                        h_, hn = hh[b2], "hh%d" % b2
                        hT_, hTn = hT[b2], "hT%d" % b2
                        pT_, pTn = pT[b2], "pT%d" % b2
                        for half in range(2):
                            for q4 in range(4):
                                kc = half * 4 + q4
                                T(lambda e, kc=kc, q4=q4, half=half: e.transpose(out=PS[2 + half][:, q4 * 128:(q4 + 1) * 128],
                                                                                 in_=h_[:, kc * 128:(kc + 1) * 128], identity=ident[:]),
                                  r=[hn, "ident"], w=[psn[2 + half]])
                            A(lambda e, half=half: e.copy(out=hT_[:, half * 4:half * 4 + 4, :],
                                                          in_=PS[2 + half][:].rearrange("p (k t) -> p k t", k=4)),
                              r=[psn[2 + half]], w=[hTn])
                        for q2 in range(2):
                            T(lambda e, q2=q2: e.transpose(out=PS[4][:, q2 * 128:(q2 + 1) * 128], in_=pi_[:, q2 * 128:(q2 + 1) * 128],
                                                           identity=ident[:]), r=[pn, "ident"], w=[psn[4]])
                        V(lambda e: e.tensor_copy(out=pT_[:], in_=PS[4][:, 0:256].rearrange("p (k t) -> p k t", k=2)),
                          r=[psn[4]], w=[pTn])

                    def Y(it):
                        ct, tp = it // 8, it % 8
                        b2 = it % 2
                        h_, hn = hh[b2], "hh%d" % b2
                        s_, sn_ = st4[b2], "st4d%d" % b2
                        hT_, hTn = hT[b2], "hT%d" % b2
                        pT_, pTn = pT[b2], "pT%d" % b2
                        g_, gn = gate[b2], "gate%d" % b2
                        oi, on = oo[b2], "oo%d" % b2
                        r0 = ct * 1024 + tp
                        for nh in range(2):
                            for kc in range(8):
                                T(lambda e, kc=kc, nh=nh: e.matmul(PS[5][:], lhsT=hT_[:, kc, :], rhs=Wg[:, kc, nh * 512:(nh + 1) * 512],
                                                                   start=(kc == 0), stop=(kc == 7)), r=[hTn, "Wg"], w=[psn[5]])
                            A(lambda e, nh=nh: e.activation(out=g_[:, nh * 512:(nh + 1) * 512], in_=PS[5][:], func=AF.Sigmoid,
                                                            scale=s_[:, 3:4]), r=[psn[5], sn_], w=[gn])
                            for kc in range(2):
                                T(lambda e, kc=kc, nh=nh: e.matmul(PS[6 + nh][:], lhsT=pT_[:, kc, :], rhs=Wp[:, kc, nh * 512:(nh + 1) * 512],
                                                                   start=(kc == 0), stop=(kc == 1)), r=[pTn, "Wp"], w=[psn[6 + nh]])
                            V(lambda e, nh=nh: e.tensor_tensor(out=oi[:, nh * 512:(nh + 1) * 512], in0=PS[6 + nh][:],
                                                               in1=g_[:, nh * 512:(nh + 1) * 512], op=ALU.mult),
                              r=[psn[6 + nh], gn], w=[on])
                        G(lambda e: e.tensor_tensor(out=oi[:], in0=oi[:], in1=h_[:], op=ALU.add), r=[on, hn], w=[on])
                        D(lambda e: e.dma_start(out=out_d[s, r0:(ct + 1) * 1024:8, :], in_=oi[:]), r=[on])

                    XL(0)
                    XL(1)
                    X1(0)
                    X2(0)
                    for it in range(16):
                        if it + 2 < 16:
                            XL(it + 2)
                        if it + 1 < 16:
                            X1(it + 1)
                        Y(it)
                        if it + 1 < 16:
                            X2(it + 1)
                    S.barrier()
                    ck("D")
        except _Stop:
            pass
        S.barrier()
    return nc


_NC_CACHE = {}


def kernel(**inputs):
    f = lambda a: np.ascontiguousarray(np.asarray(a, dtype=np.float32))
    x = f(inputs["x"])
    p = f(inputs["p"])[0]
    shared = {}
    for k in ("mix_norm", "w_in", "q_norm", "k_norm", "lambda_re", "lambda_im", "log_dt", "b_re", "b_im", "c_re", "c_im",
              "d_skip", "w_glu", "b_glu", "w_out", "ple_norm", "w_ple_gate", "w_ple_proj"):
        shared[k] = f(inputs[k])[0]
    if "nc" not in _NC_CACHE:
        _NC_CACHE["nc"] = build_nc()
    nc = _NC_CACHE["nc"]
    in_maps = []
    for c in range(NCORES):
        m = {"x": np.ascontiguousarray(x[2 * c:2 * c + 2]), "p": np.ascontiguousarray(p[2 * c:2 * c + 2])}
        m.update(shared)
        in_maps.append(m)
    res = run_bass_kernel_spmd(nc, in_maps, core_ids=list(range(NCORES)))
    out = np.concatenate([np.asarray(r["out"], dtype=np.float32) for r in res.results], axis=0)
    return out
```

```python
import math
from contextlib import ExitStack
import numpy as np
import concourse.bass as bass
import concourse.mybir as mybir
from concourse.bass_utils import run_bass_kernel_spmd

F32 = mybir.dt.float32
BF16 = mybir.dt.bfloat16
F32R = mybir.dt.float32r
I32 = mybir.dt.int32
ALU = mybir.AluOpType
AF = mybir.ActivationFunctionType
AX = mybir.AxisListType

NCORES = 8
SEQ = 2048
DM = 1024
EPS = 1e-6
MAGIC = 12582912.0
TWO_PI = 2.0 * math.pi
EXACT_MASK = False
ALPHA = 1.0 if EXACT_MASK else 38.2305


class Sched:
    ENGS = ("pe", "act", "dve", "pool", "sp")

    def __init__(self, nc, n_dma_sems=24):
        self.nc = nc
        self.cnt = {e: 0 for e in self.ENGS}
        self.sem = {}
        self.seen = {e: {} for e in self.ENGS}
        self.last_w = {}
        self.readers = {}
        self.dma_sems = []
        self.dma_uses = []
        self.n_dma_sems = n_dma_sems
        self.dma_rr = 0

    def alloc(self, stack):
        for e in self.ENGS:
            if e == "sp":
                continue
            self.sem[e] = stack.enter_context(self.nc.semaphore("s_" + e))
        for i in range(self.n_dma_sems):
            self.dma_sems.append(stack.enter_context(self.nc.semaphore("d%d" % i)))
            self.dma_uses.append(0)

    def _need(self, eng, tok, waits):
        key, sem, val, src = tok
        if self.seen[eng].get(key, 0) >= val:
            return
        self.seen[eng][key] = val
        waits.append((sem, val))

    dead = False

    def op(self, eng, fn, reads=(), writes=(), dma=False):
        if self.dead:
            return None
        waits = []
        for r in reads:
            t = self.last_w.get(r)
            if t is not None:
                self._need(eng, t, waits)
        for w in writes:
            t = self.last_w.get(w)
            if t is not None and (dma or t[3] != eng or eng != "pe"):
                self._need(eng, t, waits)
            for t in self.readers.get(w, {}).values():
                if dma or t[3] != eng or eng != "pe":
                    self._need(eng, t, waits)
        if dma:
            j = self.dma_rr
            self.dma_rr = (self.dma_rr + 1) % self.n_dma_sems
            k = self.dma_uses[j]
            if k > 0:
                self._need(eng, ("d%d" % j, self.dma_sems[j], 16 * k, None), waits)
            self.dma_uses[j] = k + 1
            tok = ("d%d" % j, self.dma_sems[j], 16 * (k + 1), None)
            inc = (self.dma_sems[j], 16)
        else:
            self.cnt[eng] += 1
            tok = (eng, self.sem[eng], self.cnt[eng], eng)
            inc = (self.sem[eng], 1)
        for w in writes:
            self.last_w[w] = tok
            self.readers[w] = {}
        for r in reads:
            self.readers.setdefault(r, {})[tok[0]] = tok
        self._emit(eng, waits, fn, inc)
        return tok

    def group(self, eng, ops):
        if self.dead:
            return
        waits = []
        for fn, reads, writes in ops:
            for r in reads:
                t = self.last_w.get(r)
                if t is not None:
                    self._need(eng, t, waits)
            for w in writes:
                t = self.last_w.get(w)
                if t is not None and (t[3] != eng or eng != "pe"):
                    self._need(eng, t, waits)
                for t in self.readers.get(w, {}).values():
                    if t[3] != eng or eng != "pe":
                        self._need(eng, t, waits)
        first = True
        for fn, reads, writes in ops:
            self.cnt[eng] += 1
            tok = (eng, self.sem[eng], self.cnt[eng], eng)
            for w in writes:
                self.last_w[w] = tok
                self.readers[w] = {}
            for r in reads:
                self.readers.setdefault(r, {})[tok[0]] = tok
            self._emit(eng, waits if first else [], fn, (self.sem[eng], 1))
            first = False

    def _eng(self, name):
        nc = self.nc
        return {"pe": nc.tensor, "act": nc.scalar, "dve": nc.vector, "pool": nc.gpsimd, "sp": nc.sync}[name]

    def _emit(self, eng, waits, fn, inc):
        e = self._eng(eng)
        for sem, val in waits:
            e.wait_ge(sem, val)
        if fn is not None:
            ins = fn(e)
            ins.then_inc(inc[0], inc[1])

    def barrier(self):
        if self.dead:
            return
        toks = []
        for e in self.ENGS:
            if e != "sp" and self.cnt[e] > 0:
                toks.append((e, self.sem[e], self.cnt[e], e))
        for j in range(self.n_dma_sems):
            if self.dma_uses[j] > 0:
                toks.append(("d%d" % j, self.dma_sems[j], 16 * self.dma_uses[j], None))
        for e in self.ENGS:
            waits = []
            for t in toks:
                self._need(e, t, waits)
            if waits:
                self._emit(e, waits, None, None)


class _Stop(Exception):
    pass


def build_nc(debug=False, stop_after=None):
    nc = bass.Bass("TRN2", target_bir_lowering=False)

    def din(name, shape):
        return nc.dram_tensor(name, shape, F32, kind="ExternalInput").ap()

    x_d = din("x", [2, SEQ, DM])
    p_d = din("p", [2, SEQ, 256])
    mixn_d = din("mix_norm", [DM])
    win_d = din("w_in", [DM, 3072])
    qn_d = din("q_norm", [64])
    kn_d = din("k_norm", [64])
    lre_d = din("lambda_re", [32, 64])
    lim_d = din("lambda_im", [32, 64])
    ldt_d = din("log_dt", [32])
    bre_d = din("b_re", [32, 64, 16])
    bim_d = din("b_im", [32, 64, 16])
    cre_d = din("c_re", [32, 16, 64])
    cim_d = din("c_im", [32, 16, 64])
    dsk_d = din("d_skip", [512])
    wglu_d = din("w_glu", [512, 512])
    bglu_d = din("b_glu", [512])
    wout_d = din("w_out", [DM, DM])
    plen_d = din("ple_norm", [DM])
    wg_d = din("w_ple_gate", [DM, DM])
    wp_d = din("w_ple_proj", [256, DM])
    out_d = nc.dram_tensor("out", [2, SEQ, DM], F32, kind="ExternalOutput").ap()
    dbg = {}
    if debug:
        dbg["xT"] = nc.dram_tensor("dbg_xT", [128, 8, SEQ], F32, kind="ExternalOutput").ap()
        dbg["attnT"] = nc.dram_tensor("dbg_attnT", [128, 4, SEQ], F32, kind="ExternalOutput").ap()
        dbg["ssmT"] = nc.dram_tensor("dbg_ssmT", [128, 4, SEQ], F32, kind="ExternalOutput").ap()
        dbg["ygT"] = nc.dram_tensor("dbg_ygT", [128, 4, SEQ], F32, kind="ExternalOutput").ap()
        dbg["qT"] = nc.dram_tensor("dbg_qT", [128, SEQ], F32, kind="ExternalOutput").ap()
        dbg["T"] = nc.dram_tensor("dbg_T", [128, 32, 128], F32, kind="ExternalOutput").ap()
        dbg["S"] = nc.dram_tensor("dbg_S", [128, 16, 2, 257], F32, kind="ExternalOutput").ap()

    with ExitStack() as top:
        S = Sched(nc)
        S.alloc(top)
        top.enter_context(nc.allow_non_contiguous_dma(reason="small strided parameter loads"))

        uid = [0]

        def mk(stack):
            def sb(name, shape, dt=F32):
                uid[0] += 1
                return stack.enter_context(nc.sbuf_tensor("%s_u%d" % (name, uid[0]), shape, dt))
            return sb

        sbP = mk(top)
        PS = [top.enter_context(nc.psum_tensor("ps%d" % i, [128, 512], F32)) for i in range(8)]
        psn = ["ps%d" % i for i in range(8)]

        def V(fn, r=(), w=()):
            S.op("dve", fn, reads=r, writes=w)

        def A(fn, r=(), w=()):
            S.op("act", fn, reads=r, writes=w)

        def G(fn, r=(), w=()):
            S.op("pool", fn, reads=r, writes=w)

        def T(fn, r=(), w=()):
            S.op("pe", fn, reads=r, writes=w)

        def D(fn, r=(), w=()):
            S.op("sp", fn, reads=r, writes=w, dma=True)

        dq = [0]

        def D2(fn, r=(), w=()):
            dq[0] += 1
            S.op("sp" if dq[0] % 2 else "act", fn, reads=r, writes=w, dma=True)

        def ck(name):
            if stop_after == name:
                S.barrier()
                S.dead = True

        try:
            ident = sbP("ident", [128, 128])
            ones_f = sbP("ones_f", [128, 128])
            identr = sbP("identr", [128, 128], F32 if EXACT_MASK else BF16)
            cneg = sbP("cneg", [128, 2])
            blk1 = sbP("blk1", [128, 128], BF16)
            MT = sbP("MT", [128, 2432], F32 if EXACT_MASK else BF16)
            mixg = sbP("mixg", [128, 8, 1])
            pleg = sbP("pleg", [128, 8, 1])
            qg = sbP("qg", [128, 1])
            kg = sbP("kg", [128, 1])
            bglu = sbP("bglu", [128, 4, 1])
            Tm = sbP("Tm", [128, 32, 128], BF16)
            PGre = sbP("PGre", [128, 32, 64], BF16)
            PGim = sbP("PGim", [128, 32, 64], BF16)
            PCre = sbP("PCre", [128, 16, 128], BF16)
            PCim = sbP("PCim", [128, 16, 128], BF16)
            APR = sbP("APR", [128, 16, 16, 2])
            API = sbP("API", [128, 16, 16, 2])
            attnT = sbP("attnT", [128, 4, SEQ], BF16)
            ssmT = sbP("ssmT", [128, 4, SEQ], BF16)

            G(lambda e: e.memset(ident[:], 1.0), w=["ident"])
            G(lambda e: e.affine_select(out=ident[:], in_=ident[:], pattern=[[-1, 128]], base=0, channel_multiplier=1,
                                        compare_op=ALU.is_equal, fill=0.0), r=["ident"], w=["ident"])
            V(lambda e: e.memset(ones_f[:], 1.0), w=["ones_f"])
            V(lambda e: e.tensor_copy(out=(identr[:].bitcast(F32R) if EXACT_MASK else identr[:]), in_=ident[:]), r=["ident"], w=["identr"])
            V(lambda e: e.memset(cneg[:, 0:1], -0.5), w=["cneg"])
            V(lambda e: e.memset(cneg[:, 1:2], -1.0), w=["cneg"])
            V(lambda e: e.memset(blk1[:], 0.0), w=["blk1"])
            V(lambda e: e.memset(blk1[0:64, 0:64], 1.0), w=["blk1"])
            V(lambda e: e.memset(blk1[64:128, 64:128], 1.0), w=["blk1"])
            ck("consts")

            with ExitStack() as su:
                sb = mk(su)
                NI = 2432
                sb_keep = sb
                msk_scope = ExitStack()
                sb = mk(msk_scope)
                di = sb("di", [128, NI], I32)
                df = sb("df", [128, NI])
                t1 = sb("mt1", [128, NI])
                t2 = sb("mt2", [128, NI])
                acc = sb("macc", [128, NI])
                G(lambda e: e.iota(di[:], pattern=[[1, NI]], base=-384, channel_multiplier=-1), w=["di"])
                V(lambda e: e.tensor_copy(out=df[:], in_=di[:]), r=["di"], w=["df"])
                V(lambda e: e.tensor_scalar(out=acc[:], in0=df[:], scalar1=128.0, scalar2=None, op0=ALU.is_le), r=["df"], w=["acc"])
                for (mod, lim) in ((4.0, 512.0), (16.0, None)):
                    V(lambda e, mod=mod: e.tensor_scalar(out=t1[:], in0=df[:], scalar1=1.0 / mod, scalar2=MAGIC, op0=ALU.mult,
                                                         op1=ALU.add), r=["df"], w=["t1"])
                    V(lambda e, mod=mod: e.tensor_scalar(out=t1[:], in0=t1[:], scalar1=MAGIC, scalar2=mod, op0=ALU.subtract,
                                                         op1=ALU.mult), r=["t1"], w=["t1"])
                    V(lambda e: e.tensor_tensor(out=t1[:], in0=t1[:], in1=df[:], op=ALU.is_equal), r=["t1", "df"], w=["t1"])
                    if lim is not None:
                        V(lambda e, lim=lim: e.tensor_scalar(out=t2[:], in0=df[:], scalar1=lim, scalar2=None, op0=ALU.is_le),
                          r=["df"], w=["t2"])
                        V(lambda e: e.tensor_tensor(out=t1[:], in0=t1[:], in1=t2[:], op=ALU.mult), r=["t1", "t2"], w=["t1"])
                    V(lambda e: e.tensor_tensor(out=acc[:], in0=acc[:], in1=t1[:], op=ALU.add), r=["acc", "t1"], w=["acc"])
                V(lambda e: e.tensor_scalar(out=t2[:], in0=df[:], scalar1=0.0, scalar2=None, op0=ALU.is_ge), r=["df"], w=["t2"])
                V(lambda e: e.tensor_tensor(out=acc[:], in0=acc[:], in1=t2[:], op=ALU.mult), r=["acc", "t2"], w=["acc"])
                if EXACT_MASK:
                    V(lambda e: e.tensor_scalar(out=t1[:], in0=acc[:], scalar1=1.0, scalar2=None, op0=ALU.max), r=["acc"], w=["t1"])
                    A(lambda e: e.activation(out=t1[:], in_=t1[:], func=AF.Ln), r=["t1"], w=["t1"])
                    V(lambda e: e.tensor_scalar(out=t2[:], in0=acc[:], scalar1=0.0, scalar2=-30000.0, op0=ALU.is_equal, op1=ALU.mult),
                      r=["acc"], w=["t2"])
                    V(lambda e: e.tensor_tensor(out=MT[:].bitcast(F32R), in0=t1[:], in1=t2[:], op=ALU.add), r=["t1", "t2"], w=["MT"])
                else:
                    V(lambda e: e.tensor_scalar(out=t1[:], in0=acc[:], scalar1=2.0, scalar2=26.5, op0=ALU.is_equal, op1=ALU.mult),
                      r=["acc"], w=["t1"])
                    V(lambda e: e.tensor_scalar(out=t2[:], in0=acc[:], scalar1=3.0, scalar2=42.0, op0=ALU.is_equal, op1=ALU.mult),
                      r=["acc"], w=["t2"])
                    V(lambda e: e.tensor_tensor(out=t1[:], in0=t1[:], in1=t2[:], op=ALU.add), r=["t1", "t2"], w=["t1"])
                    V(lambda e: e.tensor_scalar(out=t2[:], in0=acc[:], scalar1=0.0, scalar2=-1.0e6, op0=ALU.is_equal, op1=ALU.mult),
                      r=["acc"], w=["t2"])
                    V(lambda e: e.tensor_tensor(out=MT[:], in0=t1[:], in1=t2[:], op=ALU.add), r=["t1", "t2"], w=["MT"])
                S.barrier()
                msk_scope.close()
                sb = sb_keep
                ck("mask")

                LR = sb("LR", [128, 16, 1])
                LI = sb("LI", [128, 16, 1])
                DT = sb("DT", [128, 16, 1])
                dtrow = sb("dtrow", [1, 32])
                D(lambda e: e.dma_start(out=dtrow[:], in_=ldt_d.rearrange("(o g) -> o g", o=1)), w=["dtrow"])
                T(lambda e: e.matmul(PS[5][:, 0:32], lhsT=ones_f[0:1, :], rhs=dtrow[:], start=True, stop=True),
                  r=["ones_f", "dtrow"], w=[psn[5]])
                for j in range(2):
                    V(lambda e, j=j: e.tensor_copy(out=DT[64 * j:64 * j + 64, :, 0], in_=PS[5][64 * j:64 * j + 64, j:32:2]),
                      r=[psn[5]], w=["DT"])
                Ln_ = [sb("Lnat%d" % i, [32, 2, 64]) for i in range(2)]
                for li, (src, dst, dname) in enumerate(((lre_d, LR, "LR"), (lim_d, LI, "LI"))):
                    D(lambda e, li=li, src=src: e.dma_start(out=Ln_[li][:, 0, :], in_=src), w=["Lnat%d" % li])
                    G(lambda e, li=li: e.tensor_copy(out=Ln_[li][:, 1, :], in_=Ln_[li][:, 0, :]), r=["Lnat%d" % li], w=["Lnat%d" % li])
                    bank = 6 + li
                    T(lambda e, li=li, bank=bank: e.transpose(out=PS[bank][:, 0:32], in_=Ln_[li][:].rearrange("p d n -> p (d n)"),
                                                              identity=ident[0:32, 0:32]), r=["Lnat%d" % li, "ident"], w=[psn[bank]])
                    for j in range(2):
                        V(lambda e, dst=dst, j=j, bank=bank: e.tensor_copy(
                            out=dst[64 * j:64 * j + 64, :, 0], in_=PS[bank][64 * j:64 * j + 64, j:32:2]), r=[psn[bank]], w=[dname])
                BR = sb("BR", [128, 16, 16])
                BI = sb("BI", [128, 16, 16])
                for j in range(2):
                    D2(lambda e, j=j: e.dma_start(out=BR[64 * j:64 * j + 64], in_=bre_d.rearrange("(gp j) n c -> j n gp c", j=2)[j]), w=["BR"])
                    D2(lambda e, j=j: e.dma_start(out=BI[64 * j:64 * j + 64], in_=bim_d.rearrange("(gp j) n c -> j n gp c", j=2)[j]), w=["BI"])
                Cn = [sb("Cn%d" % i, [128, 4, 2, 64]) for i in range(2)]
                for ci, src in enumerate((cre_d, cim_d)):
                    D(lambda e, ci=ci, src=src: e.dma_start(out=Cn[ci][:, :, 0, :],
                                                            in_=src.rearrange("(t g) c n -> (g c) t n", t=4)), w=["Cn%d" % ci])
                    G(lambda e, ci=ci: e.tensor_copy(out=Cn[ci][:, :, 1, :], in_=Cn[ci][:, :, 0, :]), r=["Cn%d" % ci], w=["Cn%d" % ci])
                gnat = sb("gnat", [20, 128])
                D(lambda e: e.dma_start(out=gnat[0:8, :], in_=mixn_d.rearrange("(k p) -> k p", p=128)), w=["gnat"])
                D(lambda e: e.dma_start(out=gnat[8:16, :], in_=plen_d.rearrange("(k p) -> k p", p=128)), w=["gnat"])
                D(lambda e: e.dma_start(out=gnat[16:20, :], in_=bglu_d.rearrange("(k p) -> k p", p=128)), w=["gnat"])
                T(lambda e: e.transpose(out=PS[4][:, 0:20], in_=gnat[:], identity=ident[0:20, 0:20]), r=["gnat", "ident"], w=[psn[4]])
                V(lambda e: e.tensor_copy(out=mixg[:, :, 0], in_=PS[4][:, 0:8]), r=[psn[4]], w=["mixg"])
                V(lambda e: e.tensor_copy(out=pleg[:, :, 0], in_=PS[4][:, 8:16]), r=[psn[4]], w=["pleg"])
                V(lambda e: e.tensor_copy(out=bglu[:, :, 0], in_=PS[4][:, 16:20]), r=[psn[4]], w=["bglu"])
                qkrow = sb("qkrow", [2, 2, 64])
                for hh in range(2):
                    D(lambda e, hh=hh: e.dma_start(out=qkrow[0:1, hh, :], in_=qn_d.rearrange("(o n) -> o n", o=1)), w=["qkrow"])
                    D(lambda e, hh=hh: e.dma_start(out=qkrow[1:2, hh, :], in_=kn_d.rearrange("(o n) -> o n", o=1)), w=["qkrow"])
                T(lambda e: e.transpose(out=PS[4][:, 32:34], in_=qkrow[:].rearrange("r h n -> r (h n)"), identity=ident[0:2, 0:2]),
                  r=["qkrow", "ident"], w=[psn[4]])
                V(lambda e: e.tensor_scalar(out=qg[:], in0=PS[4][:, 32:33], scalar1=0.125 * ALPHA, scalar2=None, op0=ALU.mult),
                  r=[psn[4]], w=["qg"])
                V(lambda e: e.tensor_copy(out=kg[:], in_=PS[4][:, 33:34]), r=[psn[4]], w=["kg"])
                CR = sb("CR", [128, 16, 16])
                CI = sb("CI", [128, 16, 16])
                for ci, (dst, dname) in enumerate(((CR, "CR"), (CI, "CI"))):
                    for t in range(4):
                        bank = (ci * 4 + t) % 8
                        T(lambda e, ci=ci, t=t, bank=bank: e.transpose(
                            out=PS[bank][:, 0:128], in_=Cn[ci][:, t, :, :].rearrange("p d n -> p (d n)"), identity=ident[:]),
                          r=["Cn%d" % ci, "ident"], w=[psn[bank]])
                        for j in range(2):
                            V(lambda e, dst=dst, t=t, j=j, bank=bank: e.tensor_copy(
                                out=dst[64 * j:64 * j + 64, 4 * t:4 * t + 4, :],
                                in_=PS[bank][64 * j:64 * j + 64, 0:128].rearrange("p (g q c) -> p g q c", q=2, c=16)[:, :, j, :]),
                              r=[psn[bank]], w=[dname])
                dcol = sb("dcol", [128, 32, 1])
                dnat = sb("dnat", [32, 16])
                d16 = sb("d16", [16, 32])
                Rrep = sb("Rrep", [16, 128])
                D(lambda e: e.dma_start(out=dnat[:], in_=dsk_d.rearrange("(g c) -> g c", c=16)), w=["dnat"])
                T(lambda e: e.transpose(out=PS[4][0:16, 64:96], in_=dnat[:], identity=ident[0:32, 0:32]), r=["dnat", "ident"], w=[psn[4]])
                V(lambda e: e.tensor_copy(out=d16[:], in_=PS[4][0:16, 64:96]), r=[psn[4]], w=["d16"])
                for sp_ in range(8):
                    V(lambda e, sp_=sp_: e.tensor_copy(out=Rrep[:, 16 * sp_:16 * sp_ + 16], in_=ident[0:16, 0:16]), r=["ident"], w=["Rrep"])
                T(lambda e: e.matmul(PS[4][:, 128:160], lhsT=Rrep[:], rhs=d16[:], start=True, stop=True), r=["Rrep", "d16"], w=[psn[4]])
                V(lambda e: e.tensor_copy(out=dcol[:, :, 0], in_=PS[4][:, 128:160]), r=[psn[4]], w=["dcol"])

                def sm(name):
                    return sb("s_" + name, [128, 16, 1])

                dt_ = sm("dt_"); th = sm("th"); lrdt = sm("lrdt"); mag = sm("mag"); magi = sm("magi")
                rr = sm("rr"); sn = sm("sn"); cs = sm("cs"); tA = sm("tA"); tB = sm("tB")
                a_re = sm("a_re"); a_im = sm("a_im"); i_re = sm("i_re"); i_im = sm("i_im")
                c_re = sm("c_re"); c_im = sm("c_im"); den = sm("den")
                A(lambda e: e.activation(out=dt_[:], in_=DT[:], func=AF.Exp), r=["DT"], w=["dt_"])
                V(lambda e: e.tensor_tensor(out=th[:], in0=LI[:], in1=dt_[:], op=ALU.mult), r=["LI", "dt_"], w=["th"])
                V(lambda e: e.tensor_tensor(out=lrdt[:], in0=LR[:], in1=dt_[:], op=ALU.mult), r=["LR", "dt_"], w=["lrdt"])
                A(lambda e: e.activation(out=mag[:], in_=lrdt[:], func=AF.Exp), r=["lrdt"], w=["mag"])
                A(lambda e: e.activation(out=magi[:], in_=lrdt[:], func=AF.Exp, scale=-1.0), r=["lrdt"], w=["magi"])

                def sin_of(dst, shift, dname):
                    V(lambda e: e.tensor_scalar(out=tA[:], in0=th[:], scalar1=shift, scalar2=1.0 / TWO_PI, op0=ALU.add, op1=ALU.mult),
                      r=["th"], w=["tA"])
                    V(lambda e: e.tensor_scalar(out=tA[:], in0=tA[:], scalar1=MAGIC, scalar2=MAGIC, op0=ALU.add, op1=ALU.subtract),
                      r=["tA"], w=["tA"])
                    V(lambda e: e.tensor_scalar(out=tB[:], in0=th[:], scalar1=shift, scalar2=None, op0=ALU.add), r=["th"], w=["tB"])
                    V(lambda e: e.scalar_tensor_tensor(out=rr[:], in0=tA[:], scalar=-TWO_PI, in1=tB[:], op0=ALU.mult, op1=ALU.add),
                      r=["tA", "tB"], w=["rr"])
                    V(lambda e: e.tensor_scalar(out=rr[:], in0=rr[:], scalar1=math.pi, scalar2=-math.pi, op0=ALU.min, op1=ALU.max),
                      r=["rr"], w=["rr"])
                    A(lambda e: e.activation(out=dst[:], in_=rr[:], func=AF.Sin), r=["rr"], w=[dname])

                sin_of(sn, 0.0, "sn")
                sin_of(cs, math.pi / 2.0, "cs")
                V(lambda e: e.tensor_tensor(out=a_re[:], in0=mag[:], in1=cs[:], op=ALU.mult), r=["mag", "cs"], w=["a_re"])
                V(lambda e: e.tensor_tensor(out=a_im[:], in0=mag[:], in1=sn[:], op=ALU.mult), r=["mag", "sn"], w=["a_im"])
                V(lambda e: e.tensor_tensor(out=i_re[:], in0=magi[:], in1=cs[:], op=ALU.mult), r=["magi", "cs"], w=["i_re"])
                V(lambda e: e.scalar_tensor_tensor(out=i_im[:], in0=magi[:], scalar=-1.0, in1=sn[:], op0=ALU.mult, op1=ALU.mult),
                  r=["magi", "sn"], w=["i_im"])
                V(lambda e: e.tensor_tensor(out=den[:], in0=LR[:], in1=LR[:], op=ALU.mult), r=["LR"], w=["den"])
                V(lambda e: e.tensor_tensor(out=tA[:], in0=LI[:], in1=LI[:], op=ALU.mult), r=["LI"], w=["tA"])
                V(lambda e: e.tensor_tensor(out=den[:], in0=den[:], in1=tA[:], op=ALU.add), r=["den", "tA"], w=["den"])
                V(lambda e: e.reciprocal(out=den[:], in_=den[:]), r=["den"], w=["den"])
                V(lambda e: e.tensor_scalar(out=tB[:], in0=a_re[:], scalar1=-1.0, scalar2=None, op0=ALU.add), r=["a_re"], w=["tB"])
                V(lambda e: e.tensor_tensor(out=c_re[:], in0=tB[:], in1=LR[:], op=ALU.mult), r=["tB", "LR"], w=["c_re"])
                V(lambda e: e.tensor_tensor(out=tA[:], in0=a_im[:], in1=LI[:], op=ALU.mult), r=["a_im", "LI"], w=["tA"])
                V(lambda e: e.tensor_tensor(out=c_re[:], in0=c_re[:], in1=tA[:], op=ALU.add), r=["c_re", "tA"], w=["c_re"])
                V(lambda e: e.tensor_tensor(out=c_re[:], in0=c_re[:], in1=den[:], op=ALU.mult), r=["c_re", "den"], w=["c_re"])
                V(lambda e: e.tensor_tensor(out=c_im[:], in0=a_im[:], in1=LR[:], op=ALU.mult), r=["a_im", "LR"], w=["c_im"])
                V(lambda e: e.tensor_tensor(out=tA[:], in0=tB[:], in1=LI[:], op=ALU.mult), r=["tB", "LI"], w=["tA"])
                V(lambda e: e.tensor_tensor(out=c_im[:], in0=c_im[:], in1=tA[:], op=ALU.subtract), r=["c_im", "tA"], w=["c_im"])
                V(lambda e: e.tensor_tensor(out=c_im[:], in0=c_im[:], in1=den[:], op=ALU.mult), r=["c_im", "den"], w=["c_im"])

                tb1 = sb("tb1", [128, 16, 16])
                tb2 = sb("tb2", [128, 16, 16])

                lit = {a_re.name: "a_re", a_im.name: "a_im", i_re.name: "i_re", i_im.name: "i_im",
                       c_re.name: "c_re", c_im.name: "c_im"}

                def nm(t):
                    return lit.get(t.name, t.name)

                tmpE = {"dve": (tb1, tb2, "tb1", "tb2", tA, tB, "tA", "tB"),
                        "pool": (sb("tb1p", [128, 16, 16]), sb("tb2p", [128, 16, 16]), "tb1p", "tb2p",
                                 sm("tAp"), sm("tBp"), "tAp", "tBp")}

                def cmul_b(eng, o_re, o_im, on, x_re, x_im, xn, s_re, s_im, neg_im=False):
                    u1, u2, n1, n2 = tmpE[eng][0:4]
                    sr = s_re[:].to_broadcast([128, 16, 16])
                    si = s_im[:].to_broadcast([128, 16, 16])
                    O = lambda fn, r, w: S.op(eng, fn, reads=r, writes=w)
                    O(lambda e: e.tensor_tensor(out=u1[:], in0=x_re, in1=sr, op=ALU.mult), xn + [nm(s_re)], [n1])
                    O(lambda e: e.tensor_tensor(out=u2[:], in0=x_im, in1=si, op=ALU.mult), xn + [nm(s_im)], [n2])
                    O(lambda e: e.tensor_tensor(out=o_re, in0=u1[:], in1=u2[:], op=ALU.subtract), [n1, n2], [on])
                    O(lambda e: e.tensor_tensor(out=u1[:], in0=x_re, in1=si, op=ALU.mult), xn + [nm(s_im)], [n1])
                    O(lambda e: e.tensor_tensor(out=u2[:], in0=x_im, in1=sr, op=ALU.mult), xn + [nm(s_re)], [n2])
                    if neg_im:
                        O(lambda e: e.tensor_tensor(out=u1[:], in0=u1[:], in1=u2[:], op=ALU.add), [n1, n2], [n1])
                        O(lambda e: e.tensor_scalar(out=o_im, in0=u1[:], scalar1=-1.0, scalar2=None, op0=ALU.mult), [n1], [on])
                    else:
                        O(lambda e: e.tensor_tensor(out=o_im, in0=u1[:], in1=u2[:], op=ALU.add), [n1, n2], [on])

                def cmul_s(eng, o_re, o_im, x_re, x_im, s_re, s_im):
                    u1, u2, n1, n2 = tmpE[eng][4:8]
                    O = lambda fn, r, w: S.op(eng, fn, reads=r, writes=w)
                    O(lambda e: e.tensor_tensor(out=u1[:], in0=x_re[:], in1=s_re[:], op=ALU.mult), [nm(x_re), nm(s_re)], [n1])
                    O(lambda e: e.tensor_tensor(out=u2[:], in0=x_im[:], in1=s_im[:], op=ALU.mult), [nm(x_im), nm(s_im)], [n2])
                    O(lambda e: e.tensor_tensor(out=o_re[:], in0=u1[:], in1=u2[:], op=ALU.subtract), [n1, n2], [nm(o_re)])
                    O(lambda e: e.tensor_tensor(out=u1[:], in0=x_re[:], in1=s_im[:], op=ALU.mult), [nm(x_re), nm(s_im)], [n1])
                    O(lambda e: e.tensor_tensor(out=u2[:], in0=x_im[:], in1=s_re[:], op=ALU.mult), [nm(x_im), nm(s_re)], [n2])
                    O(lambda e: e.tensor_tensor(out=o_im[:], in0=u1[:], in1=u2[:], op=ALU.add), [n1, n2], [nm(o_im)])

                BBr = sb("BBr", [128, 16, 16])
                BBi = sb("BBi", [128, 16, 16])
                cmul_b("dve", BBr[:], BBi[:], "BB", BR[:], BI[:], ["BR", "BI"], c_re, c_im)
                pw_re = [sm("pwr%d" % i) for i in range(9)]
                pw_im = [sm("pwi%d" % i) for i in range(9)]
                iw_re = [sm("iwr%d" % i) for i in range(8)]
                iw_im = [sm("iwi%d" % i) for i in range(8)]
                CPr = sb("CPr", [128, 16, 9, 16])
                CPi = sb("CPi", [128, 16, 9, 16])
                for (eng, lst_re, lst_im) in (("dve", pw_re, pw_im), ("pool", iw_re, iw_im)):
                    S.op(eng, lambda e, t=lst_re[0]: e.memset(t[:], 1.0), writes=[lst_re[0].name])
                    S.op(eng, lambda e, t=lst_im[0]: e.memset(t[:], 0.0), writes=[lst_im[0].name])
                for i in range(1, 8):
                    cmul_s("pool", iw_re[i], iw_im[i], iw_re[i - 1], iw_im[i - 1], i_re, i_im)
                for tau in range(9):
                    if tau + 1 < 9:
                        cmul_s("dve", pw_re[tau + 1], pw_im[tau + 1], pw_re[tau], pw_im[tau], a_re, a_im)
                    cmul_b("dve", CPr[:, :, tau, :], CPi[:, :, tau, :], "CP%d" % tau, CR[:], CI[:], ["CR", "CI"],
                           pw_re[tau], pw_im[tau], neg_im=True)
                Qr = sb("Qr", [128, 16, 8, 16])
                Qi = sb("Qi", [128, 16, 8, 16])
                Hr = sb("Hr", [128, 16, 8, 16])
                Hi = sb("Hi", [128, 16, 8, 16])
                for s_ in range(8):
                    cmul_b("pool", Qr[:, :, s_, :], Qi[:, :, s_, :], "Q%d" % s_, BBr[:], BBi[:], ["BB"], iw_re[s_], iw_im[s_])
                    cmul_b("dve" if s_ < 3 else "pool", Hr[:, :, s_, :], Hi[:, :, s_, :], "H%d" % s_, BBr[:], BBi[:], ["BB"],
                           pw_re[7 - s_], pw_im[7 - s_])
                CPn = ["CP%d" % t for t in range(9)]
                Qn = ["Q%d" % t for t in range(8)]
                Hn = ["H%d" % t for t in range(8)]
                V(lambda e: e.tensor_copy(out=PCre[:].rearrange("p g (t c) -> p g t c", c=16), in_=CPr[:, :, 1:9, :]), r=CPn, w=["PCre"])
                V(lambda e: e.tensor_copy(out=PCim[:].rearrange("p g (t c) -> p g t c", c=16), in_=CPi[:, :, 1:9, :]), r=CPn, w=["PCim"])
                qw_re = [pw_re[8]] + [sm("qwr%d" % i) for i in range(1, 16)]
                qw_im = [pw_im[8]] + [sm("qwi%d" % i) for i in range(1, 16)]
                for jj in range(16):
                    if jj > 0:
                        cmul_s("pool", qw_re[jj], qw_im[jj], qw_re[jj - 1], qw_im[jj - 1], pw_re[8], pw_im[8])
                    G(lambda e, jj=jj: e.tensor_copy(out=APR[:, jj, :, 0:1], in_=qw_re[jj][:]), r=[qw_re[jj].name], w=["APR"])
                    G(lambda e, jj=jj: e.tensor_copy(out=APR[:, jj, :, 1:2], in_=qw_re[jj][:]), r=[qw_re[jj].name], w=["APR"])
                    G(lambda e, jj=jj: e.tensor_scalar(out=API[:, jj, :, 0:1], in0=qw_im[jj][:], scalar1=-1.0, scalar2=None,
                                                       op0=ALU.mult), r=[qw_im[jj].name], w=["API"])
                    G(lambda e, jj=jj: e.tensor_copy(out=API[:, jj, :, 1:2], in_=qw_im[jj][:]), r=[qw_im[jj].name], w=["API"])
                for gp in range(16):
                    for ci, (src, dst, dn_) in enumerate(((Hr, PGre, "PGre"), (Hi, PGim, "PGim"))):
                        bank = (2 * gp + ci) % 8
                        T(lambda e, src=src, gp=gp, bank=bank: e.transpose(
                            out=PS[bank][:, 0:128], in_=src[:, gp, :, :].rearrange("p s c -> p (s c)"), identity=ident[:]),
                          r=Hn + ["ident"], w=[psn[bank]])
                        fcp = (lambda e, dst=dst, gp=gp, bank=bank: e.tensor_copy(
                            out=dst[:, 2 * gp:2 * gp + 2, :], in_=PS[bank][:, 0:128].rearrange("p (j n) -> p j n", j=2)))
                        if ci == 0:
                            V(fcp, r=[psn[bank]], w=[dn_])
                        else:
                            A(lambda e, dst=dst, gp=gp, bank=bank: e.copy(
                                out=dst[:, 2 * gp:2 * gp + 2, :], in_=PS[bank][:, 0:128].rearrange("p (j n) -> p j n", j=2)),
                              r=[psn[bank]], w=[dn_])
                tmask = sb("tmask", [128, 128])
                ttmp = sb("ttmp", [128, 128])
                G(lambda e: e.memset(tmask[:], 1.0), w=["tmask"])
                G(lambda e: e.affine_select(out=tmask[:], in_=tmask[:], pattern=[[16, 8], [0, 16]], base=15, channel_multiplier=-1,
                                            compare_op=ALU.is_ge, fill=0.0), r=["tmask"], w=["tmask"])
                for g in range(32):
                    gp, j = g // 2, g % 2
                    bank = g % 8
                    pb = 64 * j
                    T(lambda e, gp=gp, pb=pb, bank=bank: e.matmul(PS[bank][:, 0:128], lhsT=Qr[pb:pb + 64, gp, :, :].rearrange("p s c -> p (s c)"),
                                                                   rhs=CPr[pb:pb + 64, gp, 0:8, :].rearrange("p s c -> p (s c)"), start=True, stop=False),
                      r=Qn + CPn, w=[psn[bank]])
                    T(lambda e, gp=gp, pb=pb, bank=bank: e.matmul(PS[bank][:, 0:128], lhsT=Qi[pb:pb + 64, gp, :, :].rearrange("p s c -> p (s c)"),
                                                                   rhs=CPi[pb:pb + 64, gp, 0:8, :].rearrange("p s c -> p (s c)"), start=False, stop=True),
                      r=Qn + CPn, w=[psn[bank]])
                    V(lambda e, bank=bank: e.tensor_tensor(out=ttmp[:], in0=PS[bank][:, 0:128], in1=tmask[:], op=ALU.mult),
                      r=[psn[bank], "tmask"], w=["ttmp"])
                    V(lambda e, g=g: e.scalar_tensor_tensor(out=Tm[:, g, :], in0=ident[:], scalar=dcol[:, g, :], in1=ttmp[:],
                                                            op0=ALU.mult, op1=ALU.add), r=["ident", "dcol", "ttmp"], w=["Tm"])
                if debug:
                    dT = sb("dT", [128, 32, 128])
                    V(lambda e: e.tensor_copy(out=dT[:], in_=Tm[:]), r=["Tm"], w=["dT"])
                    D(lambda e: e.dma_start(out=dbg["T"], in_=dT[:]), r=["dT"])
                S.barrier()

            for s in range(2):
                S.barrier()
                with ExitStack() as abc:
                    sb = mk(abc)
                    xT = sb("xT", [128, 8, SEQ], BF16)
                    wst = [None]
                    wbf = [None, None]
                    wctr = [0]

                    def load_w(c0, slot):
                        i = 0
                        D(lambda e: e.dma_start(out=wst[i][:], in_=win_d[:, c0:c0 + 128].rearrange("(k p) c -> p k c", p=128)),
                          w=["wsta%d" % i])
                        G(lambda e: e.tensor_tensor(out=wbf[slot][:], in0=wst[i][:], in1=mixg[:].to_broadcast([128, 8, 128]),
                                                    op=ALU.mult), r=["wsta%d" % i, "mixg"], w=["wbf%d" % slot])
                        return wbf[slot], "wbf%d" % slot

                    ck("ssmsetup")
                    with ExitStack() as a0:
                        sb0 = mk(a0)
                        xt = [sb0("xt%d" % i, [128, DM]) for i in range(4)]
                        sqs = [sb0("sq%d" % i, [128, DM]) for i in range(2)]
                        st4s = [sb0("st4%d" % i, [128, 4]) for i in range(2)]
                        dgs = [sb0("dg%d" % i, [128, 128]) for i in range(2)]
                        rbcs = [sb0("rbc%d" % i, [128, 128]) for i in range(2)]
                        def a0_load(i):
                            xi = xt[i % 4]
                            xn = "xt%d" % (i % 4)
                            sq, sqn = sqs[i % 2], "sq%d" % (i % 2)
                            D(lambda e: e.dma_start(out=xi[:], in_=x_d[s, i * 128:(i + 1) * 128, :]), w=[xn])
                            A(lambda e: e.activation(out=sq[:], in_=xi[:], func=AF.Square), r=[xn], w=[sqn])

                        def a0_stats(i):
                            p2 = i % 2
                            xi = xt[i % 4]
                            xn = "xt%d" % (i % 4)
                            sq, sqn = sqs[p2], "sq%d" % p2
                            st4, stn = st4s[p2], "st4%d" % p2
                            dg, dgn = dgs[p2], "dg%d" % p2
                            rbc, rbn = rbcs[p2], "rbc%d" % p2
                            rb = 2 + p2
                            V(lambda e: e.tensor_reduce(out=st4[:, 0:1], in_=sq[:], axis=AX.X, op=ALU.add), r=[sqn], w=[stn])
                            V(lambda e: e.tensor_scalar(out=st4[:, 1:2], in0=st4[:, 0:1], scalar1=1.0 / DM, scalar2=EPS,
                                                        op0=ALU.mult, op1=ALU.add), r=[stn], w=[stn])
                            G(lambda e: e.tensor_tensor(out=st4[:, 3:4], in0=st4[:, 1:2], in1=cneg[:, 0:1], op=ALU.pow),
                              r=[stn, "cneg"], w=[stn])

                        def a0_stats_b(i):
                            p2 = i % 2
                            st4, stn = st4s[p2], "st4%d" % p2
                            xi = xt[i % 4]
                            xn = "xt%d" % (i % 4)
                            A(lambda e: e.activation(out=xi[:], in_=xi[:], func=AF.Copy, scale=st4[:, 3:4]), r=[xn, stn], w=[xn])

                        def a0_xpose(i):
                            p2 = i % 2
                            xi = xt[i % 4]
                            xn = "xt%d" % (i % 4)
                            for half in range(2):
                                tbk = 4 * p2 + half
                                for q4 in range(4):
                                    kc = half * 4 + q4
                                    T(lambda e, kc=kc, q4=q4, tbk=tbk: e.transpose(
                                        out=PS[tbk][:, q4 * 128:(q4 + 1) * 128], in_=xi[:, kc * 128:(kc + 1) * 128], identity=ident[:]),
                                      r=[xn, "ident"], w=[psn[tbk]])
                                if half == 0:
                                    V(lambda e, half=half, tbk=tbk: e.tensor_copy(
                                        out=xT[:, half * 4:half * 4 + 4, i * 128:(i + 1) * 128],
                                        in_=PS[tbk][:].rearrange("p (k t) -> p k t", k=4)), r=[psn[tbk]], w=["xT"])
                                else:
                                    A(lambda e, half=half, tbk=tbk: e.copy(
                                        out=xT[:, half * 4:half * 4 + 4, i * 128:(i + 1) * 128],
                                        in_=PS[tbk][:].rearrange("p (k t) -> p k t", k=4)), r=[psn[tbk]], w=["xT"])

                        a0_load(0)
                        a0_load(1)
                        a0_stats(0)
                        a0_stats_b(0)
                        for i in range(16):
                            if i + 2 < 16:
                                a0_load(i + 2)
                            if i + 1 < 16:
                                a0_stats(i + 1)
                            a0_xpose(i)
                            if i + 1 < 16:
                                a0_stats_b(i + 1)
                        S.barrier()
                    if debug and s == 0:
                        with ExitStack() as dd:
                            d32 = mk(dd)("d32", [128, 8, SEQ])
                            V(lambda e: e.tensor_copy(out=d32[:], in_=xT[:]), r=["xT"], w=["d32"])
                            D(lambda e: e.dma_start(out=dbg["xT"], in_=d32[:]), r=["d32"])
                            S.barrier()

                    ck("A0")
                    with ExitStack() as bs:
                        sbB = mk(bs)
                        qT = [sbB("qT%d" % i, [128, SEQ], BF16) for i in range(2)]
                        kT = [sbB("kT%d" % i, [128, SEQ], BF16) for i in range(2)]
                        gaT = [sbB("gaT%d" % i, [128, SEQ], BF16) for i in range(2)]
                        Vaug = [sbB("Vaug%d" % i, [128, 16, 2, 128], BF16) for i in range(2)]
                        wB = [sbB("wB%d" % i, [128, 8, 128], BF16) for i in range(8)]
                        wst[0] = sbB("wstaB", [128, 8, 128])
                        sqb = [sbB("sqb%d" % i, [128, 512], BF16) for i in range(2)]
                        tt = [sbB("tt%d" % i, [128, 512]) for i in range(2)]
                        NE = 4
                        Eb = [sbB("Eb%d" % i, [128, 512], BF16) for i in range(NE)]
                        rd = [sbB("rd%d" % i, [128, 512]) for i in range(2)]
                        ot = [sbB("ot%d" % i, [128, 512]) for i in range(2)]
                        oc = [sbB("oc%d" % i, [128, 512]) for i in range(2)]
                        SCB = (4, 5, 6, 7)
                        for i in range(2):
                            V(lambda e, i=i: e.memset(Vaug[i][:], 1.0), w=["Vaug%d" % i])

                        def load_wB(c0, slot):
                            i = 0
                            D(lambda e: e.dma_start(out=wst[i][:], in_=win_d[:, c0:c0 + 128].rearrange("(k p) c -> p k c", p=128)),
                              w=["wsta%d" % i])
                            G(lambda e: e.tensor_tensor(out=wB[slot][:], in0=wst[i][:], in1=mixg[:].to_broadcast([128, 8, 128]),
                                                        op=ALU.mult), r=["wsta%d" % i, "mixg"], w=["wB%d" % slot])
                            return wB[slot], "wB%d" % slot

                        def prep_units(hp):
                            par = hp % 2
                            wq, wqn = load_wB(hp * 128, par * 4 + 0)
                            wk, wkn = load_wB(512 + hp * 128, par * 4 + 1)
                            yield
                            wv, wvn = load_wB(1024 + hp * 128, par * 4 + 2)
                            wa, wan = load_wB(1536 + hp * 128, par * 4 + 3)
                            yield
                            blocks = []
                            for (w_, wn_, dstT, dn, gain, gn) in ((wq, wqn, qT[par], "qT%d" % par, qg, "qg"),
                                                                  (wk, wkn, kT[par], "kT%d" % par, kg, "kg")):
                                for tg in range(4):
                                    blocks.append((w_, wn_, dstT, dn, gain, gn, tg))

                            def S1(bi):
                                w_, wn_, dstT, dn, gain, gn, tg = blocks[bi]
                                pbk = bi % 2
                                for kc in range(8):
                                    T(lambda e, kc=kc: e.matmul(PS[pbk][:], lhsT=w_[:, kc, :], rhs=xT[:, kc, tg * 512:(tg + 1) * 512],
                                                                start=(kc == 0), stop=(kc == 7)), r=[wn_, "xT"], w=[psn[pbk]])

                            def S2(bi):
                                pbk = bi % 2
                                sq_, sqn = sqb[bi % 2], "sqb%d" % (bi % 2)
                                t_, tn = tt[bi % 2], "tt%d" % (bi % 2)
                                A(lambda e: e.activation(out=sq_[:], in_=PS[pbk][:], func=AF.Square), r=[psn[pbk]], w=[sqn])
                                T(lambda e: e.matmul(PS[2][:], lhsT=blk1[:], rhs=sq_[:], start=True, stop=True),
                                  r=["blk1", sqn], w=[psn[2]])
                                V(lambda e: e.tensor_scalar(out=t_[:], in0=PS[2][:], scalar1=1.0 / 64.0, scalar2=EPS, op0=ALU.mult,
                                                            op1=ALU.add), r=[psn[2]], w=[tn])

                            def S3(bi):
                                w_, wn_, dstT, dn, gain, gn, tg = blocks[bi]
                                pbk = bi % 2
                                t_, tn = tt[bi % 2], "tt%d" % (bi % 2)
                                A(lambda e: e.activation(out=t_[:], in_=t_[:], func=AF.Ln), r=[tn], w=[tn])
                                A(lambda e: e.activation(out=t_[:], in_=t_[:], func=AF.Exp, scale=-0.5), r=[tn], w=[tn])
                                V(lambda e: e.scalar_tensor_tensor(out=dstT[:, tg * 512:(tg + 1) * 512], in0=PS[pbk][:],
                                                                   scalar=gain[:, 0:1], in1=t_[:], op0=ALU.mult, op1=ALU.mult),
                                  r=[psn[pbk], gn, tn], w=[dn])

                            for t in range(8 + 2):
                                if 0 <= t - 2 < 8:
                                    S3(t - 2)
                                if 0 <= t - 1 < 8:
                                    S2(t - 1)
                                if t < 8:
                                    S1(t)
                                yield

                            def G1(tg):
                                pbk = tg % 2
                                for kc in range(8):
                                    T(lambda e, kc=kc: e.matmul(PS[pbk][:], lhsT=wa[:, kc, :], rhs=xT[:, kc, tg * 512:(tg + 1) * 512],
                                                                start=(kc == 0), stop=(kc == 7)), r=[wan, "xT"], w=[psn[pbk]])

                            def G2(tg):
                                pbk = tg % 2
                                t_, tn = tt[tg % 2], "tt%d" % (tg % 2)
                                A(lambda e: e.activation(out=t_[:], in_=PS[pbk][:], func=AF.Exp, scale=-1.0), r=[psn[pbk]], w=[tn])
                                A(lambda e: e.activation(out=t_[:], in_=t_[:], func=AF.Ln, bias=1.0), r=[tn], w=[tn])
                                A(lambda e: e.activation(out=t_[:], in_=t_[:], func=AF.Exp, scale=-1.0), r=[tn], w=[tn])

                            def G3(tg):
                                pbk = tg % 2
                                t_, tn = tt[tg % 2], "tt%d" % (tg % 2)
                                V(lambda e: e.tensor_tensor(out=gaT[par][:, tg * 512:(tg + 1) * 512], in0=PS[pbk][:], in1=t_[:],
                                                            op=ALU.mult), r=[tn, psn[pbk]], w=["gaT%d" % par])

                            for t in range(4 + 2):
                                if 0 <= t - 2 < 4:
                                    G3(t - 2)
                                if 0 <= t - 1 < 4:
                                    G2(t - 1)
                                if t < 4:
                                    G1(t)
                                yield

                            def V1(i4):
                                pbk = i4 % 2
                                for ii in range(4):
                                    i = i4 * 4 + ii
                                    for kc in range(8):
                                        T(lambda e, kc=kc, i=i, ii=ii: e.matmul(
                                            PS[pbk][:, ii * 128:(ii + 1) * 128], lhsT=xT[:, kc, i * 128:(i + 1) * 128], rhs=wv[:, kc, :],
                                            start=(kc == 0), stop=(kc == 7)), r=[wvn, "xT"], w=[psn[pbk]])

                            def V2(i4):
                                pbk = i4 % 2
                                pv = PS[pbk][:].rearrange("p (i c) -> p i c", i=4)
                                V(lambda e: e.tensor_copy(out=Vaug[par][:, i4 * 4:i4 * 4 + 4, 0, 0:64], in_=pv[:, :, 0:64]),
                                  r=[psn[pbk]], w=["Vaug%d" % par])
                                A(lambda e: e.copy(out=Vaug[par][:, i4 * 4:i4 * 4 + 4, 1, 64:128], in_=pv[:, :, 64:128]),
                                  r=[psn[pbk]], w=["Vaug%d" % par])

                            for t in range(4 + 1):
                                if 0 <= t - 1 < 4:
                                    V2(t - 1)
                                if t < 4:
                                    V1(t)
                                yield

                        def core_units(hp):
                            par = hp % 2
                            qn_, kn_, gn_, vn_ = "qT%d" % par, "kT%d" % par, "gaT%d" % par, "Vaug%d" % par
                            blocks = []
                            for h2 in range(2):
                                for Qg in range(4):
                                    nkb = 4 * Qg + 4
                                    for kb in range(nkb):
                                        blocks.append((h2, Qg, kb, nkb))
                            N = len(blocks)
                            NP = N // 2
                            for pi in range(NP + 1):
                                if pi < NP:
                                    qk_ops = []
                                    for idx in (2 * pi, 2 * pi + 1):
                                        h2, Qg, kb, nkb = blocks[idx]
                                        pb = 64 * h2
                                        sbk = SCB[idx % 4]
                                        off = Qg * 512 - kb * 128 + 384
                                        c0 = max(0, kb - 4 * Qg) * 128
                                        qk_ops.append((lambda e, kb=kb, Qg=Qg, sbk=sbk, pb=pb, c0=c0: e.matmul(
                                            PS[sbk][:, c0:512], lhsT=kT[par][pb:pb + 64, kb * 128:(kb + 1) * 128],
                                            rhs=qT[par][pb:pb + 64, Qg * 512 + c0:(Qg + 1) * 512], start=True, stop=False),
                                            [kn_, qn_], [psn[sbk]]))
                                        qk_ops.append((lambda e, sbk=sbk, off=off, c0=c0: e.matmul(
                                            PS[sbk][:, c0:512], lhsT=(identr[:].bitcast(F32R) if EXACT_MASK else identr[:]),
                                            rhs=(MT[:, off + c0:off + 512].bitcast(F32R) if EXACT_MASK else MT[:, off + c0:off + 512]),
                                            start=False, stop=True), ["identr", "MT"], [psn[sbk]]))
                                    S.group("pe", qk_ops)
                                if pi >= 1:
                                    pv_ops = []
                                    for j in (2 * pi - 2, 2 * pi - 1):
                                        h2, Qg, kb, nkb = blocks[j]
                                        eb = j % NE
                                        c0 = max(0, kb - 4 * Qg) * 128
                                        pv_ops.append((lambda e, kb=kb, h2=h2, eb=eb, nkb=nkb, c0=c0: e.matmul(
                                            PS[3][:, c0:512], lhsT=Vaug[par][:, kb, h2, :], rhs=Eb[eb][:, c0:512],
                                            start=(kb == 0), stop=(kb == nkb - 1)), [vn_, "Eb%d" % eb], [psn[3]]))
                                    pv_grp = pv_ops
                                else:
                                    pv_grp = None
                                if pi < NP:
                                    for idx in (2 * pi, 2 * pi + 1):
                                        h2, Qg, kb, nkb = blocks[idx]
                                        c0 = max(0, kb - 4 * Qg) * 128
                                        sbk = SCB[idx % 4]
                                        eb = idx % NE
                                        A(lambda e, sbk=sbk, eb=eb, c0=c0: e.activation(out=Eb[eb][:, c0:512], in_=PS[sbk][:, c0:512],
                                                                                        func=AF.Exp, scale=1.0 / ALPHA),
                                          r=[psn[sbk]], w=["Eb%d" % eb])
                                if pv_grp is not None:
                                    S.group("pe", pv_grp)
                                if pi >= 1:
                                    h2, Qg, kb, nkb = blocks[2 * pi - 1]
                                    ob = 3
                                    if kb == nkb - 1:
                                        fi = (h2 * 4 + Qg) % 2
                                        rd_, rdn = rd[fi], "rd%d" % fi
                                        ot_, otn = ot[fi], "ot%d" % fi
                                        oc_, ocn = oc[fi], "oc%d" % fi
                                        dlo, olo = (64, 0) if h2 == 0 else (0, 64)
                                        V(lambda e, oc_=oc_: e.tensor_copy(out=oc_[:], in_=PS[ob][:]), r=[psn[ob]], w=[ocn])
                                        A(lambda e, dlo=dlo, olo=olo, rd_=rd_, oc_=oc_: e.activation(
                                            out=rd_[olo:olo + 64, :], in_=oc_[dlo:dlo + 64, :], func=AF.Ln), r=[ocn], w=[rdn])
                                        A(lambda e, olo=olo, rd_=rd_: e.activation(out=rd_[olo:olo + 64, :], in_=rd_[olo:olo + 64, :],
                                                                                   func=AF.Exp, scale=-1.0), r=[rdn], w=[rdn])
                                        V(lambda e, olo=olo, rd_=rd_, ot_=ot_, oc_=oc_: e.tensor_tensor(
                                            out=ot_[olo:olo + 64, :], in0=oc_[olo:olo + 64, :], in1=rd_[olo:olo + 64, :],
                                            op=ALU.mult), r=[ocn, rdn], w=[otn])
                                        G(lambda e, olo=olo, Qg=Qg, ot_=ot_: e.tensor_tensor(
                                            out=attnT[olo:olo + 64, hp, Qg * 512:(Qg + 1) * 512], in0=ot_[olo:olo + 64, :],
                                            in1=gaT[par][olo:olo + 64, Qg * 512:(Qg + 1) * 512], op=ALU.mult),
                                          r=[otn, gn_], w=["attnT"])
                                yield

                        for _ in prep_units(0):
                            pass
                        for hp in range(4):
                            nxt = prep_units(hp + 1) if hp < 3 else None
                            for si, _ in enumerate(core_units(hp)):
                                if nxt is not None and si % 2 == 1:
                                    try:
                                        next(nxt)
                                    except StopIteration:
                                        nxt = None
                            if nxt is not None:
                                for _ in nxt:
                                    pass
                        S.barrier()
                    ck("B")
                    with ExitStack() as cs_:
                        sbC0 = mk(cs_)
                        ygT = sbC0("ygT", [128, 4, SEQ], BF16)
                        c12 = ExitStack()
                        sbC = mk(c12)
                        Ug = sbC("Ug", [128, 32, 256], BF16)
                        Sb = sbC("Sb", [128, 16, 2, 16, 16])
                        sc_scope = ExitStack()
                        scA = [[mk(sc_scope)("scA%d_%d" % (c, i), [128, (12, 4)[c], 2, 16]) for i in range(2)] for c in range(2)]
                        u_scope = ExitStack()
                        sbU = mk(u_scope)
                        U = sbU("U", [128, 4, 8, 8, 16])
                        wst[0] = sbU("wstaC", [128, 8, 128])
                        wu_all = sbU("wu_all", [128, 8, 512], BF16)
                        for cb in range(4):
                            c0w = 2048 + cb * 128
                            D(lambda e, c0w=c0w: e.dma_start(out=wst[0][:], in_=win_d[:, c0w:c0w + 128].rearrange("(k p) c -> p k c", p=128)),
                              w=["wsta0"])
                            G(lambda e, cb=cb: e.tensor_tensor(out=wu_all[:, :, cb * 128:(cb + 1) * 128], in0=wst[0][:],
                                                               in1=mixg[:].to_broadcast([128, 8, 128]), op=ALU.mult),
                              r=["wsta0", "mixg"], w=["wu_all"])
                        for ct in range(2):
                            for sp_ in range(8):
                                bank = sp_ % 2
                                for kc in range(8):
                                    T(lambda e, kc=kc, sp_=sp_, bank=bank: e.matmul(
                                        PS[bank][:], lhsT=xT[:, kc, ct * 1024 + sp_:(ct + 1) * 1024:8], rhs=wu_all[:, kc, :],
                                        start=(kc == 0), stop=(kc == 7)), r=["wu_all", "xT"], w=[psn[bank]])
                                fev = (lambda e, sp_=sp_, bank=bank: e.copy(
                                    out=U[:, :, :, sp_, :], in_=PS[bank][:].rearrange("p (b g c) -> p b g c", b=4, g=8)))
                                A(fev, r=[psn[bank]], w=["U"])
                            for g4 in range(8):
                                bank = 2 + g4 % 2
                                for gi in range(4):
                                    g = g4 * 4 + gi
                                    T(lambda e, g=g, gi=gi, bank=bank: e.transpose(
                                        out=PS[bank][:, gi * 128:(gi + 1) * 128],
                                        in_=U[:, g // 8, g % 8, :, :].rearrange("p s c -> p (s c)"), identity=ident[:]),
                                      r=["U", "ident"], w=[psn[bank]])
                                V(lambda e, g4=g4, bank=bank: e.tensor_copy(out=Ug[:, g4 * 4:g4 * 4 + 4, ct * 128:(ct + 1) * 128],
                                                                            in_=PS[bank][:].rearrange("p (g k) -> p g k", g=4)),
                                  r=[psn[bank]], w=["Ug"])
                        for gp in range(16):
                            bank = 4 + gp % 2
                            for j in range(2):
                                g = 2 * gp + j
                                T(lambda e, g=g, j=j, bank=bank: e.matmul(PS[bank][64 * j:64 * j + 64, 0:256], lhsT=PGre[:, g, :],
                                                                          rhs=Ug[:, g, :], start=True, stop=True),
                                  r=["PGre", "Ug"], w=[psn[bank]])
                                T(lambda e, g=g, j=j, bank=bank: e.matmul(PS[bank][64 * j:64 * j + 64, 256:512], lhsT=PGim[:, g, :],
                                                                          rhs=Ug[:, g, :], start=True, stop=True),
                                  r=["PGim", "Ug"], w=[psn[bank]])
                            A(lambda e, gp=gp, bank=bank: e.copy(out=Sb[:, gp, :, :, :],
                                                                 in_=PS[bank][:].rearrange("p (c K j) -> p c j K", c=2, K=16)),
                              r=[psn[bank]], w=["Sb%d" % (0 if gp < 12 else 1)])
                        S.barrier()
                        u_scope.close()
                        ck("C1")
                        SPLIT = ((0, 12, "dve"), (12, 16, "pool"))
                        for ci, (g0, g1, eng) in enumerate(SPLIT):
                            ng = g1 - g0
                            ta, tb_ = scA[ci]
                            na, nb = "scA%d_0" % ci, "scA%d_1" % ci
                            rn = "Sb%d" % ci
                            def colv(j, k0, nk, rev=False, g0=g0, g1=g1):
                                cs_ = slice(None, None, -1) if rev else slice(None)
                                return Sb[:, g0:g1, cs_, j, k0:k0 + nk]
                            for j in range(1, 16):
                                prev, prev_sw, cur = colv(j - 1, 0, 16), colv(j - 1, 0, 16, True), colv(j, 0, 16)
                                ar = APR[:, 0, g0:g1, :].unsqueeze(3).to_broadcast([128, ng, 2, 16])
                                ai = API[:, 0, g0:g1, :].unsqueeze(3).to_broadcast([128, ng, 2, 16])
                                S.op(eng, lambda e, prev=prev, ar=ar: e.tensor_tensor(out=ta[:], in0=prev, in1=ar, op=ALU.mult),
                                     reads=[rn, "APR"], writes=[na])
                                S.op(eng, lambda e, prev_sw=prev_sw, ai=ai: e.tensor_tensor(out=tb_[:], in0=prev_sw, in1=ai, op=ALU.mult),
                                     reads=[rn, "API"], writes=[nb])
                                S.op(eng, lambda e: e.tensor_tensor(out=ta[:], in0=ta[:], in1=tb_[:], op=ALU.add), reads=[na, nb], writes=[na])
                                S.op(eng, lambda e, cur=cur: e.tensor_tensor(out=cur, in0=cur, in1=ta[:], op=ALU.add), reads=[na, rn], writes=[rn])
                            for K in range(1, 16):
                                prev, prev_sw, cur = colv(15, K - 1, 1), colv(15, K - 1, 1, True), colv(15, K, 1)
                                ar = APR[:, 15, g0:g1, :].unsqueeze(3)
                                ai = API[:, 15, g0:g1, :].unsqueeze(3)
                                S.op(eng, lambda e, prev=prev, ar=ar: e.tensor_tensor(out=ta[:, :, :, 0:1], in0=prev, in1=ar, op=ALU.mult),
                                     reads=[rn, "APR"], writes=[na])
                                S.op(eng, lambda e, prev_sw=prev_sw, ai=ai: e.tensor_tensor(out=tb_[:, :, :, 0:1], in0=prev_sw, in1=ai,
                                                                                         op=ALU.mult), reads=[rn, "API"], writes=[nb])
                                S.op(eng, lambda e: e.tensor_tensor(out=ta[:, :, :, 0:1], in0=ta[:, :, :, 0:1], in1=tb_[:, :, :, 0:1],
                                                                    op=ALU.add), reads=[na, nb], writes=[na])
                                S.op(eng, lambda e, cur=cur: e.tensor_tensor(out=cur, in0=cur, in1=ta[:, :, :, 0:1], op=ALU.add),
                                     reads=[na, rn], writes=[rn])
                            for j in range(15):
                                prev, prev_sw, cur = colv(15, 0, 15), colv(15, 0, 15, True), colv(j, 1, 15)
                                ar = APR[:, j, g0:g1, :].unsqueeze(3).to_broadcast([128, ng, 2, 15])
                                ai = API[:, j, g0:g1, :].unsqueeze(3).to_broadcast([128, ng, 2, 15])
                                S.op(eng, lambda e, prev=prev, ar=ar: e.tensor_tensor(out=ta[:, :, :, 0:15], in0=prev, in1=ar, op=ALU.mult),
                                     reads=[rn, "APR"], writes=[na])
                                S.op(eng, lambda e, prev_sw=prev_sw, ai=ai: e.tensor_tensor(out=tb_[:, :, :, 0:15], in0=prev_sw, in1=ai,
                                                                                         op=ALU.mult), reads=[rn, "API"], writes=[nb])
                                S.op(eng, lambda e: e.tensor_tensor(out=ta[:, :, :, 0:15], in0=ta[:, :, :, 0:15], in1=tb_[:, :, :, 0:15],
                                                                    op=ALU.add), reads=[na, nb], writes=[na])
                                S.op(eng, lambda e, cur=cur: e.tensor_tensor(out=cur, in0=cur, in1=ta[:, :, :, 0:15], op=ALU.add),
                                     reads=[na, rn], writes=[rn])
                        S.barrier()
                        sc_scope.close()
                        Sh = sbC("Sh", [128, 16, 2, 256], BF16)
                        for ci, (g0, g1, eng) in enumerate(SPLIT):
                            rn = "Sb%d" % ci
                            S.op(eng, lambda e, g0=g0, g1=g1: e.memset(Sh[:, g0:g1, :, 0:1], 0.0), writes=["Sh%d" % ci])
                            for c in range(2):
                                S.op(eng, lambda e, g0=g0, g1=g1, c=c: e.tensor_copy(
                                    out=Sh[:, g0:g1, c, 1:241].rearrange("p g (K j) -> p g K j", j=16),
                                    in_=Sb[:, g0:g1, c, :, 0:15].rearrange("p g j K -> p g K j")), reads=[rn], writes=["Sh%d" % ci])
                                S.op(eng, lambda e, g0=g0, g1=g1, c=c: e.tensor_copy(
                                    out=Sh[:, g0:g1, c, 241:256], in_=Sb[:, g0:g1, c, 0:15, 15]), reads=[rn], writes=["Sh%d" % ci])
                        ck("C2")
                        with ExitStack() as c3:
                            sb3 = mk(c3)
                            Ysbs = [sb3("Ysb%d" % i, [128, 2, 128]) for i in range(2)]
                            YG = sb3("YG", [128, 8, 512])
                            def mm3(ct, gp):
                                bank = gp % 2
                                for j in range(2):
                                    g = 2 * gp + j
                                    pb = 64 * j
                                    o_ = PS[bank][:, j * 128:(j + 1) * 128]
                                    T(lambda e, g=g, o_=o_: e.matmul(o_, lhsT=Tm[:, g, :], rhs=Ug[:, g, ct * 128:(ct + 1) * 128],
                                                                     start=True, stop=False), r=["Tm", "Ug"], w=[psn[bank]])
                                    T(lambda e, pb=pb, o_=o_: e.matmul(
                                        o_, lhsT=PCre[pb:pb + 64, gp, :], rhs=Sh[pb:pb + 64, gp, 0, ct * 128:(ct + 1) * 128],
                                        start=False, stop=False), r=["PCre", "Sh0", "Sh1"], w=[psn[bank]])
                                    T(lambda e, pb=pb, o_=o_: e.matmul(
                                        o_, lhsT=PCim[pb:pb + 64, gp, :], rhs=Sh[pb:pb + 64, gp, 1, ct * 128:(ct + 1) * 128],
                                        start=False, stop=True), r=["PCim", "Sh0", "Sh1"], w=[psn[bank]])

                            def ev3(ct, gp):
                                bank = gp % 2
                                Ysb, Ysn = Ysbs[gp % 2], "Ysb%d" % (gp % 2)
                                A(lambda e: e.copy(out=Ysb[:], in_=PS[bank][:, 0:256].rearrange("p (j k) -> p j k", j=2)),
                                  r=[psn[bank]], w=[Ysn])
                                tb = 2 + gp % 2
                                for j in range(2):
                                    T(lambda e, j=j: e.transpose(out=PS[tb][:, j * 128:(j + 1) * 128], in_=Ysb[:, j, :],
                                                                 identity=ident[:]), r=[Ysn, "ident"], w=[psn[tb]])
                                for j in range(2):
                                    g = 2 * gp + j
                                    A(lambda e, j=j, g=g: e.activation(
                                        out=YG[:, :, g * 16:(g + 1) * 16],
                                        in_=PS[tb][:, j * 128:(j + 1) * 128].rearrange("p (t c) -> p t c", t=8),
                                        func=AF.Gelu), r=[psn[tb]], w=["YG"])

                            for ct in range(2):
                                mm3(ct, 0)
                                for gp in range(16):
                                    if gp + 1 < 16:
                                        mm3(ct, gp + 1)
                                    ev3(ct, gp)
                                for tp in range(8):
                                    bank = 4 + tp % 2
                                    for cbi in range(4):
                                        T(lambda e, tp=tp, cbi=cbi, bank=bank: e.transpose(
                                            out=PS[bank][:, cbi * 128:(cbi + 1) * 128], in_=YG[:, tp, cbi * 128:(cbi + 1) * 128],
                                            identity=ident[:]), r=["YG", "ident"], w=[psn[bank]])
                                    col = (ct * 8 + tp) * 128
                                    V(lambda e, bank=bank, col=col: e.tensor_copy(out=ygT[:, :, col:col + 128],
                                                                                  in_=PS[bank][:].rearrange("p (a t) -> p a t", a=4)),
                                      r=[psn[bank]], w=["ygT"])
                            S.barrier()
                        S.barrier()
                        c12.close()
                        ck("C3")
                        with ExitStack() as c5:
                            sb5 = mk(c5)
                            wgl = sb5("wgl", [128, 4, 512], BF16)
                            wst[0] = sb5("wsta5", [128, 8, 128])
                            wbf[0] = sb5("wbf5a", [128, 8, 128], BF16)
                            wbf[1] = sb5("wbf5b", [128, 8, 128], BF16)
                            wgs_st = sb5("wgs_st", [128, 512])
                            sig = [sb5("sig%d" % i, [128, 512]) for i in range(2)]
                            gsl = [sb5("gsl%d" % i, [128, 512], BF16) for i in range(2)]
                            gth = [sb5("gth%d" % i, [128, 512]) for i in range(2)]
                            tm5 = [sb5("tm5%d" % i, [128, 512]) for i in range(2)]
                            for kc in range(4):
                                D(lambda e, kc=kc: e.dma_start(out=wgs_st[:], in_=wglu_d[kc * 128:(kc + 1) * 128, :]), w=["wgs_st"])
                                V(lambda e, kc=kc: e.tensor_copy(out=wgl[:, kc, :], in_=wgs_st[:]), r=["wgs_st"], w=["wgl"])
                            steps = [(cb, tg) for cb in range(4) for tg in range(4)]
                            wcur = {}

                            def pvw(ap, half):
                                return ap.rearrange("p (t q) -> p t q", t=8)[:, :, half * 64:(half + 1) * 64]

                            def mm5(si):
                                cb, tg = steps[si]
                                if tg == 0:
                                    wcur[cb] = load_w(2560 + cb * 128, cb % 2)
                                wgs, wgsn = wcur[cb]
                                ct, half = tg // 2, tg % 2
                                b0, b1 = 2 * (si % 4), 2 * (si % 4) + 1
                                for kc in range(4):
                                    T(lambda e, kc=kc: e.matmul(PS[b0][:], lhsT=wgl[:, kc, cb * 128:(cb + 1) * 128],
                                                                rhs=pvw(ygT[:, kc, ct * 1024:(ct + 1) * 1024], half),
                                                                start=(kc == 0), stop=(kc == 3)), r=["wgl", "ygT"], w=[psn[b0]])
                                for kc in range(8):
                                    T(lambda e, kc=kc: e.matmul(PS[b1][:], lhsT=wgs[:, kc, :], rhs=xT[:, kc, tg * 512:(tg + 1) * 512],
                                                                start=(kc == 0), stop=(kc == 7)), r=[wgsn, "xT"], w=[psn[b1]])

                            def ev5(si):
                                cb, tg = steps[si]
                                ct, half = tg // 2, tg % 2
                                i2 = si % 2
                                b0, b1 = 2 * (si % 4), 2 * (si % 4) + 1
                                sg, sgn = sig[i2], "sig%d" % i2
                                gh, ghn = gth[i2], "gth%d" % i2
                                gl, gln = gsl[i2], "gsl%d" % i2
                                t5, t5n = tm5[i2], "tm5%d" % i2
                                A(lambda e: e.activation(out=sg[:], in_=PS[b0][:], func=AF.Sigmoid, bias=bglu[:, cb, :]),
                                  r=[psn[b0], "bglu"], w=[sgn])
                                A(lambda e: e.activation(out=gh[:], in_=PS[b1][:], func=AF.Sigmoid), r=[psn[b1]], w=[ghn])
                                V(lambda e: e.tensor_tensor(out=gl[:].rearrange("p (t q) -> p q t", t=8),
                                                            in0=PS[b1][:].rearrange("p (q t) -> p q t", t=8),
                                                            in1=gh[:].rearrange("p (q t) -> p q t", t=8), op=ALU.mult),
                                  r=[ghn, psn[b1]], w=[gln])
                                G(lambda e: e.tensor_tensor(out=t5[:].rearrange("p (t q) -> p t q", t=8),
                                                            in0=pvw(ygT[:, cb, ct * 1024:(ct + 1) * 1024], half),
                                                            in1=sg[:].rearrange("p (t q) -> p t q", t=8), op=ALU.mult),
                                  r=["ygT", sgn], w=[t5n])
                                G(lambda e: e.tensor_tensor(out=pvw(ssmT[:, cb, ct * 1024:(ct + 1) * 1024], half),
                                                            in0=t5[:].rearrange("p (t q) -> p t q", t=8),
                                                            in1=gl[:].rearrange("p (t q) -> p t q", t=8), op=ALU.mult),
                                  r=[t5n, gln], w=["ssmT"])

                            mm5(0)
                            mm5(1)
                            mm5(2)
                            for si in range(16):
                                if si + 3 < 16:
                                    mm5(si + 3)
                                ev5(si)
                            if debug and s == 0:
                                with ExitStack() as dd:
                                    d32 = mk(dd)("d32y", [128, 4, SEQ])
                                    for nm, src in (("attnT", attnT), ("ssmT", ssmT), ("ygT", ygT)):
                                        S.barrier()
                                        V(lambda e, src=src: e.tensor_copy(out=d32[:], in_=src[:]), w=["d32y"])
                                        D(lambda e, nm=nm: e.dma_start(out=dbg[nm], in_=d32[:]), r=["d32y"])
                                    S.barrier()
                            S.barrier()
                    S.barrier()

                ck("C5")
                S.barrier()
                with ExitStack() as ds:
                    sbD = mk(ds)
                    Wout = sbD("Wout", [128, 8, DM], BF16)
                    Wg = sbD("Wg", [128, 8, DM], BF16)
                    Wp = sbD("Wp", [128, 2, DM], BF16)
                    wstd = [sbD("wstd%d" % i, [128, DM]) for i in range(2)]
                    cnt = 0
                    for (src, dst, dn_, nk, fold) in ((wout_d, Wout, "Wout", 8, None), (wg_d, Wg, "Wg", 8, pleg), (wp_d, Wp, "Wp", 2, None)):
                        for kc in range(nk):
                            st = wstd[cnt % 2]
                            rn = "wstd%d" % (cnt % 2)
                            D(lambda e, st=st, src=src, kc=kc: e.dma_start(out=st[:], in_=src[kc * 128:(kc + 1) * 128, :]), w=[rn])
                            if fold is None:
                                if cnt % 2 == 0:
                                    V(lambda e, st=st, dst=dst, kc=kc: e.tensor_copy(out=dst[:, kc, :], in_=st[:]), r=[rn], w=[dn_])
                                else:
                                    A(lambda e, st=st, dst=dst, kc=kc: e.copy(out=dst[:, kc, :], in_=st[:]), r=[rn], w=[dn_])
                            else:
                                V(lambda e, st=st, dst=dst, kc=kc, fold=fold: e.tensor_scalar(
                                    out=dst[:, kc, :], in0=st[:], scalar1=fold[:, kc, :], scalar2=None, op0=ALU.mult),
                                  r=[rn, "pleg"], w=[dn_])
                            cnt += 1
                    xd = [sbD("xd%d" % i, [128, DM]) for i in range(3)]
                    pd = [sbD("pd%d" % i, [128, 256]) for i in range(3)]
                    hh = [sbD("hh%d" % i, [128, DM]) for i in range(2)]
                    sq = sbD("sqd", [128, DM])
                    st4 = [sbD("st4d%d" % i, [128, 4]) for i in range(2)]
                    hT = [sbD("hT%d" % i, [128, 8, 128], BF16) for i in range(2)]
                    pT = [sbD("pT%d" % i, [128, 2, 128], BF16) for i in range(2)]
                    gate = [sbD("gate%d" % i, [128, DM]) for i in range(2)]
                    oo = [sbD("oo%d" % i, [128, DM]) for i in range(2)]

                    def XL(it):
                        ct, tp = it // 8, it % 8
                        xi, xn = xd[it % 3], "xd%d" % (it % 3)
                        pi_, pn = pd[it % 3], "pd%d" % (it % 3)
                        r0 = ct * 1024 + tp
                        D(lambda e: e.dma_start(out=xi[:], in_=x_d[s, r0:(ct + 1) * 1024:8, :]), w=[xn])
                        D(lambda e: e.dma_start(out=pi_[:], in_=p_d[s, r0:(ct + 1) * 1024:8, :]), w=[pn])

                    def X1(it):
                        ct, tp = it // 8, it % 8
                        b2 = it % 2
                        xi, xn = xd[it % 3], "xd%d" % (it % 3)
                        h_, hn = hh[b2], "hh%d" % b2
                        s_, sn_ = st4[b2], "st4d%d" % b2
                        r0 = ct * 1024 + tp
                        col = it * 128
                        for nh in range(2):
                            for kc in range(8):
                                if kc < 4:
                                    lt = attnT[:, kc, r0:(ct + 1) * 1024:8]
                                    rn = "attnT"
                                else:
                                    lt = ssmT[:, kc - 4, col:col + 128]
                                    rn = "ssmT"
                                T(lambda e, lt=lt, kc=kc, nh=nh: e.matmul(PS[nh][:], lhsT=lt, rhs=Wout[:, kc, nh * 512:(nh + 1) * 512],
                                                                          start=(kc == 0), stop=(kc == 7)), r=[rn, "Wout"], w=[psn[nh]])
                            V(lambda e, nh=nh: e.tensor_tensor(out=h_[:, nh * 512:(nh + 1) * 512], in0=PS[nh][:],
                                                               in1=xi[:, nh * 512:(nh + 1) * 512], op=ALU.add),
                              r=[psn[nh], xn], w=[hn])
                        A(lambda e: e.activation(out=sq[:], in_=h_[:], func=AF.Square), r=[hn], w=["sqd"])
                        V(lambda e: e.tensor_reduce(out=s_[:, 0:1], in_=sq[:], axis=AX.X, op=ALU.add), r=["sqd"], w=[sn_])
                        V(lambda e: e.tensor_scalar(out=s_[:, 1:2], in0=s_[:, 0:1], scalar1=1.0 / DM, scalar2=EPS, op0=ALU.mult,
                                                    op1=ALU.add), r=[sn_], w=[sn_])
                        G(lambda e: e.tensor_tensor(out=s_[:, 3:4], in0=s_[:, 1:2], in1=cneg[:, 0:1], op=ALU.pow),
                          r=[sn_, "cneg"], w=[sn_])

                    def X2(it):
                        b2 = it % 2
                        pi_, pn = pd[it % 3], "pd%d" % (it % 3)
                        h_, hn = hh[b2], "hh%d" % b2
                        hT_, hTn = hT[b2], "hT%d" % b2
                        pT_, pTn = pT[b2], "pT%d" % b2
                        for half in range(2):
                            for q4 in range(4):
                                kc = half * 4 + q4
                                T(lambda e, kc=kc, q4=q4, half=half: e.transpose(out=PS[2 + half][:, q4 * 128:(q4 + 1) * 128],
                                                                                 in_=h_[:, kc * 128:(kc + 1) * 128], identity=ident[:]),
                                  r=[hn, "ident"], w=[psn[2 + half]])
                            A(lambda e, half=half: e.copy(out=hT_[:, half * 4:half * 4 + 4, :],
                                                          in_=PS[2 + half][:].rearrange("p (k t) -> p k t", k=4)),
                              r=[psn[2 + half]], w=[hTn])
                        for q2 in range(2):
                            T(lambda e, q2=q2: e.transpose(out=PS[4][:, q2 * 128:(q2 + 1) * 128], in_=pi_[:, q2 * 128:(q2 + 1) * 128],
                                                           identity=ident[:]), r=[pn, "ident"], w=[psn[4]])
                        V(lambda e: e.tensor_copy(out=pT_[:], in_=PS[4][:, 0:256].rearrange("p (k t) -> p k t", k=2)),
                          r=[psn[4]], w=[pTn])

                    def Y(it):
                        ct, tp = it // 8, it % 8
                        b2 = it % 2
                        h_, hn = hh[b2], "hh%d" % b2
                        s_, sn_ = st4[b2], "st4d%d" % b2
                        hT_, hTn = hT[b2], "hT%d" % b2
                        pT_, pTn = pT[b2], "pT%d" % b2
                        g_, gn = gate[b2], "gate%d" % b2
                        oi, on = oo[b2], "oo%d" % b2
                        r0 = ct * 1024 + tp
                        for nh in range(2):
                            for kc in range(8):
                                T(lambda e, kc=kc, nh=nh: e.matmul(PS[5][:], lhsT=hT_[:, kc, :], rhs=Wg[:, kc, nh * 512:(nh + 1) * 512],
                                                                   start=(kc == 0), stop=(kc == 7)), r=[hTn, "Wg"], w=[psn[5]])
                            A(lambda e, nh=nh: e.activation(out=g_[:, nh * 512:(nh + 1) * 512], in_=PS[5][:], func=AF.Sigmoid,
                                                            scale=s_[:, 3:4]), r=[psn[5], sn_], w=[gn])
                            for kc in range(2):
                                T(lambda e, kc=kc, nh=nh: e.matmul(PS[6 + nh][:], lhsT=pT_[:, kc, :], rhs=Wp[:, kc, nh * 512:(nh + 1) * 512],
                                                                   start=(kc == 0), stop=(kc == 1)), r=[pTn, "Wp"], w=[psn[6 + nh]])
                            V(lambda e, nh=nh: e.tensor_tensor(out=oi[:, nh * 512:(nh + 1) * 512], in0=PS[6 + nh][:],
                                                               in1=g_[:, nh * 512:(nh + 1) * 512], op=ALU.mult),
                              r=[psn[6 + nh], gn], w=[on])
                        V(lambda e: e.tensor_tensor(out=oi[:], in0=oi[:], in1=h_[:], op=ALU.add), r=[on, hn], w=[on])
                        D(lambda e: e.dma_start(out=out_d[s, r0:(ct + 1) * 1024:8, :], in_=oi[:]), r=[on])

                    XL(0)
                    XL(1)
                    X1(0)
                    X2(0)
                    for it in range(16):
                        if it + 2 < 16:
                            XL(it + 2)
                        if it + 1 < 16:
                            X1(it + 1)
                        Y(it)
                        if it + 1 < 16:
                            X2(it + 1)
                    S.barrier()
                    ck("D")
        except _Stop:
            pass
        S.barrier()
    return nc


_NC_CACHE = {}


def kernel(**inputs):
    f = lambda a: np.ascontiguousarray(np.asarray(a, dtype=np.float32))
    x = f(inputs["x"])
    p = f(inputs["p"])[0]
    shared = {}
    for k in ("mix_norm", "w_in", "q_norm", "k_norm", "lambda_re", "lambda_im", "log_dt", "b_re", "b_im", "c_re", "c_im",
              "d_skip", "w_glu", "b_glu", "w_out", "ple_norm", "w_ple_gate", "w_ple_proj"):
        shared[k] = f(inputs[k])[0]
    if "nc" not in _NC_CACHE:
        _NC_CACHE["nc"] = build_nc()
    nc = _NC_CACHE["nc"]
    in_maps = []
    for c in range(NCORES):
        m = {"x": np.ascontiguousarray(x[2 * c:2 * c + 2]), "p": np.ascontiguousarray(p[2 * c:2 * c + 2])}
        m.update(shared)
        in_maps.append(m)
    res = run_bass_kernel_spmd(nc, in_maps, core_ids=list(range(NCORES)))
    out = np.concatenate([np.asarray(r["out"], dtype=np.float32) for r in res.results], axis=0)
    return out
```
